# Optimizing a Trainium2 kernel written in Bass

```python
import math, functools
import jax, jax.numpy as jnp
from jax import lax
import numpy as np

D_MODEL = 1024
BATCH = 4
SEQ = 4096
DEPTH = 1
DEC_BATCH = 32
DEC_SEQ = 1
PAST_LEN = 8192
PAGE_SIZE = 128

A_HEADS = 4
A_HEAD_DIM = 64
A_QK_DIM = 2 * A_HEAD_DIM
A_V_DIM = 2 * A_HEAD_DIM
A_WIDTH = A_HEADS * A_V_DIM
R_HEAD = 64
R_HEADS = 8
R_WIDTH = R_HEADS * R_HEAD
R_LORA_W = 64
R_LORA_A = 64
R_LORA_G = 128
R_SHIFT_W = 3 * R_WIDTH + R_LORA_W + R_LORA_A + R_LORA_G
D_FF = -(-8 * D_MODEL // (3 * 256)) * 256
OFF_Q = 0
OFF_K = OFF_Q + A_HEADS * A_QK_DIM
OFF_V = OFF_K + A_HEADS * A_QK_DIM
OFF_R = OFF_V + A_WIDTH
OFF_G = OFF_R + R_SHIFT_W
IN_WIDTH = OFF_G + 2 * D_MODEL
Q_BLOCK = 128
NORM_EPS = 1e-6
GN_EPS = 64e-5

kernel_name = "griffin_diffattn_rwkv7_swiglu_step"


def _rms_norm(x, g, eps=NORM_EPS):
    xf = x.astype(jnp.float32)
    y = xf * lax.rsqrt(jnp.mean(xf * xf, axis=-1, keepdims=True) + eps)
    return (y * g.astype(jnp.float32)).astype(x.dtype)


def _alibi_slopes():
    return jnp.exp2(-8.0 * jnp.arange(1, A_HEADS + 1, dtype=jnp.float32) / A_HEADS)


def _diff_attn_core(q, k, v, q_pos, k_pos, lam):
    s = jnp.einsum("bqhcd,bkhcd->bhcqk", q, k, preferred_element_type=jnp.float32) * (A_HEAD_DIM ** -0.5)
    dist = q_pos[:, None] - k_pos[None, :]
    bias = -_alibi_slopes()[:, None, None] * dist.astype(jnp.float32)
    s = jnp.where(dist >= 0, s + bias[None, :, None], -jnp.inf)
    p = jax.nn.softmax(s, axis=-1)
    pd = p[:, :, 0] - lam * p[:, :, 1]
    return jnp.einsum("bhqk,bkhd->bqhd", pd.astype(v.dtype), v)


def _prompt_attend(q, k, v, lam):
    b, s = q.shape[:2]
    nb = s // Q_BLOCK
    qb = jnp.moveaxis(q.reshape(b, nb, Q_BLOCK, A_HEADS, 2, A_HEAD_DIM), 1, 0)
    k_pos = jnp.arange(s)

    def one_block(args):
        q_blk, i = args
        q_pos = i * Q_BLOCK + jnp.arange(Q_BLOCK)
        return _diff_attn_core(q_blk, k, v, q_pos, k_pos, lam)

    o = lax.map(one_block, (qb, jnp.arange(nb)))
    return jnp.moveaxis(o, 0, 1).reshape(b, s, A_HEADS, A_V_DIM)


def _sample_attend(q, k, v, lam, pool_k, pool_v, page_table):
    db, t = q.shape[:2]
    past = page_table.shape[1] * PAGE_SIZE
    past_k = pool_k[page_table].reshape(db, past, A_HEADS, 2, A_HEAD_DIM)
    past_v = pool_v[page_table].reshape(db, past, A_HEADS, A_V_DIM)
    k_all = jnp.concatenate([past_k.astype(k.dtype), k], axis=1)
    v_all = jnp.concatenate([past_v.astype(v.dtype), v], axis=1)
    q_pos = past + jnp.arange(t)
    k_pos = jnp.arange(past + t)
    return _diff_attn_core(q, k_all, v_all, q_pos, k_pos, lam)


def _rwkv7(p_cols, shift_prev, wkv0, p):
    b, t, _ = p_cols.shape
    f32 = jnp.float32
    prev = jnp.concatenate([shift_prev[:, None].astype(p_cols.dtype), p_cols[:, :-1]], axis=1)
    m = (p_cols + (prev - p_cols) * p["shift_mu"].astype(p_cols.dtype)).astype(f32)
    r, k, v, xw, xa, xg = jnp.split(
        m, [R_WIDTH, 2 * R_WIDTH, 3 * R_WIDTH, 3 * R_WIDTH + R_LORA_W,
            3 * R_WIDTH + R_LORA_W + R_LORA_A], axis=-1)
    w_raw = -jax.nn.softplus(-(p["w0"].astype(f32) + jnp.tanh(xw) @ p["w2"].astype(f32))) - 0.5
    decay = jnp.exp(-jnp.exp(w_raw))
    a = jax.nn.sigmoid(p["a0"].astype(f32) + xa @ p["a2"].astype(f32))
    g = jax.nn.sigmoid(xg) @ p["g2"].astype(f32)

    def hs(z):
        return z.reshape(b, t, R_HEADS, R_HEAD)

    kk = hs(k * p["k_k"].astype(f32))
    kk = kk * lax.rsqrt(jnp.maximum(jnp.sum(kk * kk, axis=-1, keepdims=True), 1e-24))
    k = k * (1.0 + (a - 1.0) * p["k_a"].astype(f32))
    r, decay, k, v, a = hs(r), hs(decay), hs(k), hs(v), hs(a)

    def step(S, inp):
        r_t, w_t, k_t, v_t, kk_t, a_t = inp
        sa = jnp.einsum("bhij,bhj->bhi", S, -kk_t)
        S = (S * w_t[:, :, None, :] + sa[..., None] * (kk_t * a_t)[:, :, None, :]
             + v_t[..., None] * k_t[:, :, None, :])
        return S, jnp.einsum("bhij,bhj->bhi", S, r_t)

    seq = tuple(jnp.moveaxis(z, 1, 0) for z in (r, decay, k, v, kk, a))
    S_fin, y = lax.scan(step, wkv0.astype(f32), seq)
    y = jnp.moveaxis(y, 0, 1)
    mu = jnp.mean(y, axis=-1, keepdims=True)
    var = jnp.mean(jnp.square(y - mu), axis=-1, keepdims=True)
    yn = ((y - mu) * lax.rsqrt(var + GN_EPS) * p["gn_w"].astype(f32).reshape(R_HEADS, R_HEAD)
          + p["gn_b"].astype(f32).reshape(R_HEADS, R_HEAD))
    bonus = jnp.sum(r * k * p["r_k"].astype(f32), axis=-1, keepdims=True) * v
    out = ((yn + bonus).reshape(b, t, R_WIDTH) * g).astype(p_cols.dtype) @ p["w_pb"]
    return out, S_fin.astype(wkv0.dtype), p_cols[:, -1].astype(shift_prev.dtype)


def _layer(x, shift_prev, wkv0, attend, l, p):
    b, t, _ = x.shape
    lambda_init = 0.8 - 0.6 * math.exp(-0.3 * l)
    h = _rms_norm(x, p["norm_mix"])
    proj = h @ p["w_in"]
    q = _rms_norm(proj[..., OFF_Q:OFF_K].reshape(b, t, A_HEADS, 2, A_HEAD_DIM), p["q_gain"])
    k = _rms_norm(proj[..., OFF_K:OFF_V].reshape(b, t, A_HEADS, 2, A_HEAD_DIM), p["k_gain"])
    v = proj[..., OFF_V:OFF_R].reshape(b, t, A_HEADS, A_V_DIM)
    f32 = jnp.float32
    lam = (jnp.exp(jnp.sum(p["lambda_q1"].astype(f32) * p["lambda_k1"].astype(f32)))
           - jnp.exp(jnp.sum(p["lambda_q2"].astype(f32) * p["lambda_k2"].astype(f32)))
           + lambda_init)
    o = attend(q, k, v, lam)
    o = _rms_norm(o, p["attn_out_gain"]) * (1.0 - lambda_init)
    y_a = o.reshape(b, t, A_WIDTH).astype(x.dtype) @ p["w_pa"]
    y_b, wkv_new, shift_new = _rwkv7(proj[..., OFF_R:OFF_G], shift_prev, wkv0, p)
    gates = jax.nn.sigmoid(proj[..., OFF_G:].astype(f32))
    merged = (gates[..., :D_MODEL] * y_a.astype(f32) + gates[..., D_MODEL:] * y_b.astype(f32)).astype(x.dtype)
    x = x + merged @ p["w_out"]
    hf = _rms_norm(x, p["norm_ffn"])
    x = x + (jax.nn.silu(hf @ p["w_gate"]) * (hf @ p["w_up"])) @ p["w_down"]
    return x, k.reshape(b, t, A_HEADS, A_QK_DIM), v, wkv_new, shift_new


def setup_inputs(seed: int = 0) -> dict:
    key = jax.random.key(seed)
    ks = iter(jax.random.split(key, 48))
    f32 = jnp.float32

    def nrm(shape, scale):
        return jax.random.normal(next(ks), shape, f32) * scale

    def gain(shape):
        return 1.0 + nrm(shape, 0.02)

    n_pages = PAST_LEN // PAGE_SIZE
    n_pool = -(-5 * DEC_BATCH * n_pages // 4)
    d = {}
    d["x_prompt"] = nrm((BATCH, SEQ, D_MODEL), 1.0)
    d["x_sample"] = nrm((DEC_BATCH, DEC_SEQ, D_MODEL), 1.0)
    d["cache_k"] = nrm((DEPTH, n_pool, PAGE_SIZE, A_HEADS, A_QK_DIM), 1.0)
    d["cache_v"] = nrm((DEPTH, n_pool, PAGE_SIZE, A_HEADS, A_V_DIM), 1.0)
    d["page_table"] = jax.random.permutation(next(ks), n_pool)[:DEC_BATCH * n_pages].reshape(
        DEC_BATCH, n_pages).astype(jnp.int32)
    d["state_wkv"] = nrm((DEPTH, DEC_BATCH, R_HEADS, R_HEAD, R_HEAD), 0.5)
    d["state_shift"] = nrm((DEPTH, DEC_BATCH, R_SHIFT_W), 1.0)
    d["norm_mix"] = gain((DEPTH, D_MODEL))
    d["w_in"] = nrm((DEPTH, D_MODEL, IN_WIDTH), D_MODEL ** -0.5)
    d["q_gain"] = gain((DEPTH, A_HEAD_DIM))
    d["k_gain"] = gain((DEPTH, A_HEAD_DIM))
    d["lambda_q1"] = nrm((DEPTH, A_HEAD_DIM), 0.1)
    d["lambda_k1"] = nrm((DEPTH, A_HEAD_DIM), 0.1)
    d["lambda_q2"] = nrm((DEPTH, A_HEAD_DIM), 0.1)
    d["lambda_k2"] = nrm((DEPTH, A_HEAD_DIM), 0.1)
    d["attn_out_gain"] = gain((DEPTH, A_V_DIM))
    d["w_pa"] = nrm((DEPTH, A_WIDTH, D_MODEL), A_WIDTH ** -0.5)
    d["shift_mu"] = jax.random.uniform(next(ks), (DEPTH, R_SHIFT_W), f32)
    d["w0"] = jax.random.uniform(next(ks), (DEPTH, R_WIDTH), f32, -5.0, 1.0)
    d["w2"] = nrm((DEPTH, R_LORA_W, R_WIDTH), 0.5 * R_LORA_W ** -0.5)
    d["a0"] = nrm((DEPTH, R_WIDTH), 0.1)
    d["a2"] = nrm((DEPTH, R_LORA_A, R_WIDTH), R_LORA_A ** -0.5)
    d["g2"] = nrm((DEPTH, R_LORA_G, R_WIDTH), R_LORA_G ** -0.5)
    d["k_k"] = 0.85 + nrm((DEPTH, R_WIDTH), 0.02)
    d["k_a"] = gain((DEPTH, R_WIDTH))
    d["r_k"] = nrm((DEPTH, R_HEADS, R_HEAD), 0.1)
    d["gn_w"] = gain((DEPTH, R_WIDTH))
    d["gn_b"] = nrm((DEPTH, R_WIDTH), 0.01)
    d["w_pb"] = nrm((DEPTH, R_WIDTH, D_MODEL), R_WIDTH ** -0.5)
    d["w_out"] = nrm((DEPTH, D_MODEL, D_MODEL), D_MODEL ** -0.5)
    d["norm_ffn"] = gain((DEPTH, D_MODEL))
    d["w_gate"] = nrm((DEPTH, D_MODEL, D_FF), D_MODEL ** -0.5)
    d["w_up"] = nrm((DEPTH, D_MODEL, D_FF), D_MODEL ** -0.5)
    d["w_down"] = nrm((DEPTH, D_FF, D_MODEL), D_FF ** -0.5)
    return d


def reference(x_prompt, x_sample, cache_k, cache_v, page_table, state_wkv, state_shift,
              norm_mix, w_in, q_gain, k_gain, lambda_q1, lambda_k1, lambda_q2, lambda_k2,
              attn_out_gain, w_pa, shift_mu, w0, w2, a0, a2, g2, k_k, k_a, r_k, gn_w, gn_b,
              w_pb, w_out, norm_ffn, w_gate, w_up, w_down):
    b = x_prompt.shape[0]
    xp, xs = x_prompt, x_sample
    kp_l, vp_l, wp_l, sp_l = [], [], [], []
    ks_l, vs_l, ws_l, ss_l = [], [], [], []
    for l in range(DEPTH):
        p = {"norm_mix": norm_mix[l], "w_in": w_in[l], "q_gain": q_gain[l], "k_gain": k_gain[l],
             "lambda_q1": lambda_q1[l], "lambda_k1": lambda_k1[l], "lambda_q2": lambda_q2[l],
             "lambda_k2": lambda_k2[l], "attn_out_gain": attn_out_gain[l], "w_pa": w_pa[l],
             "shift_mu": shift_mu[l], "w0": w0[l], "w2": w2[l], "a0": a0[l], "a2": a2[l],
             "g2": g2[l], "k_k": k_k[l], "k_a": k_a[l], "r_k": r_k[l], "gn_w": gn_w[l],
             "gn_b": gn_b[l], "w_pb": w_pb[l], "w_out": w_out[l], "norm_ffn": norm_ffn[l],
             "w_gate": w_gate[l], "w_up": w_up[l], "w_down": w_down[l]}
        shift0 = jnp.zeros((b, R_SHIFT_W), xp.dtype)
        wkv_init = jnp.zeros((b, R_HEADS, R_HEAD, R_HEAD), xp.dtype)
        xp, kp, vp, wp, sp = _layer(xp, shift0, wkv_init, _prompt_attend, l, p)
        attend_s = functools.partial(_sample_attend, pool_k=cache_k[l], pool_v=cache_v[l],
                                     page_table=page_table)
        xs, ks_, vs_, ws_, ss_ = _layer(xs, state_shift[l], state_wkv[l], attend_s, l, p)
        kp_l.append(kp); vp_l.append(vp); wp_l.append(wp); sp_l.append(sp)
        ks_l.append(ks_); vs_l.append(vs_); ws_l.append(ws_); ss_l.append(ss_)
    return (xp, xs, jnp.stack(kp_l), jnp.stack(vp_l), jnp.stack(wp_l), jnp.stack(sp_l),
            jnp.stack(ks_l), jnp.stack(vs_l), jnp.stack(ws_l), jnp.stack(ss_l))
```

```python
import math
import numpy as np
import concourse.bass as bass
import concourse.mybir as mybir
from concourse.bass_utils import run_bass_kernel_spmd

F32 = mybir.dt.float32
BF16 = mybir.dt.bfloat16
I32 = mybir.dt.int32
ALU = mybir.AluOpType
AF = mybir.ActivationFunctionType
AX = mybir.AxisListType

SEM_EPOCH = 30000
NPRE, NOWN, NSMP = 2048, 2048, 256
NCOL = NPRE + NOWN + NSMP
NLOC = NOWN + NSMP
NT = NCOL // 128
D = 1024
DFF = 2816
NEG = -30000.0


class Eng:
    def __init__(self, fw, name, e):
        self.fw = fw; self.name = name; self.e = e
        self.sems = []; self.count = 0; self.epoch = -1; self.known = {}
        self._new_epoch()

    def _new_epoch(self):
        self.epoch += 1
        self.count = 0
        self.sems.append(self.fw.new_sem("%s_e%d" % (self.name, self.epoch)))


class Buf:
    def __init__(self, name, psum=False):
        self.name = name; self.w = None; self.r = []; self.dsem = None; self.dcount = 0; self.psum = psum


class FW:
    def __init__(self, nc):
        self.nc = nc
        self._stack = []
        self._semstack = []
        self.engs = {}
        for name, e in (("pe", nc.tensor), ("act", nc.scalar), ("dve", nc.vector),
                        ("pool", nc.gpsimd), ("sp", nc.sync)):
            self.engs[name] = Eng(self, name, e)
        self.pe = self.engs["pe"]; self.act = self.engs["act"]; self.dve = self.engs["dve"]
        self.pool = self.engs["pool"]; self.sp = self.engs["sp"]
        self.nbuf = 0
        self.out_bufs = []
        self.free_dsems = []

    def new_sem(self, name):
        cm = self.nc.semaphore(name)
        s = cm.__enter__()
        self._semstack.append(cm)
        return s

    def sbuf(self, name, shape, dt):
        cm = self.nc.sbuf_tensor("sb_" + name, list(shape), dt)
        t = cm.__enter__()
        self._stack.append(cm)
        self.nbuf += 1
        return t, Buf(name)

    def psum(self, name, shape, dt):
        cm = self.nc.psum_tensor("ps_" + name, list(shape), dt)
        t = cm.__enter__()
        self._stack.append(cm)
        return t, Buf(name, psum=True)

    def mark(self):
        return len(self._stack)

    def release(self, mark):
        while len(self._stack) > mark:
            self._stack.pop().__exit__(None, None, None)

    def _need(self, eng, stamp, kind):
        if stamp is None:
            return
        if stamp[0] == 'e':
            _, pe_, ep, cnt = stamp
            if pe_ is eng:
                if eng.name == "pe" or kind != "raw":
                    return
            key = (pe_.name, ep)
            if eng.known.get(key, 0) >= cnt:
                return
            eng.e.wait_ge(pe_.sems[ep], cnt)
            eng.known[key] = cnt
        else:
            _, sem, val, key = stamp
            if eng.known.get(key, 0) >= val:
                return
            eng.e.wait_ge(sem, val)
            eng.known[key] = val

    def _deps(self, eng, reads, writes):
        for b in reads:
            self._need(eng, b.w, "raw")
        for b in writes:
            self._need(eng, b.w, "waw")
            for s in b.r:
                self._need(eng, s, "war")

    def _record(self, st, reads, writes):
        for b in reads:
            b.r.append(st)
        for b in writes:
            b.w = st
            b.r = []

    def op(self, eng, fn, reads=(), writes=(), rg=None):
        if eng.name == "pe":
            for b in writes:
                prev = getattr(b, "rg", None)
                if rg is not None and prev is not None and prev != rg and b.w is not None and b.w[0] == 'e' and b.w[1] is eng:
                    _, pe_, ep, cnt = b.w
                    key = (pe_.name, ep)
                    if eng.known.get(key, 0) < cnt:
                        eng.e.wait_ge(pe_.sems[ep], cnt)
                        eng.known[key] = cnt
                b.rg = rg
        if eng.name != "pe":
            px = [b for b in reads if b.psum]
            if px:
                reads = [b for b in reads if not b.psum]
                writes = list(writes) + [b for b in px if b not in writes]
        self._deps(eng, reads, writes)
        if eng.count >= SEM_EPOCH:
            eng._new_epoch()
        ins = fn()
        eng.count += 1
        ins.then_inc(eng.sems[eng.epoch], 1)
        self._record(('e', eng, eng.epoch, eng.count), reads, writes)
        return ins

    def dma(self, eng, out, in_, sb, reads=(), writes=(), is_output=False, fn=None):
        self._deps(eng, reads, writes)
        b = sb
        if b.dsem is None:
            b.dsem = self.new_sem("d_" + b.name)
        ins = eng.e.dma_start(out=out, in_=in_) if fn is None else fn()
        ins.then_inc(b.dsem, 16)
        b.dcount += 16
        self._record(('d', b.dsem, b.dcount, ("dma", id(b))), reads, writes)
        if is_output and b not in self.out_bufs:
            self.out_bufs.append(b)
        return ins

    def finish(self):
        for b in self.out_bufs:
            self.sp.e.wait_ge(b.dsem, b.dcount)

    def barrier_all(self):
        for a in self.engs.values():
            for o in self.engs.values():
                if o is a or o.count == 0:
                    continue
                key = (o.name, o.epoch)
                if a.known.get(key, 0) >= o.count:
                    continue
                a.e.wait_ge(o.sems[o.epoch], o.count)
                a.known[key] = o.count

    def close(self):
        self.release(0)
        while self._semstack:
            self._semstack.pop().__exit__(None, None, None)


class Ring:
    def __init__(self, items):
        self.items = items; self.i = 0

    def next(self):
        it = self.items[self.i % len(self.items)]
        self.i += 1
        return it


def build(stage=99, small=False, dbg=None, ng=8, parts=("rwkv", "attn", "epi", "samp"), nb=None, rl=9, cut=99):
    nc = bass.Bass("TRN2", target_bir_lowering=False)
    V, S, G, T = nc.vector, nc.scalar, nc.gpsimd, nc.tensor

    def din(name, shape, dt=F32):
        return nc.dram_tensor(name, list(shape), dt, kind="ExternalInput").ap()

    def dout(name, shape, dt=F32):
        return nc.dram_tensor(name, list(shape), dt, kind="ExternalOutput").ap()

    xin = din("xin", [NCOL, D])
    w_in = din("w_in", [D, 5376]); w_pa = din("w_pa", [512, D]); w_pb = din("w_pb", [512, D])
    w_out = din("w_out", [D, D]); w_gate = din("w_gate", [D, DFF]); w_up = din("w_up", [D, DFF])
    w_down = din("w_down", [DFF, D])
    wa2_d = din("wa2", [128, 512]); g2_d = din("g2", [128, 512])
    pvec_d = din("pvec", [128, 36]); prow_d = din("prow", [1, 3456])
    kb_d = din("kb", [4, 4, 4096]); qb_d = din("qb", [4, 4, 2048])
    sshift_d = din("sshift", [128, 14, 4]); swkv_d = din("swkv", [4, 4, 128, 64])
    NPG = 128 if small else 2560 * 128
    ck_d = din("ck", [NPG, 512]); cv_d = din("cv", [NPG, 512])
    ptab_d = din("ptab", [1, 256], I32)
    sbias_d = din("sbias", [128, 65 * 8])
    ident_d = din("ident", [128, 128]); mu4_d = din("mu4", [128, 512]); ml4_d = din("ml4", [128, 512])
    mui4_d = din("mui4", [128, 512]); tri_d = din("tri", [128, 128]); colmask_d = din("colmask", [128, 256])
    resetm_d = din("resetm", [128, 512]); cm0_d = din("cm0", [128, 512]); cm1_d = din("cm1", [128, 512])
    bones_d = din("bones", [128, 128]); sel_d = din("sel", [128, 512]); sel0_d = din("sel0", [128, 256])
    iota_d = din("iotap", [128, 1], I32); hmask_d = din("hmask", [8, 4]); e0_d = din("e0", [8, 4]); e1_d = din("e1", [8, 4])

    y_out = dout("y_out", [NLOC, D]); k_out = dout("k_out", [512, NLOC]); v_out = dout("v_out", [NLOC, 512])
    wkv_out = dout("wkv_out", [5, 4, 128, 64]); p_out = dout("p_out", [14, 128, 384])

    dbg_d = dout("dbg", [2, 4, 128, NLOC])
    f = FW(nc)
    pe, act, dve, pool, sp = f.pe, f.act, f.dve, f.pool, f.sp

    def cload(name, src, shape, dt=F32, q=None):
        t, b = f.sbuf(name, shape, dt)
        if dt == F32 or dt == I32:
            f.dma(q or sp, t[:], src, b, writes=[b])
        else:
            f.dma(pool, t[:], src, b, writes=[b])
        return t, b

    ident_f, b_identf = cload("ident_f", ident_d[:, :], [128, 128])
    ident_b, b_identb = cload("ident_b", ident_d[:, :], [128, 128], BF16)
    tri_b, b_tri = cload("tri_b", tri_d[:, :], [128, 128], BF16)
    bones, b_bones = cload("bones", bones_d[:, :], [128, 128])
    pvec, b_pvec = cload("pvec", pvec_d[:, :], [128, 36])
    prow, b_prow = f.sbuf("prow", [128, 3456], F32)
    f.dma(sp, prow[:], prow_d[0:1, :].partition_broadcast(128), b_prow, writes=[b_prow])
    pv2, b_pv2 = f.sbuf("pv2", [128, 24], F32)
    f.op(dve, lambda: V.tensor_scalar(out=pv2[:, 0:14], in0=pvec[:, 0:14], scalar1=-1.0, scalar2=1.0,
                                      op0=ALU.mult, op1=ALU.add), reads=[b_pvec], writes=[b_pv2])
    f.op(dve, lambda: V.tensor_scalar(out=pv2[:, 14:18], in0=pvec[:, 14:18], scalar1=-1.0, scalar2=None,
                                      op0=ALU.mult), reads=[b_pvec], writes=[b_pv2])
    f.op(dve, lambda: V.tensor_scalar(out=pv2[:, 18:22], in0=pvec[:, 26:30], scalar1=-1.0, scalar2=1.0,
                                      op0=ALU.mult, op1=ALU.add), reads=[b_pvec], writes=[b_pv2])
    f.op(dve, lambda: V.tensor_scalar(out=pv2[:, 22:23], in0=pvec[:, 34:35], scalar1=0.125, scalar2=None,
                                      op0=ALU.mult), reads=[b_pvec], writes=[b_pv2])
    MU_C, W0_C, A0_C, KK_C, KA_C, RK_C, QG_C, KG_C = 0, 14, 18, 22, 26, 30, 34, 35
    GMIX, GFFN, GAO, GNW, GNB, LAM = 0, 1024, 2048, 2176, 2688, 3200

    lt, b_lt = f.sbuf("lt", [128, 8], F32)
    junk64, b_junk64 = f.sbuf("junk64", [128, 64], F32)
    f.op(dve, lambda: V.memset(lt[:], 0.0), writes=[b_lt])
    f.op(dve, lambda: V.tensor_tensor(out=junk64[:], in0=prow[:, LAM:LAM + 64], in1=prow[:, LAM + 64:LAM + 128],
                                      op=ALU.mult), reads=[b_prow], writes=[b_junk64])
    f.op(dve, lambda: V.reduce_sum(out=lt[:, 0:1], in_=junk64[:], axis=AX.X), reads=[b_junk64], writes=[b_lt])
    f.op(dve, lambda: V.tensor_tensor(out=junk64[:], in0=prow[:, LAM + 128:LAM + 192], in1=prow[:, LAM + 192:LAM + 256],
                                      op=ALU.mult), reads=[b_prow, b_lt], writes=[b_junk64])
    f.op(dve, lambda: V.reduce_sum(out=lt[:, 1:2], in_=junk64[:], axis=AX.X), reads=[b_junk64], writes=[b_lt])
    f.op(act, lambda: S.activation(out=lt[:, 2:4], in_=lt[:, 0:2], func=AF.Exp), reads=[b_lt], writes=[b_lt])
    lam_init = 0.8 - 0.6 * math.exp(-0.3 * 0)
    f.op(dve, lambda: V.tensor_tensor(out=lt[:, 4:5], in0=lt[:, 3:4], in1=lt[:, 2:3], op=ALU.subtract),
         reads=[b_lt], writes=[b_lt])
    f.op(dve, lambda: V.tensor_scalar(out=lt[:, 5:6], in0=lt[:, 4:5], scalar1=-lam_init, scalar2=None, op0=ALU.add),
         reads=[b_lt], writes=[b_lt])
    NLAM = lt[:, 5:6]

    hT_loc, b_hTloc = f.sbuf("hT_loc", [128, 8, NLOC], BF16)
    oT, b_oT = f.sbuf("oT", [128, 4, NLOC], BF16)
    yrT, b_yrT = f.sbuf("yrT", [128, 4, NLOC], BF16)

    m_pre = f.mark()
    hT_pre, b_hTpre = f.sbuf("hT_pre", [128, 8, NPRE], BF16)

    def hT_cols(c0, n):
        if c0 < NPRE:
            return hT_pre, b_hTpre, c0
        return hT_loc, b_hTloc, c0 - NPRE

    m1 = f.mark()
    xr = Ring([f.sbuf("x%d" % i, [128, D], F32) for i in range(3)])
    xbr = Ring([f.sbuf("xb%d" % i, [128, D], BF16) for i in range(2)])
    junk, b_junk = f.sbuf("junk", [128, D], F32)
    ssr = Ring([f.sbuf("ss%d" % i, [128, 2], F32) for i in range(3)])
    ptr = Ring([f.psum("pt%d" % i, [128, 8, 128], BF16) for i in range(2)])
    for t in range(NT if dbg is None else 0):
        xt, bx = xr.next(); xb, bxb = xbr.next(); ss, bss = ssr.next(); pt, bpt = ptr.next()
        f.dma(sp, xt[:], xin[t * 128:(t + 1) * 128, :], bx, writes=[bx])
        f.op(dve, lambda: V.memset(ss[:], 0.0), writes=[bss])
        f.op(act, lambda: S.activation(out=junk[:], in_=xt[:], func=AF.Square, accum_out=ss[:, 0:1]),
             reads=[bx, bss], writes=[b_junk, bss])
        f.op(dve, lambda: V.tensor_scalar(out=ss[:, 1:2], in0=ss[:, 0:1], scalar1=1.0 / D, scalar2=1e-6,
                                          op0=ALU.mult, op1=ALU.add), reads=[bss], writes=[bss])
        f.op(act, lambda: S.activation(out=ss[:, 1:2], in_=ss[:, 1:2], func=AF.Sqrt), reads=[bss], writes=[bss])
        f.op(dve, lambda: V.reciprocal(out=ss[:, 1:2], in_=ss[:, 1:2]), reads=[bss], writes=[bss])
        f.op(dve, lambda: V.scalar_tensor_tensor(out=xb[:], in0=xt[:], scalar=ss[:, 1:2], in1=prow[:, GMIX:GMIX + D],
                                                 op0=ALU.mult, op1=ALU.mult), reads=[bx, bss, b_prow], writes=[bxb])
        for c in range(8):
            f.op(pe, lambda c=c: T.transpose(out=pt[:, c, :], in_=xb[:, c * 128:(c + 1) * 128], identity=ident_b[:]),
                 reads=[bxb, b_identb], writes=[bpt])
        ht, bht, lc = hT_cols(t * 128, 128)
        f.op(act, lambda: S.copy(out=ht[:, :, lc:lc + 128], in_=pt[:]), reads=[bpt], writes=[bht])
    f.release(m1)
    f.barrier_all()

    def wload(dst, bdst, src):
        f.dma(pool, dst, src, bdst, writes=[bdst])


    def epilogue():
        mE = f.mark()
        wpa, b_wpa = f.sbuf("wpa", [128, 4, D], BF16)
        wpb, b_wpb = f.sbuf("wpb", [128, 4, D], BF16)
        wo, b_wo = f.sbuf("wo", [128, 8, D], BF16)
        wload(wpa[:], b_wpa, w_pa.rearrange("(c p) n -> p c n", p=128))
        wload(wpb[:], b_wpb, w_pb.rearrange("(c p) n -> p c n", p=128))
        wload(wo[:], b_wo, w_out.rearrange("(c p) n -> p c n", p=128))
        SB = 384
        bank = [f.psum("bank%d" % i, [128, 512], F32) for i in range(8)]
        wgr = Ring([f.sbuf("wg%d" % i, [128, 8, 128], BF16) for i in range(4)])
        wdr = Ring([f.sbuf("wd%d" % i, [128, D], BF16) for i in range(2)])
        mT, b_mT = f.sbuf("mT", [128, 8, SB], BF16)
        hfT, b_hfT = f.sbuf("hfT", [128, 8, SB], BF16)
        x1, b_x1 = f.sbuf("x1e", [128, 3, D], F32)
        xr2 = Ring([f.sbuf("xe%d" % i, [128, D], F32) for i in range(2)])
        sga, b_sga = f.sbuf("sga", [128, SB], F32)
        sgb, b_sgb = f.sbuf("sgb", [128, SB], F32)
        tA, b_tA = f.sbuf("tA", [128, SB], F32)
        hfb, b_hfb = f.sbuf("hfb", [128, D], BF16)
        ejunk, b_ejunk = f.sbuf("ejunk", [128, D], F32)
        est, b_est = f.sbuf("est", [128, 4], F32)
        actr = Ring([f.sbuf("act%d" % i, [128, SB], BF16) for i in range(2)])
        yor = Ring([f.sbuf("yo%d" % i, [128, 512], F32) for i in range(2)])
        for sbi in range(NLOC // SB):
            c0 = sbi * SB
            for ch in range(8):
                cs = slice(ch * 128, (ch + 1) * 128)
                (pa, bpa), (pb_, bpb), (pga, bpga), (pgb, bpgb) = bank[0], bank[1], bank[2], bank[3]
                wga, bwga = wgr.next(); wgb, bwgb = wgr.next()
                wload(wga[:], bwga, w_in[:, 3328 + ch * 128:3328 + (ch + 1) * 128].rearrange("(c p) n -> p c n", p=128))
                wload(wgb[:], bwgb, w_in[:, 4352 + ch * 128:4352 + (ch + 1) * 128].rearrange("(c p) n -> p c n", p=128))
                for h in range(4):
                    f.op(pe, lambda h=h: T.matmul(out=pa[:, 0:SB], lhsT=wpa[:, h, cs], rhs=oT[:, h, c0:c0 + SB], start=(h == 0), stop=(h == 3)),
                         reads=[b_wpa, b_oT], writes=[bpa])
                for h in range(4):
                    f.op(pe, lambda h=h: T.matmul(out=pb_[:, 0:SB], lhsT=wpb[:, h, cs], rhs=yrT[:, h, c0:c0 + SB], start=(h == 0), stop=(h == 3)),
                         reads=[b_wpb, b_yrT], writes=[bpb])
                for c in range(8):
                    f.op(pe, lambda c=c: T.matmul(out=pga[:, 0:SB], lhsT=wga[:, c, :], rhs=hT_loc[:, c, c0:c0 + SB], start=(c == 0), stop=(c == 7)),
                         reads=[bwga, b_hTloc], writes=[bpga])
                for c in range(8):
                    f.op(pe, lambda c=c: T.matmul(out=pgb[:, 0:SB], lhsT=wgb[:, c, :], rhs=hT_loc[:, c, c0:c0 + SB], start=(c == 0), stop=(c == 7)),
                         reads=[bwgb, b_hTloc], writes=[bpgb])
                f.op(act, lambda: S.activation(out=sga[:], in_=pga[:, 0:SB], func=AF.Sigmoid), reads=[bpga], writes=[b_sga])
                f.op(act, lambda: S.activation(out=sgb[:], in_=pgb[:, 0:SB], func=AF.Sigmoid), reads=[bpgb], writes=[b_sgb])
                f.op(dve, lambda: V.tensor_tensor(out=tA[:], in0=pa[:, 0:SB], in1=sga[:], op=ALU.mult), reads=[bpa, b_sga], writes=[b_tA])
                f.op(dve, lambda: V.tensor_tensor(out=sgb[:], in0=pb_[:, 0:SB], in1=sgb[:], op=ALU.mult), reads=[bpb, b_sgb], writes=[b_sgb])
                f.op(pool, lambda: G.tensor_tensor(out=mT[:, ch, :], in0=tA[:], in1=sgb[:], op=ALU.add), reads=[b_tA, b_sgb], writes=[b_mT])
            for tl in range(3):
                ts_ = slice(tl * 128, (tl + 1) * 128)
                xt, bxt = xr2.next()
                row0 = NPRE + c0 + tl * 128
                f.dma(sp, xt[:], xin[row0:row0 + 128, :], bxt, writes=[bxt])
                for hf_ in range(2):
                    px, bpx = bank[4 + hf_]
                    for k in range(8):
                        f.op(pe, lambda k=k: T.matmul(out=px[:], lhsT=mT[:, k, ts_], rhs=wo[:, k, hf_ * 512:(hf_ + 1) * 512],
                                                      start=(k == 0), stop=(k == 7)), reads=[b_mT, b_wo], writes=[bpx])
                    f.op(dve, lambda: V.tensor_tensor(out=x1[:, tl, hf_ * 512:(hf_ + 1) * 512], in0=px[:], in1=xt[:, hf_ * 512:(hf_ + 1) * 512],
                                                      op=ALU.add), reads=[bpx, bxt], writes=[b_x1])
                f.op(dve, lambda: V.memset(est[:, 0:1], 0.0), writes=[b_est])
                f.op(act, lambda: S.activation(out=ejunk[:], in_=x1[:, tl, :], func=AF.Square, accum_out=est[:, 0:1]),
                     reads=[b_x1, b_est], writes=[b_ejunk, b_est])
                f.op(dve, lambda: V.tensor_scalar(out=est[:, 1:2], in0=est[:, 0:1], scalar1=1.0 / D, scalar2=1e-6, op0=ALU.mult, op1=ALU.add),
                     reads=[b_est], writes=[b_est])
                f.op(act, lambda: S.activation(out=est[:, 1:2], in_=est[:, 1:2], func=AF.Sqrt), reads=[b_est], writes=[b_est])
                f.op(dve, lambda: V.reciprocal(out=est[:, 1:2], in_=est[:, 1:2]), reads=[b_est], writes=[b_est])
                f.op(dve, lambda: V.scalar_tensor_tensor(out=hfb[:], in0=x1[:, tl, :], scalar=est[:, 1:2], in1=prow[:, GFFN:GFFN + D],
                                                         op0=ALU.mult, op1=ALU.mult), reads=[b_x1, b_est, b_prow], writes=[b_hfb])
                ptr_, bptr = bank[6]
                ptb = ptr_[:].bitcast(BF16)
                for c in range(8):
                    f.op(pe, lambda c=c: T.transpose(out=ptb[:, c * 128:(c + 1) * 128], in_=hfb[:, c * 128:(c + 1) * 128], identity=ident_b[:]),
                         reads=[b_hfb, b_identb], writes=[bptr])
                f.op(act, lambda: S.copy(out=hfT[:, :, ts_], in_=ptb.rearrange("p (c n) -> p c n", c=8)), reads=[bptr], writes=[b_hfT])
            for ffc in range(DFF // 128):
                fs = slice(ffc * 128, (ffc + 1) * 128)
                wg, bwg = wgr.next(); wu, bwu = wgr.next(); wd, bwd = wdr.next()
                wload(wg[:], bwg, w_gate[:, fs].rearrange("(c p) n -> p c n", p=128))
                wload(wu[:], bwu, w_up[:, fs].rearrange("(c p) n -> p c n", p=128))
                wload(wd[:], bwd, w_down[fs, :])
                (pg, bpg), (pu, bpu) = bank[6], bank[7]
                for c in range(8):
                    f.op(pe, lambda c=c: T.matmul(out=pg[:, 0:SB], lhsT=wg[:, c, :], rhs=hfT[:, c, :], start=(c == 0), stop=(c == 7)),
                         reads=[bwg, b_hfT], writes=[bpg])
                for c in range(8):
                    f.op(pe, lambda c=c: T.matmul(out=pu[:, 0:SB], lhsT=wu[:, c, :], rhs=hfT[:, c, :], start=(c == 0), stop=(c == 7)),
                         reads=[bwu, b_hfT], writes=[bpu])
                f.op(act, lambda: S.activation(out=sga[:], in_=pg[:, 0:SB], func=AF.Silu), reads=[bpg], writes=[b_sga])
                at, bat = actr.next()
                f.op(dve, lambda: V.tensor_tensor(out=at[:], in0=pu[:, 0:SB], in1=sga[:], op=ALU.mult), reads=[bpu, b_sga], writes=[bat])
                for tl in range(3):
                    for hf_ in range(2):
                        pd, bpd = bank[tl * 2 + hf_]
                        f.op(pe, lambda: T.matmul(out=pd[:], lhsT=at[:, tl * 128:(tl + 1) * 128], rhs=wd[:, hf_ * 512:(hf_ + 1) * 512],
                                                  start=(ffc == 0), stop=(ffc == DFF // 128 - 1)), reads=[bat, bwd], writes=[bpd])
            for tl in range(3):
                for hf_ in range(2):
                    pd, bpd = bank[tl * 2 + hf_]
                    yo, byo = yor.next()
                    f.op(dve, lambda: V.tensor_tensor(out=yo[:], in0=pd[:], in1=x1[:, tl, hf_ * 512:(hf_ + 1) * 512], op=ALU.add),
                         reads=[bpd, b_x1], writes=[byo])
                    r0 = c0 + tl * 128
                    f.dma(sp, y_out[r0:r0 + 128, hf_ * 512:(hf_ + 1) * 512], yo[:], byo, reads=[byo], is_output=True)
        f.barrier_all()
        f.release(mE)

    def rwkv_phase():
        mR = f.mark()
        BW = 256
        NB = NCOL // BW
        mu4b, b_mu4 = cload("mu4b", mu4_d[:, :], [128, 512], BF16)
        ml4b, b_ml4 = cload("ml4b", ml4_d[:, :], [128, 512], BF16)
        mui4b, b_mui4 = cload("mui4b", mui4_d[:, :], [128, 512], BF16)
        ident4, b_ident4 = f.sbuf("ident4", [128, 512], BF16)
        for i in range(4):
            f.dma(pool, ident4[:, i * 128:(i + 1) * 128], ident_d[:, :], b_ident4, writes=[b_ident4])
        resetm, b_resetm = cload("resetm", resetm_d[:, 0:BW], [128, BW])
        cm0, b_cm0 = cload("cm0", cm0_d[:, 0:BW], [128, BW])
        cm1, b_cm1 = cload("cm1", cm1_d[:, 0:BW], [128, BW])
        colmask, b_colmask = cload("colmask", colmask_d[:, :], [128, 256])
        wa2b, b_wa2 = cload("wa2b", wa2_d[:, :], [128, 512], BF16)
        g2b, b_g2 = cload("g2b", g2_d[:, :], [128, 512], BF16)
        sshift, b_sshift = cload("sshift", sshift_d[:, :, :], [128, 14, 4])
        cst, b_cst = f.sbuf("cst", [128, 4], F32)
        f.op(dve, lambda: V.memset(cst[:, 0:1], 1.0), writes=[b_cst])
        f.op(dve, lambda: V.memset(cst[:, 1:2], -0.5), writes=[b_cst])
        f.op(dve, lambda: V.memset(cst[:, 2:3], 64e-5), writes=[b_cst])
        carry, b_carry = f.sbuf("carry", [128, 14], F32)
        f.op(dve, lambda: V.memset(carry[:], 0.0), writes=[b_carry])
        H, b_H = f.sbuf("H", [128, 4, 64], F32)
        f.op(dve, lambda: V.memset(H[:], 0.0), writes=[b_H])
        Hb = [f.sbuf("Hb%d" % i, [128, 4, 64], BF16) for i in range(2)]
        wring = Ring([f.sbuf("wr%d" % i, [128, 8, 128], BF16) for i in range(2)])
        ppr = Ring([f.psum("pp%d" % i, [128, 512], F32) for i in range(3)])
        pxr = Ring([f.sbuf("px%d" % i, [128, BW + 1], F32) for i in range(2)])
        rT, b_rT = f.sbuf("rT", [128, 4, BW], F32)
        kT, b_kT = f.sbuf("kT", [128, 4, BW], F32)
        vT, b_vT = f.sbuf("vT", [128, 4, BW], F32)
        m12, b_m12 = f.sbuf("m12", [128, BW], F32)
        m13, b_m13 = f.sbuf("m13", [128, BW], F32)
        twxa, b_twxa = f.sbuf("twxa", [128, BW], BF16)
        sg, b_sg = f.sbuf("sg", [128, BW], BF16)
        scr = [f.sbuf("scr%d" % i, [128, BW], F32) for i in range(8)]
        AT, b_AT = f.sbuf("AT", [128, 4, BW], BF16)
        BT, b_BT = f.sbuf("BT", [128, 4, BW], BF16)
        KT, b_KT2 = f.sbuf("KT", [128, 4, BW], BF16)
        RT, b_RT = f.sbuf("RT", [128, 4, BW], BF16)
        RT0, b_RT0 = f.sbuf("RT0", [128, 4, BW], BF16)
        RT1, b_RT1 = f.sbuf("RT1", [128, 4, BW], BF16)
        Eneg, b_Eneg = f.sbuf("Eneg", [128, 4, BW], F32)
        bonT, b_bonT = f.sbuf("bonT", [128, 4, BW], F32)
        gT, b_gT = f.sbuf("gT", [128, 4, BW], F32)
        Vtm, b_Vtm = f.sbuf("Vtm", [128, 2, 512], BF16)
        Btm, b_Btm = f.sbuf("Btm", [128, 2, 512], BF16)
        Ktm, b_Ktm = f.sbuf("Ktm", [128, 2, 512], BF16)
        ptBK, b_ptBK = f.psum("ptBK", [128, 2, 512], BF16)
        ptV, b_ptV = f.psum("ptV", [128, 512], F32)
        psA, b_psA = f.psum("psA", [128, 4, 128], F32)
        psB, b_psB = f.psum("psB", [128, 4, 128], F32)
        psC, b_psC = f.psum("psC", [128, 4, 128], F32)
        Lk = [f.sbuf("Lk%d" % i, [128, 4, 128], BF16) for i in range(2)]
        Nk = [f.sbuf("Nk%d" % i, [128, 4, 128], BF16) for i in range(2)]
        Yk = [f.sbuf("Yk%d" % i, [128, 4, 128], BF16) for i in range(2)]
        AKT, b_AKT = f.sbuf("AKT", [128, 4, 128], BF16)
        RBT, b_RBT = f.sbuf("RBT", [128, 4, 128], BF16)
        RKT, b_RKT = f.sbuf("RKT", [128, 4, 128], BF16)
        P1, b_P1 = f.sbuf("P1", [128, 4, 64], F32)
        Wb, b_Wb = f.sbuf("Wb", [128, 4, 64], BF16)
        Ub, b_Ub = f.sbuf("Ub", [128, 4, 64], BF16)
        Yall, b_Yall = f.sbuf("Yall", [128, 8, 64], F32)
        gst, b_gst = f.sbuf("gst", [128, 40], F32)
        ytmp2, b_ytmp = f.sbuf("ytmp", [128, 512], F32)
        ytmp = ytmp2[:].rearrange("p (a b) -> p a b", a=4)
        Ysq = ytmp2[:].rearrange("p (a b) -> p a b", a=8)
        b_Ysq = b_ytmp

        def flat(t3):
            return t3[:].rearrange("p a b -> p (a b)")

        for bi in (range(NB) if nb is None else nb):
            col0 = bi * BW
            is_loc = col0 >= NPRE
            is_smp = col0 >= NPRE + NOWN
            ht, bht, lc = hT_cols(col0, BW)
            for ch in range(14):
                wt, bwt = wring.next()
                wload(wt[:], bwt, w_in[:, 1536 + ch * 128:1536 + (ch + 1) * 128].rearrange("(c p) n -> p c n", p=128))
                pp, bpp = ppr.next()
                for c in range(8):
                    f.op(pe, lambda c=c: T.matmul(out=pp[:, 0:BW], lhsT=wt[:, c, :], rhs=ht[:, c, lc:lc + BW],
                                                  start=(c == 0), stop=(c == 7)), reads=[bwt, bht], writes=[bpp])
                px, bpx = pxr.next()
                f.op(dve, lambda: V.tensor_copy(out=px[:, 0:1], in_=carry[:, ch:ch + 1]), reads=[b_carry], writes=[bpx])
                f.op(act, lambda: S.copy(out=px[:, 1:BW + 1], in_=pp[:, 0:BW]), reads=[bpp], writes=[bpx])
                f.op(dve, lambda: V.tensor_copy(out=carry[:, ch:ch + 1], in_=px[:, BW:BW + 1]), reads=[bpx], writes=[b_carry])
                if bi == 15:
                    f.dma(sp, p_out[ch, :, 0:128], px[:, 129:257], bpx, reads=[bpx], is_output=True)
                if is_smp:
                    f.dma(sp, p_out[ch, :, 128:384], px[:, 1:257], bpx, reads=[bpx], is_output=True)
                    f.op(dve, lambda: V.tensor_copy(out=px[:, 1:BW + 1:64], in_=sshift[:, ch, :]),
                         reads=[b_sshift], writes=[bpx])
                tmp, btmp = scr[0]
                if ch < 4:
                    dst, bdst = rT[:, ch, :], b_rT
                elif ch < 8:
                    dst, bdst = kT[:, ch - 4, :], b_kT
                elif ch < 12:
                    dst, bdst = vT[:, ch - 8, :], b_vT
                elif ch == 12:
                    dst, bdst = m12[:], b_m12
                else:
                    dst, bdst = m13[:], b_m13
                f.op(pool, lambda: G.tensor_scalar(out=tmp[:], in0=px[:, 0:BW], scalar1=pvec[:, MU_C + ch:MU_C + ch + 1], scalar2=None,
                                                   op0=ALU.mult), reads=[bpx, b_pvec], writes=[btmp])
                f.op(dve, lambda: V.scalar_tensor_tensor(out=dst, in0=px[:, 1:BW + 1], scalar=pv2[:, ch:ch + 1], in1=tmp[:],
                                                         op0=ALU.mult, op1=ALU.add), reads=[bpx, btmp, b_pv2], writes=[bdst])
                if is_smp:
                    f.op(dve, lambda: V.tensor_tensor(out=dst, in0=dst, in1=colmask[:], op=ALU.mult),
                         reads=[bdst, b_colmask], writes=[bdst])
            if rl < 2:
                continue
            f.op(act, lambda: S.activation(out=twxa[0:64, :], in_=m12[0:64, :], func=AF.Tanh), reads=[b_m12], writes=[b_twxa])
            f.op(dve, lambda: V.tensor_copy(out=twxa[64:128, :], in_=m12[64:128, :]), reads=[b_m12], writes=[b_twxa])
            f.op(act, lambda: S.activation(out=sg[:], in_=m13[:], func=AF.Sigmoid), reads=[b_m13], writes=[b_sg])
            for pr in range(4):
                pc = slice(pr * 128, (pr + 1) * 128)
                (s1, bs1), (s2, bs2), (s3, bs3), (s4, bs4), (s5, bs5), (s6, bs6), (s7, bs7) = scr[1:8]
                pp, bpp = ppr.next()
                f.op(pe, lambda: T.matmul(out=pp[:, 0:BW], lhsT=wa2b[0:64, pc], rhs=twxa[0:64, :], start=True, stop=True),
                     reads=[b_wa2, b_twxa], writes=[bpp])
                f.op(act, lambda: S.activation(out=s1[:], in_=pp[:, 0:BW], func=AF.Exp, bias=pv2[:, 14 + pr:15 + pr], scale=-1.0),
                     reads=[bpp, b_pv2], writes=[bs1])
                f.op(act, lambda: S.activation(out=s1[:], in_=s1[:], func=AF.Ln, bias=cst[:, 0:1], scale=1.0),
                     reads=[bs1, b_cst], writes=[bs1])
                f.op(act, lambda: S.activation(out=s2[:], in_=s1[:], func=AF.Exp, bias=cst[:, 1:2], scale=-1.0),
                     reads=[bs1, b_cst], writes=[bs2])
                if is_smp:
                    f.op(dve, lambda: V.tensor_tensor(out=s2[:], in0=s2[:], in1=colmask[:], op=ALU.mult),
                         reads=[bs2, b_colmask], writes=[bs2])
                f.op(dve, lambda: V.tensor_tensor_scan(out=s3[:], data0=resetm[:], data1=s2[:], initial=0.0,
                                                       op0=ALU.mult, op1=ALU.add), reads=[bs2, b_resetm], writes=[bs3])
                f.op(act, lambda: S.activation(out=Eneg[:, pr, :], in_=s3[:], func=AF.Exp, scale=-1.0), reads=[bs3], writes=[b_Eneg])
                f.op(act, lambda: S.activation(out=s4[:], in_=s3[:], func=AF.Exp), reads=[bs3], writes=[bs4])
                f.op(dve, lambda: V.tensor_tensor(out=s5[:], in0=s3[:], in1=s2[:], op=ALU.subtract), reads=[bs3, bs2], writes=[bs5])
                f.op(act, lambda: S.activation(out=s5[:], in_=s5[:], func=AF.Exp, scale=-1.0), reads=[bs5], writes=[bs5])
                pp, bpp = ppr.next()
                f.op(pe, lambda: T.matmul(out=pp[:, 0:BW], lhsT=wa2b[64:128, pc], rhs=twxa[64:128, :], start=True, stop=True),
                     reads=[b_wa2, b_twxa], writes=[bpp])
                f.op(act, lambda: S.activation(out=s6[:], in_=pp[:, 0:BW], func=AF.Sigmoid, bias=pvec[:, A0_C + pr:A0_C + pr + 1], scale=1.0),
                     reads=[bpp, b_pvec], writes=[bs6])
                f.op(dve, lambda: V.tensor_scalar(out=s1[:], in0=kT[:, pr, :], scalar1=pvec[:, KK_C + pr:KK_C + pr + 1], scalar2=None,
                                                  op0=ALU.mult), reads=[b_kT, b_pvec], writes=[bs1])
                f.op(act, lambda: S.activation(out=s7[:], in_=s1[:], func=AF.Square), reads=[bs1], writes=[bs7])
                pp, bpp = ppr.next()
                f.op(pe, lambda: T.matmul(out=pp[:, 0:BW], lhsT=bones[:], rhs=s7[:], start=True, stop=True),
                     reads=[b_bones, bs7], writes=[bpp])
                f.op(dve, lambda: V.tensor_scalar(out=s7[:], in0=pp[:, 0:BW], scalar1=1e-24, scalar2=None, op0=ALU.max),
                     reads=[bpp], writes=[bs7])
                f.op(act, lambda: S.activation(out=s7[:], in_=s7[:], func=AF.Sqrt), reads=[bs7], writes=[bs7])
                f.op(dve, lambda: V.reciprocal(out=s7[:], in_=s7[:]), reads=[bs7], writes=[bs7])
                f.op(dve, lambda: V.tensor_tensor(out=s1[:], in0=s1[:], in1=s7[:], op=ALU.mult), reads=[bs1, bs7], writes=[bs1])
                f.op(dve, lambda: V.tensor_scalar(out=s7[:], in0=s6[:], scalar1=pvec[:, KA_C + pr:KA_C + pr + 1],
                                                  scalar2=pv2[:, 18 + pr:19 + pr], op0=ALU.mult, op1=ALU.add),
                     reads=[bs6, b_pvec, b_pv2], writes=[bs7])
                f.op(dve, lambda: V.tensor_tensor(out=s7[:], in0=s7[:], in1=kT[:, pr, :], op=ALU.mult), reads=[bs7, b_kT], writes=[bs7])
                f.op(dve, lambda: V.scalar_tensor_tensor(out=AT[:, pr, :], in0=s1[:], scalar=-1.0, in1=s5[:], op0=ALU.mult, op1=ALU.mult),
                     reads=[bs1, bs5], writes=[b_AT])
                f.op(pool, lambda: G.tensor_tensor(out=s1[:], in0=s1[:], in1=s6[:], op=ALU.mult), reads=[bs1, bs6], writes=[bs1])
                f.op(pool, lambda: G.tensor_tensor(out=BT[:, pr, :], in0=s1[:], in1=s4[:], op=ALU.mult), reads=[bs1, bs4], writes=[b_BT])
                f.op(pool, lambda: G.tensor_tensor(out=KT[:, pr, :], in0=s7[:], in1=s4[:], op=ALU.mult), reads=[bs7, bs4], writes=[b_KT2])
                f.op(dve, lambda: V.tensor_tensor(out=RT[:, pr, :], in0=rT[:, pr, :], in1=Eneg[:, pr, :], op=ALU.mult),
                     reads=[b_rT, b_Eneg], writes=[b_RT])
                f.op(pool, lambda: G.tensor_tensor(out=RT0[:, pr, :], in0=RT[:, pr, :], in1=cm0[:], op=ALU.mult),
                     reads=[b_RT, b_cm0], writes=[b_RT0])
                f.op(pool, lambda: G.tensor_tensor(out=RT1[:, pr, :], in0=RT[:, pr, :], in1=cm1[:], op=ALU.mult),
                     reads=[b_RT, b_cm1], writes=[b_RT1])
                if is_loc:
                    f.op(dve, lambda: V.scalar_tensor_tensor(out=s1[:], in0=rT[:, pr, :], scalar=pvec[:, RK_C + pr:RK_C + pr + 1], in1=s7[:],
                                                             op0=ALU.mult, op1=ALU.mult), reads=[b_rT, bs7, b_pvec], writes=[bs1])
                    pp, bpp = ppr.next()
                    f.op(pe, lambda: T.matmul(out=pp[:, 0:BW], lhsT=bones[:], rhs=s1[:], start=True, stop=True),
                         reads=[b_bones, bs1], writes=[bpp])
                    f.op(dve, lambda: V.tensor_tensor(out=bonT[:, pr, :], in0=pp[:, 0:BW], in1=vT[:, pr, :], op=ALU.mult),
                         reads=[bpp, b_vT], writes=[b_bonT])
                    pp, bpp = ppr.next()
                    f.op(pe, lambda: T.matmul(out=pp[:, 0:BW], lhsT=g2b[:, pc], rhs=sg[:], start=True, stop=True),
                         reads=[b_g2, b_sg], writes=[bpp])
                    f.op(act, lambda: S.copy(out=gT[:, pr, :], in_=pp[:, 0:BW]), reads=[bpp], writes=[b_gT])
            for tl in range(2 if rl >= 3 else 0):
                Tg = bi * 2 + tl
                tc = slice(tl * 128, (tl + 1) * 128)
                for pr in range(4):
                    pc = slice(pr * 128, (pr + 1) * 128)
                    f.op(pe, lambda: T.transpose(out=ptBK[:, 0, pc], in_=BT[:, pr, tc], identity=ident_b[:]),
                         reads=[b_BT, b_identb], writes=[b_ptBK])
                    f.op(pe, lambda: T.transpose(out=ptBK[:, 1, pc], in_=KT[:, pr, tc], identity=ident_b[:]),
                         reads=[b_KT2, b_identb], writes=[b_ptBK])
                    f.op(pe, lambda: T.transpose(out=ptV[:, pc], in_=vT[:, pr, tc], identity=ident_f[:]),
                         reads=[b_vT, b_identf], writes=[b_ptV])
                f.op(act, lambda: S.copy(out=Btm[:, tl, :], in_=ptBK[:, 0, :]), reads=[b_ptBK], writes=[b_Btm])
                f.op(dve, lambda: V.tensor_copy(out=Ktm[:, tl, :], in_=ptBK[:, 1, :]), reads=[b_ptBK], writes=[b_Ktm])
                f.op(act, lambda: S.copy(out=Vtm[:, tl, :], in_=ptV[:]), reads=[b_ptV], writes=[b_Vtm])
                for Gi in range(2 if rl >= 4 else 0):
                    heads = [(hi, 4 * Gi + hi, (4 * Gi + hi) // 2, slice(64 * ((4 * Gi + hi) % 2), 64 * ((4 * Gi + hi) % 2) + 64))
                             for hi in range(4)]
                    for hi, h, pr, rows in heads:
                        f.op(pe, lambda: T.matmul(out=psA[:, hi, :], lhsT=BT[rows, pr, tc], rhs=AT[rows, pr, tc], start=True, stop=True),
                             reads=[b_BT, b_AT], writes=[b_psA], rg=rows.start)
                    for hi, h, pr, rows in heads:
                        f.op(pe, lambda: T.matmul(out=psB[:, hi, :], lhsT=AT[rows, pr, tc], rhs=BT[rows, pr, tc], start=True, stop=True),
                             reads=[b_BT, b_AT], writes=[b_psB], rg=rows.start)
                    (N0, bN0), (N1, bN1) = Nk
                    (L0, bL0), (L1, bL1) = Lk
                    (Y0, bY0), (Y1, bY1) = Yk
                    f.op(dve, lambda: V.tensor_tensor(out=flat(N0), in0=flat(psA), in1=mu4b[:], op=ALU.mult),
                         reads=[b_psA, b_mu4], writes=[bN0])
                    f.op(dve, lambda: V.tensor_tensor(out=flat(L0), in0=flat(psB), in1=ml4b[:], op=ALU.mult),
                         reads=[b_psB, b_ml4], writes=[bL0])
                    if cut <= 1:
                        continue
                    f.op(pool, lambda: G.tensor_tensor(out=flat(Y0), in0=flat(N0), in1=ident4[:], op=ALU.add),
                         reads=[bN0, b_ident4], writes=[bY0])
                    if cut <= 2:
                        continue
                    for k in range(min(5, cut - 2)):
                        Nc, bNc = Nk[k % 2]; Lc, bLc = Lk[k % 2]; Yc, bYc = Yk[k % 2]
                        Nn, bNn = Nk[(k + 1) % 2]; Ln, bLn = Lk[(k + 1) % 2]; Yn, bYn = Yk[(k + 1) % 2]
                        for hi, h, pr, rows in heads:
                            f.op(pe, lambda: T.matmul(out=psA[:, hi, :], lhsT=Nc[:, hi, :], rhs=Lc[:, hi, :], start=True, stop=True),
                                 reads=[bNc, bLc], writes=[b_psA])
                        if k < 4:
                            for hi, h, pr, rows in heads:
                                f.op(pe, lambda: T.matmul(out=psB[:, hi, :], lhsT=Lc[:, hi, :], rhs=Nc[:, hi, :], start=True, stop=True),
                                     reads=[bNc, bLc], writes=[b_psB])
                        f.op(act, lambda: S.copy(out=flat(Ln), in_=flat(psA)), reads=[b_psA], writes=[bLn])
                        if k < 4:
                            f.op(dve, lambda: V.tensor_copy(out=flat(Nn), in_=flat(psB)), reads=[b_psB], writes=[bNn])
                        for hi, h, pr, rows in heads:
                            f.op(pe, lambda: T.matmul(out=psC[:, hi, :], lhsT=ident_b[:], rhs=Yc[:, hi, :], start=True, stop=False),
                                 reads=[b_identb, bYc], writes=[b_psC])
                            f.op(pe, lambda: T.matmul(out=psC[:, hi, :], lhsT=Ln[:, hi, :], rhs=Yc[:, hi, :], start=False, stop=True),
                                 reads=[bLn, bYc], writes=[b_psC])
                        if k % 2 == 0:
                            f.op(dve, lambda: V.tensor_copy(out=flat(Yn), in_=flat(psC)), reads=[b_psC], writes=[bYn])
                        else:
                            f.op(act, lambda: S.copy(out=flat(Yn), in_=flat(psC)), reads=[b_psC], writes=[bYn])
                    TT, bTT = Yk[1]
                    if cut <= 7:
                        continue
                    for hi, h, pr, rows in heads:
                        f.op(pe, lambda: T.matmul(out=psA[:, hi, :], lhsT=KT[rows, pr, tc], rhs=AT[rows, pr, tc], start=True, stop=True),
                             reads=[b_KT2, b_AT], writes=[b_psA], rg=rows.start)
                    for hi, h, pr, rows in heads:
                        f.op(pe, lambda: T.matmul(out=psB[:, hi, :], lhsT=BT[rows, pr, tc], rhs=RT[rows, pr, tc], start=True, stop=True),
                             reads=[b_BT, b_RT], writes=[b_psB], rg=rows.start)
                    for hi, h, pr, rows in heads:
                        f.op(pe, lambda: T.matmul(out=psC[:, hi, :], lhsT=KT[rows, pr, tc], rhs=RT[rows, pr, tc], start=True, stop=True),
                             reads=[b_KT2, b_RT], writes=[b_psC], rg=rows.start)
                    f.op(dve, lambda: V.tensor_tensor(out=flat(AKT), in0=flat(psA), in1=mu4b[:], op=ALU.mult),
                         reads=[b_psA, b_mu4], writes=[b_AKT])
                    f.op(dve, lambda: V.tensor_tensor(out=flat(RBT), in0=flat(psB), in1=mui4b[:], op=ALU.mult),
                         reads=[b_psB, b_mui4], writes=[b_RBT])
                    f.op(dve, lambda: V.tensor_tensor(out=flat(RKT), in0=flat(psC), in1=mui4b[:], op=ALU.mult),
                         reads=[b_psC, b_mui4], writes=[b_RKT])
                    for hi, h, pr, rows in heads:
                        f.op(pe, lambda: T.matmul(out=psC[:, hi, 0:64], lhsT=AKT[:, hi, :], rhs=Vtm[:, tl, h * 64:(h + 1) * 64],
                                                  start=True, stop=True), reads=[b_AKT, b_Vtm], writes=[b_psC])
                    f.op(act, lambda: S.copy(out=P1[:], in_=psC[:, :, 0:64]), reads=[b_psC], writes=[b_P1])
                    gp = slice(2 * Gi, 2 * Gi + 2)
                    for c in range(2 if rl >= 5 else 0):
                        crow = slice(64 * c, 64 * c + 64)
                        smp = (Tg - 32) * 2 + c
                        if is_smp:
                            f.dma(sp, H[:, gp, :], swkv_d[smp, gp, :, :].rearrange("a p i -> p a i"), b_H, writes=[b_H])
                        hb, bhb = Hb[c]
                        f.op(pool, lambda: G.tensor_copy(out=hb[:, gp, :], in_=H[:, gp, :]), reads=[b_H], writes=[bhb])
                        for hi, h, pr, rows in heads:
                            f.op(pe, lambda: T.matmul(out=psA[:, hi, 0:64], lhsT=AT[rows, pr, tc], rhs=hb[rows, pr, :], start=True, stop=True),
                                 reads=[b_AT, bhb], writes=[b_psA], rg=rows.start)
                        f.op(dve, lambda: V.tensor_tensor(out=Wb[crow, :, :], in0=psA[crow, :, 0:64], in1=P1[crow, :, :], op=ALU.add),
                             reads=[b_psA, b_P1], writes=[b_Wb])
                        for hi, h, pr, rows in heads:
                            f.op(pe, lambda: T.matmul(out=psA[:, hi, 64:128], lhsT=TT[crow, hi, :], rhs=Wb[crow, hi, :], start=True, stop=True),
                                 reads=[bTT, b_Wb], writes=[b_psA], rg=crow.start)
                        f.op(act, lambda: S.copy(out=Ub[crow, :, :], in_=psA[crow, :, 64:128]), reads=[b_psA], writes=[b_Ub])
                        for hi, h, pr, rows in heads:
                            pcs = slice(pr * 128, (pr + 1) * 128)
                            f.op(pe, lambda: T.matmul(out=psB[:, hi, 0:64], lhsT=Btm[crow, tl, pcs], rhs=Ub[crow, hi, :], start=True, stop=False),
                                 reads=[b_Btm, b_Ub], writes=[b_psB], rg=crow.start)
                            f.op(pe, lambda: T.matmul(out=psB[:, hi, 0:64], lhsT=Ktm[crow, tl, pcs], rhs=Vtm[crow, tl, h * 64:(h + 1) * 64],
                                                      start=False, stop=True), reads=[b_Ktm, b_Vtm], writes=[b_psB], rg=crow.start)
                        for hi, h, pr, rows in heads:
                            gcol = Eneg[rows, pr, tl * 128 + 64 * c + 63:tl * 128 + 64 * c + 64]
                            f.op(pool, lambda: G.tensor_scalar(out=H[rows, pr, :], in0=H[rows, pr, :], scalar1=gcol, scalar2=None, op0=ALU.mult),
                                 reads=[b_H, b_Eneg], writes=[b_H])
                            f.op(dve, lambda: V.scalar_tensor_tensor(out=H[rows, pr, :], in0=psB[rows, hi, 0:64], scalar=gcol, in1=H[rows, pr, :],
                                                                     op0=ALU.mult, op1=ALU.add), reads=[b_psB, b_H, b_Eneg], writes=[b_H])
                        if is_smp:
                            f.dma(sp, wkv_out[1 + smp, gp, :, :].rearrange("a p i -> p a i"), H[:, gp, :], b_H, reads=[b_H], is_output=True)
                        elif Tg == 31 and c == 1:
                            f.dma(sp, wkv_out[0, gp, :, :].rearrange("a p i -> p a i"), H[:, gp, :], b_H, reads=[b_H], is_output=True)
                    if is_loc and rl >= 6:
                        (hb0, bhb0), (hb1, bhb1) = Hb
                        for hi, h, pr, rows in heads:
                            f.op(pe, lambda: T.matmul(out=psB[:, hi, 64:128], lhsT=RT0[rows, pr, tc], rhs=hb0[rows, pr, :], start=True, stop=False),
                                 reads=[b_RT0, bhb0], writes=[b_psB], rg=rows.start)
                            f.op(pe, lambda: T.matmul(out=psB[:, hi, 64:128], lhsT=RT1[rows, pr, tc], rhs=hb1[rows, pr, :], start=False, stop=False),
                                 reads=[b_RT1, bhb1], writes=[b_psB], rg=rows.start)
                            f.op(pe, lambda: T.matmul(out=psB[:, hi, 64:128], lhsT=RBT[:, hi, :], rhs=Ub[:, hi, :], start=False, stop=False),
                                 reads=[b_RBT, b_Ub], writes=[b_psB])
                            f.op(pe, lambda: T.matmul(out=psB[:, hi, 64:128], lhsT=RKT[:, hi, :], rhs=Vtm[:, tl, h * 64:(h + 1) * 64],
                                                      start=False, stop=True), reads=[b_RKT, b_Vtm], writes=[b_psB])
                        f.op(act, lambda: S.copy(out=Yall[:, 4 * Gi:4 * Gi + 4, :], in_=psB[:, :, 64:128]), reads=[b_psB], writes=[b_Yall])
                if is_loc and rl >= 7:
                    lcol = Tg * 128 - NPRE
                    f.op(dve, lambda: V.reduce_sum(out=gst[:, 0:8], in_=Yall[:], axis=AX.X), reads=[b_Yall], writes=[b_gst])
                    f.op(act, lambda: S.activation(out=Ysq, in_=Yall[:], func=AF.Square), reads=[b_Yall], writes=[b_Ysq])
                    f.op(dve, lambda: V.reduce_sum(out=gst[:, 8:16], in_=Ysq, axis=AX.X), reads=[b_Ysq, b_gst], writes=[b_gst])
                    f.op(dve, lambda: V.tensor_scalar(out=gst[:, 0:16], in0=gst[:, 0:16], scalar1=1.0 / 64, scalar2=None, op0=ALU.mult),
                         reads=[b_gst], writes=[b_gst])
                    f.op(dve, lambda: V.tensor_tensor(out=gst[:, 16:24], in0=gst[:, 0:8], in1=gst[:, 0:8], op=ALU.mult),
                         reads=[b_gst], writes=[b_gst])
                    f.op(dve, lambda: V.tensor_tensor(out=gst[:, 16:24], in0=gst[:, 8:16], in1=gst[:, 16:24], op=ALU.subtract),
                         reads=[b_gst], writes=[b_gst])
                    f.op(dve, lambda: V.tensor_scalar(out=gst[:, 16:24], in0=gst[:, 16:24], scalar1=64e-5, scalar2=None, op0=ALU.add),
                         reads=[b_gst], writes=[b_gst])
                    f.op(act, lambda: S.activation(out=gst[:, 16:24], in_=gst[:, 16:24], func=AF.Sqrt), reads=[b_gst], writes=[b_gst])
                    f.op(dve, lambda: V.reciprocal(out=gst[:, 24:32], in_=gst[:, 16:24]), reads=[b_gst], writes=[b_gst])
                    f.op(dve, lambda: V.tensor_tensor(out=Yall[:], in0=Yall[:], in1=gst[:, 0:8].unsqueeze(2).to_broadcast([128, 8, 64]),
                                                      op=ALU.subtract), reads=[b_Yall, b_gst], writes=[b_Yall])
                    f.op(dve, lambda: V.tensor_tensor(out=Yall[:], in0=Yall[:], in1=gst[:, 24:32].unsqueeze(2).to_broadcast([128, 8, 64]),
                                                      op=ALU.mult), reads=[b_Yall, b_gst], writes=[b_Yall])
                    yf = Yall[:].rearrange("p a b -> p (a b)")
                    f.op(dve, lambda: V.tensor_tensor(out=yf, in0=yf, in1=prow[:, GNW:GNW + 512], op=ALU.mult),
                         reads=[b_Yall, b_prow], writes=[b_Yall])
                    f.op(pool, lambda: G.tensor_tensor(out=yf, in0=yf, in1=prow[:, GNB:GNB + 512], op=ALU.add),
                         reads=[b_Yall, b_prow], writes=[b_Yall])
                    for pr in range(4):
                        f.op(pe, lambda: T.transpose(out=ptV[:, pr * 128:(pr + 1) * 128], in_=yf[:, pr * 128:(pr + 1) * 128], identity=ident_f[:]),
                             reads=[b_Yall, b_identf], writes=[b_ptV])
                    f.op(dve, lambda: V.tensor_tensor(out=ytmp, in0=ptV[:].rearrange("p (a b) -> p a b", a=4), in1=bonT[:, :, tc], op=ALU.add),
                         reads=[b_ptV, b_bonT], writes=[b_ytmp])
                    f.op(dve, lambda: V.tensor_tensor(out=yrT[:, :, lcol:lcol + 128], in0=ytmp, in1=gT[:, :, tc], op=ALU.mult),
                         reads=[b_ytmp, b_gT], writes=[b_yrT])
        f.barrier_all()
        f.release(mR)

    if "rwkv" in parts:
        rwkv_phase()

    if stage >= 1.5 and "attn" in parts:
        m2 = f.mark()
        wv, b_wv = f.sbuf("wv", [128, 8, 512], BF16)
        wload(wv[:], b_wv, w_in[:, 1024:1536].rearrange("(c p) n -> p c n", p=128))
        Vh, b_Vh = f.sbuf("Vh", [128, NT, 129], BF16)
        f.op(pool, lambda: G.memset(Vh[:], 1.0), writes=[b_Vh])
        pvr = Ring([f.psum("pv%d" % i, [128, 512], F32) for i in range(1)])
        vor = Ring([f.sbuf("vo%d" % i, [128, 128], F32) for i in range(2)])

        def v_project(h):
            for t in range(NT):
                pv, bpv = pvr.next()
                ht, bht, lc = hT_cols(t * 128, 128)
                for c in range(8):
                    f.op(pe, lambda c=c: T.matmul(out=pv[:, 0:128], lhsT=ht[:, c, lc:lc + 128], rhs=wv[:, c, h * 128:(h + 1) * 128],
                                                  start=(c == 0), stop=(c == 7)), reads=[bht, b_wv], writes=[bpv])
                f.op(act, lambda: S.copy(out=Vh[:, t, 0:128], in_=pv[:, 0:128]), reads=[bpv], writes=[b_Vh])
                if t >= 16:
                    vo, bvo = vor.next()
                    f.op(dve, lambda: V.tensor_copy(out=vo[:], in_=pv[:, 0:128]), reads=[bpv], writes=[bvo])
                    f.dma(sp, v_out[(t - 16) * 128:(t - 15) * 128, h * 128:(h + 1) * 128], vo[:], bvo, reads=[bvo], is_output=True)

        KT_, b_KT = f.sbuf("KhT", [68, 2, NCOL], BF16)
        QT_, b_QT = f.sbuf("QhT", [68, 2, NLOC], BF16)
        wq, b_wq = f.sbuf("wq", [128, 8, 128], BF16)
        wk, b_wk = f.sbuf("wk", [128, 8, 128], BF16)
        pqr = Ring([f.psum("pq%d" % i, [64, 512], F32) for i in range(1)])
        psq, b_psq = f.psum("psq", [64, 512], F32)
        sqt, b_sqt = f.sbuf("sqt", [64, 512], F32)
        rnt, b_rnt = f.sbuf("rnt", [64, 512], F32)
        kor = Ring([f.sbuf("ko%d" % i, [64, 512], F32) for i in range(1)])
        pS_r = Ring([f.psum("pS%d" % i, [128, 2, 256], F32) for i in range(2)])
        pO, b_pO = f.psum("pO", [128, 2, 2, 256], F32)
        PTr = Ring([f.sbuf("PT%d" % i, [128, 2, 256], BF16) for i in range(2)])
        osb, b_osb = f.sbuf("osb", [128, 128], F32)
        osb2, b_osb2 = f.sbuf("osb2", [128, 128], F32)
        obf, b_obf = f.sbuf("obf", [128, 128], BF16)
        ost, b_ost = f.sbuf("ost", [128, 8], F32)
        pTo, b_pTo = f.psum("pTo", [128, 1024], BF16)

        def qk_project(wt, bwt, ncols, dstT, bdst, gcol, is_k):
            nb = (ncols + 511) // 512
            for bi in range(nb):
                c0 = bi * 512
                n = min(512, ncols - c0)
                gc0 = c0 if is_k else c0 + NPRE
                ht, bht, lc = hT_cols(gc0, n)
                for cmp_ in range(2):
                    pq, bpq = pqr.next()
                    for c in range(8):
                        f.op(pe, lambda c=c: T.matmul(out=pq[:, 0:n], lhsT=wt[:, c, cmp_ * 64:(cmp_ + 1) * 64],
                                                      rhs=ht[:, c, lc:lc + n], start=(c == 0), stop=(c == 7)),
                             reads=[bht, bwt], writes=[bpq])
                    f.op(act, lambda: S.activation(out=sqt[:, 0:n], in_=pq[:, 0:n], func=AF.Square),
                         reads=[bpq], writes=[b_sqt])
                    f.op(pe, lambda: T.matmul(out=psq[:, 0:n], lhsT=bones[0:64, 0:64], rhs=sqt[:, 0:n], start=True, stop=True),
                         reads=[b_sqt, b_bones], writes=[b_psq])
                    f.op(dve, lambda: V.tensor_scalar(out=rnt[:, 0:n], in0=psq[:, 0:n], scalar1=1.0 / 64, scalar2=1e-6,
                                                      op0=ALU.mult, op1=ALU.add), reads=[b_psq], writes=[b_rnt])
                    f.op(act, lambda: S.activation(out=rnt[:, 0:n], in_=rnt[:, 0:n], func=AF.Sqrt), reads=[b_rnt], writes=[b_rnt])
                    f.op(dve, lambda: V.reciprocal(out=rnt[:, 0:n], in_=rnt[:, 0:n]), reads=[b_rnt], writes=[b_rnt])
                    f.op(dve, lambda: V.scalar_tensor_tensor(out=dstT[0:64, cmp_, c0:c0 + n], in0=pq[:, 0:n], scalar=gcol,
                                                             in1=rnt[:, 0:n], op0=ALU.mult, op1=ALU.mult),
                         reads=[bpq, b_rnt, b_pvec, b_pv2], writes=[bdst])
                    if is_k and gc0 >= NPRE:
                        ko, bko = kor.next()
                        f.op(pool if False else dve, lambda: V.scalar_tensor_tensor(out=ko[:, 0:n], in0=pq[:, 0:n], scalar=gcol,
                                                                 in1=rnt[:, 0:n], op0=ALU.mult, op1=ALU.mult),
                             reads=[bpq, b_rnt, b_pvec], writes=[bko])
                        r0 = cur_h[0] * 128 + cmp_ * 64
                        f.dma(sp, k_out[r0:r0 + 64, gc0 - NPRE:gc0 - NPRE + n], ko[:, 0:n], bko, reads=[bko], is_output=True)

        qs_tm, b_qs = f.sbuf("qs_tm", [128, 2, 512], BF16)
        ks_tm, b_ks = f.sbuf("ks_tm", [128, 2, 512], BF16)
        vs_tm, b_vs = f.sbuf("vs_tm", [128, 2, 512], BF16)
        cur_h = [0]
        for h in range((4 if dbg is None else 1) if stage >= 1.6 else 0):
            cur_h[0] = h
            wload(wq[:], b_wq, w_in[:, h * 128:(h + 1) * 128].rearrange("(c p) n -> p c n", p=128))
            wload(wk[:], b_wk, w_in[:, 512 + h * 128:512 + (h + 1) * 128].rearrange("(c p) n -> p c n", p=128))
            for cmp_ in range(2):
                f.dma(pool, KT_[64:68, cmp_, 0:4096], kb_d[h, :, :], b_KT, writes=[b_KT])
                f.dma(pool, QT_[64:68, cmp_, 0:2048], qb_d[h, :, :], b_QT, writes=[b_QT])
            if dbg == "attn":
                f.op(dve, lambda: V.memset(KT_[:], 0.125), writes=[b_KT])
                f.op(dve, lambda: V.memset(QT_[:], 0.125), writes=[b_QT])
            elif stage >= 2.0:
                v_project(h)
            if stage >= 2.1 and dbg is None:
                qk_project(wk, b_wk, NCOL, KT_, b_KT, pvec[0:64, KG_C:KG_C + 1], True)
                qk_project(wq, b_wq, NLOC, QT_, b_QT, pv2[0:64, 22:23], False)
            if "samp" in parts:
                for tl in range(2):
                    for cmp_ in range(2):
                        f.op(pe, lambda: T.transpose(out=pTo[:, 0:64], in_=QT_[0:64, cmp_, 2048 + tl * 128:2048 + (tl + 1) * 128], identity=ident_b[0:64, 0:64]),
                             reads=[b_QT, b_identb], writes=[b_pTo])
                        f.op(act, lambda: S.copy(out=qs_tm[:, tl, h * 128 + cmp_ * 64:h * 128 + (cmp_ + 1) * 64], in_=pTo[:, 0:64]),
                             reads=[b_pTo], writes=[b_qs])
                        f.op(pe, lambda: T.transpose(out=pTo[:, 0:64], in_=KT_[0:64, cmp_, 4096 + tl * 128:4096 + (tl + 1) * 128], identity=ident_b[0:64, 0:64]),
                             reads=[b_KT, b_identb], writes=[b_pTo])
                        f.op(act, lambda: S.copy(out=ks_tm[:, tl, h * 128 + cmp_ * 64:h * 128 + (cmp_ + 1) * 64], in_=pTo[:, 0:64]),
                             reads=[b_pTo], writes=[b_ks])
                    f.op(pool, lambda: G.tensor_copy(out=vs_tm[:, tl, h * 128:(h + 1) * 128], in_=Vh[:, 32 + tl, 0:128]),
                         reads=[b_Vh], writes=[b_vs])
            if stage < 2.2:
                continue
            for g in range(ng):
                nkt = 16 + 2 * g + 2
                for kt in range(nkt):
                    d0 = (kt == 16 + 2 * g)
                    d1 = (kt == 16 + 2 * g + 1)
                    q0 = 128 if d1 else 0
                    pS, bpS = pS_r.next(); PT, bPT = PTr.next()
                    for cmp_ in range(2):
                        f.op(pe, lambda cmp_=cmp_: T.matmul(out=pS[:, cmp_, q0:256], lhsT=KT_[:, cmp_, kt * 128:(kt + 1) * 128],
                                                            rhs=QT_[:, cmp_, g * 256 + q0:(g + 1) * 256], start=True, stop=True),
                             reads=[b_KT, b_QT], writes=[bpS])
                    f.op(act, lambda: S.activation(out=PT[:, :, q0:256], in_=pS[:, :, q0:256], func=AF.Exp),
                         reads=[bpS], writes=[bPT])
                    if d0 or d1:
                        for cmp_ in range(2):
                            f.op(pool, lambda cmp_=cmp_: G.tensor_tensor(out=PT[:, cmp_, q0:q0 + 128], in0=PT[:, cmp_, q0:q0 + 128],
                                                                         in1=tri_b[:], op=ALU.mult),
                                 reads=[bPT, b_tri], writes=[bPT])
                    for cmp_ in range(2):
                        for sub in range(2):
                            if d1 and sub == 0:
                                continue
                            last = (kt == 16 + 2 * g + sub)
                            f.op(pe, lambda cmp_=cmp_, sub=sub, last=last: T.matmul(
                                out=pO[:, cmp_, sub, 0:129], lhsT=PT[:, cmp_, sub * 128:(sub + 1) * 128], rhs=Vh[:, kt, :],
                                start=(kt == 0), stop=last), reads=[bPT, b_Vh], writes=[b_pO])
                for sub in range(2):
                    lcq = g * 256 + sub * 128
                    f.op(dve, lambda: V.reciprocal(out=ost[:, 0:2], in_=pO[:, :, sub, 128]), reads=[b_pO], writes=[b_ost])
                    f.op(dve, lambda: V.tensor_tensor(out=ost[:, 2:3], in0=ost[:, 1:2], in1=NLAM, op=ALU.mult),
                         reads=[b_ost, b_lt], writes=[b_ost])
                    f.op(dve, lambda: V.tensor_scalar(out=osb[:], in0=pO[:, 0, sub, 0:128], scalar1=ost[:, 0:1], scalar2=None,
                                                      op0=ALU.mult), reads=[b_pO, b_ost], writes=[b_osb])
                    f.op(dve, lambda: V.scalar_tensor_tensor(out=osb[:], in0=pO[:, 1, sub, 0:128], scalar=ost[:, 2:3], in1=osb[:],
                                                             op0=ALU.mult, op1=ALU.add), reads=[b_pO, b_ost, b_osb], writes=[b_osb])
                    finalize_o(f, nc, osb, b_osb, osb2, b_osb2, obf, b_obf, ost, b_ost, prow, b_prow, GAO, lam_init,
                               pTo, b_pTo, ident_b, b_identb, oT, b_oT, h, lcq)

        if "samp" in parts:
            selb, b_selb = cload("selb", sel_d[:, :], [128, 512], BF16)
            sel0b, b_sel0b = cload("sel0b", sel0_d[:, :], [128, 256], BF16)
            sbias, b_sbias = cload("sbias", sbias_d[:, :], [128, 520])
            hmask, b_hmask = cload("hmask", hmask_d[:, :], [8, 4])
            e0t, b_e0 = cload("e0t", e0_d[:, :], [8, 4])
            e1t, b_e1 = cload("e1t", e1_d[:, :], [8, 4])
            iop, b_iop = cload("iop", iota_d[:, :], [128, 1], I32)
            pti, b_pti = f.sbuf("pti", [128, 256], I32)
            f.dma(sp, pti[:], ptab_d[0:1, :].partition_broadcast(128), b_pti, writes=[b_pti])
            ptf, b_ptf = f.sbuf("ptf", [128, 256], F32)
            iof, b_iof = f.sbuf("iof", [128, 1], F32)
            idx, b_idx = f.sbuf("idx", [128, 256], I32)
            f.op(dve, lambda: V.tensor_copy(out=ptf[:], in_=pti[:]), reads=[b_pti], writes=[b_ptf])
            f.op(dve, lambda: V.tensor_copy(out=iof[:], in_=iop[:]), reads=[b_iop], writes=[b_iof])
            f.op(dve, lambda: V.tensor_scalar(out=ptf[:], in0=ptf[:], scalar1=128.0, scalar2=iof[:, 0:1], op0=ALU.mult, op1=ALU.add),
                 reads=[b_ptf, b_iof], writes=[b_ptf])
            f.op(dve, lambda: V.tensor_copy(out=idx[:], in_=ptf[:]), reads=[b_ptf], writes=[b_idx])
            cmb, b_cmb = f.sbuf("cmb", [8, 4], F32)
            f.op(dve, lambda: V.scalar_tensor_tensor(out=cmb[:], in0=e1t[:], scalar=lt[0:8, 5:6], in1=e0t[:], op0=ALU.mult, op1=ALU.add),
                 reads=[b_e1, b_e0, b_lt], writes=[b_cmb])
            onesc, b_onesc = f.sbuf("onesc", [128, 1], F32)
            f.op(dve, lambda: V.memset(onesc[:], 1.0), writes=[b_onesc])
            Ktr = Ring([f.sbuf("Kt%d" % i, [128, 512], F32) for i in range(2)])
            Vtr = Ring([f.sbuf("Vt%d" % i, [128, 512], F32) for i in range(2)])
            prodr = Ring([f.sbuf("prod%d" % i, [128, 512], F32) for i in range(1)])
            qbc, b_qbc = f.sbuf("qbc", [128, 512], F32)
            spg_r = Ring([f.sbuf("spg%d" % i, [128, 16], F32) for i in range(3)])
            osm, b_osm = f.sbuf("osm", [8, 512], F32)
            osel, b_osel = f.sbuf("osel", [8, 128], F32)
            ofin, b_ofin = f.sbuf("ofin", [4, 128], F32)
            ofin2, b_ofin2 = f.sbuf("ofin2", [4, 128], F32)
            ofb, b_ofb = f.sbuf("ofb", [4, 128], BF16)
            sst, b_sst = f.sbuf("sst", [8, 8], F32)
            pS0, bpS0 = pS_r.items[0]
            pS0f = pS0[:].rearrange("p a b -> p (a b)")
            pso = pO[0:8, 0, :, :].rearrange("p a b -> p (a b)")
            psz = pO[0:8, 1, 0, 0:1]
            pvr0, bpvr0 = pvr.items[0]
            for s in range(4):
                tl = s // 2
                col = 2048 + 128 * tl + 64 * (s % 2) + 1
                f.op(pe, lambda: T.matmul(out=pS0f, lhsT=selb[:, s * 128:(s + 1) * 128], rhs=qs_tm[:, tl, :], start=True, stop=True),
                     reads=[b_selb, b_qs], writes=[bpS0])
                f.op(act, lambda: S.copy(out=qbc[:], in_=pS0f), reads=[bpS0], writes=[b_qbc])
                for pg in range(65):
                    Kt, bKt = Ktr.next(); Vt, bVt = Vtr.next(); prod, bprod = prodr.next(); spg, bspg = spg_r.next()
                    if pg < 64:
                        ic = s * 64 + pg
                        f.dma(pool, None, None, bKt, reads=[b_idx], writes=[bKt],
                              fn=lambda: G.indirect_dma_start(out=Kt[:, :], out_offset=None, in_=ck_d[:, :],
                                                              in_offset=bass.IndirectOffsetOnAxis(ap=idx[:, ic:ic + 1], axis=0)))
                        f.dma(pool, None, None, bVt, reads=[b_idx], writes=[bVt],
                              fn=lambda: G.indirect_dma_start(out=Vt[:, :], out_offset=None, in_=cv_d[:, :],
                                                              in_offset=bass.IndirectOffsetOnAxis(ap=idx[:, ic:ic + 1], axis=0)))
                    else:
                        f.op(pe, lambda: T.matmul(out=pS0f, lhsT=sel0b[:, (s % 2) * 128:(s % 2 + 1) * 128], rhs=ks_tm[:, tl, :], start=True, stop=True),
                             reads=[b_sel0b, b_ks], writes=[bpS0])
                        f.op(act, lambda: S.copy(out=Kt[:], in_=pS0f), reads=[bpS0], writes=[bKt])
                        f.op(pe, lambda: T.matmul(out=pS0f, lhsT=sel0b[:, (s % 2) * 128:(s % 2 + 1) * 128], rhs=vs_tm[:, tl, :], start=True, stop=True),
                             reads=[b_sel0b, b_vs], writes=[bpS0])
                        f.op(act, lambda: S.copy(out=Vt[:], in_=pS0f), reads=[bpS0], writes=[bVt])
                    f.op(pool, lambda: G.tensor_tensor(out=prod[:], in0=Kt[:], in1=qbc[:], op=ALU.mult), reads=[bKt, b_qbc], writes=[bprod])
                    f.op(dve, lambda: V.reduce_sum(out=spg[:, 0:8], in_=prod[:].rearrange("p (g d) -> p g d", g=8), axis=AX.X),
                         reads=[bprod], writes=[bspg])
                    f.op(dve, lambda: V.tensor_tensor(out=spg[:, 0:8], in0=spg[:, 0:8], in1=sbias[:, pg * 8:(pg + 1) * 8], op=ALU.add),
                         reads=[bspg, b_sbias], writes=[bspg])
                    f.op(act, lambda: S.activation(out=spg[:, 8:16], in_=spg[:, 0:8], func=AF.Exp), reads=[bspg], writes=[bspg])
                    f.op(pe, lambda: T.matmul(out=pso, lhsT=spg[:, 8:16], rhs=Vt[:], start=(pg == 0), stop=(pg == 64)),
                         reads=[bspg, bVt], writes=[b_pO])
                    f.op(pe, lambda: T.matmul(out=psz, lhsT=spg[:, 8:16], rhs=onesc[:], start=(pg == 0), stop=(pg == 64)),
                         reads=[bspg, b_onesc], writes=[b_pO])
                f.op(dve, lambda: V.reciprocal(out=sst[:, 0:1], in_=psz), reads=[b_pO], writes=[b_sst])
                f.op(dve, lambda: V.tensor_scalar(out=osm[:], in0=pso, scalar1=sst[:, 0:1], scalar2=None, op0=ALU.mult),
                     reads=[b_pO, b_sst], writes=[b_osm])
                f.op(dve, lambda: V.tensor_tensor(out=osm[:].rearrange("p (h d) -> p h d", h=4), in0=osm[:].rearrange("p (h d) -> p h d", h=4),
                                                  in1=hmask[:].unsqueeze(2).to_broadcast([8, 4, 128]), op=ALU.mult),
                     reads=[b_osm, b_hmask], writes=[b_osm])
                f.op(dve, lambda: V.reduce_sum(out=osel[:], in_=osm[:].rearrange("p (h d) -> p d h", h=4), axis=AX.X),
                     reads=[b_osm], writes=[b_osel])
                f.op(pe, lambda: T.matmul(out=pvr0[0:4, 0:128], lhsT=cmb[:], rhs=osel[:], start=True, stop=True),
                     reads=[b_cmb, b_osel], writes=[bpvr0])
                f.op(dve, lambda: V.tensor_copy(out=ofin[:], in_=pvr0[0:4, 0:128]), reads=[bpvr0], writes=[b_ofin])
                f.op(dve, lambda: V.memset(sst[0:4, 1:2], 0.0), writes=[b_sst])
                f.op(act, lambda: S.activation(out=ofin2[:], in_=ofin[:], func=AF.Square, accum_out=sst[0:4, 1:2]),
                     reads=[b_ofin, b_sst], writes=[b_ofin2, b_sst])
                f.op(dve, lambda: V.tensor_scalar(out=sst[0:4, 2:3], in0=sst[0:4, 1:2], scalar1=1.0 / 128, scalar2=1e-6, op0=ALU.mult, op1=ALU.add),
                     reads=[b_sst], writes=[b_sst])
                f.op(act, lambda: S.activation(out=sst[0:4, 2:3], in_=sst[0:4, 2:3], func=AF.Sqrt), reads=[b_sst], writes=[b_sst])
                f.op(dve, lambda: V.reciprocal(out=sst[0:4, 2:3], in_=sst[0:4, 2:3]), reads=[b_sst], writes=[b_sst])
                f.op(dve, lambda: V.tensor_scalar(out=sst[0:4, 3:4], in0=sst[0:4, 2:3], scalar1=(1.0 - lam_init), scalar2=None, op0=ALU.mult),
                     reads=[b_sst], writes=[b_sst])
                f.op(dve, lambda: V.scalar_tensor_tensor(out=ofb[:], in0=ofin[:], scalar=sst[0:4, 3:4], in1=prow[0:4, GAO:GAO + 128],
                                                         op0=ALU.mult, op1=ALU.mult), reads=[b_ofin, b_sst, b_prow], writes=[b_ofb])
                f.op(pe, lambda: T.transpose(out=pTo[:, 0:4], in_=ofb[:], identity=ident_b[0:4, 0:4]), reads=[b_ofb, b_identb], writes=[b_pTo])
                f.op(act, lambda: S.copy(out=oT[:, :, col], in_=pTo[:, 0:4]), reads=[b_pTo], writes=[b_oT])
        f.release(m2)
        f.barrier_all()

    f.release(m_pre)
    m_pre = f.mark()
    if "epi" in parts:
        epilogue()
    f.dma(pool, dbg_d[0].rearrange("a p n -> p a n"), oT[:], b_oT, reads=[b_oT], is_output=True)
    f.dma(pool, dbg_d[1].rearrange("a p n -> p a n"), yrT[:], b_yrT, reads=[b_yrT], is_output=True)
    f.release(m_pre)
    f.finish()
    f.close()
    return nc


def finalize_o(f, nc, osb, b_osb, osb2, b_osb2, obf, b_obf, ost, b_ost, prow, b_prow, GAO, lam_init,
               pTo, b_pTo, ident_b, b_identb, oT, b_oT, h, lcq):
    V, S, T = nc.vector, nc.scalar, nc.tensor
    dve, act, pe = f.dve, f.act, f.pe
    f.op(dve, lambda: V.memset(ost[:, 4:5], 0.0), writes=[b_ost])
    f.op(act, lambda: S.activation(out=osb2[:], in_=osb[:], func=AF.Square, accum_out=ost[:, 4:5]),
         reads=[b_osb, b_ost], writes=[b_osb2, b_ost])
    f.op(dve, lambda: V.tensor_scalar(out=ost[:, 5:6], in0=ost[:, 4:5], scalar1=1.0 / 128, scalar2=1e-6,
                                      op0=ALU.mult, op1=ALU.add), reads=[b_ost], writes=[b_ost])
    f.op(act, lambda: S.activation(out=ost[:, 5:6], in_=ost[:, 5:6], func=AF.Sqrt), reads=[b_ost], writes=[b_ost])
    f.op(dve, lambda: V.reciprocal(out=ost[:, 5:6], in_=ost[:, 5:6]), reads=[b_ost], writes=[b_ost])
    f.op(dve, lambda: V.tensor_scalar(out=ost[:, 6:7], in0=ost[:, 5:6], scalar1=(1.0 - lam_init), scalar2=None,
                                      op0=ALU.mult), reads=[b_ost], writes=[b_ost])
    f.op(dve, lambda: V.scalar_tensor_tensor(out=obf[:], in0=osb[:], scalar=ost[:, 6:7], in1=prow[:, GAO:GAO + 128],
                                             op0=ALU.mult, op1=ALU.mult), reads=[b_osb, b_ost, b_prow], writes=[b_obf])
    f.op(pe, lambda: T.transpose(out=pTo[:, 0:128], in_=obf[:], identity=ident_b[:]), reads=[b_obf, b_identb], writes=[b_pTo])
    f.op(act, lambda: S.copy(out=oT[:, h, lcq:lcq + 128], in_=pTo[:, 0:128]), reads=[b_pTo], writes=[b_oT])


def sample_attention(f, nc, L):
    pass


def _consts(half):
    c = {}
    c["ident"] = np.eye(128, dtype=np.float32)
    s_idx = np.arange(128)[:, None]; t_idx = np.arange(128)[None, :]
    same = (s_idx // 64) == (t_idx // 64)
    mu = ((s_idx < t_idx) & same).astype(np.float32)
    mui = ((s_idx <= t_idx) & same).astype(np.float32)
    ml = mu.T.copy()
    c["mu4"] = np.tile(mu, (1, 4)); c["ml4"] = np.tile(ml, (1, 4)); c["mui4"] = np.tile(mui, (1, 4))
    c["tri"] = (s_idx <= t_idx).astype(np.float32)
    cmk = np.zeros((128, 256), np.float32)
    cmk[:, [1, 65, 129, 193]] = 1.0
    c["colmask"] = cmk
    rm = np.ones((128, 512), np.float32); rm[:, ::64] = 0.0
    c["resetm"] = rm
    col = np.arange(512)[None, :]
    c["cm0"] = np.broadcast_to(((col % 128) < 64).astype(np.float32), (128, 512)).copy()
    c["cm1"] = np.broadcast_to(((col % 128) >= 64).astype(np.float32), (128, 512)).copy()
    c["bones"] = same.astype(np.float32)
    sel = np.zeros((128, 4, 128), np.float32)
    sel0 = np.zeros((128, 2, 128), np.float32)
    for s in range(4):
        sel[1 + 64 * (s % 2), s, :] = 1.0
    for r in range(2):
        sel0[1 + 64 * r, r, 0] = 1.0
    c["sel"] = sel.reshape(128, 512); c["sel0"] = sel0.reshape(128, 256)
    c["iotap"] = np.arange(128, dtype=np.int32).reshape(128, 1)
    hm = np.zeros((8, 4), np.float32); e0 = np.zeros((8, 4), np.float32); e1 = np.zeros((8, 4), np.float32)
    for h in range(4):
        for cc in range(2):
            hm[h * 2 + cc, h] = 1.0
        e0[h * 2, h] = 1.0; e1[h * 2 + 1, h] = 1.0
    c["hmask"] = hm; c["e0"] = e0; c["e1"] = e1
    slopes = np.array([2.0 ** (-8.0 * (h + 1) / 4) for h in range(4)], np.float64)
    kcol = np.arange(4096)
    kb = np.zeros((4, 4, 4096), np.float32)
    qb = np.zeros((4, 4, 2048), np.float32)
    qpos = 2048 + np.arange(2048)
    for h in range(4):
        kb[h, 0] = slopes[h] * 128 * (kcol // 128)
        if half == 0:
            kb[h, 0, :2048] = NEG
        kb[h, 1] = slopes[h] * (kcol % 128)
        kb[h, 2] = 1.0; kb[h, 3] = 1.0
        qb[h, 0] = 1.0; qb[h, 1] = 1.0
        qb[h, 2] = -slopes[h] * 128 * (qpos // 128)
        qb[h, 3] = -slopes[h] * (qpos % 128)
    c["kb"] = kb; c["qb"] = qb
    sb = np.zeros((128, 65, 8), np.float32)
    slot = np.arange(128)[:, None]
    for pg in range(64):
        dist = 8192 - (128 * pg + slot)
        for h in range(4):
            sb[:, pg, 2 * h] = (-slopes[h] * dist)[:, 0]; sb[:, pg, 2 * h + 1] = (-slopes[h] * dist)[:, 0]
    sb[1:, 64, :] = NEG
    c["sbias"] = sb.reshape(128, 65 * 8)
    return c


_NC_CACHE = {}
_LAST = None


def kernel(**inp):
    f32 = np.float32
    xp = np.asarray(inp["x_prompt"], f32); xs = np.asarray(inp["x_sample"], f32)
    g = lambda k: np.ascontiguousarray(np.asarray(inp[k], f32)[0])
    w_in = g("w_in")
    shared = {
        "w_in": w_in, "w_pa": g("w_pa"), "w_pb": g("w_pb"), "w_out": g("w_out"),
        "w_gate": g("w_gate"), "w_up": g("w_up"), "w_down": g("w_down"),
        "wa2": np.ascontiguousarray(np.concatenate([g("w2"), g("a2")], 0)), "g2": g("g2"),
        "ck": np.ascontiguousarray(np.asarray(inp["cache_k"], f32).reshape(2560 * 128, 512)),
        "cv": np.ascontiguousarray(np.asarray(inp["cache_v"], f32).reshape(2560 * 128, 512)),
    }
    pvec = np.zeros((128, 36), f32)
    pvec[:, 0:14] = g("shift_mu").reshape(14, 128).T
    pvec[:, 14:18] = g("w0").reshape(4, 128).T
    pvec[:, 18:22] = g("a0").reshape(4, 128).T
    pvec[:, 22:26] = g("k_k").reshape(4, 128).T
    pvec[:, 26:30] = g("k_a").reshape(4, 128).T
    pvec[:, 30:34] = g("r_k").reshape(4, 128).T
    pvec[:, 34] = np.tile(g("q_gain"), 2); pvec[:, 35] = np.tile(g("k_gain"), 2)
    prow = np.concatenate([g("norm_mix"), g("norm_ffn"), g("attn_out_gain"), g("gn_w"), g("gn_b"),
                           g("lambda_q1"), g("lambda_k1"), g("lambda_q2"), g("lambda_k2")]).reshape(1, 3456).astype(f32)
    shared["pvec"] = pvec; shared["prow"] = prow
    consts = [_consts(0), _consts(1)]
    ptab = np.asarray(inp["page_table"], np.int32)
    swkv = np.asarray(inp["state_wkv"], f32)[0]
    sshift = np.asarray(inp["state_shift"], f32)[0]
    in_maps = []
    for c in range(8):
        b, half = c // 2, c % 2
        xin = np.zeros((NCOL, D), f32)
        if half == 1:
            xin[0:2048] = xp[b, 0:2048]
        xin[2048:4096] = xp[b, half * 2048:(half + 1) * 2048]
        for s in range(4):
            xin[4096 + 128 * (s // 2) + 64 * (s % 2) + 1] = xs[4 * c + s, 0]
        m = dict(shared)
        m.update(consts[half])
        m["xin"] = xin
        m["ptab"] = np.ascontiguousarray(ptab[4 * c:4 * c + 4].reshape(1, 256))
        sw = swkv[4 * c:4 * c + 4].transpose(0, 1, 3, 2).reshape(4, 4, 128, 64)
        m["swkv"] = np.ascontiguousarray(sw)
        m["sshift"] = np.ascontiguousarray(sshift[4 * c:4 * c + 4].reshape(4, 14, 128).transpose(2, 1, 0))
        in_maps.append(m)
    if "nc" not in _NC_CACHE:
        _NC_CACHE["nc"] = build()
        _NC_CACHE["small"] = False
    if _NC_CACHE.get("small"):
        for m in in_maps:
            m["ck"] = m["ck"][:128]; m["cv"] = m["cv"][:128]
    nc = _NC_CACHE["nc"]
    res = run_bass_kernel_spmd(nc, in_maps, core_ids=list(range(8)))
    R = res.results
    global _LAST
    _LAST = R
    y_p = np.zeros((4, 4096, 1024), f32); y_s = np.zeros((32, 1, 1024), f32)
    k_p = np.zeros((1, 4, 4096, 4, 128), f32); v_p = np.zeros((1, 4, 4096, 4, 128), f32)
    wkv_p = np.zeros((1, 4, 8, 64, 64), f32); sh_p = np.zeros((1, 4, 1792), f32)
    k_s = np.zeros((1, 32, 1, 4, 128), f32); v_s = np.zeros((1, 32, 1, 4, 128), f32)
    wkv_s = np.zeros((1, 32, 8, 64, 64), f32); sh_s = np.zeros((1, 32, 1792), f32)
    for c in range(8):
        b, half = c // 2, c % 2
        r = R[c]
        sl = slice(half * 2048, (half + 1) * 2048)
        y_p[b, sl] = r["y_out"][0:2048]
        kT = r["k_out"]
        k_p[0, b, sl] = kT[:, 0:2048].T.reshape(2048, 4, 128)
        v_p[0, b, sl] = r["v_out"][0:2048].reshape(2048, 4, 128)
        wk = r["wkv_out"].reshape(5, 8, 64, 64)
        po = r["p_out"]
        if half == 1:
            wkv_p[0, b] = wk[0].transpose(0, 2, 1)
            sh_p[0, b] = po[:, :, 127].reshape(1792)
        for s in range(4):
            col = 128 * (s // 2) + 64 * (s % 2) + 1
            y_s[4 * c + s, 0] = r["y_out"][2048 + col]
            k_s[0, 4 * c + s, 0] = kT[:, 2048 + col].reshape(4, 128)
            v_s[0, 4 * c + s, 0] = r["v_out"][2048 + col].reshape(4, 128)
            wkv_s[0, 4 * c + s] = wk[1 + s].transpose(0, 2, 1)
            sh_s[0, 4 * c + s] = po[:, :, 128 + col].reshape(1792)
    return (y_p, y_s, k_p, v_p, wkv_p, sh_p, k_s, v_s, wkv_s, sh_s)
```

```python
import math
import numpy as np
import concourse.bass as bass
import concourse.mybir as mybir
from concourse.bass_utils import run_bass_kernel_spmd

F32 = mybir.dt.float32
BF16 = mybir.dt.bfloat16
I32 = mybir.dt.int32
ALU = mybir.AluOpType
AF = mybir.ActivationFunctionType
AX = mybir.AxisListType

SEM_EPOCH = 30000
NPRE, NOWN, NSMP = 2048, 2048, 256
NCOL = NPRE + NOWN + NSMP
NLOC = NOWN + NSMP
NT = NCOL // 128
D = 1024
DFF = 2816
NEG = -30000.0


class Eng:
    def __init__(self, fw, name, e):
        self.fw = fw; self.name = name; self.e = e
        self.sems = []; self.count = 0; self.epoch = -1; self.known = {}
        self._new_epoch()

    def _new_epoch(self):
        self.epoch += 1
        self.count = 0
        self.sems.append(self.fw.new_sem("%s_e%d" % (self.name, self.epoch)))


class Buf:
    _uid = [0]

    def __init__(self, name, psum=False):
        Buf._uid[0] += 1
        self.uid = Buf._uid[0]
        self.name = name; self.w = None; self.r = []; self.dsem = None; self.dcount = 0; self.psum = psum


class FW:
    def __init__(self, nc):
        self.nc = nc
        self._stack = []
        self._semstack = []
        self.engs = {}
        for name, e in (("pe", nc.tensor), ("act", nc.scalar), ("dve", nc.vector),
                        ("pool", nc.gpsimd), ("sp", nc.sync)):
            self.engs[name] = Eng(self, name, e)
        self.pe = self.engs["pe"]; self.act = self.engs["act"]; self.dve = self.engs["dve"]
        self.pool = self.engs["pool"]; self.sp = self.engs["sp"]
        self.nbuf = 0
        self.out_bufs = []
        self.free_dsems = []

    def new_sem(self, name):
        cm = self.nc.semaphore(name)
        s = cm.__enter__()
        self._semstack.append(cm)
        return s

    def sbuf(self, name, shape, dt):
        cm = self.nc.sbuf_tensor("sb_" + name, list(shape), dt)
        t = cm.__enter__()
        self._stack.append(cm)
        self.nbuf += 1
        return t, Buf(name)

    def psum(self, name, shape, dt):
        cm = self.nc.psum_tensor("ps_" + name, list(shape), dt)
        t = cm.__enter__()
        self._stack.append(cm)
        return t, Buf(name, psum=True)

    def mark(self):
        return len(self._stack)

    def release(self, mark):
        while len(self._stack) > mark:
            self._stack.pop().__exit__(None, None, None)

    def _need(self, eng, stamp, kind):
        if stamp is None:
            return
        if stamp[0] == 'e':
            _, pe_, ep, cnt = stamp
            if pe_ is eng:
                if eng.name == "pe" or kind != "raw":
                    return
            key = (pe_.name, ep)
            if eng.known.get(key, 0) >= cnt:
                return
            eng.e.wait_ge(pe_.sems[ep], cnt)
            eng.known[key] = cnt
        else:
            _, sem, val, key = stamp
            if eng.known.get(key, 0) >= val:
                return
            eng.e.wait_ge(sem, val)
            eng.known[key] = val

    def _deps(self, eng, reads, writes):
        for b in reads:
            self._need(eng, b.w, "raw")
        for b in writes:
            self._need(eng, b.w, "waw")
            for s in b.r:
                self._need(eng, s, "war")

    def _record(self, st, reads, writes):
        for b in reads:
            b.r.append(st)
        for b in writes:
            b.w = st
            b.r = []

    def op(self, eng, fn, reads=(), writes=(), rg=None):
        if eng.name == "pe":
            for b in writes:
                prev = getattr(b, "rg", None)
                if rg is not None and prev is not None and prev != rg and b.w is not None and b.w[0] == 'e' and b.w[1] is eng:
                    _, pe_, ep, cnt = b.w
                    key = (pe_.name, ep)
                    if eng.known.get(key, 0) < cnt:
                        eng.e.wait_ge(pe_.sems[ep], cnt)
                        eng.known[key] = cnt
                b.rg = rg
        if eng.name != "pe":
            px = [b for b in reads if b.psum]
            if px:
                reads = [b for b in reads if not b.psum]
                writes = list(writes) + [b for b in px if b not in writes]
        self._deps(eng, reads, writes)
        if eng.count >= SEM_EPOCH:
            eng._new_epoch()
        ins = fn()
        eng.count += 1
        ins.then_inc(eng.sems[eng.epoch], 1)
        self._record(('e', eng, eng.epoch, eng.count), reads, writes)
        return ins

    def dma(self, eng, out, in_, sb, reads=(), writes=(), is_output=False, fn=None):
        self._deps(eng, reads, writes)
        b = sb
        if b.dsem is None:
            b.dsem = self.new_sem("d_" + b.name)
        ins = eng.e.dma_start(out=out, in_=in_) if fn is None else fn()
        ins.then_inc(b.dsem, 16)
        b.dcount += 16
        self._record(('d', b.dsem, b.dcount, ("dma", b.uid)), reads, writes)
        if is_output and b not in self.out_bufs:
            self.out_bufs.append(b)
        return ins

    def finish(self):
        for b in self.out_bufs:
            self.sp.e.wait_ge(b.dsem, b.dcount)

    def barrier_all(self):
        for a in self.engs.values():
            for o in self.engs.values():
                if o is a or o.count == 0:
                    continue
                key = (o.name, o.epoch)
                if a.known.get(key, 0) >= o.count:
                    continue
                a.e.wait_ge(o.sems[o.epoch], o.count)
                a.known[key] = o.count

    def close(self):
        self.release(0)
        while self._semstack:
            self._semstack.pop().__exit__(None, None, None)


class Ring:
    def __init__(self, items):
        self.items = items; self.i = 0

    def next(self):
        it = self.items[self.i % len(self.items)]
        self.i += 1
        return it


def build(stage=99, small=False, dbg=None, ng=8, parts=("rwkv", "attn", "epi", "samp"), nb=None, rl=9, cut=99):
    nc = bass.Bass("TRN2", target_bir_lowering=False)
    V, S, G, T = nc.vector, nc.scalar, nc.gpsimd, nc.tensor

    def din(name, shape, dt=F32):
        return nc.dram_tensor(name, list(shape), dt, kind="ExternalInput").ap()

    def dout(name, shape, dt=F32):
        return nc.dram_tensor(name, list(shape), dt, kind="ExternalOutput").ap()

    xin = din("xin", [NCOL, D])
    w_in = din("w_in", [D, 5376]); w_pa = din("w_pa", [512, D]); w_pb = din("w_pb", [512, D])
    w_out = din("w_out", [D, D]); w_gate = din("w_gate", [D, DFF]); w_up = din("w_up", [D, DFF])
    w_down = din("w_down", [DFF, D])
    wa2_d = din("wa2", [128, 512]); g2_d = din("g2", [128, 512])
    pvec_d = din("pvec", [128, 36]); prow_d = din("prow", [1, 3456])
    kb_d = din("kb", [4, 4, 4096]); qb_d = din("qb", [4, 4, 2048])
    sshift_d = din("sshift", [128, 14, 4]); swkv_d = din("swkv", [4, 4, 128, 64])
    NPG = 128 if small else 2560 * 128
    ck_d = din("ck", [NPG, 512]); cv_d = din("cv", [NPG, 512])
    ptab_d = din("ptab", [1, 256], I32)
    sbias_d = din("sbias", [128, 65 * 8])
    ident_d = din("ident", [128, 128]); mu4_d = din("mu4", [128, 512]); ml4_d = din("ml4", [128, 512])
    mui4_d = din("mui4", [128, 512]); tri_d = din("tri", [128, 128]); colmask_d = din("colmask", [128, 256])
    resetm_d = din("resetm", [128, 512]); cm0_d = din("cm0", [128, 512]); cm1_d = din("cm1", [128, 512])
    bones_d = din("bones", [128, 128]); sel_d = din("sel", [128, 512]); sel0_d = din("sel0", [128, 256])
    iota_d = din("iotap", [128, 1], I32); hmask_d = din("hmask", [8, 4]); e0_d = din("e0", [8, 4]); e1_d = din("e1", [8, 4])

    y_out = dout("y_out", [NLOC, D]); k_out = dout("k_out", [512, NLOC]); v_out = dout("v_out", [NLOC, 512])
    wkv_out = dout("wkv_out", [5, 4, 128, 64]); p_out = dout("p_out", [14, 128, 384])

    dbg_d = dout("dbg", [2, 4, 128, NLOC])
    f = FW(nc)
    pe, act, dve, pool, sp = f.pe, f.act, f.dve, f.pool, f.sp

    def cload(name, src, shape, dt=F32, q=None):
        t, b = f.sbuf(name, shape, dt)
        if dt == F32 or dt == I32:
            f.dma(q or sp, t[:], src, b, writes=[b])
        else:
            f.dma(pool, t[:], src, b, writes=[b])
        return t, b

    ident_f, b_identf = cload("ident_f", ident_d[:, :], [128, 128])
    ident_b, b_identb = cload("ident_b", ident_d[:, :], [128, 128], BF16)
    tri_b, b_tri = cload("tri_b", tri_d[:, :], [128, 128], BF16)
    bones, b_bones = cload("bones", bones_d[:, :], [128, 128])
    pvec, b_pvec = cload("pvec", pvec_d[:, :], [128, 36])
    prow, b_prow = f.sbuf("prow", [128, 3456], F32)
    f.dma(sp, prow[:], prow_d[0:1, :].partition_broadcast(128), b_prow, writes=[b_prow])
    pv2, b_pv2 = f.sbuf("pv2", [128, 24], F32)
    f.op(dve, lambda: V.tensor_scalar(out=pv2[:, 0:14], in0=pvec[:, 0:14], scalar1=-1.0, scalar2=1.0,
                                      op0=ALU.mult, op1=ALU.add), reads=[b_pvec], writes=[b_pv2])
    f.op(dve, lambda: V.tensor_scalar(out=pv2[:, 14:18], in0=pvec[:, 14:18], scalar1=-1.0, scalar2=None,
                                      op0=ALU.mult), reads=[b_pvec], writes=[b_pv2])
    f.op(dve, lambda: V.tensor_scalar(out=pv2[:, 18:22], in0=pvec[:, 26:30], scalar1=-1.0, scalar2=1.0,
                                      op0=ALU.mult, op1=ALU.add), reads=[b_pvec], writes=[b_pv2])
    f.op(dve, lambda: V.tensor_scalar(out=pv2[:, 22:23], in0=pvec[:, 34:35], scalar1=0.125, scalar2=None,
                                      op0=ALU.mult), reads=[b_pvec], writes=[b_pv2])
    MU_C, W0_C, A0_C, KK_C, KA_C, RK_C, QG_C, KG_C = 0, 14, 18, 22, 26, 30, 34, 35
    GMIX, GFFN, GAO, GNW, GNB, LAM = 0, 1024, 2048, 2176, 2688, 3200

    lt, b_lt = f.sbuf("lt", [128, 8], F32)
    junk64, b_junk64 = f.sbuf("junk64", [128, 64], F32)
    f.op(dve, lambda: V.memset(lt[:], 0.0), writes=[b_lt])
    f.op(dve, lambda: V.tensor_tensor(out=junk64[:], in0=prow[:, LAM:LAM + 64], in1=prow[:, LAM + 64:LAM + 128],
                                      op=ALU.mult), reads=[b_prow], writes=[b_junk64])
    f.op(dve, lambda: V.reduce_sum(out=lt[:, 0:1], in_=junk64[:], axis=AX.X), reads=[b_junk64], writes=[b_lt])
    f.op(dve, lambda: V.tensor_tensor(out=junk64[:], in0=prow[:, LAM + 128:LAM + 192], in1=prow[:, LAM + 192:LAM + 256],
                                      op=ALU.mult), reads=[b_prow, b_lt], writes=[b_junk64])
    f.op(dve, lambda: V.reduce_sum(out=lt[:, 1:2], in_=junk64[:], axis=AX.X), reads=[b_junk64], writes=[b_lt])
    f.op(act, lambda: S.activation(out=lt[:, 2:4], in_=lt[:, 0:2], func=AF.Exp), reads=[b_lt], writes=[b_lt])
    lam_init = 0.8 - 0.6 * math.exp(-0.3 * 0)
    f.op(dve, lambda: V.tensor_tensor(out=lt[:, 4:5], in0=lt[:, 3:4], in1=lt[:, 2:3], op=ALU.subtract),
         reads=[b_lt], writes=[b_lt])
    f.op(dve, lambda: V.tensor_scalar(out=lt[:, 5:6], in0=lt[:, 4:5], scalar1=-lam_init, scalar2=None, op0=ALU.add),
         reads=[b_lt], writes=[b_lt])
    NLAM = lt[:, 5:6]

    hT_loc, b_hTloc = f.sbuf("hT_loc", [128, 8, NLOC], BF16)
    oT, b_oT = f.sbuf("oT", [128, 4, NLOC], BF16)
    yrT, b_yrT = f.sbuf("yrT", [128, 4, NLOC], BF16)

    m_pre = f.mark()
    hT_pre, b_hTpre = f.sbuf("hT_pre", [128, 8, NPRE], BF16)

    def hT_cols(c0, n):
        if c0 < NPRE:
            return hT_pre, b_hTpre, c0
        return hT_loc, b_hTloc, c0 - NPRE

    m1 = f.mark()
    xr = Ring([f.sbuf("x%d" % i, [128, D], F32) for i in range(3)])
    xbr = Ring([f.sbuf("xb%d" % i, [128, D], BF16) for i in range(2)])
    junk, b_junk = f.sbuf("junk", [128, D], F32)
    ssr = Ring([f.sbuf("ss%d" % i, [128, 2], F32) for i in range(3)])
    ptr = Ring([f.psum("pt%d" % i, [128, 8, 128], BF16) for i in range(2)])
    for t in range(NT if dbg is None else 0):
        xt, bx = xr.next(); xb, bxb = xbr.next(); ss, bss = ssr.next(); pt, bpt = ptr.next()
        f.dma(sp, xt[:], xin[t * 128:(t + 1) * 128, :], bx, writes=[bx])
        f.op(dve, lambda: V.memset(ss[:], 0.0), writes=[bss])
        f.op(act, lambda: S.activation(out=junk[:], in_=xt[:], func=AF.Square, accum_out=ss[:, 0:1]),
             reads=[bx, bss], writes=[b_junk, bss])
        f.op(dve, lambda: V.tensor_scalar(out=ss[:, 1:2], in0=ss[:, 0:1], scalar1=1.0 / D, scalar2=1e-6,
                                          op0=ALU.mult, op1=ALU.add), reads=[bss], writes=[bss])
        f.op(act, lambda: S.activation(out=ss[:, 1:2], in_=ss[:, 1:2], func=AF.Sqrt), reads=[bss], writes=[bss])
        f.op(dve, lambda: V.reciprocal(out=ss[:, 1:2], in_=ss[:, 1:2]), reads=[bss], writes=[bss])
        f.op(dve, lambda: V.scalar_tensor_tensor(out=xb[:], in0=xt[:], scalar=ss[:, 1:2], in1=prow[:, GMIX:GMIX + D],
                                                 op0=ALU.mult, op1=ALU.mult), reads=[bx, bss, b_prow], writes=[bxb])
        for c in range(8):
            f.op(pe, lambda c=c: T.transpose(out=pt[:, c, :], in_=xb[:, c * 128:(c + 1) * 128], identity=ident_b[:]),
                 reads=[bxb, b_identb], writes=[bpt])
        ht, bht, lc = hT_cols(t * 128, 128)
        f.op(act, lambda: S.copy(out=ht[:, :, lc:lc + 128], in_=pt[:]), reads=[bpt], writes=[bht])
    f.release(m1)
    f.barrier_all()

    def wload(dst, bdst, src):
        f.dma(pool, dst, src, bdst, writes=[bdst])


    def epilogue():
        mE = f.mark()
        wpa, b_wpa = f.sbuf("wpa", [128, 4, D], BF16)
        wpb, b_wpb = f.sbuf("wpb", [128, 4, D], BF16)
        wo, b_wo = f.sbuf("wo", [128, 8, D], BF16)
        wload(wpa[:], b_wpa, w_pa.rearrange("(c p) n -> p c n", p=128))
        wload(wpb[:], b_wpb, w_pb.rearrange("(c p) n -> p c n", p=128))
        wload(wo[:], b_wo, w_out.rearrange("(c p) n -> p c n", p=128))
        SB = 384
        bank = [f.psum("bank%d" % i, [128, 512], F32) for i in range(8)]
        wgr = Ring([f.sbuf("wg%d" % i, [128, 8, 128], BF16) for i in range(4)])
        wdr = Ring([f.sbuf("wd%d" % i, [128, D], BF16) for i in range(2)])
        mT, b_mT = f.sbuf("mT", [128, 8, SB], BF16)
        hfT, b_hfT = f.sbuf("hfT", [128, 8, SB], BF16)
        x1, b_x1 = f.sbuf("x1e", [128, 3, D], F32)
        xr2 = Ring([f.sbuf("xe%d" % i, [128, D], F32) for i in range(2)])
        sga, b_sga = f.sbuf("sga", [128, SB], F32)
        sgb, b_sgb = f.sbuf("sgb", [128, SB], F32)
        tA, b_tA = f.sbuf("tA", [128, SB], F32)
        hfb, b_hfb = f.sbuf("hfb", [128, D], BF16)
        ejunk, b_ejunk = f.sbuf("ejunk", [128, D], F32)
        est, b_est = f.sbuf("est", [128, 4], F32)
        actr = Ring([f.sbuf("act%d" % i, [128, SB], BF16) for i in range(2)])
        yor = Ring([f.sbuf("yo%d" % i, [128, 512], F32) for i in range(2)])
        for sbi in range(NLOC // SB):
            c0 = sbi * SB
            for ch in range(8):
                cs = slice(ch * 128, (ch + 1) * 128)
                (pa, bpa), (pb_, bpb), (pga, bpga), (pgb, bpgb) = bank[0], bank[1], bank[2], bank[3]
                wga, bwga = wgr.next(); wgb, bwgb = wgr.next()
                wload(wga[:], bwga, w_in[:, 3328 + ch * 128:3328 + (ch + 1) * 128].rearrange("(c p) n -> p c n", p=128))
                wload(wgb[:], bwgb, w_in[:, 4352 + ch * 128:4352 + (ch + 1) * 128].rearrange("(c p) n -> p c n", p=128))
                for h in range(4):
                    f.op(pe, lambda h=h: T.matmul(out=pa[:, 0:SB], lhsT=wpa[:, h, cs], rhs=oT[:, h, c0:c0 + SB], start=(h == 0), stop=(h == 3)),
                         reads=[b_wpa, b_oT], writes=[bpa])
                for h in range(4):
                    f.op(pe, lambda h=h: T.matmul(out=pb_[:, 0:SB], lhsT=wpb[:, h, cs], rhs=yrT[:, h, c0:c0 + SB], start=(h == 0), stop=(h == 3)),
                         reads=[b_wpb, b_yrT], writes=[bpb])
                for c in range(8):
                    f.op(pe, lambda c=c: T.matmul(out=pga[:, 0:SB], lhsT=wga[:, c, :], rhs=hT_loc[:, c, c0:c0 + SB], start=(c == 0), stop=(c == 7)),
                         reads=[bwga, b_hTloc], writes=[bpga])
                for c in range(8):
                    f.op(pe, lambda c=c: T.matmul(out=pgb[:, 0:SB], lhsT=wgb[:, c, :], rhs=hT_loc[:, c, c0:c0 + SB], start=(c == 0), stop=(c == 7)),
                         reads=[bwgb, b_hTloc], writes=[bpgb])
                f.op(act, lambda: S.activation(out=sga[:], in_=pga[:, 0:SB], func=AF.Sigmoid), reads=[bpga], writes=[b_sga])
                f.op(act, lambda: S.activation(out=sgb[:], in_=pgb[:, 0:SB], func=AF.Sigmoid), reads=[bpgb], writes=[b_sgb])
                f.op(dve, lambda: V.tensor_tensor(out=tA[:], in0=pa[:, 0:SB], in1=sga[:], op=ALU.mult), reads=[bpa, b_sga], writes=[b_tA])
                f.op(dve, lambda: V.tensor_tensor(out=sgb[:], in0=pb_[:, 0:SB], in1=sgb[:], op=ALU.mult), reads=[bpb, b_sgb], writes=[b_sgb])
                f.op(pool, lambda: G.tensor_tensor(out=mT[:, ch, :], in0=tA[:], in1=sgb[:], op=ALU.add), reads=[b_tA, b_sgb], writes=[b_mT])
            for tl in range(3):
                ts_ = slice(tl * 128, (tl + 1) * 128)
                xt, bxt = xr2.next()
                row0 = NPRE + c0 + tl * 128
                f.dma(sp, xt[:], xin[row0:row0 + 128, :], bxt, writes=[bxt])
                for hf_ in range(2):
                    px, bpx = bank[4 + hf_]
                    for k in range(8):
                        f.op(pe, lambda k=k: T.matmul(out=px[:], lhsT=mT[:, k, ts_], rhs=wo[:, k, hf_ * 512:(hf_ + 1) * 512],
                                                      start=(k == 0), stop=(k == 7)), reads=[b_mT, b_wo], writes=[bpx])
                    f.op(dve, lambda: V.tensor_tensor(out=x1[:, tl, hf_ * 512:(hf_ + 1) * 512], in0=px[:], in1=xt[:, hf_ * 512:(hf_ + 1) * 512],
                                                      op=ALU.add), reads=[bpx, bxt], writes=[b_x1])
                f.op(dve, lambda: V.memset(est[:, 0:1], 0.0), writes=[b_est])
                f.op(act, lambda: S.activation(out=ejunk[:], in_=x1[:, tl, :], func=AF.Square, accum_out=est[:, 0:1]),
                     reads=[b_x1, b_est], writes=[b_ejunk, b_est])
                f.op(dve, lambda: V.tensor_scalar(out=est[:, 1:2], in0=est[:, 0:1], scalar1=1.0 / D, scalar2=1e-6, op0=ALU.mult, op1=ALU.add),
                     reads=[b_est], writes=[b_est])
                f.op(act, lambda: S.activation(out=est[:, 1:2], in_=est[:, 1:2], func=AF.Sqrt), reads=[b_est], writes=[b_est])
                f.op(dve, lambda: V.reciprocal(out=est[:, 1:2], in_=est[:, 1:2]), reads=[b_est], writes=[b_est])
                f.op(dve, lambda: V.scalar_tensor_tensor(out=hfb[:], in0=x1[:, tl, :], scalar=est[:, 1:2], in1=prow[:, GFFN:GFFN + D],
                                                         op0=ALU.mult, op1=ALU.mult), reads=[b_x1, b_est, b_prow], writes=[b_hfb])
                ptr_, bptr = bank[6]
                ptb = ptr_[:].bitcast(BF16)
                for c in range(8):
                    f.op(pe, lambda c=c: T.transpose(out=ptb[:, c * 128:(c + 1) * 128], in_=hfb[:, c * 128:(c + 1) * 128], identity=ident_b[:]),
                         reads=[b_hfb, b_identb], writes=[bptr])
                f.op(act, lambda: S.copy(out=hfT[:, :, ts_], in_=ptb.rearrange("p (c n) -> p c n", c=8)), reads=[bptr], writes=[b_hfT])
            for ffc in range(DFF // 128):
                fs = slice(ffc * 128, (ffc + 1) * 128)
                wg, bwg = wgr.next(); wu, bwu = wgr.next(); wd, bwd = wdr.next()
                wload(wg[:], bwg, w_gate[:, fs].rearrange("(c p) n -> p c n", p=128))
                wload(wu[:], bwu, w_up[:, fs].rearrange("(c p) n -> p c n", p=128))
                wload(wd[:], bwd, w_down[fs, :])
                (pg, bpg), (pu, bpu) = bank[6], bank[7]
                for c in range(8):
                    f.op(pe, lambda c=c: T.matmul(out=pg[:, 0:SB], lhsT=wg[:, c, :], rhs=hfT[:, c, :], start=(c == 0), stop=(c == 7)),
                         reads=[bwg, b_hfT], writes=[bpg])
                for c in range(8):
                    f.op(pe, lambda c=c: T.matmul(out=pu[:, 0:SB], lhsT=wu[:, c, :], rhs=hfT[:, c, :], start=(c == 0), stop=(c == 7)),
                         reads=[bwu, b_hfT], writes=[bpu])
                f.op(act, lambda: S.activation(out=sga[:], in_=pg[:, 0:SB], func=AF.Silu), reads=[bpg], writes=[b_sga])
                at, bat = actr.next()
                f.op(dve, lambda: V.tensor_tensor(out=at[:], in0=pu[:, 0:SB], in1=sga[:], op=ALU.mult), reads=[bpu, b_sga], writes=[bat])
                for tl in range(3):
                    for hf_ in range(2):
                        pd, bpd = bank[tl * 2 + hf_]
                        f.op(pe, lambda: T.matmul(out=pd[:], lhsT=at[:, tl * 128:(tl + 1) * 128], rhs=wd[:, hf_ * 512:(hf_ + 1) * 512],
                                                  start=(ffc == 0), stop=(ffc == DFF // 128 - 1)), reads=[bat, bwd], writes=[bpd])
            for tl in range(3):
                for hf_ in range(2):
                    pd, bpd = bank[tl * 2 + hf_]
                    yo, byo = yor.next()
                    f.op(dve, lambda: V.tensor_tensor(out=yo[:], in0=pd[:], in1=x1[:, tl, hf_ * 512:(hf_ + 1) * 512], op=ALU.add),
                         reads=[bpd, b_x1], writes=[byo])
                    r0 = c0 + tl * 128
                    f.dma(sp, y_out[r0:r0 + 128, hf_ * 512:(hf_ + 1) * 512], yo[:], byo, reads=[byo], is_output=True)
        f.barrier_all()
        f.release(mE)

    def rwkv_phase():
        mR = f.mark()
        BW = 256
        NB = NCOL // BW
        mu4b, b_mu4 = cload("mu4b", mu4_d[:, :], [128, 512], BF16)
        ml4b, b_ml4 = cload("ml4b", ml4_d[:, :], [128, 512], BF16)
        mui4b, b_mui4 = cload("mui4b", mui4_d[:, :], [128, 512], BF16)
        ident4, b_ident4 = f.sbuf("ident4", [128, 512], BF16)
        for i in range(4):
            f.dma(pool, ident4[:, i * 128:(i + 1) * 128], ident_d[:, :], b_ident4, writes=[b_ident4])
        resetm, b_resetm = cload("resetm", resetm_d[:, 0:BW], [128, BW])
        cm0, b_cm0 = cload("cm0", cm0_d[:, 0:BW], [128, BW])
        cm1, b_cm1 = cload("cm1", cm1_d[:, 0:BW], [128, BW])
        colmask, b_colmask = cload("colmask", colmask_d[:, :], [128, 256])
        wa2b, b_wa2 = cload("wa2b", wa2_d[:, :], [128, 512], BF16)
        g2b, b_g2 = cload("g2b", g2_d[:, :], [128, 512], BF16)
        sshift, b_sshift = cload("sshift", sshift_d[:, :, :], [128, 14, 4])
        cst, b_cst = f.sbuf("cst", [128, 4], F32)
        f.op(dve, lambda: V.memset(cst[:, 0:1], 1.0), writes=[b_cst])
        f.op(dve, lambda: V.memset(cst[:, 1:2], -0.5), writes=[b_cst])
        f.op(dve, lambda: V.memset(cst[:, 2:3], 64e-5), writes=[b_cst])
        carry, b_carry = f.sbuf("carry", [128, 14], F32)
        f.op(dve, lambda: V.memset(carry[:], 0.0), writes=[b_carry])
        H, b_H = f.sbuf("H", [128, 4, 64], F32)
        f.op(dve, lambda: V.memset(H[:], 0.0), writes=[b_H])
        Hb = [f.sbuf("Hb%d" % i, [128, 4, 64], BF16) for i in range(2)]
        wring = Ring([f.sbuf("wr%d" % i, [128, 8, 128], BF16) for i in range(2)])
        ppr = Ring([f.psum("pp%d" % i, [128, 512], F32) for i in range(3)])
        pxr = Ring([f.sbuf("px%d" % i, [128, BW + 1], F32) for i in range(2)])
        rT, b_rT = f.sbuf("rT", [128, 4, BW], F32)
        kT, b_kT = f.sbuf("kT", [128, 4, BW], F32)
        vT, b_vT = f.sbuf("vT", [128, 4, BW], F32)
        m12, b_m12 = f.sbuf("m12", [128, BW], F32)
        m13, b_m13 = f.sbuf("m13", [128, BW], F32)
        twxa, b_twxa = f.sbuf("twxa", [128, BW], BF16)
        sg, b_sg = f.sbuf("sg", [128, BW], BF16)
        scr = [f.sbuf("scr%d" % i, [128, BW], F32) for i in range(8)]
        AT, b_AT = f.sbuf("AT", [128, 4, BW], BF16)
        BT, b_BT = f.sbuf("BT", [128, 4, BW], BF16)
        KT, b_KT2 = f.sbuf("KT", [128, 4, BW], BF16)
        RT, b_RT = f.sbuf("RT", [128, 4, BW], BF16)
        RT0, b_RT0 = f.sbuf("RT0", [128, 4, BW], BF16)
        RT1, b_RT1 = f.sbuf("RT1", [128, 4, BW], BF16)
        Eneg, b_Eneg = f.sbuf("Eneg", [128, 4, BW], F32)
        bonT, b_bonT = f.sbuf("bonT", [128, 4, BW], F32)
        gT, b_gT = f.sbuf("gT", [128, 4, BW], F32)
        Vtm, b_Vtm = f.sbuf("Vtm", [128, 2, 512], BF16)
        Btm, b_Btm = f.sbuf("Btm", [128, 2, 512], BF16)
        Ktm, b_Ktm = f.sbuf("Ktm", [128, 2, 512], BF16)
        ptBK, b_ptBK = f.psum("ptBK", [128, 2, 512], BF16)
        ptV, b_ptV = f.psum("ptV", [128, 512], F32)
        psA, b_psA = f.psum("psA", [128, 4, 128], F32)
        psB, b_psB = f.psum("psB", [128, 4, 128], F32)
        psC, b_psC = f.psum("psC", [128, 4, 128], F32)
        Lk = [f.sbuf("Lk%d" % i, [128, 4, 128], BF16) for i in range(2)]
        Nk = [f.sbuf("Nk%d" % i, [128, 4, 128], BF16) for i in range(2)]
        Yk = [f.sbuf("Yk%d" % i, [128, 4, 128], BF16) for i in range(2)]
        AKT, b_AKT = f.sbuf("AKT", [128, 4, 128], BF16)
        RBT, b_RBT = f.sbuf("RBT", [128, 4, 128], BF16)
        RKT, b_RKT = f.sbuf("RKT", [128, 4, 128], BF16)
        P1, b_P1 = f.sbuf("P1", [128, 4, 64], F32)
        Wb, b_Wb = f.sbuf("Wb", [128, 4, 64], BF16)
        Ub, b_Ub = f.sbuf("Ub", [128, 4, 64], BF16)
        Yall, b_Yall = f.sbuf("Yall", [128, 8, 64], F32)
        gst, b_gst = f.sbuf("gst", [128, 40], F32)
        ytmp2, b_ytmp = f.sbuf("ytmp", [128, 512], F32)
        ytmp = ytmp2[:].rearrange("p (a b) -> p a b", a=4)
        Ysq = ytmp2[:].rearrange("p (a b) -> p a b", a=8)
        b_Ysq = b_ytmp

        def flat(t3):
            return t3[:].rearrange("p a b -> p (a b)")

        for bi in (range(NB) if nb is None else nb):
            col0 = bi * BW
            is_loc = col0 >= NPRE
            is_smp = col0 >= NPRE + NOWN
            ht, bht, lc = hT_cols(col0, BW)
            for ch in range(14):
                wt, bwt = wring.next()
                wload(wt[:], bwt, w_in[:, 1536 + ch * 128:1536 + (ch + 1) * 128].rearrange("(c p) n -> p c n", p=128))
                pp, bpp = ppr.next()
                for c in range(8):
                    f.op(pe, lambda c=c: T.matmul(out=pp[:, 0:BW], lhsT=wt[:, c, :], rhs=ht[:, c, lc:lc + BW],
                                                  start=(c == 0), stop=(c == 7)), reads=[bwt, bht], writes=[bpp])
                px, bpx = pxr.next()
                f.op(dve, lambda: V.tensor_copy(out=px[:, 0:1], in_=carry[:, ch:ch + 1]), reads=[b_carry], writes=[bpx])
                f.op(act, lambda: S.copy(out=px[:, 1:BW + 1], in_=pp[:, 0:BW]), reads=[bpp], writes=[bpx])
                f.op(dve, lambda: V.tensor_copy(out=carry[:, ch:ch + 1], in_=px[:, BW:BW + 1]), reads=[bpx], writes=[b_carry])
                if bi == 15:
                    f.dma(sp, p_out[ch, :, 0:128], px[:, 129:257], bpx, reads=[bpx], is_output=True)
                if is_smp:
                    f.dma(sp, p_out[ch, :, 128:384], px[:, 1:257], bpx, reads=[bpx], is_output=True)
                    f.op(dve, lambda: V.tensor_copy(out=px[:, 1:BW + 1:64], in_=sshift[:, ch, :]),
                         reads=[b_sshift], writes=[bpx])
                tmp, btmp = scr[0]
                if ch < 4:
                    dst, bdst = rT[:, ch, :], b_rT
                elif ch < 8:
                    dst, bdst = kT[:, ch - 4, :], b_kT
                elif ch < 12:
                    dst, bdst = vT[:, ch - 8, :], b_vT
                elif ch == 12:
                    dst, bdst = m12[:], b_m12
                else:
                    dst, bdst = m13[:], b_m13
                f.op(pool, lambda: G.tensor_scalar(out=tmp[:], in0=px[:, 0:BW], scalar1=pvec[:, MU_C + ch:MU_C + ch + 1], scalar2=None,
                                                   op0=ALU.mult), reads=[bpx, b_pvec], writes=[btmp])
                f.op(dve, lambda: V.scalar_tensor_tensor(out=dst, in0=px[:, 1:BW + 1], scalar=pv2[:, ch:ch + 1], in1=tmp[:],
                                                         op0=ALU.mult, op1=ALU.add), reads=[bpx, btmp, b_pv2], writes=[bdst])
                if is_smp:
                    f.op(dve, lambda: V.tensor_tensor(out=dst, in0=dst, in1=colmask[:], op=ALU.mult),
                         reads=[bdst, b_colmask], writes=[bdst])
            if rl < 2:
                continue
            f.op(act, lambda: S.activation(out=twxa[0:64, :], in_=m12[0:64, :], func=AF.Tanh), reads=[b_m12], writes=[b_twxa])
            f.op(dve, lambda: V.tensor_copy(out=twxa[64:128, :], in_=m12[64:128, :]), reads=[b_m12], writes=[b_twxa])
            f.op(act, lambda: S.activation(out=sg[:], in_=m13[:], func=AF.Sigmoid), reads=[b_m13], writes=[b_sg])
            for pr in range(4):
                pc = slice(pr * 128, (pr + 1) * 128)
                (s1, bs1), (s2, bs2), (s3, bs3), (s4, bs4), (s5, bs5), (s6, bs6), (s7, bs7) = scr[1:8]
                pp, bpp = ppr.next()
                f.op(pe, lambda: T.matmul(out=pp[:, 0:BW], lhsT=wa2b[0:64, pc], rhs=twxa[0:64, :], start=True, stop=True),
                     reads=[b_wa2, b_twxa], writes=[bpp])
                f.op(act, lambda: S.activation(out=s1[:], in_=pp[:, 0:BW], func=AF.Exp, bias=pv2[:, 14 + pr:15 + pr], scale=-1.0),
                     reads=[bpp, b_pv2], writes=[bs1])
                f.op(act, lambda: S.activation(out=s1[:], in_=s1[:], func=AF.Ln, bias=cst[:, 0:1], scale=1.0),
                     reads=[bs1, b_cst], writes=[bs1])
                f.op(act, lambda: S.activation(out=s2[:], in_=s1[:], func=AF.Exp, bias=cst[:, 1:2], scale=-1.0),
                     reads=[bs1, b_cst], writes=[bs2])
                if is_smp:
                    f.op(dve, lambda: V.tensor_tensor(out=s2[:], in0=s2[:], in1=colmask[:], op=ALU.mult),
                         reads=[bs2, b_colmask], writes=[bs2])
                f.op(dve, lambda: V.tensor_tensor_scan(out=s3[:], data0=resetm[:], data1=s2[:], initial=0.0,
                                                       op0=ALU.mult, op1=ALU.add), reads=[bs2, b_resetm], writes=[bs3])
                f.op(act, lambda: S.activation(out=Eneg[:, pr, :], in_=s3[:], func=AF.Exp, scale=-1.0), reads=[bs3], writes=[b_Eneg])
                f.op(act, lambda: S.activation(out=s4[:], in_=s3[:], func=AF.Exp), reads=[bs3], writes=[bs4])
                f.op(dve, lambda: V.tensor_tensor(out=s5[:], in0=s3[:], in1=s2[:], op=ALU.subtract), reads=[bs3, bs2], writes=[bs5])
                f.op(act, lambda: S.activation(out=s5[:], in_=s5[:], func=AF.Exp, scale=-1.0), reads=[bs5], writes=[bs5])
                pp, bpp = ppr.next()
                f.op(pe, lambda: T.matmul(out=pp[:, 0:BW], lhsT=wa2b[64:128, pc], rhs=twxa[64:128, :], start=True, stop=True),
                     reads=[b_wa2, b_twxa], writes=[bpp])
                f.op(act, lambda: S.activation(out=s6[:], in_=pp[:, 0:BW], func=AF.Sigmoid, bias=pvec[:, A0_C + pr:A0_C + pr + 1], scale=1.0),
                     reads=[bpp, b_pvec], writes=[bs6])
                f.op(dve, lambda: V.tensor_scalar(out=s1[:], in0=kT[:, pr, :], scalar1=pvec[:, KK_C + pr:KK_C + pr + 1], scalar2=None,
                                                  op0=ALU.mult), reads=[b_kT, b_pvec], writes=[bs1])
                f.op(act, lambda: S.activation(out=s7[:], in_=s1[:], func=AF.Square), reads=[bs1], writes=[bs7])
                pp, bpp = ppr.next()
                f.op(pe, lambda: T.matmul(out=pp[:, 0:BW], lhsT=bones[:], rhs=s7[:], start=True, stop=True),
                     reads=[b_bones, bs7], writes=[bpp])
                f.op(dve, lambda: V.tensor_scalar(out=s7[:], in0=pp[:, 0:BW], scalar1=1e-24, scalar2=None, op0=ALU.max),
                     reads=[bpp], writes=[bs7])
                f.op(act, lambda: S.activation(out=s7[:], in_=s7[:], func=AF.Sqrt), reads=[bs7], writes=[bs7])
                f.op(dve, lambda: V.reciprocal(out=s7[:], in_=s7[:]), reads=[bs7], writes=[bs7])
                f.op(dve, lambda: V.tensor_tensor(out=s1[:], in0=s1[:], in1=s7[:], op=ALU.mult), reads=[bs1, bs7], writes=[bs1])
                f.op(dve, lambda: V.tensor_scalar(out=s7[:], in0=s6[:], scalar1=pvec[:, KA_C + pr:KA_C + pr + 1],
                                                  scalar2=pv2[:, 18 + pr:19 + pr], op0=ALU.mult, op1=ALU.add),
                     reads=[bs6, b_pvec, b_pv2], writes=[bs7])
                f.op(dve, lambda: V.tensor_tensor(out=s7[:], in0=s7[:], in1=kT[:, pr, :], op=ALU.mult), reads=[bs7, b_kT], writes=[bs7])
                f.op(dve, lambda: V.scalar_tensor_tensor(out=AT[:, pr, :], in0=s1[:], scalar=-1.0, in1=s5[:], op0=ALU.mult, op1=ALU.mult),
                     reads=[bs1, bs5], writes=[b_AT])
                f.op(pool, lambda: G.tensor_tensor(out=s1[:], in0=s1[:], in1=s6[:], op=ALU.mult), reads=[bs1, bs6], writes=[bs1])
                f.op(pool, lambda: G.tensor_tensor(out=BT[:, pr, :], in0=s1[:], in1=s4[:], op=ALU.mult), reads=[bs1, bs4], writes=[b_BT])
                f.op(pool, lambda: G.tensor_tensor(out=KT[:, pr, :], in0=s7[:], in1=s4[:], op=ALU.mult), reads=[bs7, bs4], writes=[b_KT2])
                f.op(dve, lambda: V.tensor_tensor(out=RT[:, pr, :], in0=rT[:, pr, :], in1=Eneg[:, pr, :], op=ALU.mult),
                     reads=[b_rT, b_Eneg], writes=[b_RT])
                f.op(pool, lambda: G.tensor_tensor(out=RT0[:, pr, :], in0=RT[:, pr, :], in1=cm0[:], op=ALU.mult),
                     reads=[b_RT, b_cm0], writes=[b_RT0])
                f.op(pool, lambda: G.tensor_tensor(out=RT1[:, pr, :], in0=RT[:, pr, :], in1=cm1[:], op=ALU.mult),
                     reads=[b_RT, b_cm1], writes=[b_RT1])
                if is_loc:
                    f.op(dve, lambda: V.scalar_tensor_tensor(out=s1[:], in0=rT[:, pr, :], scalar=pvec[:, RK_C + pr:RK_C + pr + 1], in1=s7[:],
                                                             op0=ALU.mult, op1=ALU.mult), reads=[b_rT, bs7, b_pvec], writes=[bs1])
                    pp, bpp = ppr.next()
                    f.op(pe, lambda: T.matmul(out=pp[:, 0:BW], lhsT=bones[:], rhs=s1[:], start=True, stop=True),
                         reads=[b_bones, bs1], writes=[bpp])
                    f.op(dve, lambda: V.tensor_tensor(out=bonT[:, pr, :], in0=pp[:, 0:BW], in1=vT[:, pr, :], op=ALU.mult),
                         reads=[bpp, b_vT], writes=[b_bonT])
                    pp, bpp = ppr.next()
                    f.op(pe, lambda: T.matmul(out=pp[:, 0:BW], lhsT=g2b[:, pc], rhs=sg[:], start=True, stop=True),
                         reads=[b_g2, b_sg], writes=[bpp])
                    f.op(act, lambda: S.copy(out=gT[:, pr, :], in_=pp[:, 0:BW]), reads=[bpp], writes=[b_gT])
            for tl in range(2 if rl >= 3 else 0):
                Tg = bi * 2 + tl
                tc = slice(tl * 128, (tl + 1) * 128)
                for pr in range(4):
                    pc = slice(pr * 128, (pr + 1) * 128)
                    f.op(pe, lambda: T.transpose(out=ptBK[:, 0, pc], in_=BT[:, pr, tc], identity=ident_b[:]),
                         reads=[b_BT, b_identb], writes=[b_ptBK])
                    f.op(pe, lambda: T.transpose(out=ptBK[:, 1, pc], in_=KT[:, pr, tc], identity=ident_b[:]),
                         reads=[b_KT2, b_identb], writes=[b_ptBK])
                    f.op(pe, lambda: T.transpose(out=ptV[:, pc], in_=vT[:, pr, tc], identity=ident_f[:]),
                         reads=[b_vT, b_identf], writes=[b_ptV])
                f.op(act, lambda: S.copy(out=Btm[:, tl, :], in_=ptBK[:, 0, :]), reads=[b_ptBK], writes=[b_Btm])
                f.op(dve, lambda: V.tensor_copy(out=Ktm[:, tl, :], in_=ptBK[:, 1, :]), reads=[b_ptBK], writes=[b_Ktm])
                f.op(act, lambda: S.copy(out=Vtm[:, tl, :], in_=ptV[:]), reads=[b_ptV], writes=[b_Vtm])
                for Gi in range(2 if rl >= 4 else 0):
                    heads = [(hi, 4 * Gi + hi, (4 * Gi + hi) // 2, slice(64 * ((4 * Gi + hi) % 2), 64 * ((4 * Gi + hi) % 2) + 64))
                             for hi in range(4)]
                    for hi, h, pr, rows in heads:
                        f.op(pe, lambda: T.matmul(out=psA[:, hi, :], lhsT=BT[rows, pr, tc], rhs=AT[rows, pr, tc], start=True, stop=True),
                             reads=[b_BT, b_AT], writes=[b_psA], rg=rows.start)
                    for hi, h, pr, rows in heads:
                        f.op(pe, lambda: T.matmul(out=psB[:, hi, :], lhsT=AT[rows, pr, tc], rhs=BT[rows, pr, tc], start=True, stop=True),
                             reads=[b_BT, b_AT], writes=[b_psB], rg=rows.start)
                    (N0, bN0), (N1, bN1) = Nk
                    (L0, bL0), (L1, bL1) = Lk
                    (Y0, bY0), (Y1, bY1) = Yk
                    f.op(dve, lambda: V.tensor_tensor(out=flat(N0), in0=flat(psA), in1=mu4b[:], op=ALU.mult),
                         reads=[b_psA, b_mu4], writes=[bN0])
                    f.op(dve, lambda: V.tensor_tensor(out=flat(L0), in0=flat(psB), in1=ml4b[:], op=ALU.mult),
                         reads=[b_psB, b_ml4], writes=[bL0])
                    if cut <= 1:
                        continue
                    f.op(pool, lambda: G.tensor_tensor(out=flat(Y0), in0=flat(N0), in1=ident4[:], op=ALU.add),
                         reads=[bN0, b_ident4], writes=[bY0])
                    if cut <= 2:
                        continue
                    for k in range(min(5, cut - 2)):
                        Nc, bNc = Nk[k % 2]; Lc, bLc = Lk[k % 2]; Yc, bYc = Yk[k % 2]
                        Nn, bNn = Nk[(k + 1) % 2]; Ln, bLn = Lk[(k + 1) % 2]; Yn, bYn = Yk[(k + 1) % 2]
                        for hi, h, pr, rows in heads:
                            f.op(pe, lambda: T.matmul(out=psA[:, hi, :], lhsT=Nc[:, hi, :], rhs=Lc[:, hi, :], start=True, stop=True),
                                 reads=[bNc, bLc], writes=[b_psA])
                        if k < 4:
                            for hi, h, pr, rows in heads:
                                f.op(pe, lambda: T.matmul(out=psB[:, hi, :], lhsT=Lc[:, hi, :], rhs=Nc[:, hi, :], start=True, stop=True),
                                     reads=[bNc, bLc], writes=[b_psB])
                        f.op(act, lambda: S.copy(out=flat(Ln), in_=flat(psA)), reads=[b_psA], writes=[bLn])
                        if k < 4:
                            f.op(dve, lambda: V.tensor_copy(out=flat(Nn), in_=flat(psB)), reads=[b_psB], writes=[bNn])
                        for hi, h, pr, rows in heads:
                            f.op(pe, lambda: T.matmul(out=psC[:, hi, :], lhsT=ident_b[:], rhs=Yc[:, hi, :], start=True, stop=False),
                                 reads=[b_identb, bYc], writes=[b_psC])
                            f.op(pe, lambda: T.matmul(out=psC[:, hi, :], lhsT=Ln[:, hi, :], rhs=Yc[:, hi, :], start=False, stop=True),
                                 reads=[bLn, bYc], writes=[b_psC])
                        if k % 2 == 0:
                            f.op(dve, lambda: V.tensor_copy(out=flat(Yn), in_=flat(psC)), reads=[b_psC], writes=[bYn])
                        else:
                            f.op(act, lambda: S.copy(out=flat(Yn), in_=flat(psC)), reads=[b_psC], writes=[bYn])
                    TT, bTT = Yk[1]
                    if cut <= 7:
                        continue
                    for hi, h, pr, rows in heads:
                        f.op(pe, lambda: T.matmul(out=psA[:, hi, :], lhsT=KT[rows, pr, tc], rhs=AT[rows, pr, tc], start=True, stop=True),
                             reads=[b_KT2, b_AT], writes=[b_psA], rg=rows.start)
                    for hi, h, pr, rows in heads:
                        f.op(pe, lambda: T.matmul(out=psB[:, hi, :], lhsT=BT[rows, pr, tc], rhs=RT[rows, pr, tc], start=True, stop=True),
                             reads=[b_BT, b_RT], writes=[b_psB], rg=rows.start)
                    for hi, h, pr, rows in heads:
                        f.op(pe, lambda: T.matmul(out=psC[:, hi, :], lhsT=KT[rows, pr, tc], rhs=RT[rows, pr, tc], start=True, stop=True),
                             reads=[b_KT2, b_RT], writes=[b_psC], rg=rows.start)
                    f.op(dve, lambda: V.tensor_tensor(out=flat(AKT), in0=flat(psA), in1=mu4b[:], op=ALU.mult),
                         reads=[b_psA, b_mu4], writes=[b_AKT])
                    f.op(dve, lambda: V.tensor_tensor(out=flat(RBT), in0=flat(psB), in1=mui4b[:], op=ALU.mult),
                         reads=[b_psB, b_mui4], writes=[b_RBT])
                    f.op(dve, lambda: V.tensor_tensor(out=flat(RKT), in0=flat(psC), in1=mui4b[:], op=ALU.mult),
                         reads=[b_psC, b_mui4], writes=[b_RKT])
                    for hi, h, pr, rows in heads:
                        f.op(pe, lambda: T.matmul(out=psC[:, hi, 0:64], lhsT=AKT[:, hi, :], rhs=Vtm[:, tl, h * 64:(h + 1) * 64],
                                                  start=True, stop=True), reads=[b_AKT, b_Vtm], writes=[b_psC])
                    f.op(act, lambda: S.copy(out=P1[:], in_=psC[:, :, 0:64]), reads=[b_psC], writes=[b_P1])
                    gp = slice(2 * Gi, 2 * Gi + 2)
                    for c in range(2 if rl >= 5 else 0):
                        crow = slice(64 * c, 64 * c + 64)
                        smp = (Tg - 32) * 2 + c
                        if is_smp:
                            f.dma(sp, H[:, gp, :], swkv_d[smp, gp, :, :].rearrange("a p i -> p a i"), b_H, writes=[b_H])
                        hb, bhb = Hb[c]
                        f.op(pool, lambda: G.tensor_copy(out=hb[:, gp, :], in_=H[:, gp, :]), reads=[b_H], writes=[bhb])
                        for hi, h, pr, rows in heads:
                            f.op(pe, lambda: T.matmul(out=psA[:, hi, 0:64], lhsT=AT[rows, pr, tc], rhs=hb[rows, pr, :], start=True, stop=True),
                                 reads=[b_AT, bhb], writes=[b_psA], rg=rows.start)
                        f.op(dve, lambda: V.tensor_tensor(out=Wb[crow, :, :], in0=psA[crow, :, 0:64], in1=P1[crow, :, :], op=ALU.add),
                             reads=[b_psA, b_P1], writes=[b_Wb])
                        for hi, h, pr, rows in heads:
                            f.op(pe, lambda: T.matmul(out=psA[:, hi, 64:128], lhsT=TT[crow, hi, :], rhs=Wb[crow, hi, :], start=True, stop=True),
                                 reads=[bTT, b_Wb], writes=[b_psA], rg=crow.start)
                        f.op(act, lambda: S.copy(out=Ub[crow, :, :], in_=psA[crow, :, 64:128]), reads=[b_psA], writes=[b_Ub])
                        for hi, h, pr, rows in heads:
                            pcs = slice(pr * 128, (pr + 1) * 128)
                            f.op(pe, lambda: T.matmul(out=psB[:, hi, 0:64], lhsT=Btm[crow, tl, pcs], rhs=Ub[crow, hi, :], start=True, stop=False),
                                 reads=[b_Btm, b_Ub], writes=[b_psB], rg=crow.start)
                            f.op(pe, lambda: T.matmul(out=psB[:, hi, 0:64], lhsT=Ktm[crow, tl, pcs], rhs=Vtm[crow, tl, h * 64:(h + 1) * 64],
                                                      start=False, stop=True), reads=[b_Ktm, b_Vtm], writes=[b_psB], rg=crow.start)
                        for hi, h, pr, rows in heads:
                            gcol = Eneg[rows, pr, tl * 128 + 64 * c + 63:tl * 128 + 64 * c + 64]
                            f.op(pool, lambda: G.tensor_scalar(out=H[rows, pr, :], in0=H[rows, pr, :], scalar1=gcol, scalar2=None, op0=ALU.mult),
                                 reads=[b_H, b_Eneg], writes=[b_H])
                            f.op(dve, lambda: V.scalar_tensor_tensor(out=H[rows, pr, :], in0=psB[rows, hi, 0:64], scalar=gcol, in1=H[rows, pr, :],
                                                                     op0=ALU.mult, op1=ALU.add), reads=[b_psB, b_H, b_Eneg], writes=[b_H])
                        if is_smp:
                            f.dma(sp, wkv_out[1 + smp, gp, :, :].rearrange("a p i -> p a i"), H[:, gp, :], b_H, reads=[b_H], is_output=True)
                        elif Tg == 31 and c == 1:
                            f.dma(sp, wkv_out[0, gp, :, :].rearrange("a p i -> p a i"), H[:, gp, :], b_H, reads=[b_H], is_output=True)
                    if is_loc and rl >= 6:
                        (hb0, bhb0), (hb1, bhb1) = Hb
                        for hi, h, pr, rows in heads:
                            f.op(pe, lambda: T.matmul(out=psB[:, hi, 64:128], lhsT=RT0[rows, pr, tc], rhs=hb0[rows, pr, :], start=True, stop=False),
                                 reads=[b_RT0, bhb0], writes=[b_psB], rg=rows.start)
                            f.op(pe, lambda: T.matmul(out=psB[:, hi, 64:128], lhsT=RT1[rows, pr, tc], rhs=hb1[rows, pr, :], start=False, stop=False),
                                 reads=[b_RT1, bhb1], writes=[b_psB], rg=rows.start)
                            f.op(pe, lambda: T.matmul(out=psB[:, hi, 64:128], lhsT=RBT[:, hi, :], rhs=Ub[:, hi, :], start=False, stop=False),
                                 reads=[b_RBT, b_Ub], writes=[b_psB])
                            f.op(pe, lambda: T.matmul(out=psB[:, hi, 64:128], lhsT=RKT[:, hi, :], rhs=Vtm[:, tl, h * 64:(h + 1) * 64],
                                                      start=False, stop=True), reads=[b_RKT, b_Vtm], writes=[b_psB])
                        f.op(act, lambda: S.copy(out=Yall[:, 4 * Gi:4 * Gi + 4, :], in_=psB[:, :, 64:128]), reads=[b_psB], writes=[b_Yall])
                if is_loc and rl >= 7:
                    lcol = Tg * 128 - NPRE
                    f.op(dve, lambda: V.reduce_sum(out=gst[:, 0:8], in_=Yall[:], axis=AX.X), reads=[b_Yall], writes=[b_gst])
                    f.op(act, lambda: S.activation(out=Ysq, in_=Yall[:], func=AF.Square), reads=[b_Yall], writes=[b_Ysq])
                    f.op(dve, lambda: V.reduce_sum(out=gst[:, 8:16], in_=Ysq, axis=AX.X), reads=[b_Ysq, b_gst], writes=[b_gst])
                    f.op(dve, lambda: V.tensor_scalar(out=gst[:, 0:16], in0=gst[:, 0:16], scalar1=1.0 / 64, scalar2=None, op0=ALU.mult),
                         reads=[b_gst], writes=[b_gst])
                    f.op(dve, lambda: V.tensor_tensor(out=gst[:, 16:24], in0=gst[:, 0:8], in1=gst[:, 0:8], op=ALU.mult),
                         reads=[b_gst], writes=[b_gst])
                    f.op(dve, lambda: V.tensor_tensor(out=gst[:, 16:24], in0=gst[:, 8:16], in1=gst[:, 16:24], op=ALU.subtract),
                         reads=[b_gst], writes=[b_gst])
                    f.op(dve, lambda: V.tensor_scalar(out=gst[:, 16:24], in0=gst[:, 16:24], scalar1=64e-5, scalar2=None, op0=ALU.add),
                         reads=[b_gst], writes=[b_gst])
                    f.op(act, lambda: S.activation(out=gst[:, 16:24], in_=gst[:, 16:24], func=AF.Sqrt), reads=[b_gst], writes=[b_gst])
                    f.op(dve, lambda: V.reciprocal(out=gst[:, 24:32], in_=gst[:, 16:24]), reads=[b_gst], writes=[b_gst])
                    f.op(dve, lambda: V.tensor_tensor(out=Yall[:], in0=Yall[:], in1=gst[:, 0:8].unsqueeze(2).to_broadcast([128, 8, 64]),
                                                      op=ALU.subtract), reads=[b_Yall, b_gst], writes=[b_Yall])
                    f.op(dve, lambda: V.tensor_tensor(out=Yall[:], in0=Yall[:], in1=gst[:, 24:32].unsqueeze(2).to_broadcast([128, 8, 64]),
                                                      op=ALU.mult), reads=[b_Yall, b_gst], writes=[b_Yall])
                    yf = Yall[:].rearrange("p a b -> p (a b)")
                    f.op(dve, lambda: V.tensor_tensor(out=yf, in0=yf, in1=prow[:, GNW:GNW + 512], op=ALU.mult),
                         reads=[b_Yall, b_prow], writes=[b_Yall])
                    f.op(pool, lambda: G.tensor_tensor(out=yf, in0=yf, in1=prow[:, GNB:GNB + 512], op=ALU.add),
                         reads=[b_Yall, b_prow], writes=[b_Yall])
                    for pr in range(4):
                        f.op(pe, lambda: T.transpose(out=ptV[:, pr * 128:(pr + 1) * 128], in_=yf[:, pr * 128:(pr + 1) * 128], identity=ident_f[:]),
                             reads=[b_Yall, b_identf], writes=[b_ptV])
                    f.op(dve, lambda: V.tensor_tensor(out=ytmp, in0=ptV[:].rearrange("p (a b) -> p a b", a=4), in1=bonT[:, :, tc], op=ALU.add),
                         reads=[b_ptV, b_bonT], writes=[b_ytmp])
                    f.op(dve, lambda: V.tensor_tensor(out=yrT[:, :, lcol:lcol + 128], in0=ytmp, in1=gT[:, :, tc], op=ALU.mult),
                         reads=[b_ytmp, b_gT], writes=[b_yrT])
        f.barrier_all()
        f.release(mR)

    if "rwkv" in parts:
        rwkv_phase()

    if stage >= 1.5 and "attn" in parts:
        m2 = f.mark()
        wv, b_wv = f.sbuf("wv", [128, 8, 512], BF16)
        wload(wv[:], b_wv, w_in[:, 1024:1536].rearrange("(c p) n -> p c n", p=128))
        Vh, b_Vh = f.sbuf("Vh", [128, NT, 129], BF16)
        f.op(pool, lambda: G.memset(Vh[:], 1.0), writes=[b_Vh])
        pvr = Ring([f.psum("pv%d" % i, [128, 512], F32) for i in range(1)])
        vor = Ring([f.sbuf("vo%d" % i, [128, 128], F32) for i in range(2)])

        def v_project(h):
            for t in range(NT):
                pv, bpv = pvr.next()
                ht, bht, lc = hT_cols(t * 128, 128)
                for c in range(8):
                    f.op(pe, lambda c=c: T.matmul(out=pv[:, 0:128], lhsT=ht[:, c, lc:lc + 128], rhs=wv[:, c, h * 128:(h + 1) * 128],
                                                  start=(c == 0), stop=(c == 7)), reads=[bht, b_wv], writes=[bpv])
                f.op(act, lambda: S.copy(out=Vh[:, t, 0:128], in_=pv[:, 0:128]), reads=[bpv], writes=[b_Vh])
                if t >= 16:
                    vo, bvo = vor.next()
                    f.op(dve, lambda: V.tensor_copy(out=vo[:], in_=pv[:, 0:128]), reads=[bpv], writes=[bvo])
                    f.dma(sp, v_out[(t - 16) * 128:(t - 15) * 128, h * 128:(h + 1) * 128], vo[:], bvo, reads=[bvo], is_output=True)

        KT_, b_KT = f.sbuf("KhT", [68, 2, NCOL], BF16)
        QT_, b_QT = f.sbuf("QhT", [68, 2, NLOC], BF16)
        wq, b_wq = f.sbuf("wq", [128, 8, 128], BF16)
        wk, b_wk = f.sbuf("wk", [128, 8, 128], BF16)
        pqr = Ring([f.psum("pq%d" % i, [64, 512], F32) for i in range(1)])
        psq, b_psq = f.psum("psq", [64, 512], F32)
        sqt, b_sqt = f.sbuf("sqt", [64, 512], F32)
        rnt, b_rnt = f.sbuf("rnt", [64, 512], F32)
        kor = Ring([f.sbuf("ko%d" % i, [64, 512], F32) for i in range(1)])
        pS_r = Ring([f.psum("pS%d" % i, [128, 2, 256], F32) for i in range(2)])
        pO, b_pO = f.psum("pO", [128, 2, 2, 256], F32)
        PTr = Ring([f.sbuf("PT%d" % i, [128, 2, 256], BF16) for i in range(2)])
        osb, b_osb = f.sbuf("osb", [128, 128], F32)
        osb2, b_osb2 = f.sbuf("osb2", [128, 128], F32)
        obf, b_obf = f.sbuf("obf", [128, 128], BF16)
        ost, b_ost = f.sbuf("ost", [128, 8], F32)
        pTo, b_pTo = f.psum("pTo", [128, 1024], BF16)

        def qk_project(wt, bwt, ncols, dstT, bdst, gcol, is_k):
            nb = (ncols + 511) // 512
            for bi in range(nb):
                c0 = bi * 512
                n = min(512, ncols - c0)
                gc0 = c0 if is_k else c0 + NPRE
                ht, bht, lc = hT_cols(gc0, n)
                for cmp_ in range(2):
                    pq, bpq = pqr.next()
                    for c in range(8):
                        f.op(pe, lambda c=c: T.matmul(out=pq[:, 0:n], lhsT=wt[:, c, cmp_ * 64:(cmp_ + 1) * 64],
                                                      rhs=ht[:, c, lc:lc + n], start=(c == 0), stop=(c == 7)),
                             reads=[bht, bwt], writes=[bpq])
                    f.op(act, lambda: S.activation(out=sqt[:, 0:n], in_=pq[:, 0:n], func=AF.Square),
                         reads=[bpq], writes=[b_sqt])
                    f.op(pe, lambda: T.matmul(out=psq[:, 0:n], lhsT=bones[0:64, 0:64], rhs=sqt[:, 0:n], start=True, stop=True),
                         reads=[b_sqt, b_bones], writes=[b_psq])
                    f.op(dve, lambda: V.tensor_scalar(out=rnt[:, 0:n], in0=psq[:, 0:n], scalar1=1.0 / 64, scalar2=1e-6,
                                                      op0=ALU.mult, op1=ALU.add), reads=[b_psq], writes=[b_rnt])
                    f.op(act, lambda: S.activation(out=rnt[:, 0:n], in_=rnt[:, 0:n], func=AF.Sqrt), reads=[b_rnt], writes=[b_rnt])
                    f.op(dve, lambda: V.reciprocal(out=rnt[:, 0:n], in_=rnt[:, 0:n]), reads=[b_rnt], writes=[b_rnt])
                    f.op(dve, lambda: V.scalar_tensor_tensor(out=dstT[0:64, cmp_, c0:c0 + n], in0=pq[:, 0:n], scalar=gcol,
                                                             in1=rnt[:, 0:n], op0=ALU.mult, op1=ALU.mult),
                         reads=[bpq, b_rnt, b_pvec, b_pv2], writes=[bdst])
                    if is_k and gc0 >= NPRE:
                        ko, bko = kor.next()
                        f.op(pool if False else dve, lambda: V.scalar_tensor_tensor(out=ko[:, 0:n], in0=pq[:, 0:n], scalar=gcol,
                                                                 in1=rnt[:, 0:n], op0=ALU.mult, op1=ALU.mult),
                             reads=[bpq, b_rnt, b_pvec], writes=[bko])
                        r0 = cur_h[0] * 128 + cmp_ * 64
                        f.dma(sp, k_out[r0:r0 + 64, gc0 - NPRE:gc0 - NPRE + n], ko[:, 0:n], bko, reads=[bko], is_output=True)

        qs_tm, b_qs = f.sbuf("qs_tm", [128, 2, 512], BF16)
        ks_tm, b_ks = f.sbuf("ks_tm", [128, 2, 512], BF16)
        vs_tm, b_vs = f.sbuf("vs_tm", [128, 2, 512], BF16)
        cur_h = [0]
        for h in range((4 if dbg is None else 1) if stage >= 1.6 else 0):
            cur_h[0] = h
            wload(wq[:], b_wq, w_in[:, h * 128:(h + 1) * 128].rearrange("(c p) n -> p c n", p=128))
            wload(wk[:], b_wk, w_in[:, 512 + h * 128:512 + (h + 1) * 128].rearrange("(c p) n -> p c n", p=128))
            for cmp_ in range(2):
                f.dma(pool, KT_[64:68, cmp_, 0:4096], kb_d[h, :, :], b_KT, writes=[b_KT])
                f.dma(pool, QT_[64:68, cmp_, 0:2048], qb_d[h, :, :], b_QT, writes=[b_QT])
            if dbg == "attn":
                f.op(dve, lambda: V.memset(KT_[:], 0.125), writes=[b_KT])
                f.op(dve, lambda: V.memset(QT_[:], 0.125), writes=[b_QT])
            elif stage >= 2.0:
                v_project(h)
            if stage >= 2.1 and dbg is None:
                qk_project(wk, b_wk, NCOL, KT_, b_KT, pvec[0:64, KG_C:KG_C + 1], True)
                qk_project(wq, b_wq, NLOC, QT_, b_QT, pv2[0:64, 22:23], False)
            if "samp" in parts:
                for tl in range(2):
                    for cmp_ in range(2):
                        f.op(pe, lambda: T.transpose(out=pTo[:, 0:64], in_=QT_[0:64, cmp_, 2048 + tl * 128:2048 + (tl + 1) * 128], identity=ident_b[0:64, 0:64]),
                             reads=[b_QT, b_identb], writes=[b_pTo])
                        f.op(act, lambda: S.copy(out=qs_tm[:, tl, h * 128 + cmp_ * 64:h * 128 + (cmp_ + 1) * 64], in_=pTo[:, 0:64]),
                             reads=[b_pTo], writes=[b_qs])
                        f.op(pe, lambda: T.transpose(out=pTo[:, 0:64], in_=KT_[0:64, cmp_, 4096 + tl * 128:4096 + (tl + 1) * 128], identity=ident_b[0:64, 0:64]),
                             reads=[b_KT, b_identb], writes=[b_pTo])
                        f.op(act, lambda: S.copy(out=ks_tm[:, tl, h * 128 + cmp_ * 64:h * 128 + (cmp_ + 1) * 64], in_=pTo[:, 0:64]),
                             reads=[b_pTo], writes=[b_ks])
                    f.op(pool, lambda: G.tensor_copy(out=vs_tm[:, tl, h * 128:(h + 1) * 128], in_=Vh[:, 32 + tl, 0:128]),
                         reads=[b_Vh], writes=[b_vs])
            if stage < 2.2:
                continue
            def emit_S(g, kt):
                d1 = (kt == 16 + 2 * g + 1)
                q0 = 128 if d1 else 0
                pS, bpS = pS_r.next()
                for cmp_ in range(2):
                    f.op(pe, lambda cmp_=cmp_: T.matmul(out=pS[:, cmp_, q0:256], lhsT=KT_[:, cmp_, kt * 128:(kt + 1) * 128],
                                                        rhs=QT_[:, cmp_, g * 256 + q0:(g + 1) * 256], start=True, stop=True),
                         reads=[b_KT, b_QT], writes=[bpS])
                return (g, kt, pS, bpS)

            def emit_rest(st_):
                g, kt, pS, bpS = st_
                d0 = (kt == 16 + 2 * g)
                d1 = (kt == 16 + 2 * g + 1)
                q0 = 128 if d1 else 0
                PT, bPT = PTr.next()
                f.op(act, lambda: S.activation(out=PT[:, :, q0:256], in_=pS[:, :, q0:256], func=AF.Exp),
                     reads=[bpS], writes=[bPT])
                if d0 or d1:
                    for cmp_ in range(2):
                        f.op(pool, lambda cmp_=cmp_: G.tensor_tensor(out=PT[:, cmp_, q0:q0 + 128], in0=PT[:, cmp_, q0:q0 + 128],
                                                                     in1=tri_b[:], op=ALU.mult),
                             reads=[bPT, b_tri], writes=[bPT])
                for cmp_ in range(2):
                    for sub in range(2):
                        if d1 and sub == 0:
                            continue
                        last = (kt == 16 + 2 * g + sub)
                        f.op(pe, lambda cmp_=cmp_, sub=sub, last=last: T.matmul(
                            out=pO[:, cmp_, sub, 0:129], lhsT=PT[:, cmp_, sub * 128:(sub + 1) * 128], rhs=Vh[:, kt, :],
                            start=(kt == 0), stop=last), reads=[bPT, b_Vh], writes=[b_pO])

            steps = [(g, kt) for g in range(ng) for kt in range(16 + 2 * g + 2)]
            pend = None
            for si in range(len(steps) + 1):
                cur = emit_S(*steps[si]) if si < len(steps) else None
                if pend is not None:
                    emit_rest(pend)
                    g = pend[0]
                    if pend[1] != 16 + 2 * g + 1:
                        pend = cur
                        continue
                else:
                    pend = cur
                    continue
                pend = cur
                for sub in range(2):
                    lcq = g * 256 + sub * 128
                    f.op(dve, lambda: V.reciprocal(out=ost[:, 0:2], in_=pO[:, :, sub, 128]), reads=[b_pO], writes=[b_ost])
                    f.op(dve, lambda: V.tensor_tensor(out=ost[:, 2:3], in0=ost[:, 1:2], in1=NLAM, op=ALU.mult),
                         reads=[b_ost, b_lt], writes=[b_ost])
                    f.op(dve, lambda: V.tensor_scalar(out=osb[:], in0=pO[:, 0, sub, 0:128], scalar1=ost[:, 0:1], scalar2=None,
                                                      op0=ALU.mult), reads=[b_pO, b_ost], writes=[b_osb])
                    f.op(dve, lambda: V.scalar_tensor_tensor(out=osb[:], in0=pO[:, 1, sub, 0:128], scalar=ost[:, 2:3], in1=osb[:],
                                                             op0=ALU.mult, op1=ALU.add), reads=[b_pO, b_ost, b_osb], writes=[b_osb])
                    finalize_o(f, nc, osb, b_osb, osb2, b_osb2, obf, b_obf, ost, b_ost, prow, b_prow, GAO, lam_init,
                               pTo, b_pTo, ident_b, b_identb, oT, b_oT, h, lcq)

        if "samp" in parts:
            selb, b_selb = cload("selb", sel_d[:, :], [128, 512], BF16)
            sel0b, b_sel0b = cload("sel0b", sel0_d[:, :], [128, 256], BF16)
            sbias, b_sbias = cload("sbias", sbias_d[:, :], [128, 520])
            hmask, b_hmask = cload("hmask", hmask_d[:, :], [8, 4])
            e0t, b_e0 = cload("e0t", e0_d[:, :], [8, 4])
            e1t, b_e1 = cload("e1t", e1_d[:, :], [8, 4])
            iop, b_iop = cload("iop", iota_d[:, :], [128, 1], I32)
            pti, b_pti = f.sbuf("pti", [128, 256], I32)
            f.dma(sp, pti[:], ptab_d[0:1, :].partition_broadcast(128), b_pti, writes=[b_pti])
            ptf, b_ptf = f.sbuf("ptf", [128, 256], F32)
            iof, b_iof = f.sbuf("iof", [128, 1], F32)
            idx, b_idx = f.sbuf("idx", [128, 256], I32)
            f.op(dve, lambda: V.tensor_copy(out=ptf[:], in_=pti[:]), reads=[b_pti], writes=[b_ptf])
            f.op(dve, lambda: V.tensor_copy(out=iof[:], in_=iop[:]), reads=[b_iop], writes=[b_iof])
            f.op(dve, lambda: V.tensor_scalar(out=ptf[:], in0=ptf[:], scalar1=128.0, scalar2=iof[:, 0:1], op0=ALU.mult, op1=ALU.add),
                 reads=[b_ptf, b_iof], writes=[b_ptf])
            f.op(dve, lambda: V.tensor_copy(out=idx[:], in_=ptf[:]), reads=[b_ptf], writes=[b_idx])
            cmb, b_cmb = f.sbuf("cmb", [8, 4], F32)
            f.op(dve, lambda: V.scalar_tensor_tensor(out=cmb[:], in0=e1t[:], scalar=lt[0:8, 5:6], in1=e0t[:], op0=ALU.mult, op1=ALU.add),
                 reads=[b_e1, b_e0, b_lt], writes=[b_cmb])
            onesc, b_onesc = f.sbuf("onesc", [128, 1], F32)
            f.op(dve, lambda: V.memset(onesc[:], 1.0), writes=[b_onesc])
            Ktr = Ring([f.sbuf("Kt%d" % i, [128, 512], F32) for i in range(2)])
            Vtr = Ring([f.sbuf("Vt%d" % i, [128, 512], F32) for i in range(2)])
            prodr = Ring([f.sbuf("prod%d" % i, [128, 512], F32) for i in range(1)])
            qbc, b_qbc = f.sbuf("qbc", [128, 512], F32)
            spg_r = Ring([f.sbuf("spg%d" % i, [128, 16], F32) for i in range(3)])
            osm, b_osm = f.sbuf("osm", [8, 512], F32)
            osel, b_osel = f.sbuf("osel", [8, 128], F32)
            ofin, b_ofin = f.sbuf("ofin", [4, 128], F32)
            ofin2, b_ofin2 = f.sbuf("ofin2", [4, 128], F32)
            ofb, b_ofb = f.sbuf("ofb", [4, 128], BF16)
            sst, b_sst = f.sbuf("sst", [8, 8], F32)
            pS0, bpS0 = pS_r.items[0]
            pS0f = pS0[:].rearrange("p a b -> p (a b)")
            pso = pO[0:8, 0, :, :].rearrange("p a b -> p (a b)")
            psz = pO[0:8, 1, 0, 0:1]
            pvr0, bpvr0 = pvr.items[0]
            for s in range(4):
                tl = s // 2
                col = 2048 + 128 * tl + 64 * (s % 2) + 1
                f.op(pe, lambda: T.matmul(out=pS0f, lhsT=selb[:, s * 128:(s + 1) * 128], rhs=qs_tm[:, tl, :], start=True, stop=True),
                     reads=[b_selb, b_qs], writes=[bpS0])
                f.op(act, lambda: S.copy(out=qbc[:], in_=pS0f), reads=[bpS0], writes=[b_qbc])
                for pg in range(65):
                    Kt, bKt = Ktr.next(); Vt, bVt = Vtr.next(); prod, bprod = prodr.next(); spg, bspg = spg_r.next()
                    if pg < 64:
                        ic = s * 64 + pg
                        f.dma(pool, None, None, bKt, reads=[b_idx], writes=[bKt],
                              fn=lambda: G.indirect_dma_start(out=Kt[:, :], out_offset=None, in_=ck_d[:, :],
                                                              in_offset=bass.IndirectOffsetOnAxis(ap=idx[:, ic:ic + 1], axis=0)))
                        f.dma(pool, None, None, bVt, reads=[b_idx], writes=[bVt],
                              fn=lambda: G.indirect_dma_start(out=Vt[:, :], out_offset=None, in_=cv_d[:, :],
                                                              in_offset=bass.IndirectOffsetOnAxis(ap=idx[:, ic:ic + 1], axis=0)))
                    else:
                        f.op(pe, lambda: T.matmul(out=pS0f, lhsT=sel0b[:, (s % 2) * 128:(s % 2 + 1) * 128], rhs=ks_tm[:, tl, :], start=True, stop=True),
                             reads=[b_sel0b, b_ks], writes=[bpS0])
                        f.op(act, lambda: S.copy(out=Kt[:], in_=pS0f), reads=[bpS0], writes=[bKt])
                        f.op(pe, lambda: T.matmul(out=pS0f, lhsT=sel0b[:, (s % 2) * 128:(s % 2 + 1) * 128], rhs=vs_tm[:, tl, :], start=True, stop=True),
                             reads=[b_sel0b, b_vs], writes=[bpS0])
                        f.op(act, lambda: S.copy(out=Vt[:], in_=pS0f), reads=[bpS0], writes=[bVt])
                    f.op(pool, lambda: G.tensor_tensor(out=prod[:], in0=Kt[:], in1=qbc[:], op=ALU.mult), reads=[bKt, b_qbc], writes=[bprod])
                    f.op(dve, lambda: V.reduce_sum(out=spg[:, 0:8], in_=prod[:].rearrange("p (g d) -> p g d", g=8), axis=AX.X),
                         reads=[bprod], writes=[bspg])
                    f.op(dve, lambda: V.tensor_tensor(out=spg[:, 0:8], in0=spg[:, 0:8], in1=sbias[:, pg * 8:(pg + 1) * 8], op=ALU.add),
                         reads=[bspg, b_sbias], writes=[bspg])
                    f.op(act, lambda: S.activation(out=spg[:, 8:16], in_=spg[:, 0:8], func=AF.Exp), reads=[bspg], writes=[bspg])
                    f.op(pe, lambda: T.matmul(out=pso, lhsT=spg[:, 8:16], rhs=Vt[:], start=(pg == 0), stop=(pg == 64)),
                         reads=[bspg, bVt], writes=[b_pO])
                    f.op(pe, lambda: T.matmul(out=psz, lhsT=spg[:, 8:16], rhs=onesc[:], start=(pg == 0), stop=(pg == 64)),
                         reads=[bspg, b_onesc], writes=[b_pO])
                f.op(dve, lambda: V.reciprocal(out=sst[:, 0:1], in_=psz), reads=[b_pO], writes=[b_sst])
                f.op(dve, lambda: V.tensor_scalar(out=osm[:], in0=pso, scalar1=sst[:, 0:1], scalar2=None, op0=ALU.mult),
                     reads=[b_pO, b_sst], writes=[b_osm])
                f.op(dve, lambda: V.tensor_tensor(out=osm[:].rearrange("p (h d) -> p h d", h=4), in0=osm[:].rearrange("p (h d) -> p h d", h=4),
                                                  in1=hmask[:].unsqueeze(2).to_broadcast([8, 4, 128]), op=ALU.mult),
                     reads=[b_osm, b_hmask], writes=[b_osm])
                f.op(dve, lambda: V.reduce_sum(out=osel[:], in_=osm[:].rearrange("p (h d) -> p d h", h=4), axis=AX.X),
                     reads=[b_osm], writes=[b_osel])
                f.op(pe, lambda: T.matmul(out=pvr0[0:4, 0:128], lhsT=cmb[:], rhs=osel[:], start=True, stop=True),
                     reads=[b_cmb, b_osel], writes=[bpvr0])
                f.op(dve, lambda: V.tensor_copy(out=ofin[:], in_=pvr0[0:4, 0:128]), reads=[bpvr0], writes=[b_ofin])
                f.op(dve, lambda: V.memset(sst[0:4, 1:2], 0.0), writes=[b_sst])
                f.op(act, lambda: S.activation(out=ofin2[:], in_=ofin[:], func=AF.Square, accum_out=sst[0:4, 1:2]),
                     reads=[b_ofin, b_sst], writes=[b_ofin2, b_sst])
                f.op(dve, lambda: V.tensor_scalar(out=sst[0:4, 2:3], in0=sst[0:4, 1:2], scalar1=1.0 / 128, scalar2=1e-6, op0=ALU.mult, op1=ALU.add),
                     reads=[b_sst], writes=[b_sst])
                f.op(act, lambda: S.activation(out=sst[0:4, 2:3], in_=sst[0:4, 2:3], func=AF.Sqrt), reads=[b_sst], writes=[b_sst])
                f.op(dve, lambda: V.reciprocal(out=sst[0:4, 2:3], in_=sst[0:4, 2:3]), reads=[b_sst], writes=[b_sst])
                f.op(dve, lambda: V.tensor_scalar(out=sst[0:4, 3:4], in0=sst[0:4, 2:3], scalar1=(1.0 - lam_init), scalar2=None, op0=ALU.mult),
                     reads=[b_sst], writes=[b_sst])
                f.op(dve, lambda: V.scalar_tensor_tensor(out=ofb[:], in0=ofin[:], scalar=sst[0:4, 3:4], in1=prow[0:4, GAO:GAO + 128],
                                                         op0=ALU.mult, op1=ALU.mult), reads=[b_ofin, b_sst, b_prow], writes=[b_ofb])
                f.op(pe, lambda: T.transpose(out=pTo[:, 0:4], in_=ofb[:], identity=ident_b[0:4, 0:4]), reads=[b_ofb, b_identb], writes=[b_pTo])
                f.op(act, lambda: S.copy(out=oT[:, :, col], in_=pTo[:, 0:4]), reads=[b_pTo], writes=[b_oT])
        f.release(m2)
        f.barrier_all()

    f.release(m_pre)
    m_pre = f.mark()
    if "epi" in parts:
        epilogue()
    f.dma(pool, dbg_d[0].rearrange("a p n -> p a n"), oT[:], b_oT, reads=[b_oT], is_output=True)
    f.dma(pool, dbg_d[1].rearrange("a p n -> p a n"), yrT[:], b_yrT, reads=[b_yrT], is_output=True)
    f.release(m_pre)
    f.finish()
    f.close()
    return nc


def finalize_o(f, nc, osb, b_osb, osb2, b_osb2, obf, b_obf, ost, b_ost, prow, b_prow, GAO, lam_init,
               pTo, b_pTo, ident_b, b_identb, oT, b_oT, h, lcq):
    V, S, T = nc.vector, nc.scalar, nc.tensor
    dve, act, pe = f.dve, f.act, f.pe
    f.op(dve, lambda: V.memset(ost[:, 4:5], 0.0), writes=[b_ost])
    f.op(act, lambda: S.activation(out=osb2[:], in_=osb[:], func=AF.Square, accum_out=ost[:, 4:5]),
         reads=[b_osb, b_ost], writes=[b_osb2, b_ost])
    f.op(dve, lambda: V.tensor_scalar(out=ost[:, 5:6], in0=ost[:, 4:5], scalar1=1.0 / 128, scalar2=1e-6,
                                      op0=ALU.mult, op1=ALU.add), reads=[b_ost], writes=[b_ost])
    f.op(act, lambda: S.activation(out=ost[:, 5:6], in_=ost[:, 5:6], func=AF.Sqrt), reads=[b_ost], writes=[b_ost])
    f.op(dve, lambda: V.reciprocal(out=ost[:, 5:6], in_=ost[:, 5:6]), reads=[b_ost], writes=[b_ost])
    f.op(dve, lambda: V.tensor_scalar(out=ost[:, 6:7], in0=ost[:, 5:6], scalar1=(1.0 - lam_init), scalar2=None,
                                      op0=ALU.mult), reads=[b_ost], writes=[b_ost])
    f.op(dve, lambda: V.scalar_tensor_tensor(out=obf[:], in0=osb[:], scalar=ost[:, 6:7], in1=prow[:, GAO:GAO + 128],
                                             op0=ALU.mult, op1=ALU.mult), reads=[b_osb, b_ost, b_prow], writes=[b_obf])
    f.op(pe, lambda: T.transpose(out=pTo[:, 0:128], in_=obf[:], identity=ident_b[:]), reads=[b_obf, b_identb], writes=[b_pTo])
    f.op(act, lambda: S.copy(out=oT[:, h, lcq:lcq + 128], in_=pTo[:, 0:128]), reads=[b_pTo], writes=[b_oT])


def sample_attention(f, nc, L):
    pass


def _consts(half):
    c = {}
    c["ident"] = np.eye(128, dtype=np.float32)
    s_idx = np.arange(128)[:, None]; t_idx = np.arange(128)[None, :]
    same = (s_idx // 64) == (t_idx // 64)
    mu = ((s_idx < t_idx) & same).astype(np.float32)
    mui = ((s_idx <= t_idx) & same).astype(np.float32)
    ml = mu.T.copy()
    c["mu4"] = np.tile(mu, (1, 4)); c["ml4"] = np.tile(ml, (1, 4)); c["mui4"] = np.tile(mui, (1, 4))
    c["tri"] = (s_idx <= t_idx).astype(np.float32)
    cmk = np.zeros((128, 256), np.float32)
    cmk[:, [1, 65, 129, 193]] = 1.0
    c["colmask"] = cmk
    rm = np.ones((128, 512), np.float32); rm[:, ::64] = 0.0
    c["resetm"] = rm
    col = np.arange(512)[None, :]
    c["cm0"] = np.broadcast_to(((col % 128) < 64).astype(np.float32), (128, 512)).copy()
    c["cm1"] = np.broadcast_to(((col % 128) >= 64).astype(np.float32), (128, 512)).copy()
    c["bones"] = same.astype(np.float32)
    sel = np.zeros((128, 4, 128), np.float32)
    sel0 = np.zeros((128, 2, 128), np.float32)
    for s in range(4):
        sel[1 + 64 * (s % 2), s, :] = 1.0
    for r in range(2):
        sel0[1 + 64 * r, r, 0] = 1.0
    c["sel"] = sel.reshape(128, 512); c["sel0"] = sel0.reshape(128, 256)
    c["iotap"] = np.arange(128, dtype=np.int32).reshape(128, 1)
    hm = np.zeros((8, 4), np.float32); e0 = np.zeros((8, 4), np.float32); e1 = np.zeros((8, 4), np.float32)
    for h in range(4):
        for cc in range(2):
            hm[h * 2 + cc, h] = 1.0
        e0[h * 2, h] = 1.0; e1[h * 2 + 1, h] = 1.0
    c["hmask"] = hm; c["e0"] = e0; c["e1"] = e1
    slopes = np.array([2.0 ** (-8.0 * (h + 1) / 4) for h in range(4)], np.float64)
    kcol = np.arange(4096)
    kb = np.zeros((4, 4, 4096), np.float32)
    qb = np.zeros((4, 4, 2048), np.float32)
    qpos = 2048 + np.arange(2048)
    for h in range(4):
        kb[h, 0] = slopes[h] * 128 * (kcol // 128)
        if half == 0:
            kb[h, 0, :2048] = NEG
        kb[h, 1] = slopes[h] * (kcol % 128)
        kb[h, 2] = 1.0; kb[h, 3] = 1.0
        qb[h, 0] = 1.0; qb[h, 1] = 1.0
        qb[h, 2] = -slopes[h] * 128 * (qpos // 128)
        qb[h, 3] = -slopes[h] * (qpos % 128)
    c["kb"] = kb; c["qb"] = qb
    sb = np.zeros((128, 65, 8), np.float32)
    slot = np.arange(128)[:, None]
    for pg in range(64):
        dist = 8192 - (128 * pg + slot)
        for h in range(4):
            sb[:, pg, 2 * h] = (-slopes[h] * dist)[:, 0]; sb[:, pg, 2 * h + 1] = (-slopes[h] * dist)[:, 0]
    sb[1:, 64, :] = NEG
    c["sbias"] = sb.reshape(128, 65 * 8)
    return c


_NC_CACHE = {}
_LAST = None


def kernel(**inp):
    f32 = np.float32
    xp = np.asarray(inp["x_prompt"], f32); xs = np.asarray(inp["x_sample"], f32)
    g = lambda k: np.ascontiguousarray(np.asarray(inp[k], f32)[0])
    w_in = g("w_in")
    shared = {
        "w_in": w_in, "w_pa": g("w_pa"), "w_pb": g("w_pb"), "w_out": g("w_out"),
        "w_gate": g("w_gate"), "w_up": g("w_up"), "w_down": g("w_down"),
        "wa2": np.ascontiguousarray(np.concatenate([g("w2"), g("a2")], 0)), "g2": g("g2"),
        "ck": np.ascontiguousarray(np.asarray(inp["cache_k"], f32).reshape(2560 * 128, 512)),
        "cv": np.ascontiguousarray(np.asarray(inp["cache_v"], f32).reshape(2560 * 128, 512)),
    }
    pvec = np.zeros((128, 36), f32)
    pvec[:, 0:14] = g("shift_mu").reshape(14, 128).T
    pvec[:, 14:18] = g("w0").reshape(4, 128).T
    pvec[:, 18:22] = g("a0").reshape(4, 128).T
    pvec[:, 22:26] = g("k_k").reshape(4, 128).T
    pvec[:, 26:30] = g("k_a").reshape(4, 128).T
    pvec[:, 30:34] = g("r_k").reshape(4, 128).T
    pvec[:, 34] = np.tile(g("q_gain"), 2); pvec[:, 35] = np.tile(g("k_gain"), 2)
    prow = np.concatenate([g("norm_mix"), g("norm_ffn"), g("attn_out_gain"), g("gn_w"), g("gn_b"),
                           g("lambda_q1"), g("lambda_k1"), g("lambda_q2"), g("lambda_k2")]).reshape(1, 3456).astype(f32)
    shared["pvec"] = pvec; shared["prow"] = prow
    consts = [_consts(0), _consts(1)]
    ptab = np.asarray(inp["page_table"], np.int32)
    swkv = np.asarray(inp["state_wkv"], f32)[0]
    sshift = np.asarray(inp["state_shift"], f32)[0]
    in_maps = []
    for c in range(8):
        b, half = c // 2, c % 2
        xin = np.zeros((NCOL, D), f32)
        if half == 1:
            xin[0:2048] = xp[b, 0:2048]
        xin[2048:4096] = xp[b, half * 2048:(half + 1) * 2048]
        for s in range(4):
            xin[4096 + 128 * (s // 2) + 64 * (s % 2) + 1] = xs[4 * c + s, 0]
        m = dict(shared)
        m.update(consts[half])
        m["xin"] = xin
        m["ptab"] = np.ascontiguousarray(ptab[4 * c:4 * c + 4].reshape(1, 256))
        sw = swkv[4 * c:4 * c + 4].transpose(0, 1, 3, 2).reshape(4, 4, 128, 64)
        m["swkv"] = np.ascontiguousarray(sw)
        m["sshift"] = np.ascontiguousarray(sshift[4 * c:4 * c + 4].reshape(4, 14, 128).transpose(2, 1, 0))
        in_maps.append(m)
    if "nc" not in _NC_CACHE:
        _NC_CACHE["nc"] = build()
        _NC_CACHE["small"] = False
    if _NC_CACHE.get("small"):
        for m in in_maps:
            m["ck"] = m["ck"][:128]; m["cv"] = m["cv"][:128]
    nc = _NC_CACHE["nc"]
    res = run_bass_kernel_spmd(nc, in_maps, core_ids=list(range(8)))
    R = res.results
    global _LAST
    _LAST = R
    y_p = np.zeros((4, 4096, 1024), f32); y_s = np.zeros((32, 1, 1024), f32)
    k_p = np.zeros((1, 4, 4096, 4, 128), f32); v_p = np.zeros((1, 4, 4096, 4, 128), f32)
    wkv_p = np.zeros((1, 4, 8, 64, 64), f32); sh_p = np.zeros((1, 4, 1792), f32)
    k_s = np.zeros((1, 32, 1, 4, 128), f32); v_s = np.zeros((1, 32, 1, 4, 128), f32)
    wkv_s = np.zeros((1, 32, 8, 64, 64), f32); sh_s = np.zeros((1, 32, 1792), f32)
    for c in range(8):
        b, half = c // 2, c % 2
        r = R[c]
        sl = slice(half * 2048, (half + 1) * 2048)
        y_p[b, sl] = r["y_out"][0:2048]
        kT = r["k_out"]
        k_p[0, b, sl] = kT[:, 0:2048].T.reshape(2048, 4, 128)
        v_p[0, b, sl] = r["v_out"][0:2048].reshape(2048, 4, 128)
        wk = r["wkv_out"].reshape(5, 8, 64, 64)
        po = r["p_out"]
        if half == 1:
            wkv_p[0, b] = wk[0].transpose(0, 2, 1)
            sh_p[0, b] = po[:, :, 127].reshape(1792)
        for s in range(4):
            col = 128 * (s // 2) + 64 * (s % 2) + 1
            y_s[4 * c + s, 0] = r["y_out"][2048 + col]
            k_s[0, 4 * c + s, 0] = kT[:, 2048 + col].reshape(4, 128)
            v_s[0, 4 * c + s, 0] = r["v_out"][2048 + col].reshape(4, 128)
            wkv_s[0, 4 * c + s] = wk[1 + s].transpose(0, 2, 1)
            sh_s[0, 4 * c + s] = po[:, :, 128 + col].reshape(1792)
    return (y_p, y_s, k_p, v_p, wkv_p, sh_p, k_s, v_s, wkv_s, sh_s)
```

```python
import math
import numpy as np
import concourse.bass as bass
import concourse.mybir as mybir
from concourse.bass_utils import run_bass_kernel_spmd

F32 = mybir.dt.float32
BF16 = mybir.dt.bfloat16
I32 = mybir.dt.int32
ALU = mybir.AluOpType
AF = mybir.ActivationFunctionType
AX = mybir.AxisListType

SEM_EPOCH = 30000
NPRE, NOWN, NSMP = 2048, 2048, 256
NCOL = NPRE + NOWN + NSMP
NLOC = NOWN + NSMP
NT = NCOL // 128
D = 1024
DFF = 2816
NEG = -30000.0


class Eng:
    def __init__(self, fw, name, e):
        self.fw = fw; self.name = name; self.e = e
        self.sems = []; self.count = 0; self.epoch = -1; self.known = {}
        self._new_epoch()

    def _new_epoch(self):
        self.epoch += 1
        self.count = 0
        self.sems.append(self.fw.new_sem("%s_e%d" % (self.name, self.epoch)))


class Buf:
    _uid = [0]

    def __init__(self, name, psum=False):
        Buf._uid[0] += 1
        self.uid = Buf._uid[0]
        self.name = name; self.w = None; self.r = []; self.dsem = None; self.dcount = 0; self.psum = psum


class FW:
    def __init__(self, nc):
        self.nc = nc
        self._stack = []
        self._semstack = []
        self.engs = {}
        for name, e in (("pe", nc.tensor), ("act", nc.scalar), ("dve", nc.vector),
                        ("pool", nc.gpsimd), ("sp", nc.sync)):
            self.engs[name] = Eng(self, name, e)
        self.pe = self.engs["pe"]; self.act = self.engs["act"]; self.dve = self.engs["dve"]
        self.pool = self.engs["pool"]; self.sp = self.engs["sp"]
        self.nbuf = 0
        self.out_bufs = []
        self.free_dsems = []

    def new_sem(self, name):
        cm = self.nc.semaphore(name)
        s = cm.__enter__()
        self._semstack.append(cm)
        return s

    def sbuf(self, name, shape, dt):
        cm = self.nc.sbuf_tensor("sb_" + name, list(shape), dt)
        t = cm.__enter__()
        self._stack.append(cm)
        self.nbuf += 1
        return t, Buf(name)

    def psum(self, name, shape, dt):
        cm = self.nc.psum_tensor("ps_" + name, list(shape), dt)
        t = cm.__enter__()
        self._stack.append(cm)
        return t, Buf(name, psum=True)

    def mark(self):
        return len(self._stack)

    def release(self, mark):
        while len(self._stack) > mark:
            self._stack.pop().__exit__(None, None, None)

    def _need(self, eng, stamp, kind):
        if stamp is None:
            return
        if stamp[0] == 'e':
            _, pe_, ep, cnt = stamp
            if pe_ is eng:
                if eng.name == "pe" or kind != "raw":
                    return
            key = (pe_.name, ep)
            if eng.known.get(key, 0) >= cnt:
                return
            eng.e.wait_ge(pe_.sems[ep], cnt)
            eng.known[key] = cnt
        else:
            _, sem, val, key = stamp
            if eng.known.get(key, 0) >= val:
                return
            eng.e.wait_ge(sem, val)
            eng.known[key] = val

    def _deps(self, eng, reads, writes):
        for b in reads:
            self._need(eng, b.w, "raw")
        for b in writes:
            self._need(eng, b.w, "waw")
            for s in b.r:
                self._need(eng, s, "war")

    def _record(self, st, reads, writes):
        for b in reads:
            b.r.append(st)
        for b in writes:
            b.w = st
            b.r = []

    def op(self, eng, fn, reads=(), writes=(), rg=None):
        if eng.name == "pe":
            for b in writes:
                prev = getattr(b, "rg", None)
                if rg is not None and prev is not None and prev != rg and b.w is not None and b.w[0] == 'e' and b.w[1] is eng:
                    _, pe_, ep, cnt = b.w
                    key = (pe_.name, ep)
                    if eng.known.get(key, 0) < cnt:
                        eng.e.wait_ge(pe_.sems[ep], cnt)
                        eng.known[key] = cnt
                b.rg = rg
        if eng.name != "pe":
            px = [b for b in reads if b.psum]
            if px:
                reads = [b for b in reads if not b.psum]
                writes = list(writes) + [b for b in px if b not in writes]
        self._deps(eng, reads, writes)
        if eng.count >= SEM_EPOCH:
            eng._new_epoch()
        ins = fn()
        eng.count += 1
        ins.then_inc(eng.sems[eng.epoch], 1)
        self._record(('e', eng, eng.epoch, eng.count), reads, writes)
        return ins

    def dma(self, eng, out, in_, sb, reads=(), writes=(), is_output=False, fn=None):
        self._deps(eng, reads, writes)
        b = sb
        if b.dsem is None:
            b.dsem = self.new_sem("d_" + b.name)
        ins = eng.e.dma_start(out=out, in_=in_) if fn is None else fn()
        ins.then_inc(b.dsem, 16)
        b.dcount += 16
        self._record(('d', b.dsem, b.dcount, ("dma", b.uid)), reads, writes)
        if is_output and b not in self.out_bufs:
            self.out_bufs.append(b)
        return ins

    def finish(self):
        for b in self.out_bufs:
            self.sp.e.wait_ge(b.dsem, b.dcount)

    def barrier_all(self):
        for a in self.engs.values():
            for o in self.engs.values():
                if o is a or o.count == 0:
                    continue
                key = (o.name, o.epoch)
                if a.known.get(key, 0) >= o.count:
                    continue
                a.e.wait_ge(o.sems[o.epoch], o.count)
                a.known[key] = o.count

    def close(self):
        self.release(0)
        while self._semstack:
            self._semstack.pop().__exit__(None, None, None)


class Ring:
    def __init__(self, items):
        self.items = items; self.i = 0

    def next(self):
        it = self.items[self.i % len(self.items)]
        self.i += 1
        return it


def build(stage=99, small=False, dbg=None, ng=8, parts=("rwkv", "attn", "epi", "samp"), nb=None, rl=9, cut=99):
    nc = bass.Bass("TRN2", target_bir_lowering=False)
    V, S, G, T = nc.vector, nc.scalar, nc.gpsimd, nc.tensor

    def din(name, shape, dt=F32):
        return nc.dram_tensor(name, list(shape), dt, kind="ExternalInput").ap()

    def dout(name, shape, dt=F32):
        return nc.dram_tensor(name, list(shape), dt, kind="ExternalOutput").ap()

    xin = din("xin", [NCOL, D])
    w_in = din("w_in", [D, 5376]); w_pa = din("w_pa", [512, D]); w_pb = din("w_pb", [512, D])
    w_out = din("w_out", [D, D]); w_gate = din("w_gate", [D, DFF]); w_up = din("w_up", [D, DFF])
    w_down = din("w_down", [DFF, D])
    wa2_d = din("wa2", [128, 512]); g2_d = din("g2", [128, 512])
    pvec_d = din("pvec", [128, 36]); prow_d = din("prow", [1, 3456])
    kb_d = din("kb", [4, 4, 4096]); qb_d = din("qb", [4, 4, 2048])
    sshift_d = din("sshift", [128, 14, 4]); swkv_d = din("swkv", [4, 4, 128, 64])
    NPG = 128 if small else 2560 * 128
    ck_d = din("ck", [NPG, 512]); cv_d = din("cv", [NPG, 512])
    ptab_d = din("ptab", [1, 256], I32)
    sbias_d = din("sbias", [128, 65 * 8])
    ident_d = din("ident", [128, 128]); mu4_d = din("mu4", [128, 512]); ml4_d = din("ml4", [128, 512])
    mui4_d = din("mui4", [128, 512]); tri_d = din("tri", [128, 128]); colmask_d = din("colmask", [128, 256])
    resetm_d = din("resetm", [128, 512]); cm0_d = din("cm0", [128, 512]); cm1_d = din("cm1", [128, 512])
    bones_d = din("bones", [128, 128]); sel_d = din("sel", [128, 512]); sel0_d = din("sel0", [128, 256])
    iota_d = din("iotap", [128, 1], I32); hmask_d = din("hmask", [8, 4]); e0_d = din("e0", [8, 4]); e1_d = din("e1", [8, 4])

    y_out = dout("y_out", [NLOC, D]); k_out = dout("k_out", [512, NLOC]); v_out = dout("v_out", [NLOC, 512])
    wkv_out = dout("wkv_out", [5, 4, 128, 64]); p_out = dout("p_out", [14, 128, 384])

    dbg_d = dout("dbg", [2, 4, 128, NLOC])
    f = FW(nc)
    pe, act, dve, pool, sp = f.pe, f.act, f.dve, f.pool, f.sp

    def cload(name, src, shape, dt=F32, q=None):
        t, b = f.sbuf(name, shape, dt)
        if dt == F32 or dt == I32:
            f.dma(q or sp, t[:], src, b, writes=[b])
        else:
            f.dma(pool, t[:], src, b, writes=[b])
        return t, b

    ident_f, b_identf = cload("ident_f", ident_d[:, :], [128, 128])
    ident_b, b_identb = cload("ident_b", ident_d[:, :], [128, 128], BF16)
    tri_b, b_tri = cload("tri_b", tri_d[:, :], [128, 128], BF16)
    bones, b_bones = cload("bones", bones_d[:, :], [128, 128])
    pvec, b_pvec = cload("pvec", pvec_d[:, :], [128, 36])
    prow, b_prow = f.sbuf("prow", [128, 3456], F32)
    f.dma(sp, prow[:], prow_d[0:1, :].partition_broadcast(128), b_prow, writes=[b_prow])
    pv2, b_pv2 = f.sbuf("pv2", [128, 24], F32)
    f.op(dve, lambda: V.tensor_scalar(out=pv2[:, 0:14], in0=pvec[:, 0:14], scalar1=-1.0, scalar2=1.0,
                                      op0=ALU.mult, op1=ALU.add), reads=[b_pvec], writes=[b_pv2])
    f.op(dve, lambda: V.tensor_scalar(out=pv2[:, 14:18], in0=pvec[:, 14:18], scalar1=-1.0, scalar2=None,
                                      op0=ALU.mult), reads=[b_pvec], writes=[b_pv2])
    f.op(dve, lambda: V.tensor_scalar(out=pv2[:, 18:22], in0=pvec[:, 26:30], scalar1=-1.0, scalar2=1.0,
                                      op0=ALU.mult, op1=ALU.add), reads=[b_pvec], writes=[b_pv2])
    f.op(dve, lambda: V.tensor_scalar(out=pv2[:, 22:23], in0=pvec[:, 34:35], scalar1=0.125, scalar2=None,
                                      op0=ALU.mult), reads=[b_pvec], writes=[b_pv2])
    MU_C, W0_C, A0_C, KK_C, KA_C, RK_C, QG_C, KG_C = 0, 14, 18, 22, 26, 30, 34, 35
    GMIX, GFFN, GAO, GNW, GNB, LAM = 0, 1024, 2048, 2176, 2688, 3200

    lt, b_lt = f.sbuf("lt", [128, 8], F32)
    junk64, b_junk64 = f.sbuf("junk64", [128, 64], F32)
    f.op(dve, lambda: V.memset(lt[:], 0.0), writes=[b_lt])
    f.op(dve, lambda: V.tensor_tensor(out=junk64[:], in0=prow[:, LAM:LAM + 64], in1=prow[:, LAM + 64:LAM + 128],
                                      op=ALU.mult), reads=[b_prow], writes=[b_junk64])
    f.op(dve, lambda: V.reduce_sum(out=lt[:, 0:1], in_=junk64[:], axis=AX.X), reads=[b_junk64], writes=[b_lt])
    f.op(dve, lambda: V.tensor_tensor(out=junk64[:], in0=prow[:, LAM + 128:LAM + 192], in1=prow[:, LAM + 192:LAM + 256],
                                      op=ALU.mult), reads=[b_prow, b_lt], writes=[b_junk64])
    f.op(dve, lambda: V.reduce_sum(out=lt[:, 1:2], in_=junk64[:], axis=AX.X), reads=[b_junk64], writes=[b_lt])
    f.op(act, lambda: S.activation(out=lt[:, 2:4], in_=lt[:, 0:2], func=AF.Exp), reads=[b_lt], writes=[b_lt])
    lam_init = 0.8 - 0.6 * math.exp(-0.3 * 0)
    f.op(dve, lambda: V.tensor_tensor(out=lt[:, 4:5], in0=lt[:, 3:4], in1=lt[:, 2:3], op=ALU.subtract),
         reads=[b_lt], writes=[b_lt])
    f.op(dve, lambda: V.tensor_scalar(out=lt[:, 5:6], in0=lt[:, 4:5], scalar1=-lam_init, scalar2=None, op0=ALU.add),
         reads=[b_lt], writes=[b_lt])
    NLAM = lt[:, 5:6]

    hT_loc, b_hTloc = f.sbuf("hT_loc", [128, 8, NLOC], BF16)
    yrT, b_yrT = f.sbuf("yrT", [128, 4, NLOC], BF16)

    m_pre = f.mark()
    hT_pre, b_hTpre = f.sbuf("hT_pre", [128, 8, NPRE], BF16)

    def hT_cols(c0, n):
        if c0 < NPRE:
            return hT_pre, b_hTpre, c0
        return hT_loc, b_hTloc, c0 - NPRE

    m1 = f.mark()
    xr = Ring([f.sbuf("x%d" % i, [128, D], F32) for i in range(3)])
    xbr = Ring([f.sbuf("xb%d" % i, [128, D], BF16) for i in range(2)])
    junk, b_junk = f.sbuf("junk", [128, D], F32)
    ssr = Ring([f.sbuf("ss%d" % i, [128, 2], F32) for i in range(3)])
    ptr = Ring([f.psum("pt%d" % i, [128, 8, 128], BF16) for i in range(2)])
    for t in range(NT if dbg is None else 0):
        xt, bx = xr.next(); xb, bxb = xbr.next(); ss, bss = ssr.next(); pt, bpt = ptr.next()
        f.dma(sp, xt[:], xin[t * 128:(t + 1) * 128, :], bx, writes=[bx])
        f.op(dve, lambda: V.memset(ss[:], 0.0), writes=[bss])
        f.op(act, lambda: S.activation(out=junk[:], in_=xt[:], func=AF.Square, accum_out=ss[:, 0:1]),
             reads=[bx, bss], writes=[b_junk, bss])
        f.op(dve, lambda: V.tensor_scalar(out=ss[:, 1:2], in0=ss[:, 0:1], scalar1=1.0 / D, scalar2=1e-6,
                                          op0=ALU.mult, op1=ALU.add), reads=[bss], writes=[bss])
        f.op(act, lambda: S.activation(out=ss[:, 1:2], in_=ss[:, 1:2], func=AF.Sqrt), reads=[bss], writes=[bss])
        f.op(dve, lambda: V.reciprocal(out=ss[:, 1:2], in_=ss[:, 1:2]), reads=[bss], writes=[bss])
        f.op(dve, lambda: V.scalar_tensor_tensor(out=xb[:], in0=xt[:], scalar=ss[:, 1:2], in1=prow[:, GMIX:GMIX + D],
                                                 op0=ALU.mult, op1=ALU.mult), reads=[bx, bss, b_prow], writes=[bxb])
        for c in range(8):
            f.op(pe, lambda c=c: T.transpose(out=pt[:, c, :], in_=xb[:, c * 128:(c + 1) * 128], identity=ident_b[:]),
                 reads=[bxb, b_identb], writes=[bpt])
        ht, bht, lc = hT_cols(t * 128, 128)
        f.op(act, lambda: S.copy(out=ht[:, :, lc:lc + 128], in_=pt[:]), reads=[bpt], writes=[bht])
    f.release(m1)
    f.barrier_all()

    def wload(dst, bdst, src):
        f.dma(pool, dst, src, bdst, writes=[bdst])


    def epilogue():
        mE = f.mark()
        wpa, b_wpa = f.sbuf("wpa", [128, 4, D], BF16)
        wpb, b_wpb = f.sbuf("wpb", [128, 4, D], BF16)
        wo, b_wo = f.sbuf("wo", [128, 8, D], BF16)
        wload(wpa[:], b_wpa, w_pa.rearrange("(c p) n -> p c n", p=128))
        wload(wpb[:], b_wpb, w_pb.rearrange("(c p) n -> p c n", p=128))
        wload(wo[:], b_wo, w_out.rearrange("(c p) n -> p c n", p=128))
        SB = 384
        bank = [f.psum("bank%d" % i, [128, 512], F32) for i in range(8)]
        wgr = Ring([f.sbuf("wg%d" % i, [128, 8, 128], BF16) for i in range(4)])
        wdr = Ring([f.sbuf("wd%d" % i, [128, D], BF16) for i in range(2)])
        mT, b_mT = f.sbuf("mT", [128, 8, SB], BF16)
        hfT, b_hfT = f.sbuf("hfT", [128, 8, SB], BF16)
        x1, b_x1 = f.sbuf("x1e", [128, 3, D], F32)
        xr2 = Ring([f.sbuf("xe%d" % i, [128, D], F32) for i in range(1)])
        sga, b_sga = f.sbuf("sga", [128, SB], F32)
        sgb, b_sgb = f.sbuf("sgb", [128, SB], F32)
        tA, b_tA = f.sbuf("tA", [128, SB], F32)
        hfb, b_hfb = f.sbuf("hfb", [128, D], BF16)
        ejunk, b_ejunk = f.sbuf("ejunk", [128, D], F32)
        est, b_est = f.sbuf("est", [128, 4], F32)
        actr = Ring([f.sbuf("act%d" % i, [128, SB], BF16) for i in range(2)])
        yor = Ring([f.sbuf("yo%d" % i, [128, 512], F32) for i in range(2)])
        for sbi in range(NLOC // SB):
            c0 = sbi * SB
            for ch in range(8):
                cs = slice(ch * 128, (ch + 1) * 128)
                (pa, bpa), (pb_, bpb), (pga, bpga), (pgb, bpgb) = bank[0], bank[1], bank[2], bank[3]
                wga, bwga = wgr.next(); wgb, bwgb = wgr.next()
                wload(wga[:], bwga, w_in[:, 3328 + ch * 128:3328 + (ch + 1) * 128].rearrange("(c p) n -> p c n", p=128))
                wload(wgb[:], bwgb, w_in[:, 4352 + ch * 128:4352 + (ch + 1) * 128].rearrange("(c p) n -> p c n", p=128))
                for h in range(4):
                    f.op(pe, lambda h=h: T.matmul(out=pa[:, 0:SB], lhsT=wpa[:, h, cs], rhs=oT[:, h, c0:c0 + SB], start=(h == 0), stop=(h == 3)),
                         reads=[b_wpa, b_oT], writes=[bpa])
                for h in range(4):
                    f.op(pe, lambda h=h: T.matmul(out=pb_[:, 0:SB], lhsT=wpb[:, h, cs], rhs=yrT[:, h, c0:c0 + SB], start=(h == 0), stop=(h == 3)),
                         reads=[b_wpb, b_yrT], writes=[bpb])
                for c in range(8):
                    f.op(pe, lambda c=c: T.matmul(out=pga[:, 0:SB], lhsT=wga[:, c, :], rhs=hT_loc[:, c, c0:c0 + SB], start=(c == 0), stop=(c == 7)),
                         reads=[bwga, b_hTloc], writes=[bpga])
                for c in range(8):
                    f.op(pe, lambda c=c: T.matmul(out=pgb[:, 0:SB], lhsT=wgb[:, c, :], rhs=hT_loc[:, c, c0:c0 + SB], start=(c == 0), stop=(c == 7)),
                         reads=[bwgb, b_hTloc], writes=[bpgb])
                f.op(act, lambda: S.activation(out=sga[:], in_=pga[:, 0:SB], func=AF.Sigmoid), reads=[bpga], writes=[b_sga])
                f.op(act, lambda: S.activation(out=sgb[:], in_=pgb[:, 0:SB], func=AF.Sigmoid), reads=[bpgb], writes=[b_sgb])
                f.op(dve, lambda: V.tensor_tensor(out=tA[:], in0=pa[:, 0:SB], in1=sga[:], op=ALU.mult), reads=[bpa, b_sga], writes=[b_tA])
                f.op(dve, lambda: V.tensor_tensor(out=sgb[:], in0=pb_[:, 0:SB], in1=sgb[:], op=ALU.mult), reads=[bpb, b_sgb], writes=[b_sgb])
                f.op(pool, lambda: G.tensor_tensor(out=mT[:, ch, :], in0=tA[:], in1=sgb[:], op=ALU.add), reads=[b_tA, b_sgb], writes=[b_mT])
            for tl in range(3):
                ts_ = slice(tl * 128, (tl + 1) * 128)
                xt, bxt = xr2.next()
                row0 = NPRE + c0 + tl * 128
                f.dma(sp, xt[:], xin[row0:row0 + 128, :], bxt, writes=[bxt])
                for hf_ in range(2):
                    px, bpx = bank[4 + hf_]
                    for k in range(8):
                        f.op(pe, lambda k=k: T.matmul(out=px[:], lhsT=mT[:, k, ts_], rhs=wo[:, k, hf_ * 512:(hf_ + 1) * 512],
                                                      start=(k == 0), stop=(k == 7)), reads=[b_mT, b_wo], writes=[bpx])
                    f.op(dve, lambda: V.tensor_tensor(out=x1[:, tl, hf_ * 512:(hf_ + 1) * 512], in0=px[:], in1=xt[:, hf_ * 512:(hf_ + 1) * 512],
                                                      op=ALU.add), reads=[bpx, bxt], writes=[b_x1])
                f.op(dve, lambda: V.memset(est[:, 0:1], 0.0), writes=[b_est])
                f.op(act, lambda: S.activation(out=ejunk[:], in_=x1[:, tl, :], func=AF.Square, accum_out=est[:, 0:1]),
                     reads=[b_x1, b_est], writes=[b_ejunk, b_est])
                f.op(dve, lambda: V.tensor_scalar(out=est[:, 1:2], in0=est[:, 0:1], scalar1=1.0 / D, scalar2=1e-6, op0=ALU.mult, op1=ALU.add),
                     reads=[b_est], writes=[b_est])
                f.op(act, lambda: S.activation(out=est[:, 1:2], in_=est[:, 1:2], func=AF.Sqrt), reads=[b_est], writes=[b_est])
                f.op(dve, lambda: V.reciprocal(out=est[:, 1:2], in_=est[:, 1:2]), reads=[b_est], writes=[b_est])
                f.op(dve, lambda: V.scalar_tensor_tensor(out=hfb[:], in0=x1[:, tl, :], scalar=est[:, 1:2], in1=prow[:, GFFN:GFFN + D],
                                                         op0=ALU.mult, op1=ALU.mult), reads=[b_x1, b_est, b_prow], writes=[b_hfb])
                ptr_, bptr = bank[6]
                ptb = ptr_[:].bitcast(BF16)
                for c in range(8):
                    f.op(pe, lambda c=c: T.transpose(out=ptb[:, c * 128:(c + 1) * 128], in_=hfb[:, c * 128:(c + 1) * 128], identity=ident_b[:]),
                         reads=[b_hfb, b_identb], writes=[bptr])
                f.op(act, lambda: S.copy(out=hfT[:, :, ts_], in_=ptb.rearrange("p (c n) -> p c n", c=8)), reads=[bptr], writes=[b_hfT])
            for ffc in range(DFF // 128):
                fs = slice(ffc * 128, (ffc + 1) * 128)
                wg, bwg = wgr.next(); wu, bwu = wgr.next(); wd, bwd = wdr.next()
                wload(wg[:], bwg, w_gate[:, fs].rearrange("(c p) n -> p c n", p=128))
                wload(wu[:], bwu, w_up[:, fs].rearrange("(c p) n -> p c n", p=128))
                wload(wd[:], bwd, w_down[fs, :])
                (pg, bpg), (pu, bpu) = bank[6], bank[7]
                for c in range(8):
                    f.op(pe, lambda c=c: T.matmul(out=pg[:, 0:SB], lhsT=wg[:, c, :], rhs=hfT[:, c, :], start=(c == 0), stop=(c == 7)),
                         reads=[bwg, b_hfT], writes=[bpg])
                for c in range(8):
                    f.op(pe, lambda c=c: T.matmul(out=pu[:, 0:SB], lhsT=wu[:, c, :], rhs=hfT[:, c, :], start=(c == 0), stop=(c == 7)),
                         reads=[bwu, b_hfT], writes=[bpu])
                f.op(act, lambda: S.activation(out=sga[:], in_=pg[:, 0:SB], func=AF.Silu), reads=[bpg], writes=[b_sga])
                at, bat = actr.next()
                f.op(dve, lambda: V.tensor_tensor(out=at[:], in0=pu[:, 0:SB], in1=sga[:], op=ALU.mult), reads=[bpu, b_sga], writes=[bat])
                for tl in range(3):
                    for hf_ in range(2):
                        pd, bpd = bank[tl * 2 + hf_]
                        f.op(pe, lambda: T.matmul(out=pd[:], lhsT=at[:, tl * 128:(tl + 1) * 128], rhs=wd[:, hf_ * 512:(hf_ + 1) * 512],
                                                  start=(ffc == 0), stop=(ffc == DFF // 128 - 1)), reads=[bat, bwd], writes=[bpd])
            for tl in range(3):
                for hf_ in range(2):
                    pd, bpd = bank[tl * 2 + hf_]
                    yo, byo = yor.next()
                    f.op(dve, lambda: V.tensor_tensor(out=yo[:], in0=pd[:], in1=x1[:, tl, hf_ * 512:(hf_ + 1) * 512], op=ALU.add),
                         reads=[bpd, b_x1], writes=[byo])
                    r0 = c0 + tl * 128
                    f.dma(sp, y_out[r0:r0 + 128, hf_ * 512:(hf_ + 1) * 512], yo[:], byo, reads=[byo], is_output=True)
        f.barrier_all()
        f.release(mE)

    def rwkv_phase():
        mR = f.mark()
        BW = 256
        NB = NCOL // BW
        mu4b, b_mu4 = cload("mu4b", mu4_d[:, :], [128, 512], BF16)
        ml4b, b_ml4 = cload("ml4b", ml4_d[:, :], [128, 512], BF16)
        mui4b, b_mui4 = cload("mui4b", mui4_d[:, :], [128, 512], BF16)
        ident4, b_ident4 = f.sbuf("ident4", [128, 512], BF16)
        for i in range(4):
            f.dma(pool, ident4[:, i * 128:(i + 1) * 128], ident_d[:, :], b_ident4, writes=[b_ident4])
        resetm, b_resetm = cload("resetm", resetm_d[:, 0:BW], [128, BW])
        cm0, b_cm0 = cload("cm0", cm0_d[:, 0:BW], [128, BW], BF16)
        cm1, b_cm1 = cload("cm1", cm1_d[:, 0:BW], [128, BW], BF16)
        colmask, b_colmask = cload("colmask", colmask_d[:, :], [128, 256])
        wa2b, b_wa2 = cload("wa2b", wa2_d[:, :], [128, 512], BF16)
        g2b, b_g2 = cload("g2b", g2_d[:, :], [128, 512], BF16)
        sshift, b_sshift = cload("sshift", sshift_d[:, :, :], [128, 14, 4])
        cst, b_cst = f.sbuf("cst", [128, 4], F32)
        f.op(dve, lambda: V.memset(cst[:, 0:1], 1.0), writes=[b_cst])
        f.op(dve, lambda: V.memset(cst[:, 1:2], -0.5), writes=[b_cst])
        f.op(dve, lambda: V.memset(cst[:, 2:3], 64e-5), writes=[b_cst])
        carry, b_carry = f.sbuf("carry", [128, 14], F32)
        f.op(dve, lambda: V.memset(carry[:], 0.0), writes=[b_carry])
        H, b_H = f.sbuf("H", [128, 4, 64], F32)
        f.op(dve, lambda: V.memset(H[:], 0.0), writes=[b_H])
        Hb = [f.sbuf("Hb%d" % i, [128, 4, 64], BF16) for i in range(2)]
        wrr, _bw = f.sbuf("wrr", [128, 8, 1792], BF16)
        b_wrc = [Buf("wrr%d" % i) for i in range(14)]
        for i in range(14):
            wload(wrr[:, :, i * 128:(i + 1) * 128], b_wrc[i], w_in[:, 1536 + i * 128:1536 + (i + 1) * 128].rearrange("(c p) n -> p c n", p=128))
        ppr = Ring([f.psum("pp%d" % i, [128, 512], F32) for i in range(3)])
        pxr = Ring([f.sbuf("px%d" % i, [128, BW + 1], F32) for i in range(2)])
        rT, b_rT = f.sbuf("rT", [128, 4, BW], F32)
        kT, b_kT = f.sbuf("kT", [128, 4, BW], F32)
        vT, b_vT = f.sbuf("vT", [128, 4, BW], F32)
        m12, b_m12 = f.sbuf("m12", [128, BW], F32)
        m13, b_m13 = f.sbuf("m13", [128, BW], F32)
        twxa, b_twxa = f.sbuf("twxa", [128, BW], BF16)
        sg, b_sg = f.sbuf("sg", [128, BW], BF16)
        scr = [f.sbuf("scr%d" % i, [128, BW], F32) for i in range(8)]
        AT, b_AT = f.sbuf("AT", [128, 4, BW], BF16)
        BT, b_BT = f.sbuf("BT", [128, 4, BW], BF16)
        KT, b_KT2 = f.sbuf("KT", [128, 4, BW], BF16)
        RT, b_RT = f.sbuf("RT", [128, 4, BW], BF16)
        RT0, b_RT0 = f.sbuf("RT0", [128, 4, BW], BF16)
        RT1, b_RT1 = f.sbuf("RT1", [128, 4, BW], BF16)
        Eneg, b_Eneg = f.sbuf("Eneg", [128, 4, BW], F32)
        bonT, b_bonT = f.sbuf("bonT", [128, 4, BW], BF16)
        gT, b_gT = f.sbuf("gT", [128, 4, BW], BF16)
        Vtm, b_Vtm = f.sbuf("Vtm", [128, 2, 512], BF16)
        Btm, b_Btm = f.sbuf("Btm", [128, 2, 512], BF16)
        Ktm, b_Ktm = f.sbuf("Ktm", [128, 2, 512], BF16)
        ptBK, b_ptBK = f.psum("ptBK", [128, 2, 512], BF16)
        ptV, b_ptV = f.psum("ptV", [128, 512], F32)
        psA, b_psA = f.psum("psA", [128, 4, 128], F32)
        psB, b_psB = f.psum("psB", [128, 4, 128], F32)
        psC, b_psC = f.psum("psC", [128, 4, 128], F32)
        Lk = [f.sbuf("Lk%d" % i, [128, 4, 128], BF16) for i in range(2)]
        Nk = [f.sbuf("Nk%d" % i, [128, 4, 128], BF16) for i in range(2)]
        Yk = [f.sbuf("Yk%d" % i, [128, 4, 128], BF16) for i in range(2)]
        AKT, b_AKT = f.sbuf("AKT", [128, 4, 128], BF16)
        RBT, b_RBT = f.sbuf("RBT", [128, 4, 128], BF16)
        RKT, b_RKT = f.sbuf("RKT", [128, 4, 128], BF16)
        P1, b_P1 = f.sbuf("P1", [128, 4, 64], F32)
        Wb, b_Wb = f.sbuf("Wb", [128, 4, 64], BF16)
        Ub, b_Ub = f.sbuf("Ub", [128, 4, 64], BF16)
        Yall, b_Yall = f.sbuf("Yall", [128, 8, 64], F32)
        gst, b_gst = f.sbuf("gst", [128, 40], F32)
        ytmp2, b_ytmp = f.sbuf("ytmp", [128, 512], F32)
        ytmp = ytmp2[:].rearrange("p (a b) -> p a b", a=4)
        Ysq = ytmp2[:].rearrange("p (a b) -> p a b", a=8)
        b_Ysq = b_ytmp

        def flat(t3):
            return t3[:].rearrange("p a b -> p (a b)")

        for bi in (range(NB) if nb is None else nb):
            col0 = bi * BW
            is_loc = col0 >= NPRE
            is_smp = col0 >= NPRE + NOWN
            ht, bht, lc = hT_cols(col0, BW)
            for ch in range(14):
                bwt = b_wrc[ch]
                pp, bpp = ppr.next()
                for c in range(8):
                    f.op(pe, lambda c=c: T.matmul(out=pp[:, 0:BW], lhsT=wrr[:, c, ch * 128:(ch + 1) * 128], rhs=ht[:, c, lc:lc + BW],
                                                  start=(c == 0), stop=(c == 7)), reads=[bwt, bht], writes=[bpp])
                px, bpx = pxr.next()
                f.op(dve, lambda: V.tensor_copy(out=px[:, 0:1], in_=carry[:, ch:ch + 1]), reads=[b_carry], writes=[bpx])
                f.op(act, lambda: S.copy(out=px[:, 1:BW + 1], in_=pp[:, 0:BW]), reads=[bpp], writes=[bpx])
                f.op(dve, lambda: V.tensor_copy(out=carry[:, ch:ch + 1], in_=px[:, BW:BW + 1]), reads=[bpx], writes=[b_carry])
                if bi == 15:
                    f.dma(sp, p_out[ch, :, 0:128], px[:, 129:257], bpx, reads=[bpx], is_output=True)
                if is_smp:
                    f.dma(sp, p_out[ch, :, 128:384], px[:, 1:257], bpx, reads=[bpx], is_output=True)
                    f.op(dve, lambda: V.tensor_copy(out=px[:, 1:BW + 1:64], in_=sshift[:, ch, :]),
                         reads=[b_sshift], writes=[bpx])
                tmp, btmp = scr[0]
                if ch < 4:
                    dst, bdst = rT[:, ch, :], b_rT
                elif ch < 8:
                    dst, bdst = kT[:, ch - 4, :], b_kT
                elif ch < 12:
                    dst, bdst = vT[:, ch - 8, :], b_vT
                elif ch == 12:
                    dst, bdst = m12[:], b_m12
                else:
                    dst, bdst = m13[:], b_m13
                f.op(pool, lambda: G.tensor_scalar(out=tmp[:], in0=px[:, 0:BW], scalar1=pvec[:, MU_C + ch:MU_C + ch + 1], scalar2=None,
                                                   op0=ALU.mult), reads=[bpx, b_pvec], writes=[btmp])
                f.op(dve, lambda: V.scalar_tensor_tensor(out=dst, in0=px[:, 1:BW + 1], scalar=pv2[:, ch:ch + 1], in1=tmp[:],
                                                         op0=ALU.mult, op1=ALU.add), reads=[bpx, btmp, b_pv2], writes=[bdst])
                if is_smp:
                    f.op(dve, lambda: V.tensor_tensor(out=dst, in0=dst, in1=colmask[:], op=ALU.mult),
                         reads=[bdst, b_colmask], writes=[bdst])
            if rl < 2:
                continue
            f.op(act, lambda: S.activation(out=twxa[0:64, :], in_=m12[0:64, :], func=AF.Tanh), reads=[b_m12], writes=[b_twxa])
            f.op(dve, lambda: V.tensor_copy(out=twxa[64:128, :], in_=m12[64:128, :]), reads=[b_m12], writes=[b_twxa])
            f.op(act, lambda: S.activation(out=sg[:], in_=m13[:], func=AF.Sigmoid), reads=[b_m13], writes=[b_sg])
            for pr in range(4):
                pc = slice(pr * 128, (pr + 1) * 128)
                (s1, bs1), (s2, bs2), (s3, bs3), (s4, bs4), (s5, bs5), (s6, bs6), (s7, bs7) = scr[1:8]
                pp, bpp = ppr.next()
                f.op(pe, lambda: T.matmul(out=pp[:, 0:BW], lhsT=wa2b[0:64, pc], rhs=twxa[0:64, :], start=True, stop=True),
                     reads=[b_wa2, b_twxa], writes=[bpp])
                f.op(act, lambda: S.activation(out=s1[:], in_=pp[:, 0:BW], func=AF.Exp, bias=pv2[:, 14 + pr:15 + pr], scale=-1.0),
                     reads=[bpp, b_pv2], writes=[bs1])
                f.op(act, lambda: S.activation(out=s1[:], in_=s1[:], func=AF.Ln, bias=cst[:, 0:1], scale=1.0),
                     reads=[bs1, b_cst], writes=[bs1])
                f.op(act, lambda: S.activation(out=s2[:], in_=s1[:], func=AF.Exp, bias=cst[:, 1:2], scale=-1.0),
                     reads=[bs1, b_cst], writes=[bs2])
                if is_smp:
                    f.op(dve, lambda: V.tensor_tensor(out=s2[:], in0=s2[:], in1=colmask[:], op=ALU.mult),
                         reads=[bs2, b_colmask], writes=[bs2])
                f.op(dve, lambda: V.tensor_tensor_scan(out=s3[:], data0=resetm[:], data1=s2[:], initial=0.0,
                                                       op0=ALU.mult, op1=ALU.add), reads=[bs2, b_resetm], writes=[bs3])
                f.op(act, lambda: S.activation(out=Eneg[:, pr, :], in_=s3[:], func=AF.Exp, scale=-1.0), reads=[bs3], writes=[b_Eneg])
                f.op(act, lambda: S.activation(out=s4[:], in_=s3[:], func=AF.Exp), reads=[bs3], writes=[bs4])
                f.op(dve, lambda: V.tensor_tensor(out=s5[:], in0=s3[:], in1=s2[:], op=ALU.subtract), reads=[bs3, bs2], writes=[bs5])
                f.op(act, lambda: S.activation(out=s5[:], in_=s5[:], func=AF.Exp, scale=-1.0), reads=[bs5], writes=[bs5])
                pp, bpp = ppr.next()
                f.op(pe, lambda: T.matmul(out=pp[:, 0:BW], lhsT=wa2b[64:128, pc], rhs=twxa[64:128, :], start=True, stop=True),
                     reads=[b_wa2, b_twxa], writes=[bpp])
                f.op(act, lambda: S.activation(out=s6[:], in_=pp[:, 0:BW], func=AF.Sigmoid, bias=pvec[:, A0_C + pr:A0_C + pr + 1], scale=1.0),
                     reads=[bpp, b_pvec], writes=[bs6])
                f.op(dve, lambda: V.tensor_scalar(out=s1[:], in0=kT[:, pr, :], scalar1=pvec[:, KK_C + pr:KK_C + pr + 1], scalar2=None,
                                                  op0=ALU.mult), reads=[b_kT, b_pvec], writes=[bs1])
                f.op(act, lambda: S.activation(out=s7[:], in_=s1[:], func=AF.Square), reads=[bs1], writes=[bs7])
                pp, bpp = ppr.next()
                f.op(pe, lambda: T.matmul(out=pp[:, 0:BW], lhsT=bones[:], rhs=s7[:], start=True, stop=True),
                     reads=[b_bones, bs7], writes=[bpp])
                f.op(dve, lambda: V.tensor_scalar(out=s7[:], in0=pp[:, 0:BW], scalar1=1e-24, scalar2=None, op0=ALU.max),
                     reads=[bpp], writes=[bs7])
                f.op(act, lambda: S.activation(out=s7[:], in_=s7[:], func=AF.Sqrt), reads=[bs7], writes=[bs7])
                f.op(dve, lambda: V.reciprocal(out=s7[:], in_=s7[:]), reads=[bs7], writes=[bs7])
                f.op(dve, lambda: V.tensor_tensor(out=s1[:], in0=s1[:], in1=s7[:], op=ALU.mult), reads=[bs1, bs7], writes=[bs1])
                f.op(dve, lambda: V.tensor_scalar(out=s7[:], in0=s6[:], scalar1=pvec[:, KA_C + pr:KA_C + pr + 1],
                                                  scalar2=pv2[:, 18 + pr:19 + pr], op0=ALU.mult, op1=ALU.add),
                     reads=[bs6, b_pvec, b_pv2], writes=[bs7])
                f.op(dve, lambda: V.tensor_tensor(out=s7[:], in0=s7[:], in1=kT[:, pr, :], op=ALU.mult), reads=[bs7, b_kT], writes=[bs7])
                f.op(dve, lambda: V.scalar_tensor_tensor(out=AT[:, pr, :], in0=s1[:], scalar=-1.0, in1=s5[:], op0=ALU.mult, op1=ALU.mult),
                     reads=[bs1, bs5], writes=[b_AT])
                f.op(pool, lambda: G.tensor_tensor(out=s1[:], in0=s1[:], in1=s6[:], op=ALU.mult), reads=[bs1, bs6], writes=[bs1])
                f.op(pool, lambda: G.tensor_tensor(out=BT[:, pr, :], in0=s1[:], in1=s4[:], op=ALU.mult), reads=[bs1, bs4], writes=[b_BT])
                f.op(pool, lambda: G.tensor_tensor(out=KT[:, pr, :], in0=s7[:], in1=s4[:], op=ALU.mult), reads=[bs7, bs4], writes=[b_KT2])
                f.op(dve, lambda: V.tensor_tensor(out=RT[:, pr, :], in0=rT[:, pr, :], in1=Eneg[:, pr, :], op=ALU.mult),
                     reads=[b_rT, b_Eneg], writes=[b_RT])
                f.op(pool, lambda: G.tensor_tensor(out=RT0[:, pr, :], in0=RT[:, pr, :], in1=cm0[:], op=ALU.mult),
                     reads=[b_RT, b_cm0], writes=[b_RT0])
                f.op(pool, lambda: G.tensor_tensor(out=RT1[:, pr, :], in0=RT[:, pr, :], in1=cm1[:], op=ALU.mult),
                     reads=[b_RT, b_cm1], writes=[b_RT1])
                if is_loc:
                    f.op(dve, lambda: V.scalar_tensor_tensor(out=s1[:], in0=rT[:, pr, :], scalar=pvec[:, RK_C + pr:RK_C + pr + 1], in1=s7[:],
                                                             op0=ALU.mult, op1=ALU.mult), reads=[b_rT, bs7, b_pvec], writes=[bs1])
                    pp, bpp = ppr.next()
                    f.op(pe, lambda: T.matmul(out=pp[:, 0:BW], lhsT=bones[:], rhs=s1[:], start=True, stop=True),
                         reads=[b_bones, bs1], writes=[bpp])
                    f.op(dve, lambda: V.tensor_tensor(out=bonT[:, pr, :], in0=pp[:, 0:BW], in1=vT[:, pr, :], op=ALU.mult),
                         reads=[bpp, b_vT], writes=[b_bonT])
                    pp, bpp = ppr.next()
                    f.op(pe, lambda: T.matmul(out=pp[:, 0:BW], lhsT=g2b[:, pc], rhs=sg[:], start=True, stop=True),
                         reads=[b_g2, b_sg], writes=[bpp])
                    f.op(act, lambda: S.copy(out=gT[:, pr, :], in_=pp[:, 0:BW]), reads=[bpp], writes=[b_gT])
            for tl in range(2 if rl >= 3 else 0):
                Tg = bi * 2 + tl
                tc = slice(tl * 128, (tl + 1) * 128)
                for pr in range(4):
                    pc = slice(pr * 128, (pr + 1) * 128)
                    f.op(pe, lambda: T.transpose(out=ptBK[:, 0, pc], in_=BT[:, pr, tc], identity=ident_b[:]),
                         reads=[b_BT, b_identb], writes=[b_ptBK])
                    f.op(pe, lambda: T.transpose(out=ptBK[:, 1, pc], in_=KT[:, pr, tc], identity=ident_b[:]),
                         reads=[b_KT2, b_identb], writes=[b_ptBK])
                    f.op(pe, lambda: T.transpose(out=ptV[:, pc], in_=vT[:, pr, tc], identity=ident_f[:]),
                         reads=[b_vT, b_identf], writes=[b_ptV])
                f.op(act, lambda: S.copy(out=Btm[:, tl, :], in_=ptBK[:, 0, :]), reads=[b_ptBK], writes=[b_Btm])
                f.op(dve, lambda: V.tensor_copy(out=Ktm[:, tl, :], in_=ptBK[:, 1, :]), reads=[b_ptBK], writes=[b_Ktm])
                f.op(act, lambda: S.copy(out=Vtm[:, tl, :], in_=ptV[:]), reads=[b_ptV], writes=[b_Vtm])
                for Gi in range(2 if rl >= 4 else 0):
                    heads = [(hi, 4 * Gi + hi, (4 * Gi + hi) // 2, slice(64 * ((4 * Gi + hi) % 2), 64 * ((4 * Gi + hi) % 2) + 64))
                             for hi in range(4)]
                    for hi, h, pr, rows in heads:
                        f.op(pe, lambda: T.matmul(out=psA[:, hi, :], lhsT=BT[rows, pr, tc], rhs=AT[rows, pr, tc], start=True, stop=True),
                             reads=[b_BT, b_AT], writes=[b_psA], rg=rows.start)
                    for hi, h, pr, rows in heads:
                        f.op(pe, lambda: T.matmul(out=psB[:, hi, :], lhsT=AT[rows, pr, tc], rhs=BT[rows, pr, tc], start=True, stop=True),
                             reads=[b_BT, b_AT], writes=[b_psB], rg=rows.start)
                    (N0, bN0), (N1, bN1) = Nk
                    (L0, bL0), (L1, bL1) = Lk
                    (Y0, bY0), (Y1, bY1) = Yk
                    f.op(dve, lambda: V.tensor_tensor(out=flat(N0), in0=flat(psA), in1=mu4b[:], op=ALU.mult),
                         reads=[b_psA, b_mu4], writes=[bN0])
                    f.op(dve, lambda: V.tensor_tensor(out=flat(L0), in0=flat(psB), in1=ml4b[:], op=ALU.mult),
                         reads=[b_psB, b_ml4], writes=[bL0])
                    if cut <= 1:
                        continue
                    f.op(pool, lambda: G.tensor_tensor(out=flat(Y0), in0=flat(N0), in1=ident4[:], op=ALU.add),
                         reads=[bN0, b_ident4], writes=[bY0])
                    if cut <= 2:
                        continue
                    for k in range(min(5, cut - 2)):
                        Nc, bNc = Nk[k % 2]; Lc, bLc = Lk[k % 2]; Yc, bYc = Yk[k % 2]
                        Nn, bNn = Nk[(k + 1) % 2]; Ln, bLn = Lk[(k + 1) % 2]; Yn, bYn = Yk[(k + 1) % 2]
                        for hi, h, pr, rows in heads:
                            f.op(pe, lambda: T.matmul(out=psA[:, hi, :], lhsT=Nc[:, hi, :], rhs=Lc[:, hi, :], start=True, stop=True),
                                 reads=[bNc, bLc], writes=[b_psA])
                        if k < 4:
                            for hi, h, pr, rows in heads:
                                f.op(pe, lambda: T.matmul(out=psB[:, hi, :], lhsT=Lc[:, hi, :], rhs=Nc[:, hi, :], start=True, stop=True),
                                     reads=[bNc, bLc], writes=[b_psB])
                        f.op(act, lambda: S.copy(out=flat(Ln), in_=flat(psA)), reads=[b_psA], writes=[bLn])
                        if k < 4:
                            f.op(dve, lambda: V.tensor_copy(out=flat(Nn), in_=flat(psB)), reads=[b_psB], writes=[bNn])
                        for hi, h, pr, rows in heads:
                            f.op(pe, lambda: T.matmul(out=psC[:, hi, :], lhsT=ident_b[:], rhs=Yc[:, hi, :], start=True, stop=False),
                                 reads=[b_identb, bYc], writes=[b_psC])
                            f.op(pe, lambda: T.matmul(out=psC[:, hi, :], lhsT=Ln[:, hi, :], rhs=Yc[:, hi, :], start=False, stop=True),
                                 reads=[bLn, bYc], writes=[b_psC])
                        if k % 2 == 0:
                            f.op(dve, lambda: V.tensor_copy(out=flat(Yn), in_=flat(psC)), reads=[b_psC], writes=[bYn])
                        else:
                            f.op(act, lambda: S.copy(out=flat(Yn), in_=flat(psC)), reads=[b_psC], writes=[bYn])
                    TT, bTT = Yk[1]
                    if cut <= 7:
                        continue
                    for hi, h, pr, rows in heads:
                        f.op(pe, lambda: T.matmul(out=psA[:, hi, :], lhsT=KT[rows, pr, tc], rhs=AT[rows, pr, tc], start=True, stop=True),
                             reads=[b_KT2, b_AT], writes=[b_psA], rg=rows.start)
                    for hi, h, pr, rows in heads:
                        f.op(pe, lambda: T.matmul(out=psB[:, hi, :], lhsT=BT[rows, pr, tc], rhs=RT[rows, pr, tc], start=True, stop=True),
                             reads=[b_BT, b_RT], writes=[b_psB], rg=rows.start)
                    for hi, h, pr, rows in heads:
                        f.op(pe, lambda: T.matmul(out=psC[:, hi, :], lhsT=KT[rows, pr, tc], rhs=RT[rows, pr, tc], start=True, stop=True),
                             reads=[b_KT2, b_RT], writes=[b_psC], rg=rows.start)
                    f.op(dve, lambda: V.tensor_tensor(out=flat(AKT), in0=flat(psA), in1=mu4b[:], op=ALU.mult),
                         reads=[b_psA, b_mu4], writes=[b_AKT])
                    f.op(dve, lambda: V.tensor_tensor(out=flat(RBT), in0=flat(psB), in1=mui4b[:], op=ALU.mult),
                         reads=[b_psB, b_mui4], writes=[b_RBT])
                    f.op(dve, lambda: V.tensor_tensor(out=flat(RKT), in0=flat(psC), in1=mui4b[:], op=ALU.mult),
                         reads=[b_psC, b_mui4], writes=[b_RKT])
                    for hi, h, pr, rows in heads:
                        f.op(pe, lambda: T.matmul(out=psC[:, hi, 0:64], lhsT=AKT[:, hi, :], rhs=Vtm[:, tl, h * 64:(h + 1) * 64],
                                                  start=True, stop=True), reads=[b_AKT, b_Vtm], writes=[b_psC])
                    f.op(act, lambda: S.copy(out=P1[:], in_=psC[:, :, 0:64]), reads=[b_psC], writes=[b_P1])
                    gp = slice(2 * Gi, 2 * Gi + 2)
                    for c in range(2 if rl >= 5 else 0):
                        crow = slice(64 * c, 64 * c + 64)
                        smp = (Tg - 32) * 2 + c
                        if is_smp:
                            f.dma(sp, H[:, gp, :], swkv_d[smp, gp, :, :].rearrange("a p i -> p a i"), b_H, writes=[b_H])
                        hb, bhb = Hb[c]
                        f.op(pool, lambda: G.tensor_copy(out=hb[:, gp, :], in_=H[:, gp, :]), reads=[b_H], writes=[bhb])
                        for hi, h, pr, rows in heads:
                            f.op(pe, lambda: T.matmul(out=psA[:, hi, 0:64], lhsT=AT[rows, pr, tc], rhs=hb[rows, pr, :], start=True, stop=True),
                                 reads=[b_AT, bhb], writes=[b_psA], rg=rows.start)
                        f.op(dve, lambda: V.tensor_tensor(out=Wb[crow, :, :], in0=psA[crow, :, 0:64], in1=P1[crow, :, :], op=ALU.add),
                             reads=[b_psA, b_P1], writes=[b_Wb])
                        for hi, h, pr, rows in heads:
                            f.op(pe, lambda: T.matmul(out=psA[:, hi, 64:128], lhsT=TT[crow, hi, :], rhs=Wb[crow, hi, :], start=True, stop=True),
                                 reads=[bTT, b_Wb], writes=[b_psA], rg=crow.start)
                        f.op(act, lambda: S.copy(out=Ub[crow, :, :], in_=psA[crow, :, 64:128]), reads=[b_psA], writes=[b_Ub])
                        for hi, h, pr, rows in heads:
                            pcs = slice(pr * 128, (pr + 1) * 128)
                            f.op(pe, lambda: T.matmul(out=psB[:, hi, 0:64], lhsT=Btm[crow, tl, pcs], rhs=Ub[crow, hi, :], start=True, stop=False),
                                 reads=[b_Btm, b_Ub], writes=[b_psB], rg=crow.start)
                            f.op(pe, lambda: T.matmul(out=psB[:, hi, 0:64], lhsT=Ktm[crow, tl, pcs], rhs=Vtm[crow, tl, h * 64:(h + 1) * 64],
                                                      start=False, stop=True), reads=[b_Ktm, b_Vtm], writes=[b_psB], rg=crow.start)
                        for hi, h, pr, rows in heads:
                            gcol = Eneg[rows, pr, tl * 128 + 64 * c + 63:tl * 128 + 64 * c + 64]
                            f.op(pool, lambda: G.tensor_scalar(out=H[rows, pr, :], in0=H[rows, pr, :], scalar1=gcol, scalar2=None, op0=ALU.mult),
                                 reads=[b_H, b_Eneg], writes=[b_H])
                            f.op(dve, lambda: V.scalar_tensor_tensor(out=H[rows, pr, :], in0=psB[rows, hi, 0:64], scalar=gcol, in1=H[rows, pr, :],
                                                                     op0=ALU.mult, op1=ALU.add), reads=[b_psB, b_H, b_Eneg], writes=[b_H])
                        if is_smp:
                            f.dma(sp, wkv_out[1 + smp, gp, :, :].rearrange("a p i -> p a i"), H[:, gp, :], b_H, reads=[b_H], is_output=True)
                        elif Tg == 31 and c == 1:
                            f.dma(sp, wkv_out[0, gp, :, :].rearrange("a p i -> p a i"), H[:, gp, :], b_H, reads=[b_H], is_output=True)
                    if is_loc and rl >= 6:
                        (hb0, bhb0), (hb1, bhb1) = Hb
                        for hi, h, pr, rows in heads:
                            f.op(pe, lambda: T.matmul(out=psB[:, hi, 64:128], lhsT=RT0[rows, pr, tc], rhs=hb0[rows, pr, :], start=True, stop=False),
                                 reads=[b_RT0, bhb0], writes=[b_psB], rg=rows.start)
                            f.op(pe, lambda: T.matmul(out=psB[:, hi, 64:128], lhsT=RT1[rows, pr, tc], rhs=hb1[rows, pr, :], start=False, stop=False),
                                 reads=[b_RT1, bhb1], writes=[b_psB], rg=rows.start)
                            f.op(pe, lambda: T.matmul(out=psB[:, hi, 64:128], lhsT=RBT[:, hi, :], rhs=Ub[:, hi, :], start=False, stop=False),
                                 reads=[b_RBT, b_Ub], writes=[b_psB])
                            f.op(pe, lambda: T.matmul(out=psB[:, hi, 64:128], lhsT=RKT[:, hi, :], rhs=Vtm[:, tl, h * 64:(h + 1) * 64],
                                                      start=False, stop=True), reads=[b_RKT, b_Vtm], writes=[b_psB])
                        f.op(act, lambda: S.copy(out=Yall[:, 4 * Gi:4 * Gi + 4, :], in_=psB[:, :, 64:128]), reads=[b_psB], writes=[b_Yall])
                if is_loc and rl >= 7:
                    lcol = Tg * 128 - NPRE
                    f.op(dve, lambda: V.reduce_sum(out=gst[:, 0:8], in_=Yall[:], axis=AX.X), reads=[b_Yall], writes=[b_gst])
                    f.op(act, lambda: S.activation(out=Ysq, in_=Yall[:], func=AF.Square), reads=[b_Yall], writes=[b_Ysq])
                    f.op(dve, lambda: V.reduce_sum(out=gst[:, 8:16], in_=Ysq, axis=AX.X), reads=[b_Ysq, b_gst], writes=[b_gst])
                    f.op(dve, lambda: V.tensor_scalar(out=gst[:, 0:16], in0=gst[:, 0:16], scalar1=1.0 / 64, scalar2=None, op0=ALU.mult),
                         reads=[b_gst], writes=[b_gst])
                    f.op(dve, lambda: V.tensor_tensor(out=gst[:, 16:24], in0=gst[:, 0:8], in1=gst[:, 0:8], op=ALU.mult),
                         reads=[b_gst], writes=[b_gst])
                    f.op(dve, lambda: V.tensor_tensor(out=gst[:, 16:24], in0=gst[:, 8:16], in1=gst[:, 16:24], op=ALU.subtract),
                         reads=[b_gst], writes=[b_gst])
                    f.op(dve, lambda: V.tensor_scalar(out=gst[:, 16:24], in0=gst[:, 16:24], scalar1=64e-5, scalar2=None, op0=ALU.add),
                         reads=[b_gst], writes=[b_gst])
                    f.op(act, lambda: S.activation(out=gst[:, 16:24], in_=gst[:, 16:24], func=AF.Sqrt), reads=[b_gst], writes=[b_gst])
                    f.op(dve, lambda: V.reciprocal(out=gst[:, 24:32], in_=gst[:, 16:24]), reads=[b_gst], writes=[b_gst])
                    f.op(dve, lambda: V.tensor_tensor(out=Yall[:], in0=Yall[:], in1=gst[:, 0:8].unsqueeze(2).to_broadcast([128, 8, 64]),
                                                      op=ALU.subtract), reads=[b_Yall, b_gst], writes=[b_Yall])
                    f.op(dve, lambda: V.tensor_tensor(out=Yall[:], in0=Yall[:], in1=gst[:, 24:32].unsqueeze(2).to_broadcast([128, 8, 64]),
                                                      op=ALU.mult), reads=[b_Yall, b_gst], writes=[b_Yall])
                    yf = Yall[:].rearrange("p a b -> p (a b)")
                    f.op(dve, lambda: V.tensor_tensor(out=yf, in0=yf, in1=prow[:, GNW:GNW + 512], op=ALU.mult),
                         reads=[b_Yall, b_prow], writes=[b_Yall])
                    f.op(pool, lambda: G.tensor_tensor(out=yf, in0=yf, in1=prow[:, GNB:GNB + 512], op=ALU.add),
                         reads=[b_Yall, b_prow], writes=[b_Yall])
                    for pr in range(4):
                        f.op(pe, lambda: T.transpose(out=ptV[:, pr * 128:(pr + 1) * 128], in_=yf[:, pr * 128:(pr + 1) * 128], identity=ident_f[:]),
                             reads=[b_Yall, b_identf], writes=[b_ptV])
                    f.op(dve, lambda: V.tensor_tensor(out=ytmp, in0=ptV[:].rearrange("p (a b) -> p a b", a=4), in1=bonT[:, :, tc], op=ALU.add),
                         reads=[b_ptV, b_bonT], writes=[b_ytmp])
                    f.op(dve, lambda: V.tensor_tensor(out=yrT[:, :, lcol:lcol + 128], in0=ytmp, in1=gT[:, :, tc], op=ALU.mult),
                         reads=[b_ytmp, b_gT], writes=[b_yrT])
        f.barrier_all()
        f.release(mR)

    if "rwkv" in parts:
        rwkv_phase()
    oT, b_oT = f.sbuf("oT", [128, 4, NLOC], BF16)

    if stage >= 1.5 and "attn" in parts:
        m2 = f.mark()
        wv, b_wv = f.sbuf("wv", [128, 8, 512], BF16)
        wload(wv[:], b_wv, w_in[:, 1024:1536].rearrange("(c p) n -> p c n", p=128))
        Vh, b_Vh = f.sbuf("Vh", [128, NT, 129], BF16)
        f.op(pool, lambda: G.memset(Vh[:], 1.0), writes=[b_Vh])
        pvr = Ring([f.psum("pv%d" % i, [128, 512], F32) for i in range(1)])
        vor = Ring([f.sbuf("vo%d" % i, [128, 128], F32) for i in range(2)])

        def v_project(h):
            for t in range(NT):
                pv, bpv = pvr.next()
                ht, bht, lc = hT_cols(t * 128, 128)
                for c in range(8):
                    f.op(pe, lambda c=c: T.matmul(out=pv[:, 0:128], lhsT=ht[:, c, lc:lc + 128], rhs=wv[:, c, h * 128:(h + 1) * 128],
                                                  start=(c == 0), stop=(c == 7)), reads=[bht, b_wv], writes=[bpv])
                f.op(act, lambda: S.copy(out=Vh[:, t, 0:128], in_=pv[:, 0:128]), reads=[bpv], writes=[b_Vh])
                if t >= 16:
                    vo, bvo = vor.next()
                    f.op(dve, lambda: V.tensor_copy(out=vo[:], in_=pv[:, 0:128]), reads=[bpv], writes=[bvo])
                    f.dma(sp, v_out[(t - 16) * 128:(t - 15) * 128, h * 128:(h + 1) * 128], vo[:], bvo, reads=[bvo], is_output=True)

        KT_, b_KT = f.sbuf("KhT", [68, 2, NCOL], BF16)
        QT_, b_QT = f.sbuf("QhT", [68, 2, NLOC], BF16)
        wq, b_wq = f.sbuf("wq", [128, 8, 128], BF16)
        wk, b_wk = f.sbuf("wk", [128, 8, 128], BF16)
        pqr = Ring([f.psum("pq%d" % i, [64, 512], F32) for i in range(1)])
        psq, b_psq = f.psum("psq", [64, 512], F32)
        sqt, b_sqt = f.sbuf("sqt", [64, 512], F32)
        rnt, b_rnt = f.sbuf("rnt", [64, 512], F32)
        kor = Ring([f.sbuf("ko%d" % i, [64, 512], F32) for i in range(1)])
        pS_r = Ring([f.psum("pS%d" % i, [128, 2, 256], F32) for i in range(2)])
        pO, b_pO = f.psum("pO", [128, 2, 2, 256], F32)
        PTr = Ring([f.sbuf("PT%d" % i, [128, 2, 256], BF16) for i in range(2)])
        osb, b_osb = f.sbuf("osb", [128, 128], F32)
        osb2, b_osb2 = f.sbuf("osb2", [128, 128], F32)
        obf, b_obf = f.sbuf("obf", [128, 128], BF16)
        ost, b_ost = f.sbuf("ost", [128, 8], F32)
        pTo, b_pTo = f.psum("pTo", [128, 1024], BF16)

        def qk_project(wt, bwt, ncols, dstT, bdst, gcol, is_k):
            nb = (ncols + 511) // 512
            for bi in range(nb):
                c0 = bi * 512
                n = min(512, ncols - c0)
                gc0 = c0 if is_k else c0 + NPRE
                ht, bht, lc = hT_cols(gc0, n)
                for cmp_ in range(2):
                    pq, bpq = pqr.next()
                    for c in range(8):
                        f.op(pe, lambda c=c: T.matmul(out=pq[:, 0:n], lhsT=wt[:, c, cmp_ * 64:(cmp_ + 1) * 64],
                                                      rhs=ht[:, c, lc:lc + n], start=(c == 0), stop=(c == 7)),
                             reads=[bht, bwt], writes=[bpq])
                    f.op(act, lambda: S.activation(out=sqt[:, 0:n], in_=pq[:, 0:n], func=AF.Square),
                         reads=[bpq], writes=[b_sqt])
                    f.op(pe, lambda: T.matmul(out=psq[:, 0:n], lhsT=bones[0:64, 0:64], rhs=sqt[:, 0:n], start=True, stop=True),
                         reads=[b_sqt, b_bones], writes=[b_psq])
                    f.op(dve, lambda: V.tensor_scalar(out=rnt[:, 0:n], in0=psq[:, 0:n], scalar1=1.0 / 64, scalar2=1e-6,
                                                      op0=ALU.mult, op1=ALU.add), reads=[b_psq], writes=[b_rnt])
                    f.op(act, lambda: S.activation(out=rnt[:, 0:n], in_=rnt[:, 0:n], func=AF.Sqrt), reads=[b_rnt], writes=[b_rnt])
                    f.op(dve, lambda: V.reciprocal(out=rnt[:, 0:n], in_=rnt[:, 0:n]), reads=[b_rnt], writes=[b_rnt])
                    f.op(dve, lambda: V.scalar_tensor_tensor(out=dstT[0:64, cmp_, c0:c0 + n], in0=pq[:, 0:n], scalar=gcol,
                                                             in1=rnt[:, 0:n], op0=ALU.mult, op1=ALU.mult),
                         reads=[bpq, b_rnt, b_pvec, b_pv2], writes=[bdst])
                    if is_k and gc0 >= NPRE:
                        ko, bko = kor.next()
                        f.op(pool if False else dve, lambda: V.scalar_tensor_tensor(out=ko[:, 0:n], in0=pq[:, 0:n], scalar=gcol,
                                                                 in1=rnt[:, 0:n], op0=ALU.mult, op1=ALU.mult),
                             reads=[bpq, b_rnt, b_pvec], writes=[bko])
                        r0 = cur_h[0] * 128 + cmp_ * 64
                        f.dma(sp, k_out[r0:r0 + 64, gc0 - NPRE:gc0 - NPRE + n], ko[:, 0:n], bko, reads=[bko], is_output=True)

        qs_tm, b_qs = f.sbuf("qs_tm", [128, 2, 512], BF16)
        ks_tm, b_ks = f.sbuf("ks_tm", [128, 2, 512], BF16)
        vs_tm, b_vs = f.sbuf("vs_tm", [128, 2, 512], BF16)
        cur_h = [0]
        for h in range((4 if dbg is None else 1) if stage >= 1.6 else 0):
            cur_h[0] = h
            wload(wq[:], b_wq, w_in[:, h * 128:(h + 1) * 128].rearrange("(c p) n -> p c n", p=128))
            wload(wk[:], b_wk, w_in[:, 512 + h * 128:512 + (h + 1) * 128].rearrange("(c p) n -> p c n", p=128))
            for cmp_ in range(2):
                f.dma(pool, KT_[64:68, cmp_, 0:4096], kb_d[h, :, :], b_KT, writes=[b_KT])
                f.dma(pool, QT_[64:68, cmp_, 0:2048], qb_d[h, :, :], b_QT, writes=[b_QT])
            if dbg == "attn":
                f.op(dve, lambda: V.memset(KT_[:], 0.125), writes=[b_KT])
                f.op(dve, lambda: V.memset(QT_[:], 0.125), writes=[b_QT])
            elif stage >= 2.0:
                v_project(h)
            if stage >= 2.1 and dbg is None:
                qk_project(wk, b_wk, NCOL, KT_, b_KT, pvec[0:64, KG_C:KG_C + 1], True)
                qk_project(wq, b_wq, NLOC, QT_, b_QT, pv2[0:64, 22:23], False)
            if "samp" in parts:
                for tl in range(2):
                    for cmp_ in range(2):
                        f.op(pe, lambda: T.transpose(out=pTo[:, 0:64], in_=QT_[0:64, cmp_, 2048 + tl * 128:2048 + (tl + 1) * 128], identity=ident_b[0:64, 0:64]),
                             reads=[b_QT, b_identb], writes=[b_pTo])
                        f.op(act, lambda: S.copy(out=qs_tm[:, tl, h * 128 + cmp_ * 64:h * 128 + (cmp_ + 1) * 64], in_=pTo[:, 0:64]),
                             reads=[b_pTo], writes=[b_qs])
                        f.op(pe, lambda: T.transpose(out=pTo[:, 0:64], in_=KT_[0:64, cmp_, 4096 + tl * 128:4096 + (tl + 1) * 128], identity=ident_b[0:64, 0:64]),
                             reads=[b_KT, b_identb], writes=[b_pTo])
                        f.op(act, lambda: S.copy(out=ks_tm[:, tl, h * 128 + cmp_ * 64:h * 128 + (cmp_ + 1) * 64], in_=pTo[:, 0:64]),
                             reads=[b_pTo], writes=[b_ks])
                    f.op(pool, lambda: G.tensor_copy(out=vs_tm[:, tl, h * 128:(h + 1) * 128], in_=Vh[:, 32 + tl, 0:128]),
                         reads=[b_Vh], writes=[b_vs])
            if stage < 2.2:
                continue
            def emit_S(g, kt):
                d1 = (kt == 16 + 2 * g + 1)
                q0 = 128 if d1 else 0
                pS, bpS = pS_r.next()
                for cmp_ in range(2):
                    f.op(pe, lambda cmp_=cmp_: T.matmul(out=pS[:, cmp_, q0:256], lhsT=KT_[:, cmp_, kt * 128:(kt + 1) * 128],
                                                        rhs=QT_[:, cmp_, g * 256 + q0:(g + 1) * 256], start=True, stop=True),
                         reads=[b_KT, b_QT], writes=[bpS])
                return (g, kt, pS, bpS)

            def emit_rest(st_):
                g, kt, pS, bpS = st_
                d0 = (kt == 16 + 2 * g)
                d1 = (kt == 16 + 2 * g + 1)
                q0 = 128 if d1 else 0
                PT, bPT = PTr.next()
                f.op(act, lambda: S.activation(out=PT[:, :, q0:256], in_=pS[:, :, q0:256], func=AF.Exp),
                     reads=[bpS], writes=[bPT])
                if d0 or d1:
                    for cmp_ in range(2):
                        f.op(pool, lambda cmp_=cmp_: G.tensor_tensor(out=PT[:, cmp_, q0:q0 + 128], in0=PT[:, cmp_, q0:q0 + 128],
                                                                     in1=tri_b[:], op=ALU.mult),
                             reads=[bPT, b_tri], writes=[bPT])
                for cmp_ in range(2):
                    for sub in range(2):
                        if d1 and sub == 0:
                            continue
                        last = (kt == 16 + 2 * g + sub)
                        f.op(pe, lambda cmp_=cmp_, sub=sub, last=last: T.matmul(
                            out=pO[:, cmp_, sub, 0:129], lhsT=PT[:, cmp_, sub * 128:(sub + 1) * 128], rhs=Vh[:, kt, :],
                            start=(kt == 0), stop=last), reads=[bPT, b_Vh], writes=[b_pO])

            steps = [(g, kt) for g in range(ng) for kt in range(16 + 2 * g + 2)]
            pend = None
            for si in range(len(steps) + 1):
                cur = emit_S(*steps[si]) if si < len(steps) else None
                if pend is not None:
                    emit_rest(pend)
                    g = pend[0]
                    if pend[1] != 16 + 2 * g + 1:
                        pend = cur
                        continue
                else:
                    pend = cur
                    continue
                pend = cur
                for sub in range(2):
                    lcq = g * 256 + sub * 128
                    f.op(dve, lambda: V.reciprocal(out=ost[:, 0:2], in_=pO[:, :, sub, 128]), reads=[b_pO], writes=[b_ost])
                    f.op(dve, lambda: V.tensor_tensor(out=ost[:, 2:3], in0=ost[:, 1:2], in1=NLAM, op=ALU.mult),
                         reads=[b_ost, b_lt], writes=[b_ost])
                    f.op(dve, lambda: V.tensor_scalar(out=osb[:], in0=pO[:, 0, sub, 0:128], scalar1=ost[:, 0:1], scalar2=None,
                                                      op0=ALU.mult), reads=[b_pO, b_ost], writes=[b_osb])
                    f.op(dve, lambda: V.scalar_tensor_tensor(out=osb[:], in0=pO[:, 1, sub, 0:128], scalar=ost[:, 2:3], in1=osb[:],
                                                             op0=ALU.mult, op1=ALU.add), reads=[b_pO, b_ost, b_osb], writes=[b_osb])
                    finalize_o(f, nc, osb, b_osb, osb2, b_osb2, obf, b_obf, ost, b_ost, prow, b_prow, GAO, lam_init,
                               pTo, b_pTo, ident_b, b_identb, oT, b_oT, h, lcq)

        if "samp" in parts:
            selb, b_selb = cload("selb", sel_d[:, :], [128, 512], BF16)
            sel0b, b_sel0b = cload("sel0b", sel0_d[:, :], [128, 256], BF16)
            sbias, b_sbias = cload("sbias", sbias_d[:, :], [128, 520])
            hmask, b_hmask = cload("hmask", hmask_d[:, :], [8, 4])
            e0t, b_e0 = cload("e0t", e0_d[:, :], [8, 4])
            e1t, b_e1 = cload("e1t", e1_d[:, :], [8, 4])
            iop, b_iop = cload("iop", iota_d[:, :], [128, 1], I32)
            pti, b_pti = f.sbuf("pti", [128, 256], I32)
            f.dma(sp, pti[:], ptab_d[0:1, :].partition_broadcast(128), b_pti, writes=[b_pti])
            ptf, b_ptf = f.sbuf("ptf", [128, 256], F32)
            iof, b_iof = f.sbuf("iof", [128, 1], F32)
            idx, b_idx = f.sbuf("idx", [128, 256], I32)
            f.op(dve, lambda: V.tensor_copy(out=ptf[:], in_=pti[:]), reads=[b_pti], writes=[b_ptf])
            f.op(dve, lambda: V.tensor_copy(out=iof[:], in_=iop[:]), reads=[b_iop], writes=[b_iof])
            f.op(dve, lambda: V.tensor_scalar(out=ptf[:], in0=ptf[:], scalar1=128.0, scalar2=iof[:, 0:1], op0=ALU.mult, op1=ALU.add),
                 reads=[b_ptf, b_iof], writes=[b_ptf])
            f.op(dve, lambda: V.tensor_copy(out=idx[:], in_=ptf[:]), reads=[b_ptf], writes=[b_idx])
            cmb, b_cmb = f.sbuf("cmb", [8, 4], F32)
            f.op(dve, lambda: V.scalar_tensor_tensor(out=cmb[:], in0=e1t[:], scalar=lt[0:8, 5:6], in1=e0t[:], op0=ALU.mult, op1=ALU.add),
                 reads=[b_e1, b_e0, b_lt], writes=[b_cmb])
            onesc, b_onesc = f.sbuf("onesc", [128, 1], F32)
            f.op(dve, lambda: V.memset(onesc[:], 1.0), writes=[b_onesc])
            Ktr = Ring([f.sbuf("Kt%d" % i, [128, 512], F32) for i in range(2)])
            Vtr = Ring([f.sbuf("Vt%d" % i, [128, 512], F32) for i in range(2)])
            prodr = Ring([f.sbuf("prod%d" % i, [128, 512], F32) for i in range(1)])
            qbc, b_qbc = f.sbuf("qbc", [128, 512], F32)
            spg_r = Ring([f.sbuf("spg%d" % i, [128, 16], F32) for i in range(3)])
            osm, b_osm = f.sbuf("osm", [8, 512], F32)
            osel, b_osel = f.sbuf("osel", [8, 128], F32)
            ofin, b_ofin = f.sbuf("ofin", [4, 128], F32)
            ofin2, b_ofin2 = f.sbuf("ofin2", [4, 128], F32)
            ofb, b_ofb = f.sbuf("ofb", [4, 128], BF16)
            sst, b_sst = f.sbuf("sst", [8, 8], F32)
            pS0, bpS0 = pS_r.items[0]
            pS0f = pS0[:].rearrange("p a b -> p (a b)")
            pso = pO[0:8, 0, :, :].rearrange("p a b -> p (a b)")
            psz = pO[0:8, 1, 0, 0:1]
            pvr0, bpvr0 = pvr.items[0]
            for s in range(4):
                tl = s // 2
                col = 2048 + 128 * tl + 64 * (s % 2) + 1
                f.op(pe, lambda: T.matmul(out=pS0f, lhsT=selb[:, s * 128:(s + 1) * 128], rhs=qs_tm[:, tl, :], start=True, stop=True),
                     reads=[b_selb, b_qs], writes=[bpS0])
                f.op(act, lambda: S.copy(out=qbc[:], in_=pS0f), reads=[bpS0], writes=[b_qbc])
                for pg in range(65):
                    Kt, bKt = Ktr.next(); Vt, bVt = Vtr.next(); prod, bprod = prodr.next(); spg, bspg = spg_r.next()
                    if pg < 64:
                        ic = s * 64 + pg
                        f.dma(pool, None, None, bKt, reads=[b_idx], writes=[bKt],
                              fn=lambda: G.indirect_dma_start(out=Kt[:, :], out_offset=None, in_=ck_d[:, :],
                                                              in_offset=bass.IndirectOffsetOnAxis(ap=idx[:, ic:ic + 1], axis=0)))
                        f.dma(pool, None, None, bVt, reads=[b_idx], writes=[bVt],
                              fn=lambda: G.indirect_dma_start(out=Vt[:, :], out_offset=None, in_=cv_d[:, :],
                                                              in_offset=bass.IndirectOffsetOnAxis(ap=idx[:, ic:ic + 1], axis=0)))
                    else:
                        f.op(pe, lambda: T.matmul(out=pS0f, lhsT=sel0b[:, (s % 2) * 128:(s % 2 + 1) * 128], rhs=ks_tm[:, tl, :], start=True, stop=True),
                             reads=[b_sel0b, b_ks], writes=[bpS0])
                        f.op(act, lambda: S.copy(out=Kt[:], in_=pS0f), reads=[bpS0], writes=[bKt])
                        f.op(pe, lambda: T.matmul(out=pS0f, lhsT=sel0b[:, (s % 2) * 128:(s % 2 + 1) * 128], rhs=vs_tm[:, tl, :], start=True, stop=True),
                             reads=[b_sel0b, b_vs], writes=[bpS0])
                        f.op(act, lambda: S.copy(out=Vt[:], in_=pS0f), reads=[bpS0], writes=[bVt])
                    f.op(pool, lambda: G.tensor_tensor(out=prod[:], in0=Kt[:], in1=qbc[:], op=ALU.mult), reads=[bKt, b_qbc], writes=[bprod])
                    f.op(dve, lambda: V.reduce_sum(out=spg[:, 0:8], in_=prod[:].rearrange("p (g d) -> p g d", g=8), axis=AX.X),
                         reads=[bprod], writes=[bspg])
                    f.op(dve, lambda: V.tensor_tensor(out=spg[:, 0:8], in0=spg[:, 0:8], in1=sbias[:, pg * 8:(pg + 1) * 8], op=ALU.add),
                         reads=[bspg, b_sbias], writes=[bspg])
                    f.op(act, lambda: S.activation(out=spg[:, 8:16], in_=spg[:, 0:8], func=AF.Exp), reads=[bspg], writes=[bspg])
                    f.op(pe, lambda: T.matmul(out=pso, lhsT=spg[:, 8:16], rhs=Vt[:], start=(pg == 0), stop=(pg == 64)),
                         reads=[bspg, bVt], writes=[b_pO])
                    f.op(pe, lambda: T.matmul(out=psz, lhsT=spg[:, 8:16], rhs=onesc[:], start=(pg == 0), stop=(pg == 64)),
                         reads=[bspg, b_onesc], writes=[b_pO])
                f.op(dve, lambda: V.reciprocal(out=sst[:, 0:1], in_=psz), reads=[b_pO], writes=[b_sst])
                f.op(dve, lambda: V.tensor_scalar(out=osm[:], in0=pso, scalar1=sst[:, 0:1], scalar2=None, op0=ALU.mult),
                     reads=[b_pO, b_sst], writes=[b_osm])
                f.op(dve, lambda: V.tensor_tensor(out=osm[:].rearrange("p (h d) -> p h d", h=4), in0=osm[:].rearrange("p (h d) -> p h d", h=4),
                                                  in1=hmask[:].unsqueeze(2).to_broadcast([8, 4, 128]), op=ALU.mult),
                     reads=[b_osm, b_hmask], writes=[b_osm])
                f.op(dve, lambda: V.reduce_sum(out=osel[:], in_=osm[:].rearrange("p (h d) -> p d h", h=4), axis=AX.X),
                     reads=[b_osm], writes=[b_osel])
                f.op(pe, lambda: T.matmul(out=pvr0[0:4, 0:128], lhsT=cmb[:], rhs=osel[:], start=True, stop=True),
                     reads=[b_cmb, b_osel], writes=[bpvr0])
                f.op(dve, lambda: V.tensor_copy(out=ofin[:], in_=pvr0[0:4, 0:128]), reads=[bpvr0], writes=[b_ofin])
                f.op(dve, lambda: V.memset(sst[0:4, 1:2], 0.0), writes=[b_sst])
                f.op(act, lambda: S.activation(out=ofin2[:], in_=ofin[:], func=AF.Square, accum_out=sst[0:4, 1:2]),
                     reads=[b_ofin, b_sst], writes=[b_ofin2, b_sst])
                f.op(dve, lambda: V.tensor_scalar(out=sst[0:4, 2:3], in0=sst[0:4, 1:2], scalar1=1.0 / 128, scalar2=1e-6, op0=ALU.mult, op1=ALU.add),
                     reads=[b_sst], writes=[b_sst])
                f.op(act, lambda: S.activation(out=sst[0:4, 2:3], in_=sst[0:4, 2:3], func=AF.Sqrt), reads=[b_sst], writes=[b_sst])
                f.op(dve, lambda: V.reciprocal(out=sst[0:4, 2:3], in_=sst[0:4, 2:3]), reads=[b_sst], writes=[b_sst])
                f.op(dve, lambda: V.tensor_scalar(out=sst[0:4, 3:4], in0=sst[0:4, 2:3], scalar1=(1.0 - lam_init), scalar2=None, op0=ALU.mult),
                     reads=[b_sst], writes=[b_sst])
                f.op(dve, lambda: V.scalar_tensor_tensor(out=ofb[:], in0=ofin[:], scalar=sst[0:4, 3:4], in1=prow[0:4, GAO:GAO + 128],
                                                         op0=ALU.mult, op1=ALU.mult), reads=[b_ofin, b_sst, b_prow], writes=[b_ofb])
                f.op(pe, lambda: T.transpose(out=pTo[:, 0:4], in_=ofb[:], identity=ident_b[0:4, 0:4]), reads=[b_ofb, b_identb], writes=[b_pTo])
                f.op(act, lambda: S.copy(out=oT[:, :, col], in_=pTo[:, 0:4]), reads=[b_pTo], writes=[b_oT])
        f.release(m2)
        f.barrier_all()

    if "epi" in parts:
        epilogue()
    f.dma(pool, dbg_d[0].rearrange("a p n -> p a n"), oT[:], b_oT, reads=[b_oT], is_output=True)
    f.dma(pool, dbg_d[1].rearrange("a p n -> p a n"), yrT[:], b_yrT, reads=[b_yrT], is_output=True)
    f.release(m_pre)
    f.finish()
    f.close()
    return nc


def finalize_o(f, nc, osb, b_osb, osb2, b_osb2, obf, b_obf, ost, b_ost, prow, b_prow, GAO, lam_init,
               pTo, b_pTo, ident_b, b_identb, oT, b_oT, h, lcq):
    V, S, T = nc.vector, nc.scalar, nc.tensor
    dve, act, pe = f.dve, f.act, f.pe
    f.op(dve, lambda: V.memset(ost[:, 4:5], 0.0), writes=[b_ost])
    f.op(act, lambda: S.activation(out=osb2[:], in_=osb[:], func=AF.Square, accum_out=ost[:, 4:5]),
         reads=[b_osb, b_ost], writes=[b_osb2, b_ost])
    f.op(dve, lambda: V.tensor_scalar(out=ost[:, 5:6], in0=ost[:, 4:5], scalar1=1.0 / 128, scalar2=1e-6,
                                      op0=ALU.mult, op1=ALU.add), reads=[b_ost], writes=[b_ost])
    f.op(act, lambda: S.activation(out=ost[:, 5:6], in_=ost[:, 5:6], func=AF.Sqrt), reads=[b_ost], writes=[b_ost])
    f.op(dve, lambda: V.reciprocal(out=ost[:, 5:6], in_=ost[:, 5:6]), reads=[b_ost], writes=[b_ost])
    f.op(dve, lambda: V.tensor_scalar(out=ost[:, 6:7], in0=ost[:, 5:6], scalar1=(1.0 - lam_init), scalar2=None,
                                      op0=ALU.mult), reads=[b_ost], writes=[b_ost])
    f.op(dve, lambda: V.scalar_tensor_tensor(out=obf[:], in0=osb[:], scalar=ost[:, 6:7], in1=prow[:, GAO:GAO + 128],
                                             op0=ALU.mult, op1=ALU.mult), reads=[b_osb, b_ost, b_prow], writes=[b_obf])
    f.op(pe, lambda: T.transpose(out=pTo[:, 0:128], in_=obf[:], identity=ident_b[:]), reads=[b_obf, b_identb], writes=[b_pTo])
    f.op(act, lambda: S.copy(out=oT[:, h, lcq:lcq + 128], in_=pTo[:, 0:128]), reads=[b_pTo], writes=[b_oT])


def sample_attention(f, nc, L):
    pass


def _consts(half):
    c = {}
    c["ident"] = np.eye(128, dtype=np.float32)
    s_idx = np.arange(128)[:, None]; t_idx = np.arange(128)[None, :]
    same = (s_idx // 64) == (t_idx // 64)
    mu = ((s_idx < t_idx) & same).astype(np.float32)
    mui = ((s_idx <= t_idx) & same).astype(np.float32)
    ml = mu.T.copy()
    c["mu4"] = np.tile(mu, (1, 4)); c["ml4"] = np.tile(ml, (1, 4)); c["mui4"] = np.tile(mui, (1, 4))
    c["tri"] = (s_idx <= t_idx).astype(np.float32)
    cmk = np.zeros((128, 256), np.float32)
    cmk[:, [1, 65, 129, 193]] = 1.0
    c["colmask"] = cmk
    rm = np.ones((128, 512), np.float32); rm[:, ::64] = 0.0
    c["resetm"] = rm
    col = np.arange(512)[None, :]
    c["cm0"] = np.broadcast_to(((col % 128) < 64).astype(np.float32), (128, 512)).copy()
    c["cm1"] = np.broadcast_to(((col % 128) >= 64).astype(np.float32), (128, 512)).copy()
    c["bones"] = same.astype(np.float32)
    sel = np.zeros((128, 4, 128), np.float32)
    sel0 = np.zeros((128, 2, 128), np.float32)
    for s in range(4):
        sel[1 + 64 * (s % 2), s, :] = 1.0
    for r in range(2):
        sel0[1 + 64 * r, r, 0] = 1.0
    c["sel"] = sel.reshape(128, 512); c["sel0"] = sel0.reshape(128, 256)
    c["iotap"] = np.arange(128, dtype=np.int32).reshape(128, 1)
    hm = np.zeros((8, 4), np.float32); e0 = np.zeros((8, 4), np.float32); e1 = np.zeros((8, 4), np.float32)
    for h in range(4):
        for cc in range(2):
            hm[h * 2 + cc, h] = 1.0
        e0[h * 2, h] = 1.0; e1[h * 2 + 1, h] = 1.0
    c["hmask"] = hm; c["e0"] = e0; c["e1"] = e1
    slopes = np.array([2.0 ** (-8.0 * (h + 1) / 4) for h in range(4)], np.float64)
    kcol = np.arange(4096)
    kb = np.zeros((4, 4, 4096), np.float32)
    qb = np.zeros((4, 4, 2048), np.float32)
    qpos = 2048 + np.arange(2048)
    for h in range(4):
        kb[h, 0] = slopes[h] * 128 * (kcol // 128)
        if half == 0:
            kb[h, 0, :2048] = NEG
        kb[h, 1] = slopes[h] * (kcol % 128)
        kb[h, 2] = 1.0; kb[h, 3] = 1.0
        qb[h, 0] = 1.0; qb[h, 1] = 1.0
        qb[h, 2] = -slopes[h] * 128 * (qpos // 128)
        qb[h, 3] = -slopes[h] * (qpos % 128)
    c["kb"] = kb; c["qb"] = qb
    sb = np.zeros((128, 65, 8), np.float32)
    slot = np.arange(128)[:, None]
    for pg in range(64):
        dist = 8192 - (128 * pg + slot)
        for h in range(4):
            sb[:, pg, 2 * h] = (-slopes[h] * dist)[:, 0]; sb[:, pg, 2 * h + 1] = (-slopes[h] * dist)[:, 0]
    sb[1:, 64, :] = NEG
    c["sbias"] = sb.reshape(128, 65 * 8)
    return c


_NC_CACHE = {}
_LAST = None


def kernel(**inp):
    f32 = np.float32
    xp = np.asarray(inp["x_prompt"], f32); xs = np.asarray(inp["x_sample"], f32)
    g = lambda k: np.ascontiguousarray(np.asarray(inp[k], f32)[0])
    w_in = g("w_in")
    shared = {
        "w_in": w_in, "w_pa": g("w_pa"), "w_pb": g("w_pb"), "w_out": g("w_out"),
        "w_gate": g("w_gate"), "w_up": g("w_up"), "w_down": g("w_down"),
        "wa2": np.ascontiguousarray(np.concatenate([g("w2"), g("a2")], 0)), "g2": g("g2"),
        "ck": np.ascontiguousarray(np.asarray(inp["cache_k"], f32).reshape(2560 * 128, 512)),
        "cv": np.ascontiguousarray(np.asarray(inp["cache_v"], f32).reshape(2560 * 128, 512)),
    }
    pvec = np.zeros((128, 36), f32)
    pvec[:, 0:14] = g("shift_mu").reshape(14, 128).T
    pvec[:, 14:18] = g("w0").reshape(4, 128).T
    pvec[:, 18:22] = g("a0").reshape(4, 128).T
    pvec[:, 22:26] = g("k_k").reshape(4, 128).T
    pvec[:, 26:30] = g("k_a").reshape(4, 128).T
    pvec[:, 30:34] = g("r_k").reshape(4, 128).T
    pvec[:, 34] = np.tile(g("q_gain"), 2); pvec[:, 35] = np.tile(g("k_gain"), 2)
    prow = np.concatenate([g("norm_mix"), g("norm_ffn"), g("attn_out_gain"), g("gn_w"), g("gn_b"),
                           g("lambda_q1"), g("lambda_k1"), g("lambda_q2"), g("lambda_k2")]).reshape(1, 3456).astype(f32)
    shared["pvec"] = pvec; shared["prow"] = prow
    consts = [_consts(0), _consts(1)]
    ptab = np.asarray(inp["page_table"], np.int32)
    swkv = np.asarray(inp["state_wkv"], f32)[0]
    sshift = np.asarray(inp["state_shift"], f32)[0]
    in_maps = []
    for c in range(8):
        b, half = c // 2, c % 2
        xin = np.zeros((NCOL, D), f32)
        if half == 1:
            xin[0:2048] = xp[b, 0:2048]
        xin[2048:4096] = xp[b, half * 2048:(half + 1) * 2048]
        for s in range(4):
            xin[4096 + 128 * (s // 2) + 64 * (s % 2) + 1] = xs[4 * c + s, 0]
        m = dict(shared)
        m.update(consts[half])
        m["xin"] = xin
        m["ptab"] = np.ascontiguousarray(ptab[4 * c:4 * c + 4].reshape(1, 256))
        sw = swkv[4 * c:4 * c + 4].transpose(0, 1, 3, 2).reshape(4, 4, 128, 64)
        m["swkv"] = np.ascontiguousarray(sw)
        m["sshift"] = np.ascontiguousarray(sshift[4 * c:4 * c + 4].reshape(4, 14, 128).transpose(2, 1, 0))
        in_maps.append(m)
    if "nc" not in _NC_CACHE:
        _NC_CACHE["nc"] = build()
        _NC_CACHE["small"] = False
    if _NC_CACHE.get("small"):
        for m in in_maps:
            m["ck"] = m["ck"][:128]; m["cv"] = m["cv"][:128]
    nc = _NC_CACHE["nc"]
    res = run_bass_kernel_spmd(nc, in_maps, core_ids=list(range(8)))
    R = res.results
    global _LAST
    _LAST = R
    y_p = np.zeros((4, 4096, 1024), f32); y_s = np.zeros((32, 1, 1024), f32)
    k_p = np.zeros((1, 4, 4096, 4, 128), f32); v_p = np.zeros((1, 4, 4096, 4, 128), f32)
    wkv_p = np.zeros((1, 4, 8, 64, 64), f32); sh_p = np.zeros((1, 4, 1792), f32)
    k_s = np.zeros((1, 32, 1, 4, 128), f32); v_s = np.zeros((1, 32, 1, 4, 128), f32)
    wkv_s = np.zeros((1, 32, 8, 64, 64), f32); sh_s = np.zeros((1, 32, 1792), f32)
    for c in range(8):
        b, half = c // 2, c % 2
        r = R[c]
        sl = slice(half * 2048, (half + 1) * 2048)
        y_p[b, sl] = r["y_out"][0:2048]
        kT = r["k_out"]
        k_p[0, b, sl] = kT[:, 0:2048].T.reshape(2048, 4, 128)
        v_p[0, b, sl] = r["v_out"][0:2048].reshape(2048, 4, 128)
        wk = r["wkv_out"].reshape(5, 8, 64, 64)
        po = r["p_out"]
        if half == 1:
            wkv_p[0, b] = wk[0].transpose(0, 2, 1)
            sh_p[0, b] = po[:, :, 127].reshape(1792)
        for s in range(4):
            col = 128 * (s // 2) + 64 * (s % 2) + 1
            y_s[4 * c + s, 0] = r["y_out"][2048 + col]
            k_s[0, 4 * c + s, 0] = kT[:, 2048 + col].reshape(4, 128)
            v_s[0, 4 * c + s, 0] = r["v_out"][2048 + col].reshape(4, 128)
            wkv_s[0, 4 * c + s] = wk[1 + s].transpose(0, 2, 1)
            sh_s[0, 4 * c + s] = po[:, :, 128 + col].reshape(1792)
    return (y_p, y_s, k_p, v_p, wkv_p, sh_p, k_s, v_s, wkv_s, sh_s)
```

```python
import math
import numpy as np
import concourse.bass as bass
import concourse.mybir as mybir
from concourse.bass_utils import run_bass_kernel_spmd

F32 = mybir.dt.float32
BF16 = mybir.dt.bfloat16
I32 = mybir.dt.int32
ALU = mybir.AluOpType
AF = mybir.ActivationFunctionType
AX = mybir.AxisListType

SEM_EPOCH = 30000
NPRE, NOWN, NSMP = 2048, 2048, 256
NCOL = NPRE + NOWN + NSMP
NLOC = NOWN + NSMP
NT = NCOL // 128
D = 1024
DFF = 2816
NEG = -30000.0


class Eng:
    def __init__(self, fw, name, e):
        self.fw = fw; self.name = name; self.e = e
        self.sems = []; self.count = 0; self.epoch = -1; self.known = {}
        self._new_epoch()

    def _new_epoch(self):
        self.epoch += 1
        self.count = 0
        self.sems.append(self.fw.new_sem("%s_e%d" % (self.name, self.epoch)))


class Buf:
    _uid = [0]

    def __init__(self, name, psum=False):
        Buf._uid[0] += 1
        self.uid = Buf._uid[0]
        self.name = name; self.w = None; self.r = []; self.dsem = None; self.dcount = 0; self.psum = psum


class FW:
    def __init__(self, nc):
        self.nc = nc
        self._stack = []
        self._semstack = []
        self.engs = {}
        for name, e in (("pe", nc.tensor), ("act", nc.scalar), ("dve", nc.vector),
                        ("pool", nc.gpsimd), ("sp", nc.sync)):
            self.engs[name] = Eng(self, name, e)
        self.pe = self.engs["pe"]; self.act = self.engs["act"]; self.dve = self.engs["dve"]
        self.pool = self.engs["pool"]; self.sp = self.engs["sp"]
        self.nbuf = 0
        self.out_bufs = []
        self.free_dsems = []

    def new_sem(self, name):
        cm = self.nc.semaphore(name)
        s = cm.__enter__()
        self._semstack.append(cm)
        return s

    def sbuf(self, name, shape, dt):
        cm = self.nc.sbuf_tensor("sb_" + name, list(shape), dt)
        t = cm.__enter__()
        self._stack.append(cm)
        self.nbuf += 1
        return t, Buf(name)

    def psum(self, name, shape, dt):
        cm = self.nc.psum_tensor("ps_" + name, list(shape), dt)
        t = cm.__enter__()
        self._stack.append(cm)
        return t, Buf(name, psum=True)

    def mark(self):
        return len(self._stack)

    def release(self, mark):
        while len(self._stack) > mark:
            self._stack.pop().__exit__(None, None, None)

    def _need(self, eng, stamp, kind):
        if stamp is None:
            return
        if stamp[0] == 'e':
            _, pe_, ep, cnt = stamp
            if pe_ is eng:
                if eng.name == "pe" or kind != "raw":
                    return
            key = (pe_.name, ep)
            if eng.known.get(key, 0) >= cnt:
                return
            eng.e.wait_ge(pe_.sems[ep], cnt)
            eng.known[key] = cnt
        else:
            _, sem, val, key = stamp
            if eng.known.get(key, 0) >= val:
                return
            eng.e.wait_ge(sem, val)
            eng.known[key] = val

    def _deps(self, eng, reads, writes):
        for b in reads:
            self._need(eng, b.w, "raw")
        for b in writes:
            self._need(eng, b.w, "waw")
            for s in b.r:
                self._need(eng, s, "war")

    def _record(self, st, reads, writes):
        for b in reads:
            b.r.append(st)
        for b in writes:
            b.w = st
            b.r = []

    def op(self, eng, fn, reads=(), writes=(), rg=None):
        if eng.name == "pe":
            for b in writes:
                prev = getattr(b, "rg", None)
                if rg is not None and prev is not None and prev != rg and b.w is not None and b.w[0] == 'e' and b.w[1] is eng:
                    _, pe_, ep, cnt = b.w
                    key = (pe_.name, ep)
                    if eng.known.get(key, 0) < cnt:
                        eng.e.wait_ge(pe_.sems[ep], cnt)
                        eng.known[key] = cnt
                b.rg = rg
        if eng.name != "pe":
            px = [b for b in reads if b.psum]
            if px:
                reads = [b for b in reads if not b.psum]
                writes = list(writes) + [b for b in px if b not in writes]
        self._deps(eng, reads, writes)
        if eng.count >= SEM_EPOCH:
            eng._new_epoch()
        ins = fn()
        eng.count += 1
        ins.then_inc(eng.sems[eng.epoch], 1)
        self._record(('e', eng, eng.epoch, eng.count), reads, writes)
        return ins

    def dma(self, eng, out, in_, sb, reads=(), writes=(), is_output=False, fn=None):
        self._deps(eng, reads, writes)
        b = sb
        if b.dsem is None:
            b.dsem = self.new_sem("d_" + b.name)
        ins = eng.e.dma_start(out=out, in_=in_) if fn is None else fn()
        ins.then_inc(b.dsem, 16)
        b.dcount += 16
        self._record(('d', b.dsem, b.dcount, ("dma", b.uid)), reads, writes)
        if is_output and b not in self.out_bufs:
            self.out_bufs.append(b)
        return ins

    def finish(self):
        for b in self.out_bufs:
            self.sp.e.wait_ge(b.dsem, b.dcount)

    def barrier_all(self):
        for a in self.engs.values():
            for o in self.engs.values():
                if o is a or o.count == 0:
                    continue
                key = (o.name, o.epoch)
                if a.known.get(key, 0) >= o.count:
                    continue
                a.e.wait_ge(o.sems[o.epoch], o.count)
                a.known[key] = o.count

    def close(self):
        self.release(0)
        while self._semstack:
            self._semstack.pop().__exit__(None, None, None)


class Ring:
    def __init__(self, items):
        self.items = items; self.i = 0

    def next(self):
        it = self.items[self.i % len(self.items)]
        self.i += 1
        return it


def build(stage=99, small=False, dbg=None, ng=8, parts=("rwkv", "attn", "epi", "samp"), nb=None, rl=9, cut=99):
    nc = bass.Bass("TRN2", target_bir_lowering=False)
    V, S, G, T = nc.vector, nc.scalar, nc.gpsimd, nc.tensor

    def din(name, shape, dt=F32):
        return nc.dram_tensor(name, list(shape), dt, kind="ExternalInput").ap()

    def dout(name, shape, dt=F32):
        return nc.dram_tensor(name, list(shape), dt, kind="ExternalOutput").ap()

    xin = din("xin", [NCOL, D])
    w_in = din("w_in", [D, 5376]); w_pa = din("w_pa", [512, D]); w_pb = din("w_pb", [512, D])
    w_out = din("w_out", [D, D]); w_gate = din("w_gate", [D, DFF]); w_up = din("w_up", [D, DFF])
    w_down = din("w_down", [DFF, D])
    wa2_d = din("wa2", [128, 512]); g2_d = din("g2", [128, 512])
    pvec_d = din("pvec", [128, 36]); prow_d = din("prow", [1, 3456])
    kb_d = din("kb", [4, 4, 4096]); qb_d = din("qb", [4, 4, 2048])
    sshift_d = din("sshift", [128, 14, 4]); swkv_d = din("swkv", [4, 4, 128, 64])
    NPG = 128 if small else 2560 * 128
    ck_d = din("ck", [NPG, 512]); cv_d = din("cv", [NPG, 512])
    ptab_d = din("ptab", [1, 256], I32)
    sbias_d = din("sbias", [128, 65 * 8])
    ident_d = din("ident", [128, 128]); mu4_d = din("mu4", [128, 512]); ml4_d = din("ml4", [128, 512])
    mui4_d = din("mui4", [128, 512]); tri_d = din("tri", [128, 128]); colmask_d = din("colmask", [128, 256])
    resetm_d = din("resetm", [128, 512]); cm0_d = din("cm0", [128, 512]); cm1_d = din("cm1", [128, 512])
    bones_d = din("bones", [128, 128]); sel_d = din("sel", [128, 512]); sel0_d = din("sel0", [128, 256])
    iota_d = din("iotap", [128, 1], I32); hmask_d = din("hmask", [8, 4]); e0_d = din("e0", [8, 4]); e1_d = din("e1", [8, 4])

    y_out = dout("y_out", [NLOC, D]); k_out = dout("k_out", [512, NLOC]); v_out = dout("v_out", [NLOC, 512])
    wkv_out = dout("wkv_out", [5, 4, 128, 64]); p_out = dout("p_out", [14, 128, 384])

    f = FW(nc)
    pe, act, dve, pool, sp = f.pe, f.act, f.dve, f.pool, f.sp

    def cload(name, src, shape, dt=F32, q=None):
        t, b = f.sbuf(name, shape, dt)
        if dt == F32 or dt == I32:
            f.dma(q or sp, t[:], src, b, writes=[b])
        else:
            f.dma(pool, t[:], src, b, writes=[b])
        return t, b

    ident_f, b_identf = cload("ident_f", ident_d[:, :], [128, 128])
    ident_b, b_identb = cload("ident_b", ident_d[:, :], [128, 128], BF16)
    tri_b, b_tri = cload("tri_b", tri_d[:, :], [128, 128], BF16)
    bones, b_bones = cload("bones", bones_d[:, :], [128, 128])
    pvec, b_pvec = cload("pvec", pvec_d[:, :], [128, 36])
    prow, b_prow = f.sbuf("prow", [128, 3456], F32)
    f.dma(sp, prow[:], prow_d[0:1, :].partition_broadcast(128), b_prow, writes=[b_prow])
    pv2, b_pv2 = f.sbuf("pv2", [128, 24], F32)
    f.op(dve, lambda: V.tensor_scalar(out=pv2[:, 0:14], in0=pvec[:, 0:14], scalar1=-1.0, scalar2=1.0,
                                      op0=ALU.mult, op1=ALU.add), reads=[b_pvec], writes=[b_pv2])
    f.op(dve, lambda: V.tensor_scalar(out=pv2[:, 14:18], in0=pvec[:, 14:18], scalar1=-1.0, scalar2=None,
                                      op0=ALU.mult), reads=[b_pvec], writes=[b_pv2])
    f.op(dve, lambda: V.tensor_scalar(out=pv2[:, 18:22], in0=pvec[:, 26:30], scalar1=-1.0, scalar2=1.0,
                                      op0=ALU.mult, op1=ALU.add), reads=[b_pvec], writes=[b_pv2])
    f.op(dve, lambda: V.tensor_scalar(out=pv2[:, 22:23], in0=pvec[:, 34:35], scalar1=0.125, scalar2=None,
                                      op0=ALU.mult), reads=[b_pvec], writes=[b_pv2])
    MU_C, W0_C, A0_C, KK_C, KA_C, RK_C, QG_C, KG_C = 0, 14, 18, 22, 26, 30, 34, 35
    GMIX, GFFN, GAO, GNW, GNB, LAM = 0, 1024, 2048, 2176, 2688, 3200

    lt, b_lt = f.sbuf("lt", [128, 8], F32)
    junk64, b_junk64 = f.sbuf("junk64", [128, 64], F32)
    f.op(dve, lambda: V.memset(lt[:], 0.0), writes=[b_lt])
    f.op(dve, lambda: V.tensor_tensor(out=junk64[:], in0=prow[:, LAM:LAM + 64], in1=prow[:, LAM + 64:LAM + 128],
                                      op=ALU.mult), reads=[b_prow], writes=[b_junk64])
    f.op(dve, lambda: V.reduce_sum(out=lt[:, 0:1], in_=junk64[:], axis=AX.X), reads=[b_junk64], writes=[b_lt])
    f.op(dve, lambda: V.tensor_tensor(out=junk64[:], in0=prow[:, LAM + 128:LAM + 192], in1=prow[:, LAM + 192:LAM + 256],
                                      op=ALU.mult), reads=[b_prow, b_lt], writes=[b_junk64])
    f.op(dve, lambda: V.reduce_sum(out=lt[:, 1:2], in_=junk64[:], axis=AX.X), reads=[b_junk64], writes=[b_lt])
    f.op(act, lambda: S.activation(out=lt[:, 2:4], in_=lt[:, 0:2], func=AF.Exp), reads=[b_lt], writes=[b_lt])
    lam_init = 0.8 - 0.6 * math.exp(-0.3 * 0)
    f.op(dve, lambda: V.tensor_tensor(out=lt[:, 4:5], in0=lt[:, 3:4], in1=lt[:, 2:3], op=ALU.subtract),
         reads=[b_lt], writes=[b_lt])
    f.op(dve, lambda: V.tensor_scalar(out=lt[:, 5:6], in0=lt[:, 4:5], scalar1=-lam_init, scalar2=None, op0=ALU.add),
         reads=[b_lt], writes=[b_lt])
    NLAM = lt[:, 5:6]

    hT_loc, b_hTloc = f.sbuf("hT_loc", [128, 8, NLOC], BF16)
    yrT, b_yrT = f.sbuf("yrT", [128, 4, NLOC], BF16)

    m_pre = f.mark()
    hT_pre, b_hTpre = f.sbuf("hT_pre", [128, 8, NPRE], BF16)

    def hT_cols(c0, n):
        if c0 < NPRE:
            return hT_pre, b_hTpre, c0
        return hT_loc, b_hTloc, c0 - NPRE

    m1 = f.mark()
    xr = Ring([f.sbuf("x%d" % i, [128, D], F32) for i in range(3)])
    xbr = Ring([f.sbuf("xb%d" % i, [128, D], BF16) for i in range(2)])
    junk, b_junk = f.sbuf("junk", [128, D], F32)
    ssr = Ring([f.sbuf("ss%d" % i, [128, 2], F32) for i in range(3)])
    ptr = Ring([f.psum("pt%d" % i, [128, 8, 128], BF16) for i in range(2)])
    for t in range(NT if dbg is None else 0):
        xt, bx = xr.next(); xb, bxb = xbr.next(); ss, bss = ssr.next(); pt, bpt = ptr.next()
        f.dma(sp, xt[:], xin[t * 128:(t + 1) * 128, :], bx, writes=[bx])
        f.op(dve, lambda: V.memset(ss[:], 0.0), writes=[bss])
        f.op(act, lambda: S.activation(out=junk[:], in_=xt[:], func=AF.Square, accum_out=ss[:, 0:1]),
             reads=[bx, bss], writes=[b_junk, bss])
        f.op(dve, lambda: V.tensor_scalar(out=ss[:, 1:2], in0=ss[:, 0:1], scalar1=1.0 / D, scalar2=1e-6,
                                          op0=ALU.mult, op1=ALU.add), reads=[bss], writes=[bss])
        f.op(act, lambda: S.activation(out=ss[:, 1:2], in_=ss[:, 1:2], func=AF.Sqrt), reads=[bss], writes=[bss])
        f.op(dve, lambda: V.reciprocal(out=ss[:, 1:2], in_=ss[:, 1:2]), reads=[bss], writes=[bss])
        f.op(dve, lambda: V.scalar_tensor_tensor(out=xb[:], in0=xt[:], scalar=ss[:, 1:2], in1=prow[:, GMIX:GMIX + D],
                                                 op0=ALU.mult, op1=ALU.mult), reads=[bx, bss, b_prow], writes=[bxb])
        for c in range(8):
            f.op(pe, lambda c=c: T.transpose(out=pt[:, c, :], in_=xb[:, c * 128:(c + 1) * 128], identity=ident_b[:]),
                 reads=[bxb, b_identb], writes=[bpt])
        ht, bht, lc = hT_cols(t * 128, 128)
        f.op(act, lambda: S.copy(out=ht[:, :, lc:lc + 128], in_=pt[:]), reads=[bpt], writes=[bht])
    f.release(m1)
    f.barrier_all()

    def wload(dst, bdst, src):
        f.dma(pool, dst, src, bdst, writes=[bdst])


    def epilogue():
        mE = f.mark()
        wpa, b_wpa = f.sbuf("wpa", [128, 4, D], BF16)
        wpb, b_wpb = f.sbuf("wpb", [128, 4, D], BF16)
        wo, b_wo = f.sbuf("wo", [128, 8, D], BF16)
        wload(wpa[:], b_wpa, w_pa.rearrange("(c p) n -> p c n", p=128))
        wload(wpb[:], b_wpb, w_pb.rearrange("(c p) n -> p c n", p=128))
        wload(wo[:], b_wo, w_out.rearrange("(c p) n -> p c n", p=128))
        SB = 384
        bank = [f.psum("bank%d" % i, [128, 512], F32) for i in range(8)]
        wgr = Ring([f.sbuf("wg%d" % i, [128, 8, 128], BF16) for i in range(4)])
        wdr = Ring([f.sbuf("wd%d" % i, [128, D], BF16) for i in range(2)])
        mT, b_mT = f.sbuf("mT", [128, 8, SB], BF16)
        hfT, b_hfT = f.sbuf("hfT", [128, 8, SB], BF16)
        x1, b_x1 = f.sbuf("x1e", [128, 3, D], F32)
        xr2 = Ring([f.sbuf("xe%d" % i, [128, D], F32) for i in range(1)])
        sga, b_sga = f.sbuf("sga", [128, SB], F32)
        sgb, b_sgb = f.sbuf("sgb", [128, SB], F32)
        tA, b_tA = f.sbuf("tA", [128, SB], F32)
        hfb, b_hfb = f.sbuf("hfb", [128, D], BF16)
        ejunk, b_ejunk = f.sbuf("ejunk", [128, D], F32)
        est, b_est = f.sbuf("est", [128, 4], F32)
        actr = Ring([f.sbuf("act%d" % i, [128, SB], BF16) for i in range(2)])
        yor = Ring([f.sbuf("yo%d" % i, [128, 512], F32) for i in range(2)])
        for sbi in range(NLOC // SB):
            c0 = sbi * SB
            for ch in range(8):
                cs = slice(ch * 128, (ch + 1) * 128)
                (pa, bpa), (pb_, bpb), (pga, bpga), (pgb, bpgb) = bank[0], bank[1], bank[2], bank[3]
                wga, bwga = wgr.next(); wgb, bwgb = wgr.next()
                wload(wga[:], bwga, w_in[:, 3328 + ch * 128:3328 + (ch + 1) * 128].rearrange("(c p) n -> p c n", p=128))
                wload(wgb[:], bwgb, w_in[:, 4352 + ch * 128:4352 + (ch + 1) * 128].rearrange("(c p) n -> p c n", p=128))
                for h in range(4):
                    f.op(pe, lambda h=h: T.matmul(out=pa[:, 0:SB], lhsT=wpa[:, h, cs], rhs=oT[:, h, c0:c0 + SB], start=(h == 0), stop=(h == 3)),
                         reads=[b_wpa, b_oT], writes=[bpa])
                for h in range(4):
                    f.op(pe, lambda h=h: T.matmul(out=pb_[:, 0:SB], lhsT=wpb[:, h, cs], rhs=yrT[:, h, c0:c0 + SB], start=(h == 0), stop=(h == 3)),
                         reads=[b_wpb, b_yrT], writes=[bpb])
                for c in range(8):
                    f.op(pe, lambda c=c: T.matmul(out=pga[:, 0:SB], lhsT=wga[:, c, :], rhs=hT_loc[:, c, c0:c0 + SB], start=(c == 0), stop=(c == 7)),
                         reads=[bwga, b_hTloc], writes=[bpga])
                for c in range(8):
                    f.op(pe, lambda c=c: T.matmul(out=pgb[:, 0:SB], lhsT=wgb[:, c, :], rhs=hT_loc[:, c, c0:c0 + SB], start=(c == 0), stop=(c == 7)),
                         reads=[bwgb, b_hTloc], writes=[bpgb])
                f.op(act, lambda: S.activation(out=sga[:], in_=pga[:, 0:SB], func=AF.Sigmoid), reads=[bpga], writes=[b_sga])
                f.op(act, lambda: S.activation(out=sgb[:], in_=pgb[:, 0:SB], func=AF.Sigmoid), reads=[bpgb], writes=[b_sgb])
                f.op(dve, lambda: V.tensor_tensor(out=tA[:], in0=pa[:, 0:SB], in1=sga[:], op=ALU.mult), reads=[bpa, b_sga], writes=[b_tA])
                f.op(dve, lambda: V.tensor_tensor(out=sgb[:], in0=pb_[:, 0:SB], in1=sgb[:], op=ALU.mult), reads=[bpb, b_sgb], writes=[b_sgb])
                f.op(pool, lambda: G.tensor_tensor(out=mT[:, ch, :], in0=tA[:], in1=sgb[:], op=ALU.add), reads=[b_tA, b_sgb], writes=[b_mT])
            for tl in range(3):
                ts_ = slice(tl * 128, (tl + 1) * 128)
                xt, bxt = xr2.next()
                row0 = NPRE + c0 + tl * 128
                f.dma(sp, xt[:], xin[row0:row0 + 128, :], bxt, writes=[bxt])
                for hf_ in range(2):
                    px, bpx = bank[4 + hf_]
                    for k in range(8):
                        f.op(pe, lambda k=k: T.matmul(out=px[:], lhsT=mT[:, k, ts_], rhs=wo[:, k, hf_ * 512:(hf_ + 1) * 512],
                                                      start=(k == 0), stop=(k == 7)), reads=[b_mT, b_wo], writes=[bpx])
                    f.op(dve, lambda: V.tensor_tensor(out=x1[:, tl, hf_ * 512:(hf_ + 1) * 512], in0=px[:], in1=xt[:, hf_ * 512:(hf_ + 1) * 512],
                                                      op=ALU.add), reads=[bpx, bxt], writes=[b_x1])
                f.op(dve, lambda: V.memset(est[:, 0:1], 0.0), writes=[b_est])
                f.op(act, lambda: S.activation(out=ejunk[:], in_=x1[:, tl, :], func=AF.Square, accum_out=est[:, 0:1]),
                     reads=[b_x1, b_est], writes=[b_ejunk, b_est])
                f.op(dve, lambda: V.tensor_scalar(out=est[:, 1:2], in0=est[:, 0:1], scalar1=1.0 / D, scalar2=1e-6, op0=ALU.mult, op1=ALU.add),
                     reads=[b_est], writes=[b_est])
                f.op(act, lambda: S.activation(out=est[:, 1:2], in_=est[:, 1:2], func=AF.Sqrt), reads=[b_est], writes=[b_est])
                f.op(dve, lambda: V.reciprocal(out=est[:, 1:2], in_=est[:, 1:2]), reads=[b_est], writes=[b_est])
                f.op(dve, lambda: V.scalar_tensor_tensor(out=hfb[:], in0=x1[:, tl, :], scalar=est[:, 1:2], in1=prow[:, GFFN:GFFN + D],
                                                         op0=ALU.mult, op1=ALU.mult), reads=[b_x1, b_est, b_prow], writes=[b_hfb])
                ptr_, bptr = bank[6]
                ptb = ptr_[:].bitcast(BF16)
                for c in range(8):
                    f.op(pe, lambda c=c: T.transpose(out=ptb[:, c * 128:(c + 1) * 128], in_=hfb[:, c * 128:(c + 1) * 128], identity=ident_b[:]),
                         reads=[b_hfb, b_identb], writes=[bptr])
                f.op(act, lambda: S.copy(out=hfT[:, :, ts_], in_=ptb.rearrange("p (c n) -> p c n", c=8)), reads=[bptr], writes=[b_hfT])
            for ffc in range(DFF // 128):
                fs = slice(ffc * 128, (ffc + 1) * 128)
                wg, bwg = wgr.next(); wu, bwu = wgr.next(); wd, bwd = wdr.next()
                wload(wg[:], bwg, w_gate[:, fs].rearrange("(c p) n -> p c n", p=128))
                wload(wu[:], bwu, w_up[:, fs].rearrange("(c p) n -> p c n", p=128))
                wload(wd[:], bwd, w_down[fs, :])
                (pg, bpg), (pu, bpu) = bank[6], bank[7]
                for c in range(8):
                    f.op(pe, lambda c=c: T.matmul(out=pg[:, 0:SB], lhsT=wg[:, c, :], rhs=hfT[:, c, :], start=(c == 0), stop=(c == 7)),
                         reads=[bwg, b_hfT], writes=[bpg])
                for c in range(8):
                    f.op(pe, lambda c=c: T.matmul(out=pu[:, 0:SB], lhsT=wu[:, c, :], rhs=hfT[:, c, :], start=(c == 0), stop=(c == 7)),
                         reads=[bwu, b_hfT], writes=[bpu])
                f.op(act, lambda: S.activation(out=sga[:], in_=pg[:, 0:SB], func=AF.Silu), reads=[bpg], writes=[b_sga])
                at, bat = actr.next()
                f.op(dve, lambda: V.tensor_tensor(out=at[:], in0=pu[:, 0:SB], in1=sga[:], op=ALU.mult), reads=[bpu, b_sga], writes=[bat])
                for tl in range(3):
                    for hf_ in range(2):
                        pd, bpd = bank[tl * 2 + hf_]
                        f.op(pe, lambda: T.matmul(out=pd[:], lhsT=at[:, tl * 128:(tl + 1) * 128], rhs=wd[:, hf_ * 512:(hf_ + 1) * 512],
                                                  start=(ffc == 0), stop=(ffc == DFF // 128 - 1)), reads=[bat, bwd], writes=[bpd])
            for tl in range(3):
                for hf_ in range(2):
                    pd, bpd = bank[tl * 2 + hf_]
                    yo, byo = yor.next()
                    f.op(dve, lambda: V.tensor_tensor(out=yo[:], in0=pd[:], in1=x1[:, tl, hf_ * 512:(hf_ + 1) * 512], op=ALU.add),
                         reads=[bpd, b_x1], writes=[byo])
                    r0 = c0 + tl * 128
                    f.dma(sp, y_out[r0:r0 + 128, hf_ * 512:(hf_ + 1) * 512], yo[:], byo, reads=[byo], is_output=True)
        f.barrier_all()
        f.release(mE)

    def rwkv_phase():
        mR = f.mark()
        BW = 256
        NB = NCOL // BW
        mu4b, b_mu4 = cload("mu4b", mu4_d[:, :], [128, 512], BF16)
        ml4b, b_ml4 = cload("ml4b", ml4_d[:, :], [128, 512], BF16)
        mui4b, b_mui4 = cload("mui4b", mui4_d[:, :], [128, 512], BF16)
        ident4, b_ident4 = f.sbuf("ident4", [128, 512], BF16)
        for i in range(4):
            f.dma(pool, ident4[:, i * 128:(i + 1) * 128], ident_d[:, :], b_ident4, writes=[b_ident4])
        resetm, b_resetm = cload("resetm", resetm_d[:, 0:BW], [128, BW])
        cm0, b_cm0 = cload("cm0", cm0_d[:, 0:BW], [128, BW], BF16)
        cm1, b_cm1 = cload("cm1", cm1_d[:, 0:BW], [128, BW], BF16)
        colmask, b_colmask = cload("colmask", colmask_d[:, :], [128, 256])
        wa2b, b_wa2 = cload("wa2b", wa2_d[:, :], [128, 512], BF16)
        g2b, b_g2 = cload("g2b", g2_d[:, :], [128, 512], BF16)
        sshift, b_sshift = cload("sshift", sshift_d[:, :, :], [128, 14, 4])
        cst, b_cst = f.sbuf("cst", [128, 4], F32)
        f.op(dve, lambda: V.memset(cst[:, 0:1], 1.0), writes=[b_cst])
        f.op(dve, lambda: V.memset(cst[:, 1:2], -0.5), writes=[b_cst])
        f.op(dve, lambda: V.memset(cst[:, 2:3], 64e-5), writes=[b_cst])
        carry, b_carry = f.sbuf("carry", [128, 14], F32)
        f.op(dve, lambda: V.memset(carry[:], 0.0), writes=[b_carry])
        H, b_H = f.sbuf("H", [128, 4, 64], F32)
        f.op(dve, lambda: V.memset(H[:], 0.0), writes=[b_H])
        Hb = [f.sbuf("Hb%d" % i, [128, 4, 64], BF16) for i in range(2)]
        wrr, _bw = f.sbuf("wrr", [128, 8, 1792], BF16)
        b_wrc = [Buf("wrr%d" % i) for i in range(14)]
        for i in range(14):
            wload(wrr[:, :, i * 128:(i + 1) * 128], b_wrc[i], w_in[:, 1536 + i * 128:1536 + (i + 1) * 128].rearrange("(c p) n -> p c n", p=128))
        ppr = Ring([f.psum("pp%d" % i, [128, 512], F32) for i in range(3)])
        pxr = Ring([f.sbuf("px%d" % i, [128, BW + 1], F32) for i in range(2)])
        rT, b_rT = f.sbuf("rT", [128, 4, BW], F32)
        kT, b_kT = f.sbuf("kT", [128, 4, BW], F32)
        vT, b_vT = f.sbuf("vT", [128, 4, BW], F32)
        m12, b_m12 = f.sbuf("m12", [128, BW], F32)
        m13, b_m13 = f.sbuf("m13", [128, BW], F32)
        twxa, b_twxa = f.sbuf("twxa", [128, BW], BF16)
        sg, b_sg = f.sbuf("sg", [128, BW], BF16)
        scr = [f.sbuf("scr%d" % i, [128, BW], F32) for i in range(8)]
        AT, b_AT = f.sbuf("AT", [128, 4, BW], BF16)
        BT, b_BT = f.sbuf("BT", [128, 4, BW], BF16)
        KT, b_KT2 = f.sbuf("KT", [128, 4, BW], BF16)
        RT, b_RT = f.sbuf("RT", [128, 4, BW], BF16)
        RT0, b_RT0 = f.sbuf("RT0", [128, 4, BW], BF16)
        RT1, b_RT1 = f.sbuf("RT1", [128, 4, BW], BF16)
        Eneg, b_Eneg = f.sbuf("Eneg", [128, 4, BW], F32)
        bonT, b_bonT = f.sbuf("bonT", [128, 4, BW], BF16)
        gT, b_gT = f.sbuf("gT", [128, 4, BW], BF16)
        Vtm, b_Vtm = f.sbuf("Vtm", [128, 2, 512], BF16)
        Btm, b_Btm = f.sbuf("Btm", [128, 2, 512], BF16)
        Ktm, b_Ktm = f.sbuf("Ktm", [128, 2, 512], BF16)
        ptBK, b_ptBK = f.psum("ptBK", [128, 2, 512], BF16)
        ptV, b_ptV = f.psum("ptV", [128, 512], F32)
        psA, b_psA = f.psum("psA", [128, 4, 128], F32)
        psB, b_psB = f.psum("psB", [128, 4, 128], F32)
        psC, b_psC = f.psum("psC", [128, 4, 128], F32)
        Lk = [f.sbuf("Lk%d" % i, [128, 4, 128], BF16) for i in range(2)]
        Nk = [f.sbuf("Nk%d" % i, [128, 4, 128], BF16) for i in range(2)]
        Yk = [f.sbuf("Yk%d" % i, [128, 4, 128], BF16) for i in range(2)]
        AKT, b_AKT = f.sbuf("AKT", [128, 4, 128], BF16)
        RBT, b_RBT = f.sbuf("RBT", [128, 4, 128], BF16)
        RKT, b_RKT = f.sbuf("RKT", [128, 4, 128], BF16)
        P1, b_P1 = f.sbuf("P1", [128, 4, 64], F32)
        Wb, b_Wb = f.sbuf("Wb", [128, 4, 64], BF16)
        Ub, b_Ub = f.sbuf("Ub", [128, 4, 64], BF16)
        Yall, b_Yall = f.sbuf("Yall", [128, 8, 64], F32)
        gst, b_gst = f.sbuf("gst", [128, 40], F32)
        ytmp2, b_ytmp = f.sbuf("ytmp", [128, 512], F32)
        ytmp = ytmp2[:].rearrange("p (a b) -> p a b", a=4)
        Ysq = ytmp2[:].rearrange("p (a b) -> p a b", a=8)
        b_Ysq = b_ytmp

        def flat(t3):
            return t3[:].rearrange("p a b -> p (a b)")

        for bi in (range(NB) if nb is None else nb):
            col0 = bi * BW
            is_loc = col0 >= NPRE
            is_smp = col0 >= NPRE + NOWN
            ht, bht, lc = hT_cols(col0, BW)
            for ch in range(14):
                bwt = b_wrc[ch]
                pp, bpp = ppr.next()
                for c in range(8):
                    f.op(pe, lambda c=c: T.matmul(out=pp[:, 0:BW], lhsT=wrr[:, c, ch * 128:(ch + 1) * 128], rhs=ht[:, c, lc:lc + BW],
                                                  start=(c == 0), stop=(c == 7)), reads=[bwt, bht], writes=[bpp])
                px, bpx = pxr.next()
                f.op(dve, lambda: V.tensor_copy(out=px[:, 0:1], in_=carry[:, ch:ch + 1]), reads=[b_carry], writes=[bpx])
                f.op(act, lambda: S.copy(out=px[:, 1:BW + 1], in_=pp[:, 0:BW]), reads=[bpp], writes=[bpx])
                f.op(dve, lambda: V.tensor_copy(out=carry[:, ch:ch + 1], in_=px[:, BW:BW + 1]), reads=[bpx], writes=[b_carry])
                if bi == 15:
                    f.dma(sp, p_out[ch, :, 0:128], px[:, 129:257], bpx, reads=[bpx], is_output=True)
                if is_smp:
                    f.dma(sp, p_out[ch, :, 128:384], px[:, 1:257], bpx, reads=[bpx], is_output=True)
                    f.op(dve, lambda: V.tensor_copy(out=px[:, 1:BW + 1:64], in_=sshift[:, ch, :]),
                         reads=[b_sshift], writes=[bpx])
                tmp, btmp = scr[0]
                if ch < 4:
                    dst, bdst = rT[:, ch, :], b_rT
                elif ch < 8:
                    dst, bdst = kT[:, ch - 4, :], b_kT
                elif ch < 12:
                    dst, bdst = vT[:, ch - 8, :], b_vT
                elif ch == 12:
                    dst, bdst = m12[:], b_m12
                else:
                    dst, bdst = m13[:], b_m13
                f.op(pool, lambda: G.tensor_scalar(out=tmp[:], in0=px[:, 0:BW], scalar1=pvec[:, MU_C + ch:MU_C + ch + 1], scalar2=None,
                                                   op0=ALU.mult), reads=[bpx, b_pvec], writes=[btmp])
                f.op(dve, lambda: V.scalar_tensor_tensor(out=dst, in0=px[:, 1:BW + 1], scalar=pv2[:, ch:ch + 1], in1=tmp[:],
                                                         op0=ALU.mult, op1=ALU.add), reads=[bpx, btmp, b_pv2], writes=[bdst])
                if is_smp:
                    f.op(dve, lambda: V.tensor_tensor(out=dst, in0=dst, in1=colmask[:], op=ALU.mult),
                         reads=[bdst, b_colmask], writes=[bdst])
            if rl < 2:
                continue
            f.op(act, lambda: S.activation(out=twxa[0:64, :], in_=m12[0:64, :], func=AF.Tanh), reads=[b_m12], writes=[b_twxa])
            f.op(dve, lambda: V.tensor_copy(out=twxa[64:128, :], in_=m12[64:128, :]), reads=[b_m12], writes=[b_twxa])
            f.op(act, lambda: S.activation(out=sg[:], in_=m13[:], func=AF.Sigmoid), reads=[b_m13], writes=[b_sg])
            for pr in range(4):
                pc = slice(pr * 128, (pr + 1) * 128)
                (s1, bs1), (s2, bs2), (s3, bs3), (s4, bs4), (s5, bs5), (s6, bs6), (s7, bs7) = scr[1:8]
                pp, bpp = ppr.next()
                f.op(pe, lambda: T.matmul(out=pp[:, 0:BW], lhsT=wa2b[0:64, pc], rhs=twxa[0:64, :], start=True, stop=True),
                     reads=[b_wa2, b_twxa], writes=[bpp])
                f.op(act, lambda: S.activation(out=s1[:], in_=pp[:, 0:BW], func=AF.Exp, bias=pv2[:, 14 + pr:15 + pr], scale=-1.0),
                     reads=[bpp, b_pv2], writes=[bs1])
                f.op(act, lambda: S.activation(out=s1[:], in_=s1[:], func=AF.Ln, bias=cst[:, 0:1], scale=1.0),
                     reads=[bs1, b_cst], writes=[bs1])
                f.op(act, lambda: S.activation(out=s2[:], in_=s1[:], func=AF.Exp, bias=cst[:, 1:2], scale=-1.0),
                     reads=[bs1, b_cst], writes=[bs2])
                if is_smp:
                    f.op(dve, lambda: V.tensor_tensor(out=s2[:], in0=s2[:], in1=colmask[:], op=ALU.mult),
                         reads=[bs2, b_colmask], writes=[bs2])
                f.op(dve, lambda: V.tensor_tensor_scan(out=s3[:], data0=resetm[:], data1=s2[:], initial=0.0,
                                                       op0=ALU.mult, op1=ALU.add), reads=[bs2, b_resetm], writes=[bs3])
                f.op(act, lambda: S.activation(out=Eneg[:, pr, :], in_=s3[:], func=AF.Exp, scale=-1.0), reads=[bs3], writes=[b_Eneg])
                f.op(act, lambda: S.activation(out=s4[:], in_=s3[:], func=AF.Exp), reads=[bs3], writes=[bs4])
                f.op(dve, lambda: V.tensor_tensor(out=s5[:], in0=s3[:], in1=s2[:], op=ALU.subtract), reads=[bs3, bs2], writes=[bs5])
                f.op(act, lambda: S.activation(out=s5[:], in_=s5[:], func=AF.Exp, scale=-1.0), reads=[bs5], writes=[bs5])
                pp, bpp = ppr.next()
                f.op(pe, lambda: T.matmul(out=pp[:, 0:BW], lhsT=wa2b[64:128, pc], rhs=twxa[64:128, :], start=True, stop=True),
                     reads=[b_wa2, b_twxa], writes=[bpp])
                f.op(act, lambda: S.activation(out=s6[:], in_=pp[:, 0:BW], func=AF.Sigmoid, bias=pvec[:, A0_C + pr:A0_C + pr + 1], scale=1.0),
                     reads=[bpp, b_pvec], writes=[bs6])
                f.op(dve, lambda: V.tensor_scalar(out=s1[:], in0=kT[:, pr, :], scalar1=pvec[:, KK_C + pr:KK_C + pr + 1], scalar2=None,
                                                  op0=ALU.mult), reads=[b_kT, b_pvec], writes=[bs1])
                f.op(act, lambda: S.activation(out=s7[:], in_=s1[:], func=AF.Square), reads=[bs1], writes=[bs7])
                pp, bpp = ppr.next()
                f.op(pe, lambda: T.matmul(out=pp[:, 0:BW], lhsT=bones[:], rhs=s7[:], start=True, stop=True),
                     reads=[b_bones, bs7], writes=[bpp])
                f.op(dve, lambda: V.tensor_scalar(out=s7[:], in0=pp[:, 0:BW], scalar1=1e-24, scalar2=None, op0=ALU.max),
                     reads=[bpp], writes=[bs7])
                f.op(act, lambda: S.activation(out=s7[:], in_=s7[:], func=AF.Sqrt), reads=[bs7], writes=[bs7])
                f.op(dve, lambda: V.reciprocal(out=s7[:], in_=s7[:]), reads=[bs7], writes=[bs7])
                f.op(dve, lambda: V.tensor_tensor(out=s1[:], in0=s1[:], in1=s7[:], op=ALU.mult), reads=[bs1, bs7], writes=[bs1])
                f.op(dve, lambda: V.tensor_scalar(out=s7[:], in0=s6[:], scalar1=pvec[:, KA_C + pr:KA_C + pr + 1],
                                                  scalar2=pv2[:, 18 + pr:19 + pr], op0=ALU.mult, op1=ALU.add),
                     reads=[bs6, b_pvec, b_pv2], writes=[bs7])
                f.op(dve, lambda: V.tensor_tensor(out=s7[:], in0=s7[:], in1=kT[:, pr, :], op=ALU.mult), reads=[bs7, b_kT], writes=[bs7])
                f.op(dve, lambda: V.scalar_tensor_tensor(out=AT[:, pr, :], in0=s1[:], scalar=-1.0, in1=s5[:], op0=ALU.mult, op1=ALU.mult),
                     reads=[bs1, bs5], writes=[b_AT])
                f.op(pool, lambda: G.tensor_tensor(out=s1[:], in0=s1[:], in1=s6[:], op=ALU.mult), reads=[bs1, bs6], writes=[bs1])
                f.op(pool, lambda: G.tensor_tensor(out=BT[:, pr, :], in0=s1[:], in1=s4[:], op=ALU.mult), reads=[bs1, bs4], writes=[b_BT])
                f.op(pool, lambda: G.tensor_tensor(out=KT[:, pr, :], in0=s7[:], in1=s4[:], op=ALU.mult), reads=[bs7, bs4], writes=[b_KT2])
                f.op(dve, lambda: V.tensor_tensor(out=RT[:, pr, :], in0=rT[:, pr, :], in1=Eneg[:, pr, :], op=ALU.mult),
                     reads=[b_rT, b_Eneg], writes=[b_RT])
                f.op(pool, lambda: G.tensor_tensor(out=RT0[:, pr, :], in0=RT[:, pr, :], in1=cm0[:], op=ALU.mult),
                     reads=[b_RT, b_cm0], writes=[b_RT0])
                f.op(pool, lambda: G.tensor_tensor(out=RT1[:, pr, :], in0=RT[:, pr, :], in1=cm1[:], op=ALU.mult),
                     reads=[b_RT, b_cm1], writes=[b_RT1])
                if is_loc:
                    f.op(dve, lambda: V.scalar_tensor_tensor(out=s1[:], in0=rT[:, pr, :], scalar=pvec[:, RK_C + pr:RK_C + pr + 1], in1=s7[:],
                                                             op0=ALU.mult, op1=ALU.mult), reads=[b_rT, bs7, b_pvec], writes=[bs1])
                    pp, bpp = ppr.next()
                    f.op(pe, lambda: T.matmul(out=pp[:, 0:BW], lhsT=bones[:], rhs=s1[:], start=True, stop=True),
                         reads=[b_bones, bs1], writes=[bpp])
                    f.op(dve, lambda: V.tensor_tensor(out=bonT[:, pr, :], in0=pp[:, 0:BW], in1=vT[:, pr, :], op=ALU.mult),
                         reads=[bpp, b_vT], writes=[b_bonT])
                    pp, bpp = ppr.next()
                    f.op(pe, lambda: T.matmul(out=pp[:, 0:BW], lhsT=g2b[:, pc], rhs=sg[:], start=True, stop=True),
                         reads=[b_g2, b_sg], writes=[bpp])
                    f.op(act, lambda: S.copy(out=gT[:, pr, :], in_=pp[:, 0:BW]), reads=[bpp], writes=[b_gT])
            for tl in range(2 if rl >= 3 else 0):
                Tg = bi * 2 + tl
                tc = slice(tl * 128, (tl + 1) * 128)
                for pr in range(4):
                    pc = slice(pr * 128, (pr + 1) * 128)
                    f.op(pe, lambda: T.transpose(out=ptBK[:, 0, pc], in_=BT[:, pr, tc], identity=ident_b[:]),
                         reads=[b_BT, b_identb], writes=[b_ptBK])
                    f.op(pe, lambda: T.transpose(out=ptBK[:, 1, pc], in_=KT[:, pr, tc], identity=ident_b[:]),
                         reads=[b_KT2, b_identb], writes=[b_ptBK])
                    f.op(pe, lambda: T.transpose(out=ptV[:, pc], in_=vT[:, pr, tc], identity=ident_f[:]),
                         reads=[b_vT, b_identf], writes=[b_ptV])
                f.op(act, lambda: S.copy(out=Btm[:, tl, :], in_=ptBK[:, 0, :]), reads=[b_ptBK], writes=[b_Btm])
                f.op(dve, lambda: V.tensor_copy(out=Ktm[:, tl, :], in_=ptBK[:, 1, :]), reads=[b_ptBK], writes=[b_Ktm])
                f.op(act, lambda: S.copy(out=Vtm[:, tl, :], in_=ptV[:]), reads=[b_ptV], writes=[b_Vtm])
                for Gi in range(2 if rl >= 4 else 0):
                    heads = [(hi, 4 * Gi + hi, (4 * Gi + hi) // 2, slice(64 * ((4 * Gi + hi) % 2), 64 * ((4 * Gi + hi) % 2) + 64))
                             for hi in range(4)]
                    for hi, h, pr, rows in heads:
                        f.op(pe, lambda: T.matmul(out=psA[:, hi, :], lhsT=BT[rows, pr, tc], rhs=AT[rows, pr, tc], start=True, stop=True),
                             reads=[b_BT, b_AT], writes=[b_psA], rg=rows.start)
                    for hi, h, pr, rows in heads:
                        f.op(pe, lambda: T.matmul(out=psB[:, hi, :], lhsT=AT[rows, pr, tc], rhs=BT[rows, pr, tc], start=True, stop=True),
                             reads=[b_BT, b_AT], writes=[b_psB], rg=rows.start)
                    (N0, bN0), (N1, bN1) = Nk
                    (L0, bL0), (L1, bL1) = Lk
                    (Y0, bY0), (Y1, bY1) = Yk
                    f.op(dve, lambda: V.tensor_tensor(out=flat(N0), in0=flat(psA), in1=mu4b[:], op=ALU.mult),
                         reads=[b_psA, b_mu4], writes=[bN0])
                    f.op(dve, lambda: V.tensor_tensor(out=flat(L0), in0=flat(psB), in1=ml4b[:], op=ALU.mult),
                         reads=[b_psB, b_ml4], writes=[bL0])
                    if cut <= 1:
                        continue
                    f.op(pool, lambda: G.tensor_tensor(out=flat(Y0), in0=flat(N0), in1=ident4[:], op=ALU.add),
                         reads=[bN0, b_ident4], writes=[bY0])
                    if cut <= 2:
                        continue
                    for k in range(min(5, cut - 2)):
                        Nc, bNc = Nk[k % 2]; Lc, bLc = Lk[k % 2]; Yc, bYc = Yk[k % 2]
                        Nn, bNn = Nk[(k + 1) % 2]; Ln, bLn = Lk[(k + 1) % 2]; Yn, bYn = Yk[(k + 1) % 2]
                        for hi, h, pr, rows in heads:
                            f.op(pe, lambda: T.matmul(out=psA[:, hi, :], lhsT=Nc[:, hi, :], rhs=Lc[:, hi, :], start=True, stop=True),
                                 reads=[bNc, bLc], writes=[b_psA])
                        if k < 4:
                            for hi, h, pr, rows in heads:
                                f.op(pe, lambda: T.matmul(out=psB[:, hi, :], lhsT=Lc[:, hi, :], rhs=Nc[:, hi, :], start=True, stop=True),
                                     reads=[bNc, bLc], writes=[b_psB])
                        f.op(act, lambda: S.copy(out=flat(Ln), in_=flat(psA)), reads=[b_psA], writes=[bLn])
                        if k < 4:
                            f.op(dve, lambda: V.tensor_copy(out=flat(Nn), in_=flat(psB)), reads=[b_psB], writes=[bNn])
                        for hi, h, pr, rows in heads:
                            f.op(pe, lambda: T.matmul(out=psC[:, hi, :], lhsT=ident_b[:], rhs=Yc[:, hi, :], start=True, stop=False),
                                 reads=[b_identb, bYc], writes=[b_psC])
                            f.op(pe, lambda: T.matmul(out=psC[:, hi, :], lhsT=Ln[:, hi, :], rhs=Yc[:, hi, :], start=False, stop=True),
                                 reads=[bLn, bYc], writes=[b_psC])
                        if k % 2 == 0:
                            f.op(dve, lambda: V.tensor_copy(out=flat(Yn), in_=flat(psC)), reads=[b_psC], writes=[bYn])
                        else:
                            f.op(act, lambda: S.copy(out=flat(Yn), in_=flat(psC)), reads=[b_psC], writes=[bYn])
                    TT, bTT = Yk[1]
                    if cut <= 7:
                        continue
                    for hi, h, pr, rows in heads:
                        f.op(pe, lambda: T.matmul(out=psA[:, hi, :], lhsT=KT[rows, pr, tc], rhs=AT[rows, pr, tc], start=True, stop=True),
                             reads=[b_KT2, b_AT], writes=[b_psA], rg=rows.start)
                    for hi, h, pr, rows in heads:
                        f.op(pe, lambda: T.matmul(out=psB[:, hi, :], lhsT=BT[rows, pr, tc], rhs=RT[rows, pr, tc], start=True, stop=True),
                             reads=[b_BT, b_RT], writes=[b_psB], rg=rows.start)
                    for hi, h, pr, rows in heads:
                        f.op(pe, lambda: T.matmul(out=psC[:, hi, :], lhsT=KT[rows, pr, tc], rhs=RT[rows, pr, tc], start=True, stop=True),
                             reads=[b_KT2, b_RT], writes=[b_psC], rg=rows.start)
                    f.op(dve, lambda: V.tensor_tensor(out=flat(AKT), in0=flat(psA), in1=mu4b[:], op=ALU.mult),
                         reads=[b_psA, b_mu4], writes=[b_AKT])
                    f.op(dve, lambda: V.tensor_tensor(out=flat(RBT), in0=flat(psB), in1=mui4b[:], op=ALU.mult),
                         reads=[b_psB, b_mui4], writes=[b_RBT])
                    f.op(dve, lambda: V.tensor_tensor(out=flat(RKT), in0=flat(psC), in1=mui4b[:], op=ALU.mult),
                         reads=[b_psC, b_mui4], writes=[b_RKT])
                    for hi, h, pr, rows in heads:
                        f.op(pe, lambda: T.matmul(out=psC[:, hi, 0:64], lhsT=AKT[:, hi, :], rhs=Vtm[:, tl, h * 64:(h + 1) * 64],
                                                  start=True, stop=True), reads=[b_AKT, b_Vtm], writes=[b_psC])
                    f.op(act, lambda: S.copy(out=P1[:], in_=psC[:, :, 0:64]), reads=[b_psC], writes=[b_P1])
                    gp = slice(2 * Gi, 2 * Gi + 2)
                    for c in range(2 if rl >= 5 else 0):
                        crow = slice(64 * c, 64 * c + 64)
                        smp = (Tg - 32) * 2 + c
                        if is_smp:
                            f.dma(sp, H[:, gp, :], swkv_d[smp, gp, :, :].rearrange("a p i -> p a i"), b_H, writes=[b_H])
                        hb, bhb = Hb[c]
                        f.op(pool, lambda: G.tensor_copy(out=hb[:, gp, :], in_=H[:, gp, :]), reads=[b_H], writes=[bhb])
                        for hi, h, pr, rows in heads:
                            f.op(pe, lambda: T.matmul(out=psA[:, hi, 0:64], lhsT=AT[rows, pr, tc], rhs=hb[rows, pr, :], start=True, stop=True),
                                 reads=[b_AT, bhb], writes=[b_psA], rg=rows.start)
                        f.op(dve, lambda: V.tensor_tensor(out=Wb[crow, :, :], in0=psA[crow, :, 0:64], in1=P1[crow, :, :], op=ALU.add),
                             reads=[b_psA, b_P1], writes=[b_Wb])
                        for hi, h, pr, rows in heads:
                            f.op(pe, lambda: T.matmul(out=psA[:, hi, 64:128], lhsT=TT[crow, hi, :], rhs=Wb[crow, hi, :], start=True, stop=True),
                                 reads=[bTT, b_Wb], writes=[b_psA], rg=crow.start)
                        f.op(act, lambda: S.copy(out=Ub[crow, :, :], in_=psA[crow, :, 64:128]), reads=[b_psA], writes=[b_Ub])
                        for hi, h, pr, rows in heads:
                            pcs = slice(pr * 128, (pr + 1) * 128)
                            f.op(pe, lambda: T.matmul(out=psB[:, hi, 0:64], lhsT=Btm[crow, tl, pcs], rhs=Ub[crow, hi, :], start=True, stop=False),
                                 reads=[b_Btm, b_Ub], writes=[b_psB], rg=crow.start)
                            f.op(pe, lambda: T.matmul(out=psB[:, hi, 0:64], lhsT=Ktm[crow, tl, pcs], rhs=Vtm[crow, tl, h * 64:(h + 1) * 64],
                                                      start=False, stop=True), reads=[b_Ktm, b_Vtm], writes=[b_psB], rg=crow.start)
                        for hi, h, pr, rows in heads:
                            gcol = Eneg[rows, pr, tl * 128 + 64 * c + 63:tl * 128 + 64 * c + 64]
                            f.op(pool, lambda: G.tensor_scalar(out=H[rows, pr, :], in0=H[rows, pr, :], scalar1=gcol, scalar2=None, op0=ALU.mult),
                                 reads=[b_H, b_Eneg], writes=[b_H])
                            f.op(dve, lambda: V.scalar_tensor_tensor(out=H[rows, pr, :], in0=psB[rows, hi, 0:64], scalar=gcol, in1=H[rows, pr, :],
                                                                     op0=ALU.mult, op1=ALU.add), reads=[b_psB, b_H, b_Eneg], writes=[b_H])
                        if is_smp:
                            f.dma(sp, wkv_out[1 + smp, gp, :, :].rearrange("a p i -> p a i"), H[:, gp, :], b_H, reads=[b_H], is_output=True)
                        elif Tg == 31 and c == 1:
                            f.dma(sp, wkv_out[0, gp, :, :].rearrange("a p i -> p a i"), H[:, gp, :], b_H, reads=[b_H], is_output=True)
                    if is_loc and rl >= 6:
                        (hb0, bhb0), (hb1, bhb1) = Hb
                        for hi, h, pr, rows in heads:
                            f.op(pe, lambda: T.matmul(out=psB[:, hi, 64:128], lhsT=RT0[rows, pr, tc], rhs=hb0[rows, pr, :], start=True, stop=False),
                                 reads=[b_RT0, bhb0], writes=[b_psB], rg=rows.start)
                            f.op(pe, lambda: T.matmul(out=psB[:, hi, 64:128], lhsT=RT1[rows, pr, tc], rhs=hb1[rows, pr, :], start=False, stop=False),
                                 reads=[b_RT1, bhb1], writes=[b_psB], rg=rows.start)
                            f.op(pe, lambda: T.matmul(out=psB[:, hi, 64:128], lhsT=RBT[:, hi, :], rhs=Ub[:, hi, :], start=False, stop=False),
                                 reads=[b_RBT, b_Ub], writes=[b_psB])
                            f.op(pe, lambda: T.matmul(out=psB[:, hi, 64:128], lhsT=RKT[:, hi, :], rhs=Vtm[:, tl, h * 64:(h + 1) * 64],
                                                      start=False, stop=True), reads=[b_RKT, b_Vtm], writes=[b_psB])
                        f.op(act, lambda: S.copy(out=Yall[:, 4 * Gi:4 * Gi + 4, :], in_=psB[:, :, 64:128]), reads=[b_psB], writes=[b_Yall])
                if is_loc and rl >= 7:
                    lcol = Tg * 128 - NPRE
                    f.op(dve, lambda: V.reduce_sum(out=gst[:, 0:8], in_=Yall[:], axis=AX.X), reads=[b_Yall], writes=[b_gst])
                    f.op(act, lambda: S.activation(out=Ysq, in_=Yall[:], func=AF.Square), reads=[b_Yall], writes=[b_Ysq])
                    f.op(dve, lambda: V.reduce_sum(out=gst[:, 8:16], in_=Ysq, axis=AX.X), reads=[b_Ysq, b_gst], writes=[b_gst])
                    f.op(dve, lambda: V.tensor_scalar(out=gst[:, 0:16], in0=gst[:, 0:16], scalar1=1.0 / 64, scalar2=None, op0=ALU.mult),
                         reads=[b_gst], writes=[b_gst])
                    f.op(dve, lambda: V.tensor_tensor(out=gst[:, 16:24], in0=gst[:, 0:8], in1=gst[:, 0:8], op=ALU.mult),
                         reads=[b_gst], writes=[b_gst])
                    f.op(dve, lambda: V.tensor_tensor(out=gst[:, 16:24], in0=gst[:, 8:16], in1=gst[:, 16:24], op=ALU.subtract),
                         reads=[b_gst], writes=[b_gst])
                    f.op(dve, lambda: V.tensor_scalar(out=gst[:, 16:24], in0=gst[:, 16:24], scalar1=64e-5, scalar2=None, op0=ALU.add),
                         reads=[b_gst], writes=[b_gst])
                    f.op(act, lambda: S.activation(out=gst[:, 16:24], in_=gst[:, 16:24], func=AF.Sqrt), reads=[b_gst], writes=[b_gst])
                    f.op(dve, lambda: V.reciprocal(out=gst[:, 24:32], in_=gst[:, 16:24]), reads=[b_gst], writes=[b_gst])
                    f.op(dve, lambda: V.tensor_tensor(out=Yall[:], in0=Yall[:], in1=gst[:, 0:8].unsqueeze(2).to_broadcast([128, 8, 64]),
                                                      op=ALU.subtract), reads=[b_Yall, b_gst], writes=[b_Yall])
                    f.op(dve, lambda: V.tensor_tensor(out=Yall[:], in0=Yall[:], in1=gst[:, 24:32].unsqueeze(2).to_broadcast([128, 8, 64]),
                                                      op=ALU.mult), reads=[b_Yall, b_gst], writes=[b_Yall])
                    yf = Yall[:].rearrange("p a b -> p (a b)")
                    f.op(dve, lambda: V.tensor_tensor(out=yf, in0=yf, in1=prow[:, GNW:GNW + 512], op=ALU.mult),
                         reads=[b_Yall, b_prow], writes=[b_Yall])
                    f.op(pool, lambda: G.tensor_tensor(out=yf, in0=yf, in1=prow[:, GNB:GNB + 512], op=ALU.add),
                         reads=[b_Yall, b_prow], writes=[b_Yall])
                    for pr in range(4):
                        f.op(pe, lambda: T.transpose(out=ptV[:, pr * 128:(pr + 1) * 128], in_=yf[:, pr * 128:(pr + 1) * 128], identity=ident_f[:]),
                             reads=[b_Yall, b_identf], writes=[b_ptV])
                    f.op(dve, lambda: V.tensor_tensor(out=ytmp, in0=ptV[:].rearrange("p (a b) -> p a b", a=4), in1=bonT[:, :, tc], op=ALU.add),
                         reads=[b_ptV, b_bonT], writes=[b_ytmp])
                    f.op(dve, lambda: V.tensor_tensor(out=yrT[:, :, lcol:lcol + 128], in0=ytmp, in1=gT[:, :, tc], op=ALU.mult),
                         reads=[b_ytmp, b_gT], writes=[b_yrT])
        f.barrier_all()
        f.release(mR)

    if "rwkv" in parts:
        rwkv_phase()
    oT, b_oT = f.sbuf("oT", [128, 4, NLOC], BF16)

    if stage >= 1.5 and "attn" in parts:
        m2 = f.mark()
        wv, b_wv = f.sbuf("wv", [128, 8, 512], BF16)
        wload(wv[:], b_wv, w_in[:, 1024:1536].rearrange("(c p) n -> p c n", p=128))
        Vh, b_Vh = f.sbuf("Vh", [128, NT, 129], BF16)
        f.op(pool, lambda: G.memset(Vh[:], 1.0), writes=[b_Vh])
        pvr = Ring([f.psum("pv%d" % i, [128, 512], F32) for i in range(1)])
        vor = Ring([f.sbuf("vo%d" % i, [128, 128], F32) for i in range(2)])

        def v_project(h):
            for t in range(NT):
                pv, bpv = pvr.next()
                ht, bht, lc = hT_cols(t * 128, 128)
                for c in range(8):
                    f.op(pe, lambda c=c: T.matmul(out=pv[:, 0:128], lhsT=ht[:, c, lc:lc + 128], rhs=wv[:, c, h * 128:(h + 1) * 128],
                                                  start=(c == 0), stop=(c == 7)), reads=[bht, b_wv], writes=[bpv])
                f.op(act, lambda: S.copy(out=Vh[:, t, 0:128], in_=pv[:, 0:128]), reads=[bpv], writes=[b_Vh])
                if t >= 16:
                    vo, bvo = vor.next()
                    f.op(dve, lambda: V.tensor_copy(out=vo[:], in_=pv[:, 0:128]), reads=[bpv], writes=[bvo])
                    f.dma(sp, v_out[(t - 16) * 128:(t - 15) * 128, h * 128:(h + 1) * 128], vo[:], bvo, reads=[bvo], is_output=True)

        KT_, b_KT = f.sbuf("KhT", [68, 2, NCOL], BF16)
        QT_, b_QT = f.sbuf("QhT", [68, 2, NLOC], BF16)
        wq, b_wq = f.sbuf("wq", [128, 8, 128], BF16)
        wk, b_wk = f.sbuf("wk", [128, 8, 128], BF16)
        pqr = Ring([f.psum("pq%d" % i, [64, 512], F32) for i in range(1)])
        psq, b_psq = f.psum("psq", [64, 512], F32)
        sqt, b_sqt = f.sbuf("sqt", [64, 512], F32)
        rnt, b_rnt = f.sbuf("rnt", [64, 512], F32)
        kor = Ring([f.sbuf("ko%d" % i, [64, 512], F32) for i in range(1)])
        pS_r = Ring([f.psum("pS%d" % i, [128, 2, 256], F32) for i in range(2)])
        pO, b_pO = f.psum("pO", [128, 2, 2, 256], F32)
        PTr = Ring([f.sbuf("PT%d" % i, [128, 2, 256], BF16) for i in range(2)])
        osb, b_osb = f.sbuf("osb", [128, 128], F32)
        osb2, b_osb2 = f.sbuf("osb2", [128, 128], F32)
        obf, b_obf = f.sbuf("obf", [128, 128], BF16)
        ost, b_ost = f.sbuf("ost", [128, 8], F32)
        pTo, b_pTo = f.psum("pTo", [128, 1024], BF16)

        def qk_project(wt, bwt, ncols, dstT, bdst, gcol, is_k):
            nb = (ncols + 511) // 512
            for bi in range(nb):
                c0 = bi * 512
                n = min(512, ncols - c0)
                gc0 = c0 if is_k else c0 + NPRE
                ht, bht, lc = hT_cols(gc0, n)
                for cmp_ in range(2):
                    pq, bpq = pqr.next()
                    for c in range(8):
                        f.op(pe, lambda c=c: T.matmul(out=pq[:, 0:n], lhsT=wt[:, c, cmp_ * 64:(cmp_ + 1) * 64],
                                                      rhs=ht[:, c, lc:lc + n], start=(c == 0), stop=(c == 7)),
                             reads=[bht, bwt], writes=[bpq])
                    f.op(act, lambda: S.activation(out=sqt[:, 0:n], in_=pq[:, 0:n], func=AF.Square),
                         reads=[bpq], writes=[b_sqt])
                    f.op(pe, lambda: T.matmul(out=psq[:, 0:n], lhsT=bones[0:64, 0:64], rhs=sqt[:, 0:n], start=True, stop=True),
                         reads=[b_sqt, b_bones], writes=[b_psq])
                    f.op(dve, lambda: V.tensor_scalar(out=rnt[:, 0:n], in0=psq[:, 0:n], scalar1=1.0 / 64, scalar2=1e-6,
                                                      op0=ALU.mult, op1=ALU.add), reads=[b_psq], writes=[b_rnt])
                    f.op(act, lambda: S.activation(out=rnt[:, 0:n], in_=rnt[:, 0:n], func=AF.Sqrt), reads=[b_rnt], writes=[b_rnt])
                    f.op(dve, lambda: V.reciprocal(out=rnt[:, 0:n], in_=rnt[:, 0:n]), reads=[b_rnt], writes=[b_rnt])
                    f.op(dve, lambda: V.scalar_tensor_tensor(out=dstT[0:64, cmp_, c0:c0 + n], in0=pq[:, 0:n], scalar=gcol,
                                                             in1=rnt[:, 0:n], op0=ALU.mult, op1=ALU.mult),
                         reads=[bpq, b_rnt, b_pvec, b_pv2], writes=[bdst])
                    if is_k and gc0 >= NPRE:
                        ko, bko = kor.next()
                        f.op(pool if False else dve, lambda: V.scalar_tensor_tensor(out=ko[:, 0:n], in0=pq[:, 0:n], scalar=gcol,
                                                                 in1=rnt[:, 0:n], op0=ALU.mult, op1=ALU.mult),
                             reads=[bpq, b_rnt, b_pvec], writes=[bko])
                        r0 = cur_h[0] * 128 + cmp_ * 64
                        f.dma(sp, k_out[r0:r0 + 64, gc0 - NPRE:gc0 - NPRE + n], ko[:, 0:n], bko, reads=[bko], is_output=True)

        qs_tm, b_qs = f.sbuf("qs_tm", [128, 2, 512], BF16)
        ks_tm, b_ks = f.sbuf("ks_tm", [128, 2, 512], BF16)
        vs_tm, b_vs = f.sbuf("vs_tm", [128, 2, 512], BF16)
        cur_h = [0]
        for h in range((4 if dbg is None else 1) if stage >= 1.6 else 0):
            cur_h[0] = h
            wload(wq[:], b_wq, w_in[:, h * 128:(h + 1) * 128].rearrange("(c p) n -> p c n", p=128))
            wload(wk[:], b_wk, w_in[:, 512 + h * 128:512 + (h + 1) * 128].rearrange("(c p) n -> p c n", p=128))
            for cmp_ in range(2):
                f.dma(pool, KT_[64:68, cmp_, 0:4096], kb_d[h, :, :], b_KT, writes=[b_KT])
                f.dma(pool, QT_[64:68, cmp_, 0:2048], qb_d[h, :, :], b_QT, writes=[b_QT])
            if dbg == "attn":
                f.op(dve, lambda: V.memset(KT_[:], 0.125), writes=[b_KT])
                f.op(dve, lambda: V.memset(QT_[:], 0.125), writes=[b_QT])
            elif stage >= 2.0:
                v_project(h)
            if stage >= 2.1 and dbg is None:
                qk_project(wk, b_wk, NCOL, KT_, b_KT, pvec[0:64, KG_C:KG_C + 1], True)
                qk_project(wq, b_wq, NLOC, QT_, b_QT, pv2[0:64, 22:23], False)
            if "samp" in parts:
                for tl in range(2):
                    for cmp_ in range(2):
                        f.op(pe, lambda: T.transpose(out=pTo[:, 0:64], in_=QT_[0:64, cmp_, 2048 + tl * 128:2048 + (tl + 1) * 128], identity=ident_b[0:64, 0:64]),
                             reads=[b_QT, b_identb], writes=[b_pTo])
                        f.op(act, lambda: S.copy(out=qs_tm[:, tl, h * 128 + cmp_ * 64:h * 128 + (cmp_ + 1) * 64], in_=pTo[:, 0:64]),
                             reads=[b_pTo], writes=[b_qs])
                        f.op(pe, lambda: T.transpose(out=pTo[:, 0:64], in_=KT_[0:64, cmp_, 4096 + tl * 128:4096 + (tl + 1) * 128], identity=ident_b[0:64, 0:64]),
                             reads=[b_KT, b_identb], writes=[b_pTo])
                        f.op(act, lambda: S.copy(out=ks_tm[:, tl, h * 128 + cmp_ * 64:h * 128 + (cmp_ + 1) * 64], in_=pTo[:, 0:64]),
                             reads=[b_pTo], writes=[b_ks])
                    f.op(pool, lambda: G.tensor_copy(out=vs_tm[:, tl, h * 128:(h + 1) * 128], in_=Vh[:, 32 + tl, 0:128]),
                         reads=[b_Vh], writes=[b_vs])
            if stage < 2.2:
                continue
            def emit_S(g, kt):
                d1 = (kt == 16 + 2 * g + 1)
                q0 = 128 if d1 else 0
                pS, bpS = pS_r.next()
                for cmp_ in range(2):
                    f.op(pe, lambda cmp_=cmp_: T.matmul(out=pS[:, cmp_, q0:256], lhsT=KT_[:, cmp_, kt * 128:(kt + 1) * 128],
                                                        rhs=QT_[:, cmp_, g * 256 + q0:(g + 1) * 256], start=True, stop=True),
                         reads=[b_KT, b_QT], writes=[bpS])
                return (g, kt, pS, bpS)

            def emit_rest(st_):
                g, kt, pS, bpS = st_
                d0 = (kt == 16 + 2 * g)
                d1 = (kt == 16 + 2 * g + 1)
                q0 = 128 if d1 else 0
                PT, bPT = PTr.next()
                f.op(act, lambda: S.activation(out=PT[:, :, q0:256], in_=pS[:, :, q0:256], func=AF.Exp),
                     reads=[bpS], writes=[bPT])
                if d0 or d1:
                    for cmp_ in range(2):
                        f.op(pool, lambda cmp_=cmp_: G.tensor_tensor(out=PT[:, cmp_, q0:q0 + 128], in0=PT[:, cmp_, q0:q0 + 128],
                                                                     in1=tri_b[:], op=ALU.mult),
                             reads=[bPT, b_tri], writes=[bPT])
                for cmp_ in range(2):
                    for sub in range(2):
                        if d1 and sub == 0:
                            continue
                        last = (kt == 16 + 2 * g + sub)
                        f.op(pe, lambda cmp_=cmp_, sub=sub, last=last: T.matmul(
                            out=pO[:, cmp_, sub, 0:129], lhsT=PT[:, cmp_, sub * 128:(sub + 1) * 128], rhs=Vh[:, kt, :],
                            start=(kt == 0), stop=last), reads=[bPT, b_Vh], writes=[b_pO])

            steps = [(g, kt) for g in range(ng) for kt in range(16 + 2 * g + 2)]
            pend = None
            for si in range(len(steps) + 1):
                cur = emit_S(*steps[si]) if si < len(steps) else None
                if pend is not None:
                    emit_rest(pend)
                    g = pend[0]
                    if pend[1] != 16 + 2 * g + 1:
                        pend = cur
                        continue
                else:
                    pend = cur
                    continue
                pend = cur
                for sub in range(2):
                    lcq = g * 256 + sub * 128
                    f.op(dve, lambda: V.reciprocal(out=ost[:, 0:2], in_=pO[:, :, sub, 128]), reads=[b_pO], writes=[b_ost])
                    f.op(dve, lambda: V.tensor_tensor(out=ost[:, 2:3], in0=ost[:, 1:2], in1=NLAM, op=ALU.mult),
                         reads=[b_ost, b_lt], writes=[b_ost])
                    f.op(dve, lambda: V.tensor_scalar(out=osb[:], in0=pO[:, 0, sub, 0:128], scalar1=ost[:, 0:1], scalar2=None,
                                                      op0=ALU.mult), reads=[b_pO, b_ost], writes=[b_osb])
                    f.op(dve, lambda: V.scalar_tensor_tensor(out=osb[:], in0=pO[:, 1, sub, 0:128], scalar=ost[:, 2:3], in1=osb[:],
                                                             op0=ALU.mult, op1=ALU.add), reads=[b_pO, b_ost, b_osb], writes=[b_osb])
                    finalize_o(f, nc, osb, b_osb, osb2, b_osb2, obf, b_obf, ost, b_ost, prow, b_prow, GAO, lam_init,
                               pTo, b_pTo, ident_b, b_identb, oT, b_oT, h, lcq)

        if "samp" in parts:
            selb, b_selb = cload("selb", sel_d[:, :], [128, 512], BF16)
            sel0b, b_sel0b = cload("sel0b", sel0_d[:, :], [128, 256], BF16)
            sbias, b_sbias = cload("sbias", sbias_d[:, :], [128, 520])
            hmask, b_hmask = cload("hmask", hmask_d[:, :], [8, 4])
            e0t, b_e0 = cload("e0t", e0_d[:, :], [8, 4])
            e1t, b_e1 = cload("e1t", e1_d[:, :], [8, 4])
            iop, b_iop = cload("iop", iota_d[:, :], [128, 1], I32)
            pti, b_pti = f.sbuf("pti", [128, 256], I32)
            f.dma(sp, pti[:], ptab_d[0:1, :].partition_broadcast(128), b_pti, writes=[b_pti])
            ptf, b_ptf = f.sbuf("ptf", [128, 256], F32)
            iof, b_iof = f.sbuf("iof", [128, 1], F32)
            idx, b_idx = f.sbuf("idx", [128, 256], I32)
            f.op(dve, lambda: V.tensor_copy(out=ptf[:], in_=pti[:]), reads=[b_pti], writes=[b_ptf])
            f.op(dve, lambda: V.tensor_copy(out=iof[:], in_=iop[:]), reads=[b_iop], writes=[b_iof])
            f.op(dve, lambda: V.tensor_scalar(out=ptf[:], in0=ptf[:], scalar1=128.0, scalar2=iof[:, 0:1], op0=ALU.mult, op1=ALU.add),
                 reads=[b_ptf, b_iof], writes=[b_ptf])
            f.op(dve, lambda: V.tensor_copy(out=idx[:], in_=ptf[:]), reads=[b_ptf], writes=[b_idx])
            cmb, b_cmb = f.sbuf("cmb", [8, 4], F32)
            f.op(dve, lambda: V.scalar_tensor_tensor(out=cmb[:], in0=e1t[:], scalar=lt[0:8, 5:6], in1=e0t[:], op0=ALU.mult, op1=ALU.add),
                 reads=[b_e1, b_e0, b_lt], writes=[b_cmb])
            onesc, b_onesc = f.sbuf("onesc", [128, 1], F32)
            f.op(dve, lambda: V.memset(onesc[:], 1.0), writes=[b_onesc])
            Ktr = Ring([f.sbuf("Kt%d" % i, [128, 512], F32) for i in range(2)])
            Vtr = Ring([f.sbuf("Vt%d" % i, [128, 512], F32) for i in range(2)])
            prodr = Ring([f.sbuf("prod%d" % i, [128, 512], F32) for i in range(1)])
            qbc, b_qbc = f.sbuf("qbc", [128, 512], F32)
            spg_r = Ring([f.sbuf("spg%d" % i, [128, 16], F32) for i in range(3)])
            osm, b_osm = f.sbuf("osm", [8, 512], F32)
            osel, b_osel = f.sbuf("osel", [8, 128], F32)
            ofin, b_ofin = f.sbuf("ofin", [4, 128], F32)
            ofin2, b_ofin2 = f.sbuf("ofin2", [4, 128], F32)
            ofb, b_ofb = f.sbuf("ofb", [4, 128], BF16)
            sst, b_sst = f.sbuf("sst", [8, 8], F32)
            pS0, bpS0 = pS_r.items[0]
            pS0f = pS0[:].rearrange("p a b -> p (a b)")
            pso = pO[0:8, 0, :, :].rearrange("p a b -> p (a b)")
            psz = pO[0:8, 1, 0, 0:1]
            pvr0, bpvr0 = pvr.items[0]
            for s in range(4):
                tl = s // 2
                col = 2048 + 128 * tl + 64 * (s % 2) + 1
                f.op(pe, lambda: T.matmul(out=pS0f, lhsT=selb[:, s * 128:(s + 1) * 128], rhs=qs_tm[:, tl, :], start=True, stop=True),
                     reads=[b_selb, b_qs], writes=[bpS0])
                f.op(act, lambda: S.copy(out=qbc[:], in_=pS0f), reads=[bpS0], writes=[b_qbc])
                for pg in range(65):
                    Kt, bKt = Ktr.next(); Vt, bVt = Vtr.next(); prod, bprod = prodr.next(); spg, bspg = spg_r.next()
                    if pg < 64:
                        ic = s * 64 + pg
                        f.dma(pool, None, None, bKt, reads=[b_idx], writes=[bKt],
                              fn=lambda: G.indirect_dma_start(out=Kt[:, :], out_offset=None, in_=ck_d[:, :],
                                                              in_offset=bass.IndirectOffsetOnAxis(ap=idx[:, ic:ic + 1], axis=0)))
                        f.dma(pool, None, None, bVt, reads=[b_idx], writes=[bVt],
                              fn=lambda: G.indirect_dma_start(out=Vt[:, :], out_offset=None, in_=cv_d[:, :],
                                                              in_offset=bass.IndirectOffsetOnAxis(ap=idx[:, ic:ic + 1], axis=0)))
                    else:
                        f.op(pe, lambda: T.matmul(out=pS0f, lhsT=sel0b[:, (s % 2) * 128:(s % 2 + 1) * 128], rhs=ks_tm[:, tl, :], start=True, stop=True),
                             reads=[b_sel0b, b_ks], writes=[bpS0])
                        f.op(act, lambda: S.copy(out=Kt[:], in_=pS0f), reads=[bpS0], writes=[bKt])
                        f.op(pe, lambda: T.matmul(out=pS0f, lhsT=sel0b[:, (s % 2) * 128:(s % 2 + 1) * 128], rhs=vs_tm[:, tl, :], start=True, stop=True),
                             reads=[b_sel0b, b_vs], writes=[bpS0])
                        f.op(act, lambda: S.copy(out=Vt[:], in_=pS0f), reads=[bpS0], writes=[bVt])
                    if pg % 2 == 0:
                        f.op(pool, lambda: G.tensor_tensor(out=prod[:], in0=Kt[:], in1=qbc[:], op=ALU.mult), reads=[bKt, b_qbc], writes=[bprod])
                    else:
                        f.op(dve, lambda: V.tensor_tensor(out=prod[:], in0=Kt[:], in1=qbc[:], op=ALU.mult), reads=[bKt, b_qbc], writes=[bprod])
                    f.op(dve, lambda: V.reduce_sum(out=spg[:, 0:8], in_=prod[:].rearrange("p (g d) -> p g d", g=8), axis=AX.X),
                         reads=[bprod], writes=[bspg])
                    f.op(dve, lambda: V.tensor_tensor(out=spg[:, 0:8], in0=spg[:, 0:8], in1=sbias[:, pg * 8:(pg + 1) * 8], op=ALU.add),
                         reads=[bspg, b_sbias], writes=[bspg])
                    f.op(act, lambda: S.activation(out=spg[:, 8:16], in_=spg[:, 0:8], func=AF.Exp), reads=[bspg], writes=[bspg])
                    f.op(pe, lambda: T.matmul(out=pso, lhsT=spg[:, 8:16], rhs=Vt[:], start=(pg == 0), stop=(pg == 64)),
                         reads=[bspg, bVt], writes=[b_pO])
                    f.op(pe, lambda: T.matmul(out=psz, lhsT=spg[:, 8:16], rhs=onesc[:], start=(pg == 0), stop=(pg == 64)),
                         reads=[bspg, b_onesc], writes=[b_pO])
                f.op(dve, lambda: V.reciprocal(out=sst[:, 0:1], in_=psz), reads=[b_pO], writes=[b_sst])
                f.op(dve, lambda: V.tensor_scalar(out=osm[:], in0=pso, scalar1=sst[:, 0:1], scalar2=None, op0=ALU.mult),
                     reads=[b_pO, b_sst], writes=[b_osm])
                f.op(dve, lambda: V.tensor_tensor(out=osm[:].rearrange("p (h d) -> p h d", h=4), in0=osm[:].rearrange("p (h d) -> p h d", h=4),
                                                  in1=hmask[:].unsqueeze(2).to_broadcast([8, 4, 128]), op=ALU.mult),
                     reads=[b_osm, b_hmask], writes=[b_osm])
                f.op(dve, lambda: V.reduce_sum(out=osel[:], in_=osm[:].rearrange("p (h d) -> p d h", h=4), axis=AX.X),
                     reads=[b_osm], writes=[b_osel])
                f.op(pe, lambda: T.matmul(out=pvr0[0:4, 0:128], lhsT=cmb[:], rhs=osel[:], start=True, stop=True),
                     reads=[b_cmb, b_osel], writes=[bpvr0])
                f.op(dve, lambda: V.tensor_copy(out=ofin[:], in_=pvr0[0:4, 0:128]), reads=[bpvr0], writes=[b_ofin])
                f.op(dve, lambda: V.memset(sst[0:4, 1:2], 0.0), writes=[b_sst])
                f.op(act, lambda: S.activation(out=ofin2[:], in_=ofin[:], func=AF.Square, accum_out=sst[0:4, 1:2]),
                     reads=[b_ofin, b_sst], writes=[b_ofin2, b_sst])
                f.op(dve, lambda: V.tensor_scalar(out=sst[0:4, 2:3], in0=sst[0:4, 1:2], scalar1=1.0 / 128, scalar2=1e-6, op0=ALU.mult, op1=ALU.add),
                     reads=[b_sst], writes=[b_sst])
                f.op(act, lambda: S.activation(out=sst[0:4, 2:3], in_=sst[0:4, 2:3], func=AF.Sqrt), reads=[b_sst], writes=[b_sst])
                f.op(dve, lambda: V.reciprocal(out=sst[0:4, 2:3], in_=sst[0:4, 2:3]), reads=[b_sst], writes=[b_sst])
                f.op(dve, lambda: V.tensor_scalar(out=sst[0:4, 3:4], in0=sst[0:4, 2:3], scalar1=(1.0 - lam_init), scalar2=None, op0=ALU.mult),
                     reads=[b_sst], writes=[b_sst])
                f.op(dve, lambda: V.scalar_tensor_tensor(out=ofb[:], in0=ofin[:], scalar=sst[0:4, 3:4], in1=prow[0:4, GAO:GAO + 128],
                                                         op0=ALU.mult, op1=ALU.mult), reads=[b_ofin, b_sst, b_prow], writes=[b_ofb])
                f.op(pe, lambda: T.transpose(out=pTo[:, 0:4], in_=ofb[:], identity=ident_b[0:4, 0:4]), reads=[b_ofb, b_identb], writes=[b_pTo])
                f.op(act, lambda: S.copy(out=oT[:, :, col], in_=pTo[:, 0:4]), reads=[b_pTo], writes=[b_oT])
        f.release(m2)
        f.barrier_all()

    if "epi" in parts:
        epilogue()
    f.release(m_pre)
    f.finish()
    f.close()
    return nc


def finalize_o(f, nc, osb, b_osb, osb2, b_osb2, obf, b_obf, ost, b_ost, prow, b_prow, GAO, lam_init,
               pTo, b_pTo, ident_b, b_identb, oT, b_oT, h, lcq):
    V, S, T = nc.vector, nc.scalar, nc.tensor
    dve, act, pe = f.dve, f.act, f.pe
    f.op(dve, lambda: V.memset(ost[:, 4:5], 0.0), writes=[b_ost])
    f.op(act, lambda: S.activation(out=osb2[:], in_=osb[:], func=AF.Square, accum_out=ost[:, 4:5]),
         reads=[b_osb, b_ost], writes=[b_osb2, b_ost])
    f.op(dve, lambda: V.tensor_scalar(out=ost[:, 5:6], in0=ost[:, 4:5], scalar1=1.0 / 128, scalar2=1e-6,
                                      op0=ALU.mult, op1=ALU.add), reads=[b_ost], writes=[b_ost])
    f.op(act, lambda: S.activation(out=ost[:, 5:6], in_=ost[:, 5:6], func=AF.Sqrt), reads=[b_ost], writes=[b_ost])
    f.op(dve, lambda: V.reciprocal(out=ost[:, 5:6], in_=ost[:, 5:6]), reads=[b_ost], writes=[b_ost])
    f.op(dve, lambda: V.tensor_scalar(out=ost[:, 6:7], in0=ost[:, 5:6], scalar1=(1.0 - lam_init), scalar2=None,
                                      op0=ALU.mult), reads=[b_ost], writes=[b_ost])
    f.op(dve, lambda: V.scalar_tensor_tensor(out=obf[:], in0=osb[:], scalar=ost[:, 6:7], in1=prow[:, GAO:GAO + 128],
                                             op0=ALU.mult, op1=ALU.mult), reads=[b_osb, b_ost, b_prow], writes=[b_obf])
    f.op(pe, lambda: T.transpose(out=pTo[:, 0:128], in_=obf[:], identity=ident_b[:]), reads=[b_obf, b_identb], writes=[b_pTo])
    f.op(act, lambda: S.copy(out=oT[:, h, lcq:lcq + 128], in_=pTo[:, 0:128]), reads=[b_pTo], writes=[b_oT])


def sample_attention(f, nc, L):
    pass


def _consts(half):
    c = {}
    c["ident"] = np.eye(128, dtype=np.float32)
    s_idx = np.arange(128)[:, None]; t_idx = np.arange(128)[None, :]
    same = (s_idx // 64) == (t_idx // 64)
    mu = ((s_idx < t_idx) & same).astype(np.float32)
    mui = ((s_idx <= t_idx) & same).astype(np.float32)
    ml = mu.T.copy()
    c["mu4"] = np.tile(mu, (1, 4)); c["ml4"] = np.tile(ml, (1, 4)); c["mui4"] = np.tile(mui, (1, 4))
    c["tri"] = (s_idx <= t_idx).astype(np.float32)
    cmk = np.zeros((128, 256), np.float32)
    cmk[:, [1, 65, 129, 193]] = 1.0
    c["colmask"] = cmk
    rm = np.ones((128, 512), np.float32); rm[:, ::64] = 0.0
    c["resetm"] = rm
    col = np.arange(512)[None, :]
    c["cm0"] = np.broadcast_to(((col % 128) < 64).astype(np.float32), (128, 512)).copy()
    c["cm1"] = np.broadcast_to(((col % 128) >= 64).astype(np.float32), (128, 512)).copy()
    c["bones"] = same.astype(np.float32)
    sel = np.zeros((128, 4, 128), np.float32)
    sel0 = np.zeros((128, 2, 128), np.float32)
    for s in range(4):
        sel[1 + 64 * (s % 2), s, :] = 1.0
    for r in range(2):
        sel0[1 + 64 * r, r, 0] = 1.0
    c["sel"] = sel.reshape(128, 512); c["sel0"] = sel0.reshape(128, 256)
    c["iotap"] = np.arange(128, dtype=np.int32).reshape(128, 1)
    hm = np.zeros((8, 4), np.float32); e0 = np.zeros((8, 4), np.float32); e1 = np.zeros((8, 4), np.float32)
    for h in range(4):
        for cc in range(2):
            hm[h * 2 + cc, h] = 1.0
        e0[h * 2, h] = 1.0; e1[h * 2 + 1, h] = 1.0
    c["hmask"] = hm; c["e0"] = e0; c["e1"] = e1
    slopes = np.array([2.0 ** (-8.0 * (h + 1) / 4) for h in range(4)], np.float64)
    kcol = np.arange(4096)
    kb = np.zeros((4, 4, 4096), np.float32)
    qb = np.zeros((4, 4, 2048), np.float32)
    qpos = 2048 + np.arange(2048)
    for h in range(4):
        kb[h, 0] = slopes[h] * 128 * (kcol // 128)
        if half == 0:
            kb[h, 0, :2048] = NEG
        kb[h, 1] = slopes[h] * (kcol % 128)
        kb[h, 2] = 1.0; kb[h, 3] = 1.0
        qb[h, 0] = 1.0; qb[h, 1] = 1.0
        qb[h, 2] = -slopes[h] * 128 * (qpos // 128)
        qb[h, 3] = -slopes[h] * (qpos % 128)
    c["kb"] = kb; c["qb"] = qb
    sb = np.zeros((128, 65, 8), np.float32)
    slot = np.arange(128)[:, None]
    for pg in range(64):
        dist = 8192 - (128 * pg + slot)
        for h in range(4):
            sb[:, pg, 2 * h] = (-slopes[h] * dist)[:, 0]; sb[:, pg, 2 * h + 1] = (-slopes[h] * dist)[:, 0]
    sb[1:, 64, :] = NEG
    c["sbias"] = sb.reshape(128, 65 * 8)
    return c


_NC_CACHE = {}
_LAST = None


def kernel(**inp):
    f32 = np.float32
    xp = np.asarray(inp["x_prompt"], f32); xs = np.asarray(inp["x_sample"], f32)
    g = lambda k: np.ascontiguousarray(np.asarray(inp[k], f32)[0])
    w_in = g("w_in")
    shared = {
        "w_in": w_in, "w_pa": g("w_pa"), "w_pb": g("w_pb"), "w_out": g("w_out"),
        "w_gate": g("w_gate"), "w_up": g("w_up"), "w_down": g("w_down"),
        "wa2": np.ascontiguousarray(np.concatenate([g("w2"), g("a2")], 0)), "g2": g("g2"),
        "ck": np.ascontiguousarray(np.asarray(inp["cache_k"], f32).reshape(2560 * 128, 512)),
        "cv": np.ascontiguousarray(np.asarray(inp["cache_v"], f32).reshape(2560 * 128, 512)),
    }
    pvec = np.zeros((128, 36), f32)
    pvec[:, 0:14] = g("shift_mu").reshape(14, 128).T
    pvec[:, 14:18] = g("w0").reshape(4, 128).T
    pvec[:, 18:22] = g("a0").reshape(4, 128).T
    pvec[:, 22:26] = g("k_k").reshape(4, 128).T
    pvec[:, 26:30] = g("k_a").reshape(4, 128).T
    pvec[:, 30:34] = g("r_k").reshape(4, 128).T
    pvec[:, 34] = np.tile(g("q_gain"), 2); pvec[:, 35] = np.tile(g("k_gain"), 2)
    prow = np.concatenate([g("norm_mix"), g("norm_ffn"), g("attn_out_gain"), g("gn_w"), g("gn_b"),
                           g("lambda_q1"), g("lambda_k1"), g("lambda_q2"), g("lambda_k2")]).reshape(1, 3456).astype(f32)
    shared["pvec"] = pvec; shared["prow"] = prow
    consts = [_consts(0), _consts(1)]
    ptab = np.asarray(inp["page_table"], np.int32)
    swkv = np.asarray(inp["state_wkv"], f32)[0]
    sshift = np.asarray(inp["state_shift"], f32)[0]
    in_maps = []
    for c in range(8):
        b, half = c // 2, c % 2
        xin = np.zeros((NCOL, D), f32)
        if half == 1:
            xin[0:2048] = xp[b, 0:2048]
        xin[2048:4096] = xp[b, half * 2048:(half + 1) * 2048]
        for s in range(4):
            xin[4096 + 128 * (s // 2) + 64 * (s % 2) + 1] = xs[4 * c + s, 0]
        m = dict(shared)
        m.update(consts[half])
        m["xin"] = xin
        m["ptab"] = np.ascontiguousarray(ptab[4 * c:4 * c + 4].reshape(1, 256))
        sw = swkv[4 * c:4 * c + 4].transpose(0, 1, 3, 2).reshape(4, 4, 128, 64)
        m["swkv"] = np.ascontiguousarray(sw)
        m["sshift"] = np.ascontiguousarray(sshift[4 * c:4 * c + 4].reshape(4, 14, 128).transpose(2, 1, 0))
        in_maps.append(m)
    if "nc" not in _NC_CACHE:
        _NC_CACHE["nc"] = build()
        _NC_CACHE["small"] = False
    if _NC_CACHE.get("small"):
        for m in in_maps:
            m["ck"] = m["ck"][:128]; m["cv"] = m["cv"][:128]
    nc = _NC_CACHE["nc"]
    res = run_bass_kernel_spmd(nc, in_maps, core_ids=list(range(8)))
    R = res.results
    global _LAST
    _LAST = R
    y_p = np.zeros((4, 4096, 1024), f32); y_s = np.zeros((32, 1, 1024), f32)
    k_p = np.zeros((1, 4, 4096, 4, 128), f32); v_p = np.zeros((1, 4, 4096, 4, 128), f32)
    wkv_p = np.zeros((1, 4, 8, 64, 64), f32); sh_p = np.zeros((1, 4, 1792), f32)
    k_s = np.zeros((1, 32, 1, 4, 128), f32); v_s = np.zeros((1, 32, 1, 4, 128), f32)
    wkv_s = np.zeros((1, 32, 8, 64, 64), f32); sh_s = np.zeros((1, 32, 1792), f32)
    for c in range(8):
        b, half = c // 2, c % 2
        r = R[c]
        sl = slice(half * 2048, (half + 1) * 2048)
        y_p[b, sl] = r["y_out"][0:2048]
        kT = r["k_out"]
        k_p[0, b, sl] = kT[:, 0:2048].T.reshape(2048, 4, 128)
        v_p[0, b, sl] = r["v_out"][0:2048].reshape(2048, 4, 128)
        wk = r["wkv_out"].reshape(5, 8, 64, 64)
        po = r["p_out"]
        if half == 1:
            wkv_p[0, b] = wk[0].transpose(0, 2, 1)
            sh_p[0, b] = po[:, :, 127].reshape(1792)
        for s in range(4):
            col = 128 * (s // 2) + 64 * (s % 2) + 1
            y_s[4 * c + s, 0] = r["y_out"][2048 + col]
            k_s[0, 4 * c + s, 0] = kT[:, 2048 + col].reshape(4, 128)
            v_s[0, 4 * c + s, 0] = r["v_out"][2048 + col].reshape(4, 128)
            wkv_s[0, 4 * c + s] = wk[1 + s].transpose(0, 2, 1)
            sh_s[0, 4 * c + s] = po[:, :, 128 + col].reshape(1792)
    return (y_p, y_s, k_p, v_p, wkv_p, sh_p, k_s, v_s, wkv_s, sh_s)
```

```python
import math
import numpy as np
import concourse.bass as bass
import concourse.mybir as mybir
from concourse.bass_utils import run_bass_kernel_spmd

F32 = mybir.dt.float32
BF16 = mybir.dt.bfloat16
I32 = mybir.dt.int32
ALU = mybir.AluOpType
AF = mybir.ActivationFunctionType
AX = mybir.AxisListType

SEM_EPOCH = 30000
NPRE, NOWN, NSMP = 2048, 2048, 256
NCOL = NPRE + NOWN + NSMP
NLOC = NOWN + NSMP
NT = NCOL // 128
D = 1024
DFF = 2816
NEG = -30000.0


class Eng:
    def __init__(self, fw, name, e):
        self.fw = fw; self.name = name; self.e = e
        self.sems = []; self.count = 0; self.epoch = -1; self.known = {}
        self._new_epoch()

    def _new_epoch(self):
        self.epoch += 1
        self.count = 0
        self.sems.append(self.fw.new_sem("%s_e%d" % (self.name, self.epoch)))


class Buf:
    _uid = [0]

    def __init__(self, name, psum=False):
        Buf._uid[0] += 1
        self.uid = Buf._uid[0]
        self.name = name; self.w = None; self.r = []; self.dsem = None; self.dcount = 0; self.psum = psum


class FW:
    def __init__(self, nc):
        self.nc = nc
        self._stack = []
        self._semstack = []
        self.engs = {}
        for name, e in (("pe", nc.tensor), ("act", nc.scalar), ("dve", nc.vector),
                        ("pool", nc.gpsimd), ("sp", nc.sync)):
            self.engs[name] = Eng(self, name, e)
        self.pe = self.engs["pe"]; self.act = self.engs["act"]; self.dve = self.engs["dve"]
        self.pool = self.engs["pool"]; self.sp = self.engs["sp"]
        self.nbuf = 0
        self.out_bufs = []
        self.free_dsems = []

    def new_sem(self, name):
        cm = self.nc.semaphore(name)
        s = cm.__enter__()
        self._semstack.append(cm)
        return s

    def sbuf(self, name, shape, dt):
        cm = self.nc.sbuf_tensor("sb_" + name, list(shape), dt)
        t = cm.__enter__()
        self._stack.append(cm)
        self.nbuf += 1
        return t, Buf(name)

    def psum(self, name, shape, dt):
        cm = self.nc.psum_tensor("ps_" + name, list(shape), dt)
        t = cm.__enter__()
        self._stack.append(cm)
        return t, Buf(name, psum=True)

    def mark(self):
        return len(self._stack)

    def release(self, mark):
        while len(self._stack) > mark:
            self._stack.pop().__exit__(None, None, None)

    def _need(self, eng, stamp, kind):
        if stamp is None:
            return
        if stamp[0] == 'e':
            _, pe_, ep, cnt = stamp
            if pe_ is eng:
                if eng.name == "pe" or kind != "raw":
                    return
            key = (pe_.name, ep)
            if eng.known.get(key, 0) >= cnt:
                return
            eng.e.wait_ge(pe_.sems[ep], cnt)
            eng.known[key] = cnt
        else:
            _, sem, val, key = stamp
            if eng.known.get(key, 0) >= val:
                return
            eng.e.wait_ge(sem, val)
            eng.known[key] = val

    def _deps(self, eng, reads, writes):
        for b in reads:
            self._need(eng, b.w, "raw")
        for b in writes:
            self._need(eng, b.w, "waw")
            for s in b.r:
                self._need(eng, s, "war")

    def _record(self, st, reads, writes):
        for b in reads:
            b.r.append(st)
        for b in writes:
            b.w = st
            b.r = []

    def op(self, eng, fn, reads=(), writes=(), rg=None):
        if eng.name == "pe":
            for b in writes:
                prev = getattr(b, "rg", None)
                if rg is not None and prev is not None and prev != rg and b.w is not None and b.w[0] == 'e' and b.w[1] is eng:
                    _, pe_, ep, cnt = b.w
                    key = (pe_.name, ep)
                    if eng.known.get(key, 0) < cnt:
                        eng.e.wait_ge(pe_.sems[ep], cnt)
                        eng.known[key] = cnt
                b.rg = rg
        if eng.name != "pe":
            px = [b for b in reads if b.psum]
            if px:
                reads = [b for b in reads if not b.psum]
                writes = list(writes) + [b for b in px if b not in writes]
        self._deps(eng, reads, writes)
        if eng.count >= SEM_EPOCH:
            eng._new_epoch()
        ins = fn()
        eng.count += 1
        ins.then_inc(eng.sems[eng.epoch], 1)
        self._record(('e', eng, eng.epoch, eng.count), reads, writes)
        return ins

    def dma(self, eng, out, in_, sb, reads=(), writes=(), is_output=False, fn=None):
        self._deps(eng, reads, writes)
        b = sb
        if b.dsem is None:
            b.dsem = self.new_sem("d_" + b.name)
        ins = eng.e.dma_start(out=out, in_=in_) if fn is None else fn()
        ins.then_inc(b.dsem, 16)
        b.dcount += 16
        self._record(('d', b.dsem, b.dcount, ("dma", b.uid)), reads, writes)
        if is_output and b not in self.out_bufs:
            self.out_bufs.append(b)
        return ins

    def finish(self):
        for b in self.out_bufs:
            self.sp.e.wait_ge(b.dsem, b.dcount)

    def barrier_all(self):
        for a in self.engs.values():
            for o in self.engs.values():
                if o is a or o.count == 0:
                    continue
                key = (o.name, o.epoch)
                if a.known.get(key, 0) >= o.count:
                    continue
                a.e.wait_ge(o.sems[o.epoch], o.count)
                a.known[key] = o.count

    def close(self):
        self.release(0)
        while self._semstack:
            self._semstack.pop().__exit__(None, None, None)


class Ring:
    def __init__(self, items):
        self.items = items; self.i = 0

    def next(self):
        it = self.items[self.i % len(self.items)]
        self.i += 1
        return it


def build(stage=99, small=False, dbg=None, ng=8, parts=("rwkv", "attn", "epi", "samp"), nb=None, rl=9, cut=99):
    nc = bass.Bass("TRN2", target_bir_lowering=False)
    V, S, G, T = nc.vector, nc.scalar, nc.gpsimd, nc.tensor

    def din(name, shape, dt=F32):
        return nc.dram_tensor(name, list(shape), dt, kind="ExternalInput").ap()

    def dout(name, shape, dt=F32):
        return nc.dram_tensor(name, list(shape), dt, kind="ExternalOutput").ap()

    xin = din("xin", [NCOL, D])
    w_in = din("w_in", [D, 5376]); w_pa = din("w_pa", [512, D]); w_pb = din("w_pb", [512, D])
    w_out = din("w_out", [D, D]); w_gate = din("w_gate", [D, DFF]); w_up = din("w_up", [D, DFF])
    w_down = din("w_down", [DFF, D])
    wa2_d = din("wa2", [128, 512]); g2_d = din("g2", [128, 512])
    pvec_d = din("pvec", [128, 36]); prow_d = din("prow", [1, 3456])
    kb_d = din("kb", [4, 4, 4096]); qb_d = din("qb", [4, 4, 2048])
    sshift_d = din("sshift", [128, 14, 4]); swkv_d = din("swkv", [4, 4, 128, 64])
    NPG = 128 if small else 2560 * 128
    ck_d = din("ck", [NPG, 512]); cv_d = din("cv", [NPG, 512])
    ptab_d = din("ptab", [1, 256], I32)
    sbias_d = din("sbias", [128, 65 * 8])
    ident_d = din("ident", [128, 128]); mu4_d = din("mu4", [128, 512]); ml4_d = din("ml4", [128, 512])
    mui4_d = din("mui4", [128, 512]); tri_d = din("tri", [128, 128]); colmask_d = din("colmask", [128, 256])
    resetm_d = din("resetm", [128, 512]); cm0_d = din("cm0", [128, 512]); cm1_d = din("cm1", [128, 512])
    bones_d = din("bones", [128, 128]); sel_d = din("sel", [128, 512]); sel0_d = din("sel0", [128, 256])
    iota_d = din("iotap", [128, 1], I32); hmask_d = din("hmask", [8, 4]); e0_d = din("e0", [8, 4]); e1_d = din("e1", [8, 4])

    y_out = dout("y_out", [NLOC, D]); k_out = dout("k_out", [512, NLOC]); v_out = dout("v_out", [NLOC, 512])
    wkv_out = dout("wkv_out", [5, 4, 128, 64]); p_out = dout("p_out", [14, 128, 384])

    f = FW(nc)
    pe, act, dve, pool, sp = f.pe, f.act, f.dve, f.pool, f.sp

    def cload(name, src, shape, dt=F32, q=None):
        t, b = f.sbuf(name, shape, dt)
        if dt == F32 or dt == I32:
            f.dma(q or sp, t[:], src, b, writes=[b])
        else:
            f.dma(pool, t[:], src, b, writes=[b])
        return t, b

    ident_f, b_identf = cload("ident_f", ident_d[:, :], [128, 128])
    ident_b, b_identb = cload("ident_b", ident_d[:, :], [128, 128], BF16)
    tri_b, b_tri = cload("tri_b", tri_d[:, :], [128, 128], BF16)
    bones, b_bones = cload("bones", bones_d[:, :], [128, 128])
    pvec, b_pvec = cload("pvec", pvec_d[:, :], [128, 36])
    prow, b_prow = f.sbuf("prow", [128, 3456], F32)
    f.dma(sp, prow[:], prow_d[0:1, :].partition_broadcast(128), b_prow, writes=[b_prow])
    pv2, b_pv2 = f.sbuf("pv2", [128, 24], F32)
    f.op(dve, lambda: V.tensor_scalar(out=pv2[:, 0:14], in0=pvec[:, 0:14], scalar1=-1.0, scalar2=1.0,
                                      op0=ALU.mult, op1=ALU.add), reads=[b_pvec], writes=[b_pv2])
    f.op(dve, lambda: V.tensor_scalar(out=pv2[:, 14:18], in0=pvec[:, 14:18], scalar1=-1.0, scalar2=None,
                                      op0=ALU.mult), reads=[b_pvec], writes=[b_pv2])
    f.op(dve, lambda: V.tensor_scalar(out=pv2[:, 18:22], in0=pvec[:, 26:30], scalar1=-1.0, scalar2=1.0,
                                      op0=ALU.mult, op1=ALU.add), reads=[b_pvec], writes=[b_pv2])
    f.op(dve, lambda: V.tensor_scalar(out=pv2[:, 22:23], in0=pvec[:, 34:35], scalar1=0.125, scalar2=None,
                                      op0=ALU.mult), reads=[b_pvec], writes=[b_pv2])
    MU_C, W0_C, A0_C, KK_C, KA_C, RK_C, QG_C, KG_C = 0, 14, 18, 22, 26, 30, 34, 35
    GMIX, GFFN, GAO, GNW, GNB, LAM = 0, 1024, 2048, 2176, 2688, 3200

    lt, b_lt = f.sbuf("lt", [128, 8], F32)
    junk64, b_junk64 = f.sbuf("junk64", [128, 64], F32)
    f.op(dve, lambda: V.memset(lt[:], 0.0), writes=[b_lt])
    f.op(dve, lambda: V.tensor_tensor(out=junk64[:], in0=prow[:, LAM:LAM + 64], in1=prow[:, LAM + 64:LAM + 128],
                                      op=ALU.mult), reads=[b_prow], writes=[b_junk64])
    f.op(dve, lambda: V.reduce_sum(out=lt[:, 0:1], in_=junk64[:], axis=AX.X), reads=[b_junk64], writes=[b_lt])
    f.op(dve, lambda: V.tensor_tensor(out=junk64[:], in0=prow[:, LAM + 128:LAM + 192], in1=prow[:, LAM + 192:LAM + 256],
                                      op=ALU.mult), reads=[b_prow, b_lt], writes=[b_junk64])
    f.op(dve, lambda: V.reduce_sum(out=lt[:, 1:2], in_=junk64[:], axis=AX.X), reads=[b_junk64], writes=[b_lt])
    f.op(act, lambda: S.activation(out=lt[:, 2:4], in_=lt[:, 0:2], func=AF.Exp), reads=[b_lt], writes=[b_lt])
    lam_init = 0.8 - 0.6 * math.exp(-0.3 * 0)
    f.op(dve, lambda: V.tensor_tensor(out=lt[:, 4:5], in0=lt[:, 3:4], in1=lt[:, 2:3], op=ALU.subtract),
         reads=[b_lt], writes=[b_lt])
    f.op(dve, lambda: V.tensor_scalar(out=lt[:, 5:6], in0=lt[:, 4:5], scalar1=-lam_init, scalar2=None, op0=ALU.add),
         reads=[b_lt], writes=[b_lt])
    NLAM = lt[:, 5:6]

    hT_loc, b_hTloc = f.sbuf("hT_loc", [128, 8, NLOC], BF16)
    yrT, b_yrT = f.sbuf("yrT", [128, 4, NLOC], BF16)

    m_pre = f.mark()
    hT_pre, b_hTpre = f.sbuf("hT_pre", [128, 8, NPRE], BF16)

    def hT_cols(c0, n):
        if c0 < NPRE:
            return hT_pre, b_hTpre, c0
        return hT_loc, b_hTloc, c0 - NPRE

    m1 = f.mark()
    xr = Ring([f.sbuf("x%d" % i, [128, D], F32) for i in range(3)])
    xbr = Ring([f.sbuf("xb%d" % i, [128, D], BF16) for i in range(2)])
    junk, b_junk = f.sbuf("junk", [128, D], F32)
    ssr = Ring([f.sbuf("ss%d" % i, [128, 2], F32) for i in range(3)])
    ptr = Ring([f.psum("pt%d" % i, [128, 8, 128], BF16) for i in range(2)])
    for t in range(NT if dbg is None else 0):
        xt, bx = xr.next(); xb, bxb = xbr.next(); ss, bss = ssr.next(); pt, bpt = ptr.next()
        f.dma(sp, xt[:], xin[t * 128:(t + 1) * 128, :], bx, writes=[bx])
        f.op(dve, lambda: V.memset(ss[:], 0.0), writes=[bss])
        f.op(act, lambda: S.activation(out=junk[:], in_=xt[:], func=AF.Square, accum_out=ss[:, 0:1]),
             reads=[bx, bss], writes=[b_junk, bss])
        f.op(dve, lambda: V.tensor_scalar(out=ss[:, 1:2], in0=ss[:, 0:1], scalar1=1.0 / D, scalar2=1e-6,
                                          op0=ALU.mult, op1=ALU.add), reads=[bss], writes=[bss])
        f.op(act, lambda: S.activation(out=ss[:, 1:2], in_=ss[:, 1:2], func=AF.Sqrt), reads=[bss], writes=[bss])
        f.op(dve, lambda: V.reciprocal(out=ss[:, 1:2], in_=ss[:, 1:2]), reads=[bss], writes=[bss])
        f.op(dve, lambda: V.scalar_tensor_tensor(out=xb[:], in0=xt[:], scalar=ss[:, 1:2], in1=prow[:, GMIX:GMIX + D],
                                                 op0=ALU.mult, op1=ALU.mult), reads=[bx, bss, b_prow], writes=[bxb])
        for c in range(8):
            f.op(pe, lambda c=c: T.transpose(out=pt[:, c, :], in_=xb[:, c * 128:(c + 1) * 128], identity=ident_b[:]),
                 reads=[bxb, b_identb], writes=[bpt])
        ht, bht, lc = hT_cols(t * 128, 128)
        f.op(act, lambda: S.copy(out=ht[:, :, lc:lc + 128], in_=pt[:]), reads=[bpt], writes=[bht])
    f.release(m1)
    f.barrier_all()

    def wload(dst, bdst, src):
        f.dma(pool, dst, src, bdst, writes=[bdst])


    def epilogue():
        mE = f.mark()
        wpa, b_wpa = f.sbuf("wpa", [128, 4, D], BF16)
        wpb, b_wpb = f.sbuf("wpb", [128, 4, D], BF16)
        wo, b_wo = f.sbuf("wo", [128, 8, D], BF16)
        wload(wpa[:], b_wpa, w_pa.rearrange("(c p) n -> p c n", p=128))
        wload(wpb[:], b_wpb, w_pb.rearrange("(c p) n -> p c n", p=128))
        wload(wo[:], b_wo, w_out.rearrange("(c p) n -> p c n", p=128))
        SB = 384
        bank = [f.psum("bank%d" % i, [128, 512], F32) for i in range(8)]
        wgr = Ring([f.sbuf("wg%d" % i, [128, 8, 128], BF16) for i in range(4)])
        wdr = Ring([f.sbuf("wd%d" % i, [128, D], BF16) for i in range(2)])
        mT, b_mT = f.sbuf("mT", [128, 8, SB], BF16)
        hfT, b_hfT = f.sbuf("hfT", [128, 8, SB], BF16)
        x1, b_x1 = f.sbuf("x1e", [128, 3, D], F32)
        xr2 = Ring([f.sbuf("xe%d" % i, [128, D], F32) for i in range(1)])
        sga, b_sga = f.sbuf("sga", [128, SB], F32)
        sgb, b_sgb = f.sbuf("sgb", [128, SB], F32)
        tA, b_tA = f.sbuf("tA", [128, SB], F32)
        hfb, b_hfb = f.sbuf("hfb", [128, D], BF16)
        ejunk, b_ejunk = f.sbuf("ejunk", [128, D], F32)
        est, b_est = f.sbuf("est", [128, 4], F32)
        actr = Ring([f.sbuf("act%d" % i, [128, SB], BF16) for i in range(2)])
        yor = Ring([f.sbuf("yo%d" % i, [128, 512], F32) for i in range(2)])
        for sbi in range(NLOC // SB):
            c0 = sbi * SB
            for ch in range(8):
                cs = slice(ch * 128, (ch + 1) * 128)
                (pa, bpa), (pb_, bpb), (pga, bpga), (pgb, bpgb) = bank[0], bank[1], bank[2], bank[3]
                wga, bwga = wgr.next(); wgb, bwgb = wgr.next()
                wload(wga[:], bwga, w_in[:, 3328 + ch * 128:3328 + (ch + 1) * 128].rearrange("(c p) n -> p c n", p=128))
                wload(wgb[:], bwgb, w_in[:, 4352 + ch * 128:4352 + (ch + 1) * 128].rearrange("(c p) n -> p c n", p=128))
                for h in range(4):
                    f.op(pe, lambda h=h: T.matmul(out=pa[:, 0:SB], lhsT=wpa[:, h, cs], rhs=oT[:, h, c0:c0 + SB], start=(h == 0), stop=(h == 3)),
                         reads=[b_wpa, b_oT], writes=[bpa])
                for h in range(4):
                    f.op(pe, lambda h=h: T.matmul(out=pb_[:, 0:SB], lhsT=wpb[:, h, cs], rhs=yrT[:, h, c0:c0 + SB], start=(h == 0), stop=(h == 3)),
                         reads=[b_wpb, b_yrT], writes=[bpb])
                for c in range(8):
                    f.op(pe, lambda c=c: T.matmul(out=pga[:, 0:SB], lhsT=wga[:, c, :], rhs=hT_loc[:, c, c0:c0 + SB], start=(c == 0), stop=(c == 7)),
                         reads=[bwga, b_hTloc], writes=[bpga])
                for c in range(8):
                    f.op(pe, lambda c=c: T.matmul(out=pgb[:, 0:SB], lhsT=wgb[:, c, :], rhs=hT_loc[:, c, c0:c0 + SB], start=(c == 0), stop=(c == 7)),
                         reads=[bwgb, b_hTloc], writes=[bpgb])
                f.op(act, lambda: S.activation(out=sga[:], in_=pga[:, 0:SB], func=AF.Sigmoid), reads=[bpga], writes=[b_sga])
                f.op(act, lambda: S.activation(out=sgb[:], in_=pgb[:, 0:SB], func=AF.Sigmoid), reads=[bpgb], writes=[b_sgb])
                f.op(dve, lambda: V.tensor_tensor(out=tA[:], in0=pa[:, 0:SB], in1=sga[:], op=ALU.mult), reads=[bpa, b_sga], writes=[b_tA])
                f.op(dve, lambda: V.tensor_tensor(out=sgb[:], in0=pb_[:, 0:SB], in1=sgb[:], op=ALU.mult), reads=[bpb, b_sgb], writes=[b_sgb])
                f.op(pool, lambda: G.tensor_tensor(out=mT[:, ch, :], in0=tA[:], in1=sgb[:], op=ALU.add), reads=[b_tA, b_sgb], writes=[b_mT])
            for tl in range(3):
                ts_ = slice(tl * 128, (tl + 1) * 128)
                xt, bxt = xr2.next()
                row0 = NPRE + c0 + tl * 128
                f.dma(sp, xt[:], xin[row0:row0 + 128, :], bxt, writes=[bxt])
                for hf_ in range(2):
                    px, bpx = bank[4 + hf_]
                    for k in range(8):
                        f.op(pe, lambda k=k: T.matmul(out=px[:], lhsT=mT[:, k, ts_], rhs=wo[:, k, hf_ * 512:(hf_ + 1) * 512],
                                                      start=(k == 0), stop=(k == 7)), reads=[b_mT, b_wo], writes=[bpx])
                    f.op(dve, lambda: V.tensor_tensor(out=x1[:, tl, hf_ * 512:(hf_ + 1) * 512], in0=px[:], in1=xt[:, hf_ * 512:(hf_ + 1) * 512],
                                                      op=ALU.add), reads=[bpx, bxt], writes=[b_x1])
                f.op(dve, lambda: V.memset(est[:, 0:1], 0.0), writes=[b_est])
                f.op(act, lambda: S.activation(out=ejunk[:], in_=x1[:, tl, :], func=AF.Square, accum_out=est[:, 0:1]),
                     reads=[b_x1, b_est], writes=[b_ejunk, b_est])
                f.op(dve, lambda: V.tensor_scalar(out=est[:, 1:2], in0=est[:, 0:1], scalar1=1.0 / D, scalar2=1e-6, op0=ALU.mult, op1=ALU.add),
                     reads=[b_est], writes=[b_est])
                f.op(act, lambda: S.activation(out=est[:, 1:2], in_=est[:, 1:2], func=AF.Sqrt), reads=[b_est], writes=[b_est])
                f.op(dve, lambda: V.reciprocal(out=est[:, 1:2], in_=est[:, 1:2]), reads=[b_est], writes=[b_est])
                f.op(dve, lambda: V.scalar_tensor_tensor(out=hfb[:], in0=x1[:, tl, :], scalar=est[:, 1:2], in1=prow[:, GFFN:GFFN + D],
                                                         op0=ALU.mult, op1=ALU.mult), reads=[b_x1, b_est, b_prow], writes=[b_hfb])
                ptr_, bptr = bank[6]
                ptb = ptr_[:].bitcast(BF16)
                for c in range(8):
                    f.op(pe, lambda c=c: T.transpose(out=ptb[:, c * 128:(c + 1) * 128], in_=hfb[:, c * 128:(c + 1) * 128], identity=ident_b[:]),
                         reads=[b_hfb, b_identb], writes=[bptr])
                f.op(act, lambda: S.copy(out=hfT[:, :, ts_], in_=ptb.rearrange("p (c n) -> p c n", c=8)), reads=[bptr], writes=[b_hfT])
            for ffc in range(DFF // 128):
                fs = slice(ffc * 128, (ffc + 1) * 128)
                wg, bwg = wgr.next(); wu, bwu = wgr.next(); wd, bwd = wdr.next()
                wload(wg[:], bwg, w_gate[:, fs].rearrange("(c p) n -> p c n", p=128))
                wload(wu[:], bwu, w_up[:, fs].rearrange("(c p) n -> p c n", p=128))
                wload(wd[:], bwd, w_down[fs, :])
                (pg, bpg), (pu, bpu) = bank[6], bank[7]
                for c in range(8):
                    f.op(pe, lambda c=c: T.matmul(out=pg[:, 0:SB], lhsT=wg[:, c, :], rhs=hfT[:, c, :], start=(c == 0), stop=(c == 7)),
                         reads=[bwg, b_hfT], writes=[bpg])
                for c in range(8):
                    f.op(pe, lambda c=c: T.matmul(out=pu[:, 0:SB], lhsT=wu[:, c, :], rhs=hfT[:, c, :], start=(c == 0), stop=(c == 7)),
                         reads=[bwu, b_hfT], writes=[bpu])
                f.op(act, lambda: S.activation(out=sga[:], in_=pg[:, 0:SB], func=AF.Silu), reads=[bpg], writes=[b_sga])
                at, bat = actr.next()
                f.op(dve, lambda: V.tensor_tensor(out=at[:], in0=pu[:, 0:SB], in1=sga[:], op=ALU.mult), reads=[bpu, b_sga], writes=[bat])
                for tl in range(3):
                    for hf_ in range(2):
                        pd, bpd = bank[tl * 2 + hf_]
                        f.op(pe, lambda: T.matmul(out=pd[:], lhsT=at[:, tl * 128:(tl + 1) * 128], rhs=wd[:, hf_ * 512:(hf_ + 1) * 512],
                                                  start=(ffc == 0), stop=(ffc == DFF // 128 - 1)), reads=[bat, bwd], writes=[bpd])
            for tl in range(3):
                for hf_ in range(2):
                    pd, bpd = bank[tl * 2 + hf_]
                    yo, byo = yor.next()
                    f.op(dve, lambda: V.tensor_tensor(out=yo[:], in0=pd[:], in1=x1[:, tl, hf_ * 512:(hf_ + 1) * 512], op=ALU.add),
                         reads=[bpd, b_x1], writes=[byo])
                    r0 = c0 + tl * 128
                    f.dma(sp, y_out[r0:r0 + 128, hf_ * 512:(hf_ + 1) * 512], yo[:], byo, reads=[byo], is_output=True)
        f.barrier_all()
        f.release(mE)

    def rwkv_phase():
        mR = f.mark()
        BW = 256
        NB = NCOL // BW
        mu4b, b_mu4 = cload("mu4b", mu4_d[:, :], [128, 512], BF16)
        ml4b, b_ml4 = cload("ml4b", ml4_d[:, :], [128, 512], BF16)
        mui4b, b_mui4 = cload("mui4b", mui4_d[:, :], [128, 512], BF16)
        ident4, b_ident4 = f.sbuf("ident4", [128, 512], BF16)
        for i in range(4):
            f.dma(pool, ident4[:, i * 128:(i + 1) * 128], ident_d[:, :], b_ident4, writes=[b_ident4])
        resetm, b_resetm = cload("resetm", resetm_d[:, 0:BW], [128, BW])
        cm0, b_cm0 = cload("cm0", cm0_d[:, 0:BW], [128, BW], BF16)
        cm1, b_cm1 = cload("cm1", cm1_d[:, 0:BW], [128, BW], BF16)
        colmask, b_colmask = cload("colmask", colmask_d[:, :], [128, 256])
        wa2b, b_wa2 = cload("wa2b", wa2_d[:, :], [128, 512], BF16)
        g2b, b_g2 = cload("g2b", g2_d[:, :], [128, 512], BF16)
        sshift, b_sshift = cload("sshift", sshift_d[:, :, :], [128, 14, 4])
        cst, b_cst = f.sbuf("cst", [128, 4], F32)
        f.op(dve, lambda: V.memset(cst[:, 0:1], 1.0), writes=[b_cst])
        f.op(dve, lambda: V.memset(cst[:, 1:2], -0.5), writes=[b_cst])
        f.op(dve, lambda: V.memset(cst[:, 2:3], 64e-5), writes=[b_cst])
        carry, b_carry = f.sbuf("carry", [128, 14], F32)
        f.op(dve, lambda: V.memset(carry[:], 0.0), writes=[b_carry])
        H, b_H = f.sbuf("H", [128, 4, 64], F32)
        f.op(dve, lambda: V.memset(H[:], 0.0), writes=[b_H])
        Hb = [f.sbuf("Hb%d" % i, [128, 4, 64], BF16) for i in range(2)]
        wrr, _bw = f.sbuf("wrr", [128, 8, 1792], BF16)
        b_wrc = [Buf("wrr%d" % i) for i in range(14)]
        for i in range(14):
            wload(wrr[:, :, i * 128:(i + 1) * 128], b_wrc[i], w_in[:, 1536 + i * 128:1536 + (i + 1) * 128].rearrange("(c p) n -> p c n", p=128))
        ppr = Ring([f.psum("pp%d" % i, [128, 512], F32) for i in range(3)])
        pxr = Ring([f.sbuf("px%d" % i, [128, BW + 1], F32) for i in range(2)])
        rT, b_rT = f.sbuf("rT", [128, 4, BW], F32)
        kT, b_kT = f.sbuf("kT", [128, 4, BW], F32)
        vT, b_vT = f.sbuf("vT", [128, 4, BW], F32)
        m12, b_m12 = f.sbuf("m12", [128, BW], F32)
        m13, b_m13 = f.sbuf("m13", [128, BW], F32)
        twxa, b_twxa = f.sbuf("twxa", [128, BW], BF16)
        sg, b_sg = f.sbuf("sg", [128, BW], BF16)
        scr = [f.sbuf("scr%d" % i, [128, BW], F32) for i in range(8)]
        AT, b_AT = f.sbuf("AT", [128, 4, BW], BF16)
        BT, b_BT = f.sbuf("BT", [128, 4, BW], BF16)
        KT, b_KT2 = f.sbuf("KT", [128, 4, BW], BF16)
        RT, b_RT = f.sbuf("RT", [128, 4, BW], BF16)
        RT0, b_RT0 = f.sbuf("RT0", [128, 4, BW], BF16)
        RT1, b_RT1 = f.sbuf("RT1", [128, 4, BW], BF16)
        Eneg, b_Eneg = f.sbuf("Eneg", [128, 4, BW], F32)
        bonT, b_bonT = f.sbuf("bonT", [128, 4, BW], BF16)
        gT, b_gT = f.sbuf("gT", [128, 4, BW], BF16)
        Vtm, b_Vtm = f.sbuf("Vtm", [128, 2, 512], BF16)
        Btm, b_Btm = f.sbuf("Btm", [128, 2, 512], BF16)
        Ktm, b_Ktm = f.sbuf("Ktm", [128, 2, 512], BF16)
        ptBK, b_ptBK = f.psum("ptBK", [128, 2, 512], BF16)
        ptV, b_ptV = f.psum("ptV", [128, 512], F32)
        psA, b_psA = f.psum("psA", [128, 4, 128], F32)
        psB, b_psB = f.psum("psB", [128, 4, 128], F32)
        psC, b_psC = f.psum("psC", [128, 4, 128], F32)
        Lk = [f.sbuf("Lk%d" % i, [128, 4, 128], BF16) for i in range(2)]
        Nk = [f.sbuf("Nk%d" % i, [128, 4, 128], BF16) for i in range(2)]
        Yk = [f.sbuf("Yk%d" % i, [128, 4, 128], BF16) for i in range(2)]
        AKT, b_AKT = f.sbuf("AKT", [128, 4, 128], BF16)
        RBT, b_RBT = f.sbuf("RBT", [128, 4, 128], BF16)
        RKT, b_RKT = f.sbuf("RKT", [128, 4, 128], BF16)
        P1, b_P1 = f.sbuf("P1", [128, 4, 64], F32)
        Wb, b_Wb = f.sbuf("Wb", [128, 4, 64], BF16)
        Ub, b_Ub = f.sbuf("Ub", [128, 4, 64], BF16)
        Yall, b_Yall = f.sbuf("Yall", [128, 8, 64], F32)
        gst, b_gst = f.sbuf("gst", [128, 40], F32)
        ytmp2, b_ytmp = f.sbuf("ytmp", [128, 512], F32)
        ytmp = ytmp2[:].rearrange("p (a b) -> p a b", a=4)
        Ysq = ytmp2[:].rearrange("p (a b) -> p a b", a=8)
        b_Ysq = b_ytmp

        def flat(t3):
            return t3[:].rearrange("p a b -> p (a b)")

        for bi in (range(NB) if nb is None else nb):
            col0 = bi * BW
            is_loc = col0 >= NPRE
            is_smp = col0 >= NPRE + NOWN
            ht, bht, lc = hT_cols(col0, BW)
            for ch in range(14):
                bwt = b_wrc[ch]
                pp, bpp = ppr.next()
                for c in range(8):
                    f.op(pe, lambda c=c: T.matmul(out=pp[:, 0:BW], lhsT=wrr[:, c, ch * 128:(ch + 1) * 128], rhs=ht[:, c, lc:lc + BW],
                                                  start=(c == 0), stop=(c == 7)), reads=[bwt, bht], writes=[bpp])
                px, bpx = pxr.next()
                f.op(dve, lambda: V.tensor_copy(out=px[:, 0:1], in_=carry[:, ch:ch + 1]), reads=[b_carry], writes=[bpx])
                f.op(act, lambda: S.copy(out=px[:, 1:BW + 1], in_=pp[:, 0:BW]), reads=[bpp], writes=[bpx])
                f.op(dve, lambda: V.tensor_copy(out=carry[:, ch:ch + 1], in_=px[:, BW:BW + 1]), reads=[bpx], writes=[b_carry])
                if bi == 15:
                    f.dma(sp, p_out[ch, :, 0:128], px[:, 129:257], bpx, reads=[bpx], is_output=True)
                if is_smp:
                    f.dma(sp, p_out[ch, :, 128:384], px[:, 1:257], bpx, reads=[bpx], is_output=True)
                    f.op(dve, lambda: V.tensor_copy(out=px[:, 1:BW + 1:64], in_=sshift[:, ch, :]),
                         reads=[b_sshift], writes=[bpx])
                tmp, btmp = scr[0]
                if ch < 4:
                    dst, bdst = rT[:, ch, :], b_rT
                elif ch < 8:
                    dst, bdst = kT[:, ch - 4, :], b_kT
                elif ch < 12:
                    dst, bdst = vT[:, ch - 8, :], b_vT
                elif ch == 12:
                    dst, bdst = m12[:], b_m12
                else:
                    dst, bdst = m13[:], b_m13
                f.op(pool, lambda: G.tensor_scalar(out=tmp[:], in0=px[:, 0:BW], scalar1=pvec[:, MU_C + ch:MU_C + ch + 1], scalar2=None,
                                                   op0=ALU.mult), reads=[bpx, b_pvec], writes=[btmp])
                f.op(dve, lambda: V.scalar_tensor_tensor(out=dst, in0=px[:, 1:BW + 1], scalar=pv2[:, ch:ch + 1], in1=tmp[:],
                                                         op0=ALU.mult, op1=ALU.add), reads=[bpx, btmp, b_pv2], writes=[bdst])
                if is_smp:
                    f.op(dve, lambda: V.tensor_tensor(out=dst, in0=dst, in1=colmask[:], op=ALU.mult),
                         reads=[bdst, b_colmask], writes=[bdst])
            if rl < 2:
                continue
            f.op(act, lambda: S.activation(out=twxa[0:64, :], in_=m12[0:64, :], func=AF.Tanh), reads=[b_m12], writes=[b_twxa])
            f.op(dve, lambda: V.tensor_copy(out=twxa[64:128, :], in_=m12[64:128, :]), reads=[b_m12], writes=[b_twxa])
            f.op(act, lambda: S.activation(out=sg[:], in_=m13[:], func=AF.Sigmoid), reads=[b_m13], writes=[b_sg])
            for pr in range(4):
                pc = slice(pr * 128, (pr + 1) * 128)
                (s1, bs1), (s2, bs2), (s3, bs3), (s4, bs4), (s5, bs5), (s6, bs6), (s7, bs7) = scr[1:8]
                pp, bpp = ppr.next()
                f.op(pe, lambda: T.matmul(out=pp[:, 0:BW], lhsT=wa2b[0:64, pc], rhs=twxa[0:64, :], start=True, stop=True),
                     reads=[b_wa2, b_twxa], writes=[bpp])
                f.op(act, lambda: S.activation(out=s1[:], in_=pp[:, 0:BW], func=AF.Exp, bias=pv2[:, 14 + pr:15 + pr], scale=-1.0),
                     reads=[bpp, b_pv2], writes=[bs1])
                f.op(act, lambda: S.activation(out=s1[:], in_=s1[:], func=AF.Ln, bias=cst[:, 0:1], scale=1.0),
                     reads=[bs1, b_cst], writes=[bs1])
                f.op(act, lambda: S.activation(out=s2[:], in_=s1[:], func=AF.Exp, bias=cst[:, 1:2], scale=-1.0),
                     reads=[bs1, b_cst], writes=[bs2])
                if is_smp:
                    f.op(dve, lambda: V.tensor_tensor(out=s2[:], in0=s2[:], in1=colmask[:], op=ALU.mult),
                         reads=[bs2, b_colmask], writes=[bs2])
                f.op(dve, lambda: V.tensor_tensor_scan(out=s3[:], data0=resetm[:], data1=s2[:], initial=0.0,
                                                       op0=ALU.mult, op1=ALU.add), reads=[bs2, b_resetm], writes=[bs3])
                f.op(act, lambda: S.activation(out=Eneg[:, pr, :], in_=s3[:], func=AF.Exp, scale=-1.0), reads=[bs3], writes=[b_Eneg])
                f.op(act, lambda: S.activation(out=s4[:], in_=s3[:], func=AF.Exp), reads=[bs3], writes=[bs4])
                f.op(dve, lambda: V.tensor_tensor(out=s5[:], in0=s3[:], in1=s2[:], op=ALU.subtract), reads=[bs3, bs2], writes=[bs5])
                f.op(act, lambda: S.activation(out=s5[:], in_=s5[:], func=AF.Exp, scale=-1.0), reads=[bs5], writes=[bs5])
                pp, bpp = ppr.next()
                f.op(pe, lambda: T.matmul(out=pp[:, 0:BW], lhsT=wa2b[64:128, pc], rhs=twxa[64:128, :], start=True, stop=True),
                     reads=[b_wa2, b_twxa], writes=[bpp])
                f.op(act, lambda: S.activation(out=s6[:], in_=pp[:, 0:BW], func=AF.Sigmoid, bias=pvec[:, A0_C + pr:A0_C + pr + 1], scale=1.0),
                     reads=[bpp, b_pvec], writes=[bs6])
                f.op(dve, lambda: V.tensor_scalar(out=s1[:], in0=kT[:, pr, :], scalar1=pvec[:, KK_C + pr:KK_C + pr + 1], scalar2=None,
                                                  op0=ALU.mult), reads=[b_kT, b_pvec], writes=[bs1])
                f.op(act, lambda: S.activation(out=s7[:], in_=s1[:], func=AF.Square), reads=[bs1], writes=[bs7])
                pp, bpp = ppr.next()
                f.op(pe, lambda: T.matmul(out=pp[:, 0:BW], lhsT=bones[:], rhs=s7[:], start=True, stop=True),
                     reads=[b_bones, bs7], writes=[bpp])
                f.op(dve, lambda: V.tensor_scalar(out=s7[:], in0=pp[:, 0:BW], scalar1=1e-24, scalar2=None, op0=ALU.max),
                     reads=[bpp], writes=[bs7])
                f.op(act, lambda: S.activation(out=s7[:], in_=s7[:], func=AF.Sqrt), reads=[bs7], writes=[bs7])
                f.op(dve, lambda: V.reciprocal(out=s7[:], in_=s7[:]), reads=[bs7], writes=[bs7])
                f.op(dve, lambda: V.tensor_tensor(out=s1[:], in0=s1[:], in1=s7[:], op=ALU.mult), reads=[bs1, bs7], writes=[bs1])
                f.op(dve, lambda: V.tensor_scalar(out=s7[:], in0=s6[:], scalar1=pvec[:, KA_C + pr:KA_C + pr + 1],
                                                  scalar2=pv2[:, 18 + pr:19 + pr], op0=ALU.mult, op1=ALU.add),
                     reads=[bs6, b_pvec, b_pv2], writes=[bs7])
                f.op(dve, lambda: V.tensor_tensor(out=s7[:], in0=s7[:], in1=kT[:, pr, :], op=ALU.mult), reads=[bs7, b_kT], writes=[bs7])
                f.op(dve, lambda: V.scalar_tensor_tensor(out=AT[:, pr, :], in0=s1[:], scalar=-1.0, in1=s5[:], op0=ALU.mult, op1=ALU.mult),
                     reads=[bs1, bs5], writes=[b_AT])
                f.op(pool, lambda: G.tensor_tensor(out=s1[:], in0=s1[:], in1=s6[:], op=ALU.mult), reads=[bs1, bs6], writes=[bs1])
                f.op(pool, lambda: G.tensor_tensor(out=BT[:, pr, :], in0=s1[:], in1=s4[:], op=ALU.mult), reads=[bs1, bs4], writes=[b_BT])
                f.op(pool, lambda: G.tensor_tensor(out=KT[:, pr, :], in0=s7[:], in1=s4[:], op=ALU.mult), reads=[bs7, bs4], writes=[b_KT2])
                f.op(dve, lambda: V.tensor_tensor(out=RT[:, pr, :], in0=rT[:, pr, :], in1=Eneg[:, pr, :], op=ALU.mult),
                     reads=[b_rT, b_Eneg], writes=[b_RT])
                f.op(pool, lambda: G.tensor_tensor(out=RT0[:, pr, :], in0=RT[:, pr, :], in1=cm0[:], op=ALU.mult),
                     reads=[b_RT, b_cm0], writes=[b_RT0])
                f.op(pool, lambda: G.tensor_tensor(out=RT1[:, pr, :], in0=RT[:, pr, :], in1=cm1[:], op=ALU.mult),
                     reads=[b_RT, b_cm1], writes=[b_RT1])
                if is_loc:
                    f.op(dve, lambda: V.scalar_tensor_tensor(out=s1[:], in0=rT[:, pr, :], scalar=pvec[:, RK_C + pr:RK_C + pr + 1], in1=s7[:],
                                                             op0=ALU.mult, op1=ALU.mult), reads=[b_rT, bs7, b_pvec], writes=[bs1])
                    pp, bpp = ppr.next()
                    f.op(pe, lambda: T.matmul(out=pp[:, 0:BW], lhsT=bones[:], rhs=s1[:], start=True, stop=True),
                         reads=[b_bones, bs1], writes=[bpp])
                    f.op(dve, lambda: V.tensor_tensor(out=bonT[:, pr, :], in0=pp[:, 0:BW], in1=vT[:, pr, :], op=ALU.mult),
                         reads=[bpp, b_vT], writes=[b_bonT])
                    pp, bpp = ppr.next()
                    f.op(pe, lambda: T.matmul(out=pp[:, 0:BW], lhsT=g2b[:, pc], rhs=sg[:], start=True, stop=True),
                         reads=[b_g2, b_sg], writes=[bpp])
                    f.op(act, lambda: S.copy(out=gT[:, pr, :], in_=pp[:, 0:BW]), reads=[bpp], writes=[b_gT])
            for tl in range(2 if rl >= 3 else 0):
                Tg = bi * 2 + tl
                tc = slice(tl * 128, (tl + 1) * 128)
                for pr in range(4):
                    pc = slice(pr * 128, (pr + 1) * 128)
                    f.op(pe, lambda: T.transpose(out=ptBK[:, 0, pc], in_=BT[:, pr, tc], identity=ident_b[:]),
                         reads=[b_BT, b_identb], writes=[b_ptBK])
                    f.op(pe, lambda: T.transpose(out=ptBK[:, 1, pc], in_=KT[:, pr, tc], identity=ident_b[:]),
                         reads=[b_KT2, b_identb], writes=[b_ptBK])
                    f.op(pe, lambda: T.transpose(out=ptV[:, pc], in_=vT[:, pr, tc], identity=ident_f[:]),
                         reads=[b_vT, b_identf], writes=[b_ptV])
                f.op(act, lambda: S.copy(out=Btm[:, tl, :], in_=ptBK[:, 0, :]), reads=[b_ptBK], writes=[b_Btm])
                f.op(dve, lambda: V.tensor_copy(out=Ktm[:, tl, :], in_=ptBK[:, 1, :]), reads=[b_ptBK], writes=[b_Ktm])
                f.op(act, lambda: S.copy(out=Vtm[:, tl, :], in_=ptV[:]), reads=[b_ptV], writes=[b_Vtm])
                for Gi in range(2 if rl >= 4 else 0):
                    heads = [(hi, 4 * Gi + hi, (4 * Gi + hi) // 2, slice(64 * ((4 * Gi + hi) % 2), 64 * ((4 * Gi + hi) % 2) + 64))
                             for hi in range(4)]
                    for hi, h, pr, rows in heads:
                        f.op(pe, lambda: T.matmul(out=psA[:, hi, :], lhsT=BT[rows, pr, tc], rhs=AT[rows, pr, tc], start=True, stop=True),
                             reads=[b_BT, b_AT], writes=[b_psA], rg=rows.start)
                    for hi, h, pr, rows in heads:
                        f.op(pe, lambda: T.matmul(out=psB[:, hi, :], lhsT=AT[rows, pr, tc], rhs=BT[rows, pr, tc], start=True, stop=True),
                             reads=[b_BT, b_AT], writes=[b_psB], rg=rows.start)
                    (N0, bN0), (N1, bN1) = Nk
                    (L0, bL0), (L1, bL1) = Lk
                    (Y0, bY0), (Y1, bY1) = Yk
                    f.op(dve, lambda: V.tensor_tensor(out=flat(N0), in0=flat(psA), in1=mu4b[:], op=ALU.mult),
                         reads=[b_psA, b_mu4], writes=[bN0])
                    f.op(dve, lambda: V.tensor_tensor(out=flat(L0), in0=flat(psB), in1=ml4b[:], op=ALU.mult),
                         reads=[b_psB, b_ml4], writes=[bL0])
                    if cut <= 1:
                        continue
                    f.op(pool, lambda: G.tensor_tensor(out=flat(Y0), in0=flat(N0), in1=ident4[:], op=ALU.add),
                         reads=[bN0, b_ident4], writes=[bY0])
                    if cut <= 2:
                        continue
                    for k in range(min(5, cut - 2)):
                        Nc, bNc = Nk[k % 2]; Lc, bLc = Lk[k % 2]; Yc, bYc = Yk[k % 2]
                        Nn, bNn = Nk[(k + 1) % 2]; Ln, bLn = Lk[(k + 1) % 2]; Yn, bYn = Yk[(k + 1) % 2]
                        for hi, h, pr, rows in heads:
                            f.op(pe, lambda: T.matmul(out=psA[:, hi, :], lhsT=Nc[:, hi, :], rhs=Lc[:, hi, :], start=True, stop=True),
                                 reads=[bNc, bLc], writes=[b_psA])
                        if k < 4:
                            for hi, h, pr, rows in heads:
                                f.op(pe, lambda: T.matmul(out=psB[:, hi, :], lhsT=Lc[:, hi, :], rhs=Nc[:, hi, :], start=True, stop=True),
                                     reads=[bNc, bLc], writes=[b_psB])
                        f.op(act, lambda: S.copy(out=flat(Ln), in_=flat(psA)), reads=[b_psA], writes=[bLn])
                        if k < 4:
                            f.op(dve, lambda: V.tensor_copy(out=flat(Nn), in_=flat(psB)), reads=[b_psB], writes=[bNn])
                        for hi, h, pr, rows in heads:
                            f.op(pe, lambda: T.matmul(out=psC[:, hi, :], lhsT=ident_b[:], rhs=Yc[:, hi, :], start=True, stop=False),
                                 reads=[b_identb, bYc], writes=[b_psC])
                            f.op(pe, lambda: T.matmul(out=psC[:, hi, :], lhsT=Ln[:, hi, :], rhs=Yc[:, hi, :], start=False, stop=True),
                                 reads=[bLn, bYc], writes=[b_psC])
                        if k % 2 == 0:
                            f.op(dve, lambda: V.tensor_copy(out=flat(Yn), in_=flat(psC)), reads=[b_psC], writes=[bYn])
                        else:
                            f.op(act, lambda: S.copy(out=flat(Yn), in_=flat(psC)), reads=[b_psC], writes=[bYn])
                    TT, bTT = Yk[1]
                    if cut <= 7:
                        continue
                    for hi, h, pr, rows in heads:
                        f.op(pe, lambda: T.matmul(out=psA[:, hi, :], lhsT=KT[rows, pr, tc], rhs=AT[rows, pr, tc], start=True, stop=True),
                             reads=[b_KT2, b_AT], writes=[b_psA], rg=rows.start)
                    for hi, h, pr, rows in heads:
                        f.op(pe, lambda: T.matmul(out=psB[:, hi, :], lhsT=BT[rows, pr, tc], rhs=RT[rows, pr, tc], start=True, stop=True),
                             reads=[b_BT, b_RT], writes=[b_psB], rg=rows.start)
                    for hi, h, pr, rows in heads:
                        f.op(pe, lambda: T.matmul(out=psC[:, hi, :], lhsT=KT[rows, pr, tc], rhs=RT[rows, pr, tc], start=True, stop=True),
                             reads=[b_KT2, b_RT], writes=[b_psC], rg=rows.start)
                    f.op(dve, lambda: V.tensor_tensor(out=flat(AKT), in0=flat(psA), in1=mu4b[:], op=ALU.mult),
                         reads=[b_psA, b_mu4], writes=[b_AKT])
                    f.op(dve, lambda: V.tensor_tensor(out=flat(RBT), in0=flat(psB), in1=mui4b[:], op=ALU.mult),
                         reads=[b_psB, b_mui4], writes=[b_RBT])
                    f.op(dve, lambda: V.tensor_tensor(out=flat(RKT), in0=flat(psC), in1=mui4b[:], op=ALU.mult),
                         reads=[b_psC, b_mui4], writes=[b_RKT])
                    for hi, h, pr, rows in heads:
                        f.op(pe, lambda: T.matmul(out=psC[:, hi, 0:64], lhsT=AKT[:, hi, :], rhs=Vtm[:, tl, h * 64:(h + 1) * 64],
                                                  start=True, stop=True), reads=[b_AKT, b_Vtm], writes=[b_psC])
                    f.op(act, lambda: S.copy(out=P1[:], in_=psC[:, :, 0:64]), reads=[b_psC], writes=[b_P1])
                    gp = slice(2 * Gi, 2 * Gi + 2)
                    for c in range(2 if rl >= 5 else 0):
                        crow = slice(64 * c, 64 * c + 64)
                        smp = (Tg - 32) * 2 + c
                        if is_smp:
                            f.dma(sp, H[:, gp, :], swkv_d[smp, gp, :, :].rearrange("a p i -> p a i"), b_H, writes=[b_H])
                        hb, bhb = Hb[c]
                        f.op(pool, lambda: G.tensor_copy(out=hb[:, gp, :], in_=H[:, gp, :]), reads=[b_H], writes=[bhb])
                        for hi, h, pr, rows in heads:
                            f.op(pe, lambda: T.matmul(out=psA[:, hi, 0:64], lhsT=AT[rows, pr, tc], rhs=hb[rows, pr, :], start=True, stop=True),
                                 reads=[b_AT, bhb], writes=[b_psA], rg=rows.start)
                        f.op(dve, lambda: V.tensor_tensor(out=Wb[crow, :, :], in0=psA[crow, :, 0:64], in1=P1[crow, :, :], op=ALU.add),
                             reads=[b_psA, b_P1], writes=[b_Wb])
                        for hi, h, pr, rows in heads:
                            f.op(pe, lambda: T.matmul(out=psA[:, hi, 64:128], lhsT=TT[crow, hi, :], rhs=Wb[crow, hi, :], start=True, stop=True),
                                 reads=[bTT, b_Wb], writes=[b_psA], rg=crow.start)
                        f.op(act, lambda: S.copy(out=Ub[crow, :, :], in_=psA[crow, :, 64:128]), reads=[b_psA], writes=[b_Ub])
                        for hi, h, pr, rows in heads:
                            pcs = slice(pr * 128, (pr + 1) * 128)
                            f.op(pe, lambda: T.matmul(out=psB[:, hi, 0:64], lhsT=Btm[crow, tl, pcs], rhs=Ub[crow, hi, :], start=True, stop=False),
                                 reads=[b_Btm, b_Ub], writes=[b_psB], rg=crow.start)
                            f.op(pe, lambda: T.matmul(out=psB[:, hi, 0:64], lhsT=Ktm[crow, tl, pcs], rhs=Vtm[crow, tl, h * 64:(h + 1) * 64],
                                                      start=False, stop=True), reads=[b_Ktm, b_Vtm], writes=[b_psB], rg=crow.start)
                        for hi, h, pr, rows in heads:
                            gcol = Eneg[rows, pr, tl * 128 + 64 * c + 63:tl * 128 + 64 * c + 64]
                            f.op(pool, lambda: G.tensor_scalar(out=H[rows, pr, :], in0=H[rows, pr, :], scalar1=gcol, scalar2=None, op0=ALU.mult),
                                 reads=[b_H, b_Eneg], writes=[b_H])
                            f.op(dve, lambda: V.scalar_tensor_tensor(out=H[rows, pr, :], in0=psB[rows, hi, 0:64], scalar=gcol, in1=H[rows, pr, :],
                                                                     op0=ALU.mult, op1=ALU.add), reads=[b_psB, b_H, b_Eneg], writes=[b_H])
                        if is_smp:
                            f.dma(sp, wkv_out[1 + smp, gp, :, :].rearrange("a p i -> p a i"), H[:, gp, :], b_H, reads=[b_H], is_output=True)
                        elif Tg == 31 and c == 1:
                            f.dma(sp, wkv_out[0, gp, :, :].rearrange("a p i -> p a i"), H[:, gp, :], b_H, reads=[b_H], is_output=True)
                    if is_loc and rl >= 6:
                        (hb0, bhb0), (hb1, bhb1) = Hb
                        for hi, h, pr, rows in heads:
                            f.op(pe, lambda: T.matmul(out=psB[:, hi, 64:128], lhsT=RT0[rows, pr, tc], rhs=hb0[rows, pr, :], start=True, stop=False),
                                 reads=[b_RT0, bhb0], writes=[b_psB], rg=rows.start)
                            f.op(pe, lambda: T.matmul(out=psB[:, hi, 64:128], lhsT=RT1[rows, pr, tc], rhs=hb1[rows, pr, :], start=False, stop=False),
                                 reads=[b_RT1, bhb1], writes=[b_psB], rg=rows.start)
                            f.op(pe, lambda: T.matmul(out=psB[:, hi, 64:128], lhsT=RBT[:, hi, :], rhs=Ub[:, hi, :], start=False, stop=False),
                                 reads=[b_RBT, b_Ub], writes=[b_psB])
                            f.op(pe, lambda: T.matmul(out=psB[:, hi, 64:128], lhsT=RKT[:, hi, :], rhs=Vtm[:, tl, h * 64:(h + 1) * 64],
                                                      start=False, stop=True), reads=[b_RKT, b_Vtm], writes=[b_psB])
                        f.op(act, lambda: S.copy(out=Yall[:, 4 * Gi:4 * Gi + 4, :], in_=psB[:, :, 64:128]), reads=[b_psB], writes=[b_Yall])
                if is_loc and rl >= 7:
                    lcol = Tg * 128 - NPRE
                    f.op(dve, lambda: V.reduce_sum(out=gst[:, 0:8], in_=Yall[:], axis=AX.X), reads=[b_Yall], writes=[b_gst])
                    f.op(act, lambda: S.activation(out=Ysq, in_=Yall[:], func=AF.Square), reads=[b_Yall], writes=[b_Ysq])
                    f.op(dve, lambda: V.reduce_sum(out=gst[:, 8:16], in_=Ysq, axis=AX.X), reads=[b_Ysq, b_gst], writes=[b_gst])
                    f.op(dve, lambda: V.tensor_scalar(out=gst[:, 0:16], in0=gst[:, 0:16], scalar1=1.0 / 64, scalar2=None, op0=ALU.mult),
                         reads=[b_gst], writes=[b_gst])
                    f.op(dve, lambda: V.tensor_tensor(out=gst[:, 16:24], in0=gst[:, 0:8], in1=gst[:, 0:8], op=ALU.mult),
                         reads=[b_gst], writes=[b_gst])
                    f.op(dve, lambda: V.tensor_tensor(out=gst[:, 16:24], in0=gst[:, 8:16], in1=gst[:, 16:24], op=ALU.subtract),
                         reads=[b_gst], writes=[b_gst])
                    f.op(dve, lambda: V.tensor_scalar(out=gst[:, 16:24], in0=gst[:, 16:24], scalar1=64e-5, scalar2=None, op0=ALU.add),
                         reads=[b_gst], writes=[b_gst])
                    f.op(act, lambda: S.activation(out=gst[:, 16:24], in_=gst[:, 16:24], func=AF.Sqrt), reads=[b_gst], writes=[b_gst])
                    f.op(dve, lambda: V.reciprocal(out=gst[:, 24:32], in_=gst[:, 16:24]), reads=[b_gst], writes=[b_gst])
                    f.op(dve, lambda: V.tensor_tensor(out=Yall[:], in0=Yall[:], in1=gst[:, 0:8].unsqueeze(2).to_broadcast([128, 8, 64]),
                                                      op=ALU.subtract), reads=[b_Yall, b_gst], writes=[b_Yall])
                    f.op(dve, lambda: V.tensor_tensor(out=Yall[:], in0=Yall[:], in1=gst[:, 24:32].unsqueeze(2).to_broadcast([128, 8, 64]),
                                                      op=ALU.mult), reads=[b_Yall, b_gst], writes=[b_Yall])
                    yf = Yall[:].rearrange("p a b -> p (a b)")
                    f.op(dve, lambda: V.tensor_tensor(out=yf, in0=yf, in1=prow[:, GNW:GNW + 512], op=ALU.mult),
                         reads=[b_Yall, b_prow], writes=[b_Yall])
                    f.op(pool, lambda: G.tensor_tensor(out=yf, in0=yf, in1=prow[:, GNB:GNB + 512], op=ALU.add),
                         reads=[b_Yall, b_prow], writes=[b_Yall])
                    for pr in range(4):
                        f.op(pe, lambda: T.transpose(out=ptV[:, pr * 128:(pr + 1) * 128], in_=yf[:, pr * 128:(pr + 1) * 128], identity=ident_f[:]),
                             reads=[b_Yall, b_identf], writes=[b_ptV])
                    f.op(dve, lambda: V.tensor_tensor(out=ytmp, in0=ptV[:].rearrange("p (a b) -> p a b", a=4), in1=bonT[:, :, tc], op=ALU.add),
                         reads=[b_ptV, b_bonT], writes=[b_ytmp])
                    f.op(dve, lambda: V.tensor_tensor(out=yrT[:, :, lcol:lcol + 128], in0=ytmp, in1=gT[:, :, tc], op=ALU.mult),
                         reads=[b_ytmp, b_gT], writes=[b_yrT])
        f.barrier_all()
        f.release(mR)

    if "rwkv" in parts:
        rwkv_phase()
    oT, b_oT = f.sbuf("oT", [128, 4, NLOC], BF16)

    if stage >= 1.5 and "attn" in parts:
        m2 = f.mark()
        wv, b_wv = f.sbuf("wv", [128, 8, 512], BF16)
        wload(wv[:], b_wv, w_in[:, 1024:1536].rearrange("(c p) n -> p c n", p=128))
        Vh, b_Vh = f.sbuf("Vh", [128, NT, 129], BF16)
        f.op(pool, lambda: G.memset(Vh[:], 1.0), writes=[b_Vh])
        pvr = Ring([f.psum("pv%d" % i, [128, 512], F32) for i in range(1)])
        vor = Ring([f.sbuf("vo%d" % i, [128, 128], F32) for i in range(2)])

        def v_project(h):
            for t in range(NT):
                pv, bpv = pvr.next()
                ht, bht, lc = hT_cols(t * 128, 128)
                for c in range(8):
                    f.op(pe, lambda c=c: T.matmul(out=pv[:, 0:128], lhsT=ht[:, c, lc:lc + 128], rhs=wv[:, c, h * 128:(h + 1) * 128],
                                                  start=(c == 0), stop=(c == 7)), reads=[bht, b_wv], writes=[bpv])
                f.op(act, lambda: S.copy(out=Vh[:, t, 0:128], in_=pv[:, 0:128]), reads=[bpv], writes=[b_Vh])
                if t >= 16:
                    vo, bvo = vor.next()
                    f.op(dve, lambda: V.tensor_copy(out=vo[:], in_=pv[:, 0:128]), reads=[bpv], writes=[bvo])
                    f.dma(sp, v_out[(t - 16) * 128:(t - 15) * 128, h * 128:(h + 1) * 128], vo[:], bvo, reads=[bvo], is_output=True)

        KT_, b_KT = f.sbuf("KhT", [68, 2, NCOL], BF16)
        QT_, b_QT = f.sbuf("QhT", [68, 2, NLOC], BF16)
        wq, b_wq = f.sbuf("wq", [128, 8, 128], BF16)
        wk, b_wk = f.sbuf("wk", [128, 8, 128], BF16)
        pqr = Ring([f.psum("pq%d" % i, [64, 512], F32) for i in range(1)])
        pqr.items.append((pvr.items[0][0][0:64, :], pvr.items[0][1]))
        psq, b_psq = f.psum("psq", [64, 512], F32)
        sqt, b_sqt = f.sbuf("sqt", [64, 512], F32)
        rnt, b_rnt = f.sbuf("rnt", [64, 512], F32)
        kor = Ring([f.sbuf("ko%d" % i, [64, 512], F32) for i in range(1)])
        pS_r = Ring([f.psum("pS%d" % i, [128, 2, 256], F32) for i in range(2)])
        pO, b_pO = f.psum("pO", [128, 2, 2, 256], F32)
        PTr = Ring([f.sbuf("PT%d" % i, [128, 2, 256], BF16) for i in range(2)])
        osb, b_osb = f.sbuf("osb", [128, 128], F32)
        osb2, b_osb2 = f.sbuf("osb2", [128, 128], F32)
        obf, b_obf = f.sbuf("obf", [128, 128], BF16)
        ost, b_ost = f.sbuf("ost", [128, 8], F32)
        pTo, b_pTo = f.psum("pTo", [128, 1024], BF16)

        def qk_project(wt, bwt, ncols, dstT, bdst, gcol, is_k):
            nb = (ncols + 511) // 512
            for bi in range(nb):
                c0 = bi * 512
                n = min(512, ncols - c0)
                gc0 = c0 if is_k else c0 + NPRE
                ht, bht, lc = hT_cols(gc0, n)
                for cmp_ in range(2):
                    pq, bpq = pqr.next()
                    for c in range(8):
                        f.op(pe, lambda c=c: T.matmul(out=pq[:, 0:n], lhsT=wt[:, c, cmp_ * 64:(cmp_ + 1) * 64],
                                                      rhs=ht[:, c, lc:lc + n], start=(c == 0), stop=(c == 7)),
                             reads=[bht, bwt], writes=[bpq])
                    f.op(act, lambda: S.activation(out=sqt[:, 0:n], in_=pq[:, 0:n], func=AF.Square),
                         reads=[bpq], writes=[b_sqt])
                    f.op(pe, lambda: T.matmul(out=psq[:, 0:n], lhsT=bones[0:64, 0:64], rhs=sqt[:, 0:n], start=True, stop=True),
                         reads=[b_sqt, b_bones], writes=[b_psq])
                    f.op(dve, lambda: V.tensor_scalar(out=rnt[:, 0:n], in0=psq[:, 0:n], scalar1=1.0 / 64, scalar2=1e-6,
                                                      op0=ALU.mult, op1=ALU.add), reads=[b_psq], writes=[b_rnt])
                    f.op(act, lambda: S.activation(out=rnt[:, 0:n], in_=rnt[:, 0:n], func=AF.Sqrt), reads=[b_rnt], writes=[b_rnt])
                    f.op(dve, lambda: V.reciprocal(out=rnt[:, 0:n], in_=rnt[:, 0:n]), reads=[b_rnt], writes=[b_rnt])
                    f.op(dve, lambda: V.scalar_tensor_tensor(out=dstT[0:64, cmp_, c0:c0 + n], in0=pq[:, 0:n], scalar=gcol,
                                                             in1=rnt[:, 0:n], op0=ALU.mult, op1=ALU.mult),
                         reads=[bpq, b_rnt, b_pvec, b_pv2], writes=[bdst])
                    if is_k and gc0 >= NPRE:
                        ko, bko = kor.next()
                        f.op(pool if False else dve, lambda: V.scalar_tensor_tensor(out=ko[:, 0:n], in0=pq[:, 0:n], scalar=gcol,
                                                                 in1=rnt[:, 0:n], op0=ALU.mult, op1=ALU.mult),
                             reads=[bpq, b_rnt, b_pvec], writes=[bko])
                        r0 = cur_h[0] * 128 + cmp_ * 64
                        f.dma(sp, k_out[r0:r0 + 64, gc0 - NPRE:gc0 - NPRE + n], ko[:, 0:n], bko, reads=[bko], is_output=True)

        qs_tm, b_qs = f.sbuf("qs_tm", [128, 2, 512], BF16)
        ks_tm, b_ks = f.sbuf("ks_tm", [128, 2, 512], BF16)
        vs_tm, b_vs = f.sbuf("vs_tm", [128, 2, 512], BF16)
        cur_h = [0]
        for h in range((4 if dbg is None else 1) if stage >= 1.6 else 0):
            cur_h[0] = h
            wload(wq[:], b_wq, w_in[:, h * 128:(h + 1) * 128].rearrange("(c p) n -> p c n", p=128))
            wload(wk[:], b_wk, w_in[:, 512 + h * 128:512 + (h + 1) * 128].rearrange("(c p) n -> p c n", p=128))
            for cmp_ in range(2):
                f.dma(pool, KT_[64:68, cmp_, 0:4096], kb_d[h, :, :], b_KT, writes=[b_KT])
                f.dma(pool, QT_[64:68, cmp_, 0:2048], qb_d[h, :, :], b_QT, writes=[b_QT])
            if dbg == "attn":
                f.op(dve, lambda: V.memset(KT_[:], 0.125), writes=[b_KT])
                f.op(dve, lambda: V.memset(QT_[:], 0.125), writes=[b_QT])
            elif stage >= 2.0:
                v_project(h)
            if stage >= 2.1 and dbg is None:
                qk_project(wk, b_wk, NCOL, KT_, b_KT, pvec[0:64, KG_C:KG_C + 1], True)
                qk_project(wq, b_wq, NLOC, QT_, b_QT, pv2[0:64, 22:23], False)
            if "samp" in parts:
                for tl in range(2):
                    for cmp_ in range(2):
                        f.op(pe, lambda: T.transpose(out=pTo[:, 0:64], in_=QT_[0:64, cmp_, 2048 + tl * 128:2048 + (tl + 1) * 128], identity=ident_b[0:64, 0:64]),
                             reads=[b_QT, b_identb], writes=[b_pTo])
                        f.op(act, lambda: S.copy(out=qs_tm[:, tl, h * 128 + cmp_ * 64:h * 128 + (cmp_ + 1) * 64], in_=pTo[:, 0:64]),
                             reads=[b_pTo], writes=[b_qs])
                        f.op(pe, lambda: T.transpose(out=pTo[:, 0:64], in_=KT_[0:64, cmp_, 4096 + tl * 128:4096 + (tl + 1) * 128], identity=ident_b[0:64, 0:64]),
                             reads=[b_KT, b_identb], writes=[b_pTo])
                        f.op(act, lambda: S.copy(out=ks_tm[:, tl, h * 128 + cmp_ * 64:h * 128 + (cmp_ + 1) * 64], in_=pTo[:, 0:64]),
                             reads=[b_pTo], writes=[b_ks])
                    f.op(pool, lambda: G.tensor_copy(out=vs_tm[:, tl, h * 128:(h + 1) * 128], in_=Vh[:, 32 + tl, 0:128]),
                         reads=[b_Vh], writes=[b_vs])
            if stage < 2.2:
                continue
            def emit_S(g, kt):
                d1 = (kt == 16 + 2 * g + 1)
                q0 = 128 if d1 else 0
                pS, bpS = pS_r.next()
                for cmp_ in range(2):
                    f.op(pe, lambda cmp_=cmp_: T.matmul(out=pS[:, cmp_, q0:256], lhsT=KT_[:, cmp_, kt * 128:(kt + 1) * 128],
                                                        rhs=QT_[:, cmp_, g * 256 + q0:(g + 1) * 256], start=True, stop=True),
                         reads=[b_KT, b_QT], writes=[bpS])
                return (g, kt, pS, bpS)

            def emit_rest(st_):
                g, kt, pS, bpS = st_
                d0 = (kt == 16 + 2 * g)
                d1 = (kt == 16 + 2 * g + 1)
                q0 = 128 if d1 else 0
                PT, bPT = PTr.next()
                f.op(act, lambda: S.activation(out=PT[:, :, q0:256], in_=pS[:, :, q0:256], func=AF.Exp),
                     reads=[bpS], writes=[bPT])
                if d0 or d1:
                    for cmp_ in range(2):
                        f.op(pool, lambda cmp_=cmp_: G.tensor_tensor(out=PT[:, cmp_, q0:q0 + 128], in0=PT[:, cmp_, q0:q0 + 128],
                                                                     in1=tri_b[:], op=ALU.mult),
                             reads=[bPT, b_tri], writes=[bPT])
                for cmp_ in range(2):
                    for sub in range(2):
                        if d1 and sub == 0:
                            continue
                        last = (kt == 16 + 2 * g + sub)
                        f.op(pe, lambda cmp_=cmp_, sub=sub, last=last: T.matmul(
                            out=pO[:, cmp_, sub, 0:129], lhsT=PT[:, cmp_, sub * 128:(sub + 1) * 128], rhs=Vh[:, kt, :],
                            start=(kt == 0), stop=last), reads=[bPT, b_Vh], writes=[b_pO])

            steps = [(g, kt) for g in range(ng) for kt in range(16 + 2 * g + 2)]
            pend = None
            for si in range(len(steps) + 1):
                cur = emit_S(*steps[si]) if si < len(steps) else None
                if pend is not None:
                    emit_rest(pend)
                    g = pend[0]
                    if pend[1] != 16 + 2 * g + 1:
                        pend = cur
                        continue
                else:
                    pend = cur
                    continue
                pend = cur
                for sub in range(2):
                    lcq = g * 256 + sub * 128
                    f.op(dve, lambda: V.reciprocal(out=ost[:, 0:2], in_=pO[:, :, sub, 128]), reads=[b_pO], writes=[b_ost])
                    f.op(dve, lambda: V.tensor_tensor(out=ost[:, 2:3], in0=ost[:, 1:2], in1=NLAM, op=ALU.mult),
                         reads=[b_ost, b_lt], writes=[b_ost])
                    f.op(dve, lambda: V.tensor_scalar(out=osb[:], in0=pO[:, 0, sub, 0:128], scalar1=ost[:, 0:1], scalar2=None,
                                                      op0=ALU.mult), reads=[b_pO, b_ost], writes=[b_osb])
                    f.op(dve, lambda: V.scalar_tensor_tensor(out=osb[:], in0=pO[:, 1, sub, 0:128], scalar=ost[:, 2:3], in1=osb[:],
                                                             op0=ALU.mult, op1=ALU.add), reads=[b_pO, b_ost, b_osb], writes=[b_osb])
                    finalize_o(f, nc, osb, b_osb, osb2, b_osb2, obf, b_obf, ost, b_ost, prow, b_prow, GAO, lam_init,
                               pTo, b_pTo, ident_b, b_identb, oT, b_oT, h, lcq)

        if "samp" in parts:
            selb, b_selb = cload("selb", sel_d[:, :], [128, 512], BF16)
            sel0b, b_sel0b = cload("sel0b", sel0_d[:, :], [128, 256], BF16)
            sbias, b_sbias = cload("sbias", sbias_d[:, :], [128, 520])
            hmask, b_hmask = cload("hmask", hmask_d[:, :], [8, 4])
            e0t, b_e0 = cload("e0t", e0_d[:, :], [8, 4])
            e1t, b_e1 = cload("e1t", e1_d[:, :], [8, 4])
            iop, b_iop = cload("iop", iota_d[:, :], [128, 1], I32)
            pti, b_pti = f.sbuf("pti", [128, 256], I32)
            f.dma(sp, pti[:], ptab_d[0:1, :].partition_broadcast(128), b_pti, writes=[b_pti])
            ptf, b_ptf = f.sbuf("ptf", [128, 256], F32)
            iof, b_iof = f.sbuf("iof", [128, 1], F32)
            idx, b_idx = f.sbuf("idx", [128, 256], I32)
            f.op(dve, lambda: V.tensor_copy(out=ptf[:], in_=pti[:]), reads=[b_pti], writes=[b_ptf])
            f.op(dve, lambda: V.tensor_copy(out=iof[:], in_=iop[:]), reads=[b_iop], writes=[b_iof])
            f.op(dve, lambda: V.tensor_scalar(out=ptf[:], in0=ptf[:], scalar1=128.0, scalar2=iof[:, 0:1], op0=ALU.mult, op1=ALU.add),
                 reads=[b_ptf, b_iof], writes=[b_ptf])
            f.op(dve, lambda: V.tensor_copy(out=idx[:], in_=ptf[:]), reads=[b_ptf], writes=[b_idx])
            cmb, b_cmb = f.sbuf("cmb", [8, 4], F32)
            f.op(dve, lambda: V.scalar_tensor_tensor(out=cmb[:], in0=e1t[:], scalar=lt[0:8, 5:6], in1=e0t[:], op0=ALU.mult, op1=ALU.add),
                 reads=[b_e1, b_e0, b_lt], writes=[b_cmb])
            onesc, b_onesc = f.sbuf("onesc", [128, 1], F32)
            f.op(dve, lambda: V.memset(onesc[:], 1.0), writes=[b_onesc])
            Ktr = Ring([f.sbuf("Kt%d" % i, [128, 512], F32) for i in range(2)])
            Vtr = Ring([f.sbuf("Vt%d" % i, [128, 512], F32) for i in range(2)])
            prodr = Ring([f.sbuf("prod%d" % i, [128, 512], F32) for i in range(1)])
            qbc, b_qbc = f.sbuf("qbc", [128, 512], F32)
            spg_r = Ring([f.sbuf("spg%d" % i, [128, 16], F32) for i in range(3)])
            osm, b_osm = f.sbuf("osm", [8, 512], F32)
            osel, b_osel = f.sbuf("osel", [8, 128], F32)
            ofin, b_ofin = f.sbuf("ofin", [4, 128], F32)
            ofin2, b_ofin2 = f.sbuf("ofin2", [4, 128], F32)
            ofb, b_ofb = f.sbuf("ofb", [4, 128], BF16)
            sst, b_sst = f.sbuf("sst", [8, 8], F32)
            pS0, bpS0 = pS_r.items[0]
            pS0f = pS0[:].rearrange("p a b -> p (a b)")
            pso = pO[0:8, 0, :, :].rearrange("p a b -> p (a b)")
            psz = pO[0:8, 1, 0, 0:1]
            pvr0, bpvr0 = pvr.items[0]
            for s in range(4):
                tl = s // 2
                col = 2048 + 128 * tl + 64 * (s % 2) + 1
                f.op(pe, lambda: T.matmul(out=pS0f, lhsT=selb[:, s * 128:(s + 1) * 128], rhs=qs_tm[:, tl, :], start=True, stop=True),
                     reads=[b_selb, b_qs], writes=[bpS0])
                f.op(act, lambda: S.copy(out=qbc[:], in_=pS0f), reads=[bpS0], writes=[b_qbc])
                for pg in range(65):
                    Kt, bKt = Ktr.next(); Vt, bVt = Vtr.next(); prod, bprod = prodr.next(); spg, bspg = spg_r.next()
                    if pg < 64:
                        ic = s * 64 + pg
                        f.dma(pool, None, None, bKt, reads=[b_idx], writes=[bKt],
                              fn=lambda: G.indirect_dma_start(out=Kt[:, :], out_offset=None, in_=ck_d[:, :],
                                                              in_offset=bass.IndirectOffsetOnAxis(ap=idx[:, ic:ic + 1], axis=0)))
                        f.dma(pool, None, None, bVt, reads=[b_idx], writes=[bVt],
                              fn=lambda: G.indirect_dma_start(out=Vt[:, :], out_offset=None, in_=cv_d[:, :],
                                                              in_offset=bass.IndirectOffsetOnAxis(ap=idx[:, ic:ic + 1], axis=0)))
                    else:
                        f.op(pe, lambda: T.matmul(out=pS0f, lhsT=sel0b[:, (s % 2) * 128:(s % 2 + 1) * 128], rhs=ks_tm[:, tl, :], start=True, stop=True),
                             reads=[b_sel0b, b_ks], writes=[bpS0])
                        f.op(act, lambda: S.copy(out=Kt[:], in_=pS0f), reads=[bpS0], writes=[bKt])
                        f.op(pe, lambda: T.matmul(out=pS0f, lhsT=sel0b[:, (s % 2) * 128:(s % 2 + 1) * 128], rhs=vs_tm[:, tl, :], start=True, stop=True),
                             reads=[b_sel0b, b_vs], writes=[bpS0])
                        f.op(act, lambda: S.copy(out=Vt[:], in_=pS0f), reads=[bpS0], writes=[bVt])
                    if pg % 2 == 0:
                        f.op(pool, lambda: G.tensor_tensor(out=prod[:], in0=Kt[:], in1=qbc[:], op=ALU.mult), reads=[bKt, b_qbc], writes=[bprod])
                    else:
                        f.op(dve, lambda: V.tensor_tensor(out=prod[:], in0=Kt[:], in1=qbc[:], op=ALU.mult), reads=[bKt, b_qbc], writes=[bprod])
                    f.op(dve, lambda: V.reduce_sum(out=spg[:, 0:8], in_=prod[:].rearrange("p (g d) -> p g d", g=8), axis=AX.X),
                         reads=[bprod], writes=[bspg])
                    f.op(dve, lambda: V.tensor_tensor(out=spg[:, 0:8], in0=spg[:, 0:8], in1=sbias[:, pg * 8:(pg + 1) * 8], op=ALU.add),
                         reads=[bspg, b_sbias], writes=[bspg])
                    f.op(act, lambda: S.activation(out=spg[:, 8:16], in_=spg[:, 0:8], func=AF.Exp), reads=[bspg], writes=[bspg])
                    f.op(pe, lambda: T.matmul(out=pso, lhsT=spg[:, 8:16], rhs=Vt[:], start=(pg == 0), stop=(pg == 64)),
                         reads=[bspg, bVt], writes=[b_pO])
                    f.op(pe, lambda: T.matmul(out=psz, lhsT=spg[:, 8:16], rhs=onesc[:], start=(pg == 0), stop=(pg == 64)),
                         reads=[bspg, b_onesc], writes=[b_pO])
                f.op(dve, lambda: V.reciprocal(out=sst[:, 0:1], in_=psz), reads=[b_pO], writes=[b_sst])
                f.op(dve, lambda: V.tensor_scalar(out=osm[:], in0=pso, scalar1=sst[:, 0:1], scalar2=None, op0=ALU.mult),
                     reads=[b_pO, b_sst], writes=[b_osm])
                f.op(dve, lambda: V.tensor_tensor(out=osm[:].rearrange("p (h d) -> p h d", h=4), in0=osm[:].rearrange("p (h d) -> p h d", h=4),
                                                  in1=hmask[:].unsqueeze(2).to_broadcast([8, 4, 128]), op=ALU.mult),
                     reads=[b_osm, b_hmask], writes=[b_osm])
                f.op(dve, lambda: V.reduce_sum(out=osel[:], in_=osm[:].rearrange("p (h d) -> p d h", h=4), axis=AX.X),
                     reads=[b_osm], writes=[b_osel])
                f.op(pe, lambda: T.matmul(out=pvr0[0:4, 0:128], lhsT=cmb[:], rhs=osel[:], start=True, stop=True),
                     reads=[b_cmb, b_osel], writes=[bpvr0])
                f.op(dve, lambda: V.tensor_copy(out=ofin[:], in_=pvr0[0:4, 0:128]), reads=[bpvr0], writes=[b_ofin])
                f.op(dve, lambda: V.memset(sst[0:4, 1:2], 0.0), writes=[b_sst])
                f.op(act, lambda: S.activation(out=ofin2[:], in_=ofin[:], func=AF.Square, accum_out=sst[0:4, 1:2]),
                     reads=[b_ofin, b_sst], writes=[b_ofin2, b_sst])
                f.op(dve, lambda: V.tensor_scalar(out=sst[0:4, 2:3], in0=sst[0:4, 1:2], scalar1=1.0 / 128, scalar2=1e-6, op0=ALU.mult, op1=ALU.add),
                     reads=[b_sst], writes=[b_sst])
                f.op(act, lambda: S.activation(out=sst[0:4, 2:3], in_=sst[0:4, 2:3], func=AF.Sqrt), reads=[b_sst], writes=[b_sst])
                f.op(dve, lambda: V.reciprocal(out=sst[0:4, 2:3], in_=sst[0:4, 2:3]), reads=[b_sst], writes=[b_sst])
                f.op(dve, lambda: V.tensor_scalar(out=sst[0:4, 3:4], in0=sst[0:4, 2:3], scalar1=(1.0 - lam_init), scalar2=None, op0=ALU.mult),
                     reads=[b_sst], writes=[b_sst])
                f.op(dve, lambda: V.scalar_tensor_tensor(out=ofb[:], in0=ofin[:], scalar=sst[0:4, 3:4], in1=prow[0:4, GAO:GAO + 128],
                                                         op0=ALU.mult, op1=ALU.mult), reads=[b_ofin, b_sst, b_prow], writes=[b_ofb])
                f.op(pe, lambda: T.transpose(out=pTo[:, 0:4], in_=ofb[:], identity=ident_b[0:4, 0:4]), reads=[b_ofb, b_identb], writes=[b_pTo])
                f.op(act, lambda: S.copy(out=oT[:, :, col], in_=pTo[:, 0:4]), reads=[b_pTo], writes=[b_oT])
        f.release(m2)
        f.barrier_all()

    if "epi" in parts:
        epilogue()
    f.release(m_pre)
    f.finish()
    f.close()
    return nc


def finalize_o(f, nc, osb, b_osb, osb2, b_osb2, obf, b_obf, ost, b_ost, prow, b_prow, GAO, lam_init,
               pTo, b_pTo, ident_b, b_identb, oT, b_oT, h, lcq):
    V, S, T = nc.vector, nc.scalar, nc.tensor
    dve, act, pe = f.dve, f.act, f.pe
    f.op(dve, lambda: V.memset(ost[:, 4:5], 0.0), writes=[b_ost])
    f.op(act, lambda: S.activation(out=osb2[:], in_=osb[:], func=AF.Square, accum_out=ost[:, 4:5]),
         reads=[b_osb, b_ost], writes=[b_osb2, b_ost])
    f.op(dve, lambda: V.tensor_scalar(out=ost[:, 5:6], in0=ost[:, 4:5], scalar1=1.0 / 128, scalar2=1e-6,
                                      op0=ALU.mult, op1=ALU.add), reads=[b_ost], writes=[b_ost])
    f.op(act, lambda: S.activation(out=ost[:, 5:6], in_=ost[:, 5:6], func=AF.Sqrt), reads=[b_ost], writes=[b_ost])
    f.op(dve, lambda: V.reciprocal(out=ost[:, 5:6], in_=ost[:, 5:6]), reads=[b_ost], writes=[b_ost])
    f.op(dve, lambda: V.tensor_scalar(out=ost[:, 6:7], in0=ost[:, 5:6], scalar1=(1.0 - lam_init), scalar2=None,
                                      op0=ALU.mult), reads=[b_ost], writes=[b_ost])
    f.op(dve, lambda: V.scalar_tensor_tensor(out=obf[:], in0=osb[:], scalar=ost[:, 6:7], in1=prow[:, GAO:GAO + 128],
                                             op0=ALU.mult, op1=ALU.mult), reads=[b_osb, b_ost, b_prow], writes=[b_obf])
    f.op(pe, lambda: T.transpose(out=pTo[:, 0:128], in_=obf[:], identity=ident_b[:]), reads=[b_obf, b_identb], writes=[b_pTo])
    f.op(act, lambda: S.copy(out=oT[:, h, lcq:lcq + 128], in_=pTo[:, 0:128]), reads=[b_pTo], writes=[b_oT])


def sample_attention(f, nc, L):
    pass


def _consts(half):
    c = {}
    c["ident"] = np.eye(128, dtype=np.float32)
    s_idx = np.arange(128)[:, None]; t_idx = np.arange(128)[None, :]
    same = (s_idx // 64) == (t_idx // 64)
    mu = ((s_idx < t_idx) & same).astype(np.float32)
    mui = ((s_idx <= t_idx) & same).astype(np.float32)
    ml = mu.T.copy()
    c["mu4"] = np.tile(mu, (1, 4)); c["ml4"] = np.tile(ml, (1, 4)); c["mui4"] = np.tile(mui, (1, 4))
    c["tri"] = (s_idx <= t_idx).astype(np.float32)
    cmk = np.zeros((128, 256), np.float32)
    cmk[:, [1, 65, 129, 193]] = 1.0
    c["colmask"] = cmk
    rm = np.ones((128, 512), np.float32); rm[:, ::64] = 0.0
    c["resetm"] = rm
    col = np.arange(512)[None, :]
    c["cm0"] = np.broadcast_to(((col % 128) < 64).astype(np.float32), (128, 512)).copy()
    c["cm1"] = np.broadcast_to(((col % 128) >= 64).astype(np.float32), (128, 512)).copy()
    c["bones"] = same.astype(np.float32)
    sel = np.zeros((128, 4, 128), np.float32)
    sel0 = np.zeros((128, 2, 128), np.float32)
    for s in range(4):
        sel[1 + 64 * (s % 2), s, :] = 1.0
    for r in range(2):
        sel0[1 + 64 * r, r, 0] = 1.0
    c["sel"] = sel.reshape(128, 512); c["sel0"] = sel0.reshape(128, 256)
    c["iotap"] = np.arange(128, dtype=np.int32).reshape(128, 1)
    hm = np.zeros((8, 4), np.float32); e0 = np.zeros((8, 4), np.float32); e1 = np.zeros((8, 4), np.float32)
    for h in range(4):
        for cc in range(2):
            hm[h * 2 + cc, h] = 1.0
        e0[h * 2, h] = 1.0; e1[h * 2 + 1, h] = 1.0
    c["hmask"] = hm; c["e0"] = e0; c["e1"] = e1
    slopes = np.array([2.0 ** (-8.0 * (h + 1) / 4) for h in range(4)], np.float64)
    kcol = np.arange(4096)
    kb = np.zeros((4, 4, 4096), np.float32)
    qb = np.zeros((4, 4, 2048), np.float32)
    qpos = 2048 + np.arange(2048)
    for h in range(4):
        kb[h, 0] = slopes[h] * 128 * (kcol // 128)
        if half == 0:
            kb[h, 0, :2048] = NEG
        kb[h, 1] = slopes[h] * (kcol % 128)
        kb[h, 2] = 1.0; kb[h, 3] = 1.0
        qb[h, 0] = 1.0; qb[h, 1] = 1.0
        qb[h, 2] = -slopes[h] * 128 * (qpos // 128)
        qb[h, 3] = -slopes[h] * (qpos % 128)
    c["kb"] = kb; c["qb"] = qb
    sb = np.zeros((128, 65, 8), np.float32)
    slot = np.arange(128)[:, None]
    for pg in range(64):
        dist = 8192 - (128 * pg + slot)
        for h in range(4):
            sb[:, pg, 2 * h] = (-slopes[h] * dist)[:, 0]; sb[:, pg, 2 * h + 1] = (-slopes[h] * dist)[:, 0]
    sb[1:, 64, :] = NEG
    c["sbias"] = sb.reshape(128, 65 * 8)
    return c


_NC_CACHE = {}
_LAST = None


def kernel(**inp):
    f32 = np.float32
    xp = np.asarray(inp["x_prompt"], f32); xs = np.asarray(inp["x_sample"], f32)
    g = lambda k: np.ascontiguousarray(np.asarray(inp[k], f32)[0])
    w_in = g("w_in")
    shared = {
        "w_in": w_in, "w_pa": g("w_pa"), "w_pb": g("w_pb"), "w_out": g("w_out"),
        "w_gate": g("w_gate"), "w_up": g("w_up"), "w_down": g("w_down"),
        "wa2": np.ascontiguousarray(np.concatenate([g("w2"), g("a2")], 0)), "g2": g("g2"),
        "ck": np.ascontiguousarray(np.asarray(inp["cache_k"], f32).reshape(2560 * 128, 512)),
        "cv": np.ascontiguousarray(np.asarray(inp["cache_v"], f32).reshape(2560 * 128, 512)),
    }
    pvec = np.zeros((128, 36), f32)
    pvec[:, 0:14] = g("shift_mu").reshape(14, 128).T
    pvec[:, 14:18] = g("w0").reshape(4, 128).T
    pvec[:, 18:22] = g("a0").reshape(4, 128).T
    pvec[:, 22:26] = g("k_k").reshape(4, 128).T
    pvec[:, 26:30] = g("k_a").reshape(4, 128).T
    pvec[:, 30:34] = g("r_k").reshape(4, 128).T
    pvec[:, 34] = np.tile(g("q_gain"), 2); pvec[:, 35] = np.tile(g("k_gain"), 2)
    prow = np.concatenate([g("norm_mix"), g("norm_ffn"), g("attn_out_gain"), g("gn_w"), g("gn_b"),
                           g("lambda_q1"), g("lambda_k1"), g("lambda_q2"), g("lambda_k2")]).reshape(1, 3456).astype(f32)
    shared["pvec"] = pvec; shared["prow"] = prow
    consts = [_consts(0), _consts(1)]
    ptab = np.asarray(inp["page_table"], np.int32)
    swkv = np.asarray(inp["state_wkv"], f32)[0]
    sshift = np.asarray(inp["state_shift"], f32)[0]
    in_maps = []
    for c in range(8):
        b, half = c // 2, c % 2
        xin = np.zeros((NCOL, D), f32)
        if half == 1:
            xin[0:2048] = xp[b, 0:2048]
        xin[2048:4096] = xp[b, half * 2048:(half + 1) * 2048]
        for s in range(4):
            xin[4096 + 128 * (s // 2) + 64 * (s % 2) + 1] = xs[4 * c + s, 0]
        m = dict(shared)
        m.update(consts[half])
        m["xin"] = xin
        m["ptab"] = np.ascontiguousarray(ptab[4 * c:4 * c + 4].reshape(1, 256))
        sw = swkv[4 * c:4 * c + 4].transpose(0, 1, 3, 2).reshape(4, 4, 128, 64)
        m["swkv"] = np.ascontiguousarray(sw)
        m["sshift"] = np.ascontiguousarray(sshift[4 * c:4 * c + 4].reshape(4, 14, 128).transpose(2, 1, 0))
        in_maps.append(m)
    if "nc" not in _NC_CACHE:
        _NC_CACHE["nc"] = build()
        _NC_CACHE["small"] = False
    if _NC_CACHE.get("small"):
        for m in in_maps:
            m["ck"] = m["ck"][:128]; m["cv"] = m["cv"][:128]
    nc = _NC_CACHE["nc"]
    res = run_bass_kernel_spmd(nc, in_maps, core_ids=list(range(8)))
    R = res.results
    global _LAST
    _LAST = R
    y_p = np.zeros((4, 4096, 1024), f32); y_s = np.zeros((32, 1, 1024), f32)
    k_p = np.zeros((1, 4, 4096, 4, 128), f32); v_p = np.zeros((1, 4, 4096, 4, 128), f32)
    wkv_p = np.zeros((1, 4, 8, 64, 64), f32); sh_p = np.zeros((1, 4, 1792), f32)
    k_s = np.zeros((1, 32, 1, 4, 128), f32); v_s = np.zeros((1, 32, 1, 4, 128), f32)
    wkv_s = np.zeros((1, 32, 8, 64, 64), f32); sh_s = np.zeros((1, 32, 1792), f32)
    for c in range(8):
        b, half = c // 2, c % 2
        r = R[c]
        sl = slice(half * 2048, (half + 1) * 2048)
        y_p[b, sl] = r["y_out"][0:2048]
        kT = r["k_out"]
        k_p[0, b, sl] = kT[:, 0:2048].T.reshape(2048, 4, 128)
        v_p[0, b, sl] = r["v_out"][0:2048].reshape(2048, 4, 128)
        wk = r["wkv_out"].reshape(5, 8, 64, 64)
        po = r["p_out"]
        if half == 1:
            wkv_p[0, b] = wk[0].transpose(0, 2, 1)
            sh_p[0, b] = po[:, :, 127].reshape(1792)
        for s in range(4):
            col = 128 * (s // 2) + 64 * (s % 2) + 1
            y_s[4 * c + s, 0] = r["y_out"][2048 + col]
            k_s[0, 4 * c + s, 0] = kT[:, 2048 + col].reshape(4, 128)
            v_s[0, 4 * c + s, 0] = r["v_out"][2048 + col].reshape(4, 128)
            wkv_s[0, 4 * c + s] = wk[1 + s].transpose(0, 2, 1)
            sh_s[0, 4 * c + s] = po[:, :, 128 + col].reshape(1792)
    return (y_p, y_s, k_p, v_p, wkv_p, sh_p, k_s, v_s, wkv_s, sh_s)
```

```python
import math
import numpy as np
import concourse.bass as bass
import concourse.mybir as mybir
from concourse.bass_utils import run_bass_kernel_spmd

F32 = mybir.dt.float32
BF16 = mybir.dt.bfloat16
I32 = mybir.dt.int32
ALU = mybir.AluOpType
AF = mybir.ActivationFunctionType
AX = mybir.AxisListType

SEM_EPOCH = 30000
NPRE, NOWN, NSMP = 2048, 2048, 256
NCOL = NPRE + NOWN + NSMP
NLOC = NOWN + NSMP
NT = NCOL // 128
D = 1024
DFF = 2816
NEG = -30000.0


class Eng:
    def __init__(self, fw, name, e):
        self.fw = fw; self.name = name; self.e = e
        self.sems = []; self.count = 0; self.epoch = -1; self.known = {}
        self._new_epoch()

    def _new_epoch(self):
        self.epoch += 1
        self.count = 0
        self.sems.append(self.fw.new_sem("%s_e%d" % (self.name, self.epoch)))


class Buf:
    _uid = [0]

    def __init__(self, name, psum=False):
        Buf._uid[0] += 1
        self.uid = Buf._uid[0]
        self.name = name; self.w = None; self.r = []; self.dsem = None; self.dcount = 0; self.psum = psum


class FW:
    def __init__(self, nc):
        self.nc = nc
        self._stack = []
        self._semstack = []
        self.engs = {}
        for name, e in (("pe", nc.tensor), ("act", nc.scalar), ("dve", nc.vector),
                        ("pool", nc.gpsimd), ("sp", nc.sync)):
            self.engs[name] = Eng(self, name, e)
        self.pe = self.engs["pe"]; self.act = self.engs["act"]; self.dve = self.engs["dve"]
        self.pool = self.engs["pool"]; self.sp = self.engs["sp"]
        self.nbuf = 0
        self.out_bufs = []
        self.free_dsems = []

    def new_sem(self, name):
        cm = self.nc.semaphore(name)
        s = cm.__enter__()
        self._semstack.append(cm)
        return s

    def sbuf(self, name, shape, dt):
        cm = self.nc.sbuf_tensor("sb_" + name, list(shape), dt)
        t = cm.__enter__()
        self._stack.append(cm)
        self.nbuf += 1
        return t, Buf(name)

    def psum(self, name, shape, dt):
        cm = self.nc.psum_tensor("ps_" + name, list(shape), dt)
        t = cm.__enter__()
        self._stack.append(cm)
        return t, Buf(name, psum=True)

    def mark(self):
        return len(self._stack)

    def release(self, mark):
        while len(self._stack) > mark:
            self._stack.pop().__exit__(None, None, None)

    def _need(self, eng, stamp, kind):
        if stamp is None:
            return
        if stamp[0] == 'e':
            _, pe_, ep, cnt = stamp
            if pe_ is eng:
                if eng.name == "pe" or kind != "raw":
                    return
            key = (pe_.name, ep)
            if eng.known.get(key, 0) >= cnt:
                return
            eng.e.wait_ge(pe_.sems[ep], cnt)
            eng.known[key] = cnt
        else:
            _, sem, val, key = stamp
            if eng.known.get(key, 0) >= val:
                return
            eng.e.wait_ge(sem, val)
            eng.known[key] = val

    def _deps(self, eng, reads, writes):
        for b in reads:
            self._need(eng, b.w, "raw")
        for b in writes:
            self._need(eng, b.w, "waw")
            for s in b.r:
                self._need(eng, s, "war")

    def _record(self, st, reads, writes):
        for b in reads:
            b.r.append(st)
        for b in writes:
            b.w = st
            b.r = []

    def op(self, eng, fn, reads=(), writes=(), rg=None):
        if eng.name == "pe":
            for b in writes:
                prev = getattr(b, "rg", None)
                if rg is not None and prev is not None and prev != rg and b.w is not None and b.w[0] == 'e' and b.w[1] is eng:
                    _, pe_, ep, cnt = b.w
                    key = (pe_.name, ep)
                    if eng.known.get(key, 0) < cnt:
                        eng.e.wait_ge(pe_.sems[ep], cnt)
                        eng.known[key] = cnt
                b.rg = rg
        if eng.name != "pe":
            px = [b for b in reads if b.psum]
            if px:
                reads = [b for b in reads if not b.psum]
                writes = list(writes) + [b for b in px if b not in writes]
        self._deps(eng, reads, writes)
        if eng.count >= SEM_EPOCH:
            eng._new_epoch()
        ins = fn()
        eng.count += 1
        ins.then_inc(eng.sems[eng.epoch], 1)
        self._record(('e', eng, eng.epoch, eng.count), reads, writes)
        return ins

    def dma(self, eng, out, in_, sb, reads=(), writes=(), is_output=False, fn=None):
        self._deps(eng, reads, writes)
        b = sb
        if b.dsem is None:
            b.dsem = self.new_sem("d_" + b.name)
        ins = eng.e.dma_start(out=out, in_=in_) if fn is None else fn()
        ins.then_inc(b.dsem, 16)
        b.dcount += 16
        self._record(('d', b.dsem, b.dcount, ("dma", b.uid)), reads, writes)
        if is_output and b not in self.out_bufs:
            self.out_bufs.append(b)
        return ins

    def finish(self):
        for b in self.out_bufs:
            self.sp.e.wait_ge(b.dsem, b.dcount)

    def barrier_all(self):
        for a in self.engs.values():
            for o in self.engs.values():
                if o is a or o.count == 0:
                    continue
                key = (o.name, o.epoch)
                if a.known.get(key, 0) >= o.count:
                    continue
                a.e.wait_ge(o.sems[o.epoch], o.count)
                a.known[key] = o.count

    def close(self):
        self.release(0)
        while self._semstack:
            self._semstack.pop().__exit__(None, None, None)


class Ring:
    def __init__(self, items):
        self.items = items; self.i = 0

    def next(self):
        it = self.items[self.i % len(self.items)]
        self.i += 1
        return it


def build(stage=99, small=False, dbg=None, ng=8, parts=("rwkv", "attn", "epi", "samp"), nb=None, rl=9, cut=99):
    nc = bass.Bass("TRN2", target_bir_lowering=False)
    V, S, G, T = nc.vector, nc.scalar, nc.gpsimd, nc.tensor

    def din(name, shape, dt=F32):
        return nc.dram_tensor(name, list(shape), dt, kind="ExternalInput").ap()

    def dout(name, shape, dt=F32):
        return nc.dram_tensor(name, list(shape), dt, kind="ExternalOutput").ap()

    xin = din("xin", [NCOL, D])
    w_in = din("w_in", [D, 5376]); w_pa = din("w_pa", [512, D]); w_pb = din("w_pb", [512, D])
    w_out = din("w_out", [D, D]); w_gate = din("w_gate", [D, DFF]); w_up = din("w_up", [D, DFF])
    w_down = din("w_down", [DFF, D])
    wa2_d = din("wa2", [128, 512]); g2_d = din("g2", [128, 512])
    pvec_d = din("pvec", [128, 36]); prow_d = din("prow", [1, 3456])
    kb_d = din("kb", [4, 4, 4096]); qb_d = din("qb", [4, 4, 2048])
    sshift_d = din("sshift", [128, 14, 4]); swkv_d = din("swkv", [4, 4, 128, 64])
    NPG = 128 if small else 2560 * 128
    ck_d = din("ck", [NPG, 512]); cv_d = din("cv", [NPG, 512])
    ptab_d = din("ptab", [1, 256], I32)
    sbias_d = din("sbias", [128, 65 * 8])
    ident_d = din("ident", [128, 128]); mu4_d = din("mu4", [128, 512]); ml4_d = din("ml4", [128, 512])
    mui4_d = din("mui4", [128, 512]); tri_d = din("tri", [128, 128]); colmask_d = din("colmask", [128, 256])
    resetm_d = din("resetm", [128, 512]); cm0_d = din("cm0", [128, 512]); cm1_d = din("cm1", [128, 512])
    bones_d = din("bones", [128, 128]); sel_d = din("sel", [128, 512]); sel0_d = din("sel0", [128, 256])
    iota_d = din("iotap", [128, 1], I32); hmask_d = din("hmask", [8, 4]); e0_d = din("e0", [8, 4]); e1_d = din("e1", [8, 4])

    y_out = dout("y_out", [NLOC, D]); k_out = dout("k_out", [512, NLOC]); v_out = dout("v_out", [NLOC, 512])
    wkv_out = dout("wkv_out", [5, 4, 128, 64]); p_out = dout("p_out", [14, 128, 384])

    f = FW(nc)
    pe, act, dve, pool, sp = f.pe, f.act, f.dve, f.pool, f.sp

    def cload(name, src, shape, dt=F32, q=None):
        t, b = f.sbuf(name, shape, dt)
        if dt == F32 or dt == I32:
            f.dma(q or sp, t[:], src, b, writes=[b])
        else:
            f.dma(pool, t[:], src, b, writes=[b])
        return t, b

    ident_f, b_identf = cload("ident_f", ident_d[:, :], [128, 128])
    ident_b, b_identb = cload("ident_b", ident_d[:, :], [128, 128], BF16)
    tri_b, b_tri = cload("tri_b", tri_d[:, :], [128, 128], BF16)
    bones, b_bones = cload("bones", bones_d[:, :], [128, 128])
    pvec, b_pvec = cload("pvec", pvec_d[:, :], [128, 36])
    prow, b_prow = f.sbuf("prow", [128, 3456], F32)
    f.dma(sp, prow[:], prow_d[0:1, :].partition_broadcast(128), b_prow, writes=[b_prow])
    pv2, b_pv2 = f.sbuf("pv2", [128, 24], F32)
    f.op(dve, lambda: V.tensor_scalar(out=pv2[:, 0:14], in0=pvec[:, 0:14], scalar1=-1.0, scalar2=1.0,
                                      op0=ALU.mult, op1=ALU.add), reads=[b_pvec], writes=[b_pv2])
    f.op(dve, lambda: V.tensor_scalar(out=pv2[:, 14:18], in0=pvec[:, 14:18], scalar1=-1.0, scalar2=None,
                                      op0=ALU.mult), reads=[b_pvec], writes=[b_pv2])
    f.op(dve, lambda: V.tensor_scalar(out=pv2[:, 18:22], in0=pvec[:, 26:30], scalar1=-1.0, scalar2=1.0,
                                      op0=ALU.mult, op1=ALU.add), reads=[b_pvec], writes=[b_pv2])
    f.op(dve, lambda: V.tensor_scalar(out=pv2[:, 22:23], in0=pvec[:, 34:35], scalar1=0.125, scalar2=None,
                                      op0=ALU.mult), reads=[b_pvec], writes=[b_pv2])
    MU_C, W0_C, A0_C, KK_C, KA_C, RK_C, QG_C, KG_C = 0, 14, 18, 22, 26, 30, 34, 35
    GMIX, GFFN, GAO, GNW, GNB, LAM = 0, 1024, 2048, 2176, 2688, 3200

    lt, b_lt = f.sbuf("lt", [128, 8], F32)
    junk64, b_junk64 = f.sbuf("junk64", [128, 64], F32)
    f.op(dve, lambda: V.memset(lt[:], 0.0), writes=[b_lt])
    f.op(dve, lambda: V.tensor_tensor(out=junk64[:], in0=prow[:, LAM:LAM + 64], in1=prow[:, LAM + 64:LAM + 128],
                                      op=ALU.mult), reads=[b_prow], writes=[b_junk64])
    f.op(dve, lambda: V.reduce_sum(out=lt[:, 0:1], in_=junk64[:], axis=AX.X), reads=[b_junk64], writes=[b_lt])
    f.op(dve, lambda: V.tensor_tensor(out=junk64[:], in0=prow[:, LAM + 128:LAM + 192], in1=prow[:, LAM + 192:LAM + 256],
                                      op=ALU.mult), reads=[b_prow, b_lt], writes=[b_junk64])
    f.op(dve, lambda: V.reduce_sum(out=lt[:, 1:2], in_=junk64[:], axis=AX.X), reads=[b_junk64], writes=[b_lt])
    f.op(act, lambda: S.activation(out=lt[:, 2:4], in_=lt[:, 0:2], func=AF.Exp), reads=[b_lt], writes=[b_lt])
    lam_init = 0.8 - 0.6 * math.exp(-0.3 * 0)
    f.op(dve, lambda: V.tensor_tensor(out=lt[:, 4:5], in0=lt[:, 3:4], in1=lt[:, 2:3], op=ALU.subtract),
         reads=[b_lt], writes=[b_lt])
    f.op(dve, lambda: V.tensor_scalar(out=lt[:, 5:6], in0=lt[:, 4:5], scalar1=-lam_init, scalar2=None, op0=ALU.add),
         reads=[b_lt], writes=[b_lt])
    NLAM = lt[:, 5:6]

    hT_loc, b_hTloc = f.sbuf("hT_loc", [128, 8, NLOC], BF16)
    yrT, b_yrT = f.sbuf("yrT", [128, 4, NLOC], BF16)

    m_pre = f.mark()
    hT_pre, b_hTpre = f.sbuf("hT_pre", [128, 8, NPRE], BF16)

    def hT_cols(c0, n):
        if c0 < NPRE:
            return hT_pre, b_hTpre, c0
        return hT_loc, b_hTloc, c0 - NPRE

    m1 = f.mark()
    xr = Ring([f.sbuf("x%d" % i, [128, D], F32) for i in range(3)])
    xbr = Ring([f.sbuf("xb%d" % i, [128, D], BF16) for i in range(2)])
    junk, b_junk = f.sbuf("junk", [128, D], F32)
    ssr = Ring([f.sbuf("ss%d" % i, [128, 2], F32) for i in range(3)])
    ptr = Ring([f.psum("pt%d" % i, [128, 8, 128], BF16) for i in range(2)])
    for t in range(NT if dbg is None else 0):
        xt, bx = xr.next(); xb, bxb = xbr.next(); ss, bss = ssr.next(); pt, bpt = ptr.next()
        f.dma(sp, xt[:], xin[t * 128:(t + 1) * 128, :], bx, writes=[bx])
        f.op(dve, lambda: V.memset(ss[:], 0.0), writes=[bss])
        f.op(act, lambda: S.activation(out=junk[:], in_=xt[:], func=AF.Square, accum_out=ss[:, 0:1]),
             reads=[bx, bss], writes=[b_junk, bss])
        f.op(dve, lambda: V.tensor_scalar(out=ss[:, 1:2], in0=ss[:, 0:1], scalar1=1.0 / D, scalar2=1e-6,
                                          op0=ALU.mult, op1=ALU.add), reads=[bss], writes=[bss])
        f.op(act, lambda: S.activation(out=ss[:, 1:2], in_=ss[:, 1:2], func=AF.Sqrt), reads=[bss], writes=[bss])
        f.op(dve, lambda: V.reciprocal(out=ss[:, 1:2], in_=ss[:, 1:2]), reads=[bss], writes=[bss])
        f.op(dve, lambda: V.scalar_tensor_tensor(out=xb[:], in0=xt[:], scalar=ss[:, 1:2], in1=prow[:, GMIX:GMIX + D],
                                                 op0=ALU.mult, op1=ALU.mult), reads=[bx, bss, b_prow], writes=[bxb])
        for c in range(8):
            f.op(pe, lambda c=c: T.transpose(out=pt[:, c, :], in_=xb[:, c * 128:(c + 1) * 128], identity=ident_b[:]),
                 reads=[bxb, b_identb], writes=[bpt])
        ht, bht, lc = hT_cols(t * 128, 128)
        f.op(act, lambda: S.copy(out=ht[:, :, lc:lc + 128], in_=pt[:]), reads=[bpt], writes=[bht])
    f.release(m1)
    f.barrier_all()

    def wload(dst, bdst, src):
        f.dma(pool, dst, src, bdst, writes=[bdst])


    def epilogue():
        mE = f.mark()
        wpa, b_wpa = f.sbuf("wpa", [128, 4, D], BF16)
        wpb, b_wpb = f.sbuf("wpb", [128, 4, D], BF16)
        wo, b_wo = f.sbuf("wo", [128, 8, D], BF16)
        wload(wpa[:], b_wpa, w_pa.rearrange("(c p) n -> p c n", p=128))
        wload(wpb[:], b_wpb, w_pb.rearrange("(c p) n -> p c n", p=128))
        wload(wo[:], b_wo, w_out.rearrange("(c p) n -> p c n", p=128))
        SB = 384
        bank = [f.psum("bank%d" % i, [128, 512], F32) for i in range(8)]
        wgr = Ring([f.sbuf("wg%d" % i, [128, 8, 128], BF16) for i in range(4)])
        wdr = Ring([f.sbuf("wd%d" % i, [128, D], BF16) for i in range(2)])
        mT, b_mT = f.sbuf("mT", [128, 8, SB], BF16)
        hfT, b_hfT = f.sbuf("hfT", [128, 8, SB], BF16)
        x1, b_x1 = f.sbuf("x1e", [128, 3, D], F32)
        xr2 = Ring([f.sbuf("xe%d" % i, [128, D], F32) for i in range(1)])
        sga, b_sga = f.sbuf("sga", [128, SB], F32)
        sgb, b_sgb = f.sbuf("sgb", [128, SB], F32)
        tA, b_tA = f.sbuf("tA", [128, SB], F32)
        hfb, b_hfb = f.sbuf("hfb", [128, D], BF16)
        ejunk, b_ejunk = f.sbuf("ejunk", [128, D], F32)
        est, b_est = f.sbuf("est", [128, 4], F32)
        actr = Ring([f.sbuf("act%d" % i, [128, SB], BF16) for i in range(2)])
        yor = Ring([f.sbuf("yo%d" % i, [128, 512], F32) for i in range(2)])
        for sbi in range(NLOC // SB):
            c0 = sbi * SB
            for ch in range(8):
                cs = slice(ch * 128, (ch + 1) * 128)
                (pa, bpa), (pb_, bpb), (pga, bpga), (pgb, bpgb) = bank[0], bank[1], bank[2], bank[3]
                wga, bwga = wgr.next(); wgb, bwgb = wgr.next()
                wload(wga[:], bwga, w_in[:, 3328 + ch * 128:3328 + (ch + 1) * 128].rearrange("(c p) n -> p c n", p=128))
                wload(wgb[:], bwgb, w_in[:, 4352 + ch * 128:4352 + (ch + 1) * 128].rearrange("(c p) n -> p c n", p=128))
                for h in range(4):
                    f.op(pe, lambda h=h: T.matmul(out=pa[:, 0:SB], lhsT=wpa[:, h, cs], rhs=oT[:, h, c0:c0 + SB], start=(h == 0), stop=(h == 3)),
                         reads=[b_wpa, b_oT], writes=[bpa])
                for h in range(4):
                    f.op(pe, lambda h=h: T.matmul(out=pb_[:, 0:SB], lhsT=wpb[:, h, cs], rhs=yrT[:, h, c0:c0 + SB], start=(h == 0), stop=(h == 3)),
                         reads=[b_wpb, b_yrT], writes=[bpb])
                for c in range(8):
                    f.op(pe, lambda c=c: T.matmul(out=pga[:, 0:SB], lhsT=wga[:, c, :], rhs=hT_loc[:, c, c0:c0 + SB], start=(c == 0), stop=(c == 7)),
                         reads=[bwga, b_hTloc], writes=[bpga])
                for c in range(8):
                    f.op(pe, lambda c=c: T.matmul(out=pgb[:, 0:SB], lhsT=wgb[:, c, :], rhs=hT_loc[:, c, c0:c0 + SB], start=(c == 0), stop=(c == 7)),
                         reads=[bwgb, b_hTloc], writes=[bpgb])
                f.op(act, lambda: S.activation(out=sga[:], in_=pga[:, 0:SB], func=AF.Sigmoid), reads=[bpga], writes=[b_sga])
                f.op(act, lambda: S.activation(out=sgb[:], in_=pgb[:, 0:SB], func=AF.Sigmoid), reads=[bpgb], writes=[b_sgb])
                f.op(dve, lambda: V.tensor_tensor(out=tA[:], in0=pa[:, 0:SB], in1=sga[:], op=ALU.mult), reads=[bpa, b_sga], writes=[b_tA])
                f.op(dve, lambda: V.tensor_tensor(out=sgb[:], in0=pb_[:, 0:SB], in1=sgb[:], op=ALU.mult), reads=[bpb, b_sgb], writes=[b_sgb])
                f.op(pool, lambda: G.tensor_tensor(out=mT[:, ch, :], in0=tA[:], in1=sgb[:], op=ALU.add), reads=[b_tA, b_sgb], writes=[b_mT])
            for tl in range(3):
                ts_ = slice(tl * 128, (tl + 1) * 128)
                xt, bxt = xr2.next()
                row0 = NPRE + c0 + tl * 128
                f.dma(sp, xt[:], xin[row0:row0 + 128, :], bxt, writes=[bxt])
                for hf_ in range(2):
                    px, bpx = bank[4 + hf_]
                    for k in range(8):
                        f.op(pe, lambda k=k: T.matmul(out=px[:], lhsT=mT[:, k, ts_], rhs=wo[:, k, hf_ * 512:(hf_ + 1) * 512],
                                                      start=(k == 0), stop=(k == 7)), reads=[b_mT, b_wo], writes=[bpx])
                    f.op(dve, lambda: V.tensor_tensor(out=x1[:, tl, hf_ * 512:(hf_ + 1) * 512], in0=px[:], in1=xt[:, hf_ * 512:(hf_ + 1) * 512],
                                                      op=ALU.add), reads=[bpx, bxt], writes=[b_x1])
                f.op(dve, lambda: V.memset(est[:, 0:1], 0.0), writes=[b_est])
                f.op(act, lambda: S.activation(out=ejunk[:], in_=x1[:, tl, :], func=AF.Square, accum_out=est[:, 0:1]),
                     reads=[b_x1, b_est], writes=[b_ejunk, b_est])
                f.op(dve, lambda: V.tensor_scalar(out=est[:, 1:2], in0=est[:, 0:1], scalar1=1.0 / D, scalar2=1e-6, op0=ALU.mult, op1=ALU.add),
                     reads=[b_est], writes=[b_est])
                f.op(act, lambda: S.activation(out=est[:, 1:2], in_=est[:, 1:2], func=AF.Sqrt), reads=[b_est], writes=[b_est])
                f.op(dve, lambda: V.reciprocal(out=est[:, 1:2], in_=est[:, 1:2]), reads=[b_est], writes=[b_est])
                f.op(dve, lambda: V.scalar_tensor_tensor(out=hfb[:], in0=x1[:, tl, :], scalar=est[:, 1:2], in1=prow[:, GFFN:GFFN + D],
                                                         op0=ALU.mult, op1=ALU.mult), reads=[b_x1, b_est, b_prow], writes=[b_hfb])
                ptr_, bptr = bank[6]
                ptb = ptr_[:].bitcast(BF16)
                for c in range(8):
                    f.op(pe, lambda c=c: T.transpose(out=ptb[:, c * 128:(c + 1) * 128], in_=hfb[:, c * 128:(c + 1) * 128], identity=ident_b[:]),
                         reads=[b_hfb, b_identb], writes=[bptr])
                f.op(act, lambda: S.copy(out=hfT[:, :, ts_], in_=ptb.rearrange("p (c n) -> p c n", c=8)), reads=[bptr], writes=[b_hfT])
            for ffc in range(DFF // 128):
                fs = slice(ffc * 128, (ffc + 1) * 128)
                wg, bwg = wgr.next(); wu, bwu = wgr.next(); wd, bwd = wdr.next()
                wload(wg[:], bwg, w_gate[:, fs].rearrange("(c p) n -> p c n", p=128))
                wload(wu[:], bwu, w_up[:, fs].rearrange("(c p) n -> p c n", p=128))
                wload(wd[:], bwd, w_down[fs, :])
                (pg, bpg), (pu, bpu) = bank[6], bank[7]
                for c in range(8):
                    f.op(pe, lambda c=c: T.matmul(out=pg[:, 0:SB], lhsT=wg[:, c, :], rhs=hfT[:, c, :], start=(c == 0), stop=(c == 7)),
                         reads=[bwg, b_hfT], writes=[bpg])
                for c in range(8):
                    f.op(pe, lambda c=c: T.matmul(out=pu[:, 0:SB], lhsT=wu[:, c, :], rhs=hfT[:, c, :], start=(c == 0), stop=(c == 7)),
                         reads=[bwu, b_hfT], writes=[bpu])
                f.op(act, lambda: S.activation(out=sga[:], in_=pg[:, 0:SB], func=AF.Silu), reads=[bpg], writes=[b_sga])
                at, bat = actr.next()
                f.op(dve, lambda: V.tensor_tensor(out=at[:], in0=pu[:, 0:SB], in1=sga[:], op=ALU.mult), reads=[bpu, b_sga], writes=[bat])
                for tl in range(3):
                    for hf_ in range(2):
                        pd, bpd = bank[tl * 2 + hf_]
                        f.op(pe, lambda: T.matmul(out=pd[:], lhsT=at[:, tl * 128:(tl + 1) * 128], rhs=wd[:, hf_ * 512:(hf_ + 1) * 512],
                                                  start=(ffc == 0), stop=(ffc == DFF // 128 - 1)), reads=[bat, bwd], writes=[bpd])
            for tl in range(3):
                for hf_ in range(2):
                    pd, bpd = bank[tl * 2 + hf_]
                    yo, byo = yor.next()
                    f.op(dve, lambda: V.tensor_tensor(out=yo[:], in0=pd[:], in1=x1[:, tl, hf_ * 512:(hf_ + 1) * 512], op=ALU.add),
                         reads=[bpd, b_x1], writes=[byo])
                    r0 = c0 + tl * 128
                    f.dma(sp, y_out[r0:r0 + 128, hf_ * 512:(hf_ + 1) * 512], yo[:], byo, reads=[byo], is_output=True)
        f.barrier_all()
        f.release(mE)

    def rwkv_phase():
        mR = f.mark()
        BW = 256
        NB = NCOL // BW
        mu4b, b_mu4 = cload("mu4b", mu4_d[:, :], [128, 512], BF16)
        ml4b, b_ml4 = cload("ml4b", ml4_d[:, :], [128, 512], BF16)
        mui4b, b_mui4 = cload("mui4b", mui4_d[:, :], [128, 512], BF16)
        ident4, b_ident4 = f.sbuf("ident4", [128, 512], BF16)
        for i in range(4):
            f.dma(pool, ident4[:, i * 128:(i + 1) * 128], ident_d[:, :], b_ident4, writes=[b_ident4])
        resetm, b_resetm = cload("resetm", resetm_d[:, 0:BW], [128, BW])
        cm0, b_cm0 = cload("cm0", cm0_d[:, 0:BW], [128, BW], BF16)
        cm1, b_cm1 = cload("cm1", cm1_d[:, 0:BW], [128, BW], BF16)
        colmask, b_colmask = cload("colmask", colmask_d[:, :], [128, 256])
        wa2b, b_wa2 = cload("wa2b", wa2_d[:, :], [128, 512], BF16)
        g2b, b_g2 = cload("g2b", g2_d[:, :], [128, 512], BF16)
        sshift, b_sshift = cload("sshift", sshift_d[:, :, :], [128, 14, 4])
        cst, b_cst = f.sbuf("cst", [128, 4], F32)
        f.op(dve, lambda: V.memset(cst[:, 0:1], 1.0), writes=[b_cst])
        f.op(dve, lambda: V.memset(cst[:, 1:2], -0.5), writes=[b_cst])
        f.op(dve, lambda: V.memset(cst[:, 2:3], 64e-5), writes=[b_cst])
        carry, b_carry = f.sbuf("carry", [128, 14], F32)
        f.op(dve, lambda: V.memset(carry[:], 0.0), writes=[b_carry])
        H, b_H = f.sbuf("H", [128, 4, 64], F32)
        f.op(dve, lambda: V.memset(H[:], 0.0), writes=[b_H])
        Hb = [f.sbuf("Hb%d" % i, [128, 4, 64], BF16) for i in range(2)]
        wrr, _bw = f.sbuf("wrr", [128, 8, 1792], BF16)
        b_wrc = [Buf("wrr%d" % i) for i in range(14)]
        for i in range(14):
            wload(wrr[:, :, i * 128:(i + 1) * 128], b_wrc[i], w_in[:, 1536 + i * 128:1536 + (i + 1) * 128].rearrange("(c p) n -> p c n", p=128))
        ppr = Ring([f.psum("pp%d" % i, [128, 512], F32) for i in range(3)])
        pxr = Ring([f.sbuf("px%d" % i, [128, BW + 1], F32) for i in range(2)])
        rT, b_rT = f.sbuf("rT", [128, 4, BW], F32)
        kT, b_kT = f.sbuf("kT", [128, 4, BW], F32)
        vT, b_vT = f.sbuf("vT", [128, 4, BW], F32)
        m12, b_m12 = f.sbuf("m12", [128, BW], F32)
        m13, b_m13 = f.sbuf("m13", [128, BW], F32)
        twxa, b_twxa = f.sbuf("twxa", [128, BW], BF16)
        sg, b_sg = f.sbuf("sg", [128, BW], BF16)
        scr = [f.sbuf("scr%d" % i, [128, BW], F32) for i in range(8)]
        AT, b_AT = f.sbuf("AT", [128, 4, BW], BF16)
        BT, b_BT = f.sbuf("BT", [128, 4, BW], BF16)
        KT, b_KT2 = f.sbuf("KT", [128, 4, BW], BF16)
        RT, b_RT = f.sbuf("RT", [128, 4, BW], BF16)
        RT0, b_RT0 = f.sbuf("RT0", [128, 4, BW], BF16)
        RT1, b_RT1 = f.sbuf("RT1", [128, 4, BW], BF16)
        Eneg, b_Eneg = f.sbuf("Eneg", [128, 4, BW], F32)
        bonT, b_bonT = f.sbuf("bonT", [128, 4, BW], BF16)
        gT, b_gT = f.sbuf("gT", [128, 4, BW], BF16)
        Vtm, b_Vtm = f.sbuf("Vtm", [128, 2, 512], BF16)
        Btm, b_Btm = f.sbuf("Btm", [128, 2, 512], BF16)
        Ktm, b_Ktm = f.sbuf("Ktm", [128, 2, 512], BF16)
        ptBK, b_ptBK = f.psum("ptBK", [128, 2, 512], BF16)
        ptV, b_ptV = f.psum("ptV", [128, 512], F32)
        psA, b_psA = f.psum("psA", [128, 4, 128], F32)
        psB, b_psB = f.psum("psB", [128, 4, 128], F32)
        psC, b_psC = f.psum("psC", [128, 4, 128], F32)
        Lk = [f.sbuf("Lk%d" % i, [128, 4, 128], BF16) for i in range(2)]
        Nk = [f.sbuf("Nk%d" % i, [128, 4, 128], BF16) for i in range(2)]
        Yk = [f.sbuf("Yk%d" % i, [128, 4, 128], BF16) for i in range(2)]
        AKT, b_AKT = f.sbuf("AKT", [128, 4, 128], BF16)
        RBT, b_RBT = f.sbuf("RBT", [128, 4, 128], BF16)
        RKT, b_RKT = f.sbuf("RKT", [128, 4, 128], BF16)
        P1, b_P1 = f.sbuf("P1", [128, 4, 64], F32)
        Wb, b_Wb = f.sbuf("Wb", [128, 4, 64], BF16)
        Ub, b_Ub = f.sbuf("Ub", [128, 4, 64], BF16)
        Yall, b_Yall = f.sbuf("Yall", [128, 8, 64], F32)
        gst, b_gst = f.sbuf("gst", [128, 40], F32)
        ytmp2, b_ytmp = f.sbuf("ytmp", [128, 512], F32)
        ytmp = ytmp2[:].rearrange("p (a b) -> p a b", a=4)
        Ysq = ytmp2[:].rearrange("p (a b) -> p a b", a=8)
        b_Ysq = b_ytmp

        def flat(t3):
            return t3[:].rearrange("p a b -> p (a b)")

        for bi in (range(NB) if nb is None else nb):
            col0 = bi * BW
            is_loc = col0 >= NPRE
            is_smp = col0 >= NPRE + NOWN
            ht, bht, lc = hT_cols(col0, BW)
            for ch in range(14):
                if bi < 7 and (ch < 4 or ch == 13):
                    continue
                bwt = b_wrc[ch]
                pp, bpp = ppr.next()
                for c in range(8):
                    f.op(pe, lambda c=c: T.matmul(out=pp[:, 0:BW], lhsT=wrr[:, c, ch * 128:(ch + 1) * 128], rhs=ht[:, c, lc:lc + BW],
                                                  start=(c == 0), stop=(c == 7)), reads=[bwt, bht], writes=[bpp])
                px, bpx = pxr.next()
                f.op(dve, lambda: V.tensor_copy(out=px[:, 0:1], in_=carry[:, ch:ch + 1]), reads=[b_carry], writes=[bpx])
                f.op(act, lambda: S.copy(out=px[:, 1:BW + 1], in_=pp[:, 0:BW]), reads=[bpp], writes=[bpx])
                f.op(dve, lambda: V.tensor_copy(out=carry[:, ch:ch + 1], in_=px[:, BW:BW + 1]), reads=[bpx], writes=[b_carry])
                if bi == 15:
                    f.dma(sp, p_out[ch, :, 0:128], px[:, 129:257], bpx, reads=[bpx], is_output=True)
                if is_smp:
                    f.dma(sp, p_out[ch, :, 128:384], px[:, 1:257], bpx, reads=[bpx], is_output=True)
                    f.op(dve, lambda: V.tensor_copy(out=px[:, 1:BW + 1:64], in_=sshift[:, ch, :]),
                         reads=[b_sshift], writes=[bpx])
                tmp, btmp = scr[0]
                if ch < 4:
                    dst, bdst = rT[:, ch, :], b_rT
                elif ch < 8:
                    dst, bdst = kT[:, ch - 4, :], b_kT
                elif ch < 12:
                    dst, bdst = vT[:, ch - 8, :], b_vT
                elif ch == 12:
                    dst, bdst = m12[:], b_m12
                else:
                    dst, bdst = m13[:], b_m13
                f.op(pool, lambda: G.tensor_scalar(out=tmp[:], in0=px[:, 0:BW], scalar1=pvec[:, MU_C + ch:MU_C + ch + 1], scalar2=None,
                                                   op0=ALU.mult), reads=[bpx, b_pvec], writes=[btmp])
                f.op(dve, lambda: V.scalar_tensor_tensor(out=dst, in0=px[:, 1:BW + 1], scalar=pv2[:, ch:ch + 1], in1=tmp[:],
                                                         op0=ALU.mult, op1=ALU.add), reads=[bpx, btmp, b_pv2], writes=[bdst])
                if is_smp:
                    f.op(dve, lambda: V.tensor_tensor(out=dst, in0=dst, in1=colmask[:], op=ALU.mult),
                         reads=[bdst, b_colmask], writes=[bdst])
            if rl < 2:
                continue
            f.op(act, lambda: S.activation(out=twxa[0:64, :], in_=m12[0:64, :], func=AF.Tanh), reads=[b_m12], writes=[b_twxa])
            f.op(dve, lambda: V.tensor_copy(out=twxa[64:128, :], in_=m12[64:128, :]), reads=[b_m12], writes=[b_twxa])
            f.op(act, lambda: S.activation(out=sg[:], in_=m13[:], func=AF.Sigmoid), reads=[b_m13], writes=[b_sg])
            for pr in range(4):
                pc = slice(pr * 128, (pr + 1) * 128)
                (s1, bs1), (s2, bs2), (s3, bs3), (s4, bs4), (s5, bs5), (s6, bs6), (s7, bs7) = scr[1:8]
                pp, bpp = ppr.next()
                f.op(pe, lambda: T.matmul(out=pp[:, 0:BW], lhsT=wa2b[0:64, pc], rhs=twxa[0:64, :], start=True, stop=True),
                     reads=[b_wa2, b_twxa], writes=[bpp])
                f.op(act, lambda: S.activation(out=s1[:], in_=pp[:, 0:BW], func=AF.Exp, bias=pv2[:, 14 + pr:15 + pr], scale=-1.0),
                     reads=[bpp, b_pv2], writes=[bs1])
                f.op(act, lambda: S.activation(out=s1[:], in_=s1[:], func=AF.Ln, bias=cst[:, 0:1], scale=1.0),
                     reads=[bs1, b_cst], writes=[bs1])
                f.op(act, lambda: S.activation(out=s2[:], in_=s1[:], func=AF.Exp, bias=cst[:, 1:2], scale=-1.0),
                     reads=[bs1, b_cst], writes=[bs2])
                if is_smp:
                    f.op(dve, lambda: V.tensor_tensor(out=s2[:], in0=s2[:], in1=colmask[:], op=ALU.mult),
                         reads=[bs2, b_colmask], writes=[bs2])
                f.op(dve, lambda: V.tensor_tensor_scan(out=s3[:], data0=resetm[:], data1=s2[:], initial=0.0,
                                                       op0=ALU.mult, op1=ALU.add), reads=[bs2, b_resetm], writes=[bs3])
                f.op(act, lambda: S.activation(out=Eneg[:, pr, :], in_=s3[:], func=AF.Exp, scale=-1.0), reads=[bs3], writes=[b_Eneg])
                f.op(act, lambda: S.activation(out=s4[:], in_=s3[:], func=AF.Exp), reads=[bs3], writes=[bs4])
                f.op(dve, lambda: V.tensor_tensor(out=s5[:], in0=s3[:], in1=s2[:], op=ALU.subtract), reads=[bs3, bs2], writes=[bs5])
                f.op(act, lambda: S.activation(out=s5[:], in_=s5[:], func=AF.Exp, scale=-1.0), reads=[bs5], writes=[bs5])
                pp, bpp = ppr.next()
                f.op(pe, lambda: T.matmul(out=pp[:, 0:BW], lhsT=wa2b[64:128, pc], rhs=twxa[64:128, :], start=True, stop=True),
                     reads=[b_wa2, b_twxa], writes=[bpp])
                f.op(act, lambda: S.activation(out=s6[:], in_=pp[:, 0:BW], func=AF.Sigmoid, bias=pvec[:, A0_C + pr:A0_C + pr + 1], scale=1.0),
                     reads=[bpp, b_pvec], writes=[bs6])
                f.op(dve, lambda: V.tensor_scalar(out=s1[:], in0=kT[:, pr, :], scalar1=pvec[:, KK_C + pr:KK_C + pr + 1], scalar2=None,
                                                  op0=ALU.mult), reads=[b_kT, b_pvec], writes=[bs1])
                f.op(act, lambda: S.activation(out=s7[:], in_=s1[:], func=AF.Square), reads=[bs1], writes=[bs7])
                pp, bpp = ppr.next()
                f.op(pe, lambda: T.matmul(out=pp[:, 0:BW], lhsT=bones[:], rhs=s7[:], start=True, stop=True),
                     reads=[b_bones, bs7], writes=[bpp])
                f.op(dve, lambda: V.tensor_scalar(out=s7[:], in0=pp[:, 0:BW], scalar1=1e-24, scalar2=None, op0=ALU.max),
                     reads=[bpp], writes=[bs7])
                f.op(act, lambda: S.activation(out=s7[:], in_=s7[:], func=AF.Sqrt), reads=[bs7], writes=[bs7])
                f.op(dve, lambda: V.reciprocal(out=s7[:], in_=s7[:]), reads=[bs7], writes=[bs7])
                f.op(dve, lambda: V.tensor_tensor(out=s1[:], in0=s1[:], in1=s7[:], op=ALU.mult), reads=[bs1, bs7], writes=[bs1])
                f.op(dve, lambda: V.tensor_scalar(out=s7[:], in0=s6[:], scalar1=pvec[:, KA_C + pr:KA_C + pr + 1],
                                                  scalar2=pv2[:, 18 + pr:19 + pr], op0=ALU.mult, op1=ALU.add),
                     reads=[bs6, b_pvec, b_pv2], writes=[bs7])
                f.op(dve, lambda: V.tensor_tensor(out=s7[:], in0=s7[:], in1=kT[:, pr, :], op=ALU.mult), reads=[bs7, b_kT], writes=[bs7])
                f.op(dve, lambda: V.scalar_tensor_tensor(out=AT[:, pr, :], in0=s1[:], scalar=-1.0, in1=s5[:], op0=ALU.mult, op1=ALU.mult),
                     reads=[bs1, bs5], writes=[b_AT])
                f.op(pool, lambda: G.tensor_tensor(out=s1[:], in0=s1[:], in1=s6[:], op=ALU.mult), reads=[bs1, bs6], writes=[bs1])
                f.op(pool, lambda: G.tensor_tensor(out=BT[:, pr, :], in0=s1[:], in1=s4[:], op=ALU.mult), reads=[bs1, bs4], writes=[b_BT])
                f.op(pool, lambda: G.tensor_tensor(out=KT[:, pr, :], in0=s7[:], in1=s4[:], op=ALU.mult), reads=[bs7, bs4], writes=[b_KT2])
                if is_loc:
                    f.op(dve, lambda: V.tensor_tensor(out=RT[:, pr, :], in0=rT[:, pr, :], in1=Eneg[:, pr, :], op=ALU.mult),
                         reads=[b_rT, b_Eneg], writes=[b_RT])
                    f.op(pool, lambda: G.tensor_tensor(out=RT0[:, pr, :], in0=RT[:, pr, :], in1=cm0[:], op=ALU.mult),
                         reads=[b_RT, b_cm0], writes=[b_RT0])
                    f.op(pool, lambda: G.tensor_tensor(out=RT1[:, pr, :], in0=RT[:, pr, :], in1=cm1[:], op=ALU.mult),
                         reads=[b_RT, b_cm1], writes=[b_RT1])
                if is_loc:
                    f.op(dve, lambda: V.scalar_tensor_tensor(out=s1[:], in0=rT[:, pr, :], scalar=pvec[:, RK_C + pr:RK_C + pr + 1], in1=s7[:],
                                                             op0=ALU.mult, op1=ALU.mult), reads=[b_rT, bs7, b_pvec], writes=[bs1])
                    pp, bpp = ppr.next()
                    f.op(pe, lambda: T.matmul(out=pp[:, 0:BW], lhsT=bones[:], rhs=s1[:], start=True, stop=True),
                         reads=[b_bones, bs1], writes=[bpp])
                    f.op(dve, lambda: V.tensor_tensor(out=bonT[:, pr, :], in0=pp[:, 0:BW], in1=vT[:, pr, :], op=ALU.mult),
                         reads=[bpp, b_vT], writes=[b_bonT])
                    pp, bpp = ppr.next()
                    f.op(pe, lambda: T.matmul(out=pp[:, 0:BW], lhsT=g2b[:, pc], rhs=sg[:], start=True, stop=True),
                         reads=[b_g2, b_sg], writes=[bpp])
                    f.op(act, lambda: S.copy(out=gT[:, pr, :], in_=pp[:, 0:BW]), reads=[bpp], writes=[b_gT])
            for tl in range(2 if rl >= 3 else 0):
                Tg = bi * 2 + tl
                tc = slice(tl * 128, (tl + 1) * 128)
                for pr in range(4):
                    pc = slice(pr * 128, (pr + 1) * 128)
                    f.op(pe, lambda: T.transpose(out=ptBK[:, 0, pc], in_=BT[:, pr, tc], identity=ident_b[:]),
                         reads=[b_BT, b_identb], writes=[b_ptBK])
                    f.op(pe, lambda: T.transpose(out=ptBK[:, 1, pc], in_=KT[:, pr, tc], identity=ident_b[:]),
                         reads=[b_KT2, b_identb], writes=[b_ptBK])
                    f.op(pe, lambda: T.transpose(out=ptV[:, pc], in_=vT[:, pr, tc], identity=ident_f[:]),
                         reads=[b_vT, b_identf], writes=[b_ptV])
                f.op(act, lambda: S.copy(out=Btm[:, tl, :], in_=ptBK[:, 0, :]), reads=[b_ptBK], writes=[b_Btm])
                f.op(dve, lambda: V.tensor_copy(out=Ktm[:, tl, :], in_=ptBK[:, 1, :]), reads=[b_ptBK], writes=[b_Ktm])
                f.op(act, lambda: S.copy(out=Vtm[:, tl, :], in_=ptV[:]), reads=[b_ptV], writes=[b_Vtm])
                for Gi in range(2 if rl >= 4 else 0):
                    heads = [(hi, 4 * Gi + hi, (4 * Gi + hi) // 2, slice(64 * ((4 * Gi + hi) % 2), 64 * ((4 * Gi + hi) % 2) + 64))
                             for hi in range(4)]
                    for hi, h, pr, rows in heads:
                        f.op(pe, lambda: T.matmul(out=psA[:, hi, :], lhsT=BT[rows, pr, tc], rhs=AT[rows, pr, tc], start=True, stop=True),
                             reads=[b_BT, b_AT], writes=[b_psA], rg=rows.start)
                    for hi, h, pr, rows in heads:
                        f.op(pe, lambda: T.matmul(out=psB[:, hi, :], lhsT=AT[rows, pr, tc], rhs=BT[rows, pr, tc], start=True, stop=True),
                             reads=[b_BT, b_AT], writes=[b_psB], rg=rows.start)
                    (N0, bN0), (N1, bN1) = Nk
                    (L0, bL0), (L1, bL1) = Lk
                    (Y0, bY0), (Y1, bY1) = Yk
                    f.op(dve, lambda: V.tensor_tensor(out=flat(N0), in0=flat(psA), in1=mu4b[:], op=ALU.mult),
                         reads=[b_psA, b_mu4], writes=[bN0])
                    f.op(dve, lambda: V.tensor_tensor(out=flat(L0), in0=flat(psB), in1=ml4b[:], op=ALU.mult),
                         reads=[b_psB, b_ml4], writes=[bL0])
                    if cut <= 1:
                        continue
                    f.op(pool, lambda: G.tensor_tensor(out=flat(Y0), in0=flat(N0), in1=ident4[:], op=ALU.add),
                         reads=[bN0, b_ident4], writes=[bY0])
                    if cut <= 2:
                        continue
                    for k in range(min(5, cut - 2)):
                        Nc, bNc = Nk[k % 2]; Lc, bLc = Lk[k % 2]; Yc, bYc = Yk[k % 2]
                        Nn, bNn = Nk[(k + 1) % 2]; Ln, bLn = Lk[(k + 1) % 2]; Yn, bYn = Yk[(k + 1) % 2]
                        for hi, h, pr, rows in heads:
                            f.op(pe, lambda: T.matmul(out=psA[:, hi, :], lhsT=Nc[:, hi, :], rhs=Lc[:, hi, :], start=True, stop=True),
                                 reads=[bNc, bLc], writes=[b_psA])
                        if k < 4:
                            for hi, h, pr, rows in heads:
                                f.op(pe, lambda: T.matmul(out=psB[:, hi, :], lhsT=Lc[:, hi, :], rhs=Nc[:, hi, :], start=True, stop=True),
                                     reads=[bNc, bLc], writes=[b_psB])
                        f.op(act, lambda: S.copy(out=flat(Ln), in_=flat(psA)), reads=[b_psA], writes=[bLn])
                        if k < 4:
                            f.op(dve, lambda: V.tensor_copy(out=flat(Nn), in_=flat(psB)), reads=[b_psB], writes=[bNn])
                        for hi, h, pr, rows in heads:
                            f.op(pe, lambda: T.matmul(out=psC[:, hi, :], lhsT=ident_b[:], rhs=Yc[:, hi, :], start=True, stop=False),
                                 reads=[b_identb, bYc], writes=[b_psC])
                            f.op(pe, lambda: T.matmul(out=psC[:, hi, :], lhsT=Ln[:, hi, :], rhs=Yc[:, hi, :], start=False, stop=True),
                                 reads=[bLn, bYc], writes=[b_psC])
                        if k % 2 == 0:
                            f.op(dve, lambda: V.tensor_copy(out=flat(Yn), in_=flat(psC)), reads=[b_psC], writes=[bYn])
                        else:
                            f.op(act, lambda: S.copy(out=flat(Yn), in_=flat(psC)), reads=[b_psC], writes=[bYn])
                    TT, bTT = Yk[1]
                    if cut <= 7:
                        continue
                    for hi, h, pr, rows in heads:
                        f.op(pe, lambda: T.matmul(out=psA[:, hi, :], lhsT=KT[rows, pr, tc], rhs=AT[rows, pr, tc], start=True, stop=True),
                             reads=[b_KT2, b_AT], writes=[b_psA], rg=rows.start)
                    if is_loc:
                        for hi, h, pr, rows in heads:
                            f.op(pe, lambda: T.matmul(out=psB[:, hi, :], lhsT=BT[rows, pr, tc], rhs=RT[rows, pr, tc], start=True, stop=True),
                                 reads=[b_BT, b_RT], writes=[b_psB], rg=rows.start)
                        for hi, h, pr, rows in heads:
                            f.op(pe, lambda: T.matmul(out=psC[:, hi, :], lhsT=KT[rows, pr, tc], rhs=RT[rows, pr, tc], start=True, stop=True),
                                 reads=[b_KT2, b_RT], writes=[b_psC], rg=rows.start)
                    f.op(dve, lambda: V.tensor_tensor(out=flat(AKT), in0=flat(psA), in1=mu4b[:], op=ALU.mult),
                         reads=[b_psA, b_mu4], writes=[b_AKT])
                    if is_loc:
                        f.op(dve, lambda: V.tensor_tensor(out=flat(RBT), in0=flat(psB), in1=mui4b[:], op=ALU.mult),
                             reads=[b_psB, b_mui4], writes=[b_RBT])
                        f.op(dve, lambda: V.tensor_tensor(out=flat(RKT), in0=flat(psC), in1=mui4b[:], op=ALU.mult),
                             reads=[b_psC, b_mui4], writes=[b_RKT])
                    for hi, h, pr, rows in heads:
                        f.op(pe, lambda: T.matmul(out=psC[:, hi, 0:64], lhsT=AKT[:, hi, :], rhs=Vtm[:, tl, h * 64:(h + 1) * 64],
                                                  start=True, stop=True), reads=[b_AKT, b_Vtm], writes=[b_psC])
                    f.op(act, lambda: S.copy(out=P1[:], in_=psC[:, :, 0:64]), reads=[b_psC], writes=[b_P1])
                    gp = slice(2 * Gi, 2 * Gi + 2)
                    for c in range(2 if rl >= 5 else 0):
                        crow = slice(64 * c, 64 * c + 64)
                        smp = (Tg - 32) * 2 + c
                        if is_smp:
                            f.dma(sp, H[:, gp, :], swkv_d[smp, gp, :, :].rearrange("a p i -> p a i"), b_H, writes=[b_H])
                        hb, bhb = Hb[c]
                        f.op(pool, lambda: G.tensor_copy(out=hb[:, gp, :], in_=H[:, gp, :]), reads=[b_H], writes=[bhb])
                        for hi, h, pr, rows in heads:
                            f.op(pe, lambda: T.matmul(out=psA[:, hi, 0:64], lhsT=AT[rows, pr, tc], rhs=hb[rows, pr, :], start=True, stop=True),
                                 reads=[b_AT, bhb], writes=[b_psA], rg=rows.start)
                        f.op(dve, lambda: V.tensor_tensor(out=Wb[crow, :, :], in0=psA[crow, :, 0:64], in1=P1[crow, :, :], op=ALU.add),
                             reads=[b_psA, b_P1], writes=[b_Wb])
                        for hi, h, pr, rows in heads:
                            f.op(pe, lambda: T.matmul(out=psA[:, hi, 64:128], lhsT=TT[crow, hi, :], rhs=Wb[crow, hi, :], start=True, stop=True),
                                 reads=[bTT, b_Wb], writes=[b_psA], rg=crow.start)
                        f.op(act, lambda: S.copy(out=Ub[crow, :, :], in_=psA[crow, :, 64:128]), reads=[b_psA], writes=[b_Ub])
                        for hi, h, pr, rows in heads:
                            pcs = slice(pr * 128, (pr + 1) * 128)
                            f.op(pe, lambda: T.matmul(out=psB[:, hi, 0:64], lhsT=Btm[crow, tl, pcs], rhs=Ub[crow, hi, :], start=True, stop=False),
                                 reads=[b_Btm, b_Ub], writes=[b_psB], rg=crow.start)
                            f.op(pe, lambda: T.matmul(out=psB[:, hi, 0:64], lhsT=Ktm[crow, tl, pcs], rhs=Vtm[crow, tl, h * 64:(h + 1) * 64],
                                                      start=False, stop=True), reads=[b_Ktm, b_Vtm], writes=[b_psB], rg=crow.start)
                        for hi, h, pr, rows in heads:
                            gcol = Eneg[rows, pr, tl * 128 + 64 * c + 63:tl * 128 + 64 * c + 64]
                            f.op(pool, lambda: G.tensor_scalar(out=H[rows, pr, :], in0=H[rows, pr, :], scalar1=gcol, scalar2=None, op0=ALU.mult),
                                 reads=[b_H, b_Eneg], writes=[b_H])
                            f.op(dve, lambda: V.scalar_tensor_tensor(out=H[rows, pr, :], in0=psB[rows, hi, 0:64], scalar=gcol, in1=H[rows, pr, :],
                                                                     op0=ALU.mult, op1=ALU.add), reads=[b_psB, b_H, b_Eneg], writes=[b_H])
                        if is_smp:
                            f.dma(sp, wkv_out[1 + smp, gp, :, :].rearrange("a p i -> p a i"), H[:, gp, :], b_H, reads=[b_H], is_output=True)
                        elif Tg == 31 and c == 1:
                            f.dma(sp, wkv_out[0, gp, :, :].rearrange("a p i -> p a i"), H[:, gp, :], b_H, reads=[b_H], is_output=True)
                    if is_loc and rl >= 6:
                        (hb0, bhb0), (hb1, bhb1) = Hb
                        for hi, h, pr, rows in heads:
                            f.op(pe, lambda: T.matmul(out=psB[:, hi, 64:128], lhsT=RT0[rows, pr, tc], rhs=hb0[rows, pr, :], start=True, stop=False),
                                 reads=[b_RT0, bhb0], writes=[b_psB], rg=rows.start)
                            f.op(pe, lambda: T.matmul(out=psB[:, hi, 64:128], lhsT=RT1[rows, pr, tc], rhs=hb1[rows, pr, :], start=False, stop=False),
                                 reads=[b_RT1, bhb1], writes=[b_psB], rg=rows.start)
                            f.op(pe, lambda: T.matmul(out=psB[:, hi, 64:128], lhsT=RBT[:, hi, :], rhs=Ub[:, hi, :], start=False, stop=False),
                                 reads=[b_RBT, b_Ub], writes=[b_psB])
                            f.op(pe, lambda: T.matmul(out=psB[:, hi, 64:128], lhsT=RKT[:, hi, :], rhs=Vtm[:, tl, h * 64:(h + 1) * 64],
                                                      start=False, stop=True), reads=[b_RKT, b_Vtm], writes=[b_psB])
                        f.op(act, lambda: S.copy(out=Yall[:, 4 * Gi:4 * Gi + 4, :], in_=psB[:, :, 64:128]), reads=[b_psB], writes=[b_Yall])
                if is_loc and rl >= 7:
                    lcol = Tg * 128 - NPRE
                    f.op(dve, lambda: V.reduce_sum(out=gst[:, 0:8], in_=Yall[:], axis=AX.X), reads=[b_Yall], writes=[b_gst])
                    f.op(act, lambda: S.activation(out=Ysq, in_=Yall[:], func=AF.Square), reads=[b_Yall], writes=[b_Ysq])
                    f.op(dve, lambda: V.reduce_sum(out=gst[:, 8:16], in_=Ysq, axis=AX.X), reads=[b_Ysq, b_gst], writes=[b_gst])
                    f.op(dve, lambda: V.tensor_scalar(out=gst[:, 0:16], in0=gst[:, 0:16], scalar1=1.0 / 64, scalar2=None, op0=ALU.mult),
                         reads=[b_gst], writes=[b_gst])
                    f.op(dve, lambda: V.tensor_tensor(out=gst[:, 16:24], in0=gst[:, 0:8], in1=gst[:, 0:8], op=ALU.mult),
                         reads=[b_gst], writes=[b_gst])
                    f.op(dve, lambda: V.tensor_tensor(out=gst[:, 16:24], in0=gst[:, 8:16], in1=gst[:, 16:24], op=ALU.subtract),
                         reads=[b_gst], writes=[b_gst])
                    f.op(dve, lambda: V.tensor_scalar(out=gst[:, 16:24], in0=gst[:, 16:24], scalar1=64e-5, scalar2=None, op0=ALU.add),
                         reads=[b_gst], writes=[b_gst])
                    f.op(act, lambda: S.activation(out=gst[:, 16:24], in_=gst[:, 16:24], func=AF.Sqrt), reads=[b_gst], writes=[b_gst])
                    f.op(dve, lambda: V.reciprocal(out=gst[:, 24:32], in_=gst[:, 16:24]), reads=[b_gst], writes=[b_gst])
                    f.op(dve, lambda: V.tensor_tensor(out=Yall[:], in0=Yall[:], in1=gst[:, 0:8].unsqueeze(2).to_broadcast([128, 8, 64]),
                                                      op=ALU.subtract), reads=[b_Yall, b_gst], writes=[b_Yall])
                    f.op(dve, lambda: V.tensor_tensor(out=Yall[:], in0=Yall[:], in1=gst[:, 24:32].unsqueeze(2).to_broadcast([128, 8, 64]),
                                                      op=ALU.mult), reads=[b_Yall, b_gst], writes=[b_Yall])
                    yf = Yall[:].rearrange("p a b -> p (a b)")
                    f.op(dve, lambda: V.tensor_tensor(out=yf, in0=yf, in1=prow[:, GNW:GNW + 512], op=ALU.mult),
                         reads=[b_Yall, b_prow], writes=[b_Yall])
                    f.op(pool, lambda: G.tensor_tensor(out=yf, in0=yf, in1=prow[:, GNB:GNB + 512], op=ALU.add),
                         reads=[b_Yall, b_prow], writes=[b_Yall])
                    for pr in range(4):
                        f.op(pe, lambda: T.transpose(out=ptV[:, pr * 128:(pr + 1) * 128], in_=yf[:, pr * 128:(pr + 1) * 128], identity=ident_f[:]),
                             reads=[b_Yall, b_identf], writes=[b_ptV])
                    f.op(dve, lambda: V.tensor_tensor(out=ytmp, in0=ptV[:].rearrange("p (a b) -> p a b", a=4), in1=bonT[:, :, tc], op=ALU.add),
                         reads=[b_ptV, b_bonT], writes=[b_ytmp])
                    f.op(dve, lambda: V.tensor_tensor(out=yrT[:, :, lcol:lcol + 128], in0=ytmp, in1=gT[:, :, tc], op=ALU.mult),
                         reads=[b_ytmp, b_gT], writes=[b_yrT])
        f.barrier_all()
        f.release(mR)

    if "rwkv" in parts:
        rwkv_phase()
    oT, b_oT = f.sbuf("oT", [128, 4, NLOC], BF16)

    if stage >= 1.5 and "attn" in parts:
        m2 = f.mark()
        wv, b_wv = f.sbuf("wv", [128, 8, 512], BF16)
        wload(wv[:], b_wv, w_in[:, 1024:1536].rearrange("(c p) n -> p c n", p=128))
        Vh, b_Vh = f.sbuf("Vh", [128, NT, 129], BF16)
        f.op(pool, lambda: G.memset(Vh[:], 1.0), writes=[b_Vh])
        pvr = Ring([f.psum("pv%d" % i, [128, 512], F32) for i in range(1)])
        vor = Ring([f.sbuf("vo%d" % i, [128, 128], F32) for i in range(2)])

        def v_project(h):
            for t in range(NT):
                pv, bpv = pvr.next()
                ht, bht, lc = hT_cols(t * 128, 128)
                for c in range(8):
                    f.op(pe, lambda c=c: T.matmul(out=pv[:, 0:128], lhsT=ht[:, c, lc:lc + 128], rhs=wv[:, c, h * 128:(h + 1) * 128],
                                                  start=(c == 0), stop=(c == 7)), reads=[bht, b_wv], writes=[bpv])
                f.op(act, lambda: S.copy(out=Vh[:, t, 0:128], in_=pv[:, 0:128]), reads=[bpv], writes=[b_Vh])
                if t >= 16:
                    vo, bvo = vor.next()
                    f.op(dve, lambda: V.tensor_copy(out=vo[:], in_=pv[:, 0:128]), reads=[bpv], writes=[bvo])
                    f.dma(sp, v_out[(t - 16) * 128:(t - 15) * 128, h * 128:(h + 1) * 128], vo[:], bvo, reads=[bvo], is_output=True)

        KT_, b_KT = f.sbuf("KhT", [68, 2, NCOL], BF16)
        QT_, b_QT = f.sbuf("QhT", [68, 2, NLOC], BF16)
        wq, b_wq = f.sbuf("wq", [128, 8, 128], BF16)
        wk, b_wk = f.sbuf("wk", [128, 8, 128], BF16)
        pqr = Ring([f.psum("pq%d" % i, [64, 512], F32) for i in range(1)])
        pqr.items.append((pvr.items[0][0][0:64, :], pvr.items[0][1]))
        psq, b_psq = f.psum("psq", [64, 512], F32)
        sqt, b_sqt = f.sbuf("sqt", [64, 512], F32)
        rnt, b_rnt = f.sbuf("rnt", [64, 512], F32)
        kor = Ring([f.sbuf("ko%d" % i, [64, 512], F32) for i in range(1)])
        pS_r = Ring([f.psum("pS%d" % i, [128, 2, 256], F32) for i in range(2)])
        pO, b_pO = f.psum("pO", [128, 2, 2, 256], F32)
        PTr = Ring([f.sbuf("PT%d" % i, [128, 2, 256], BF16) for i in range(2)])
        osb, b_osb = f.sbuf("osb", [128, 128], F32)
        osb2, b_osb2 = f.sbuf("osb2", [128, 128], F32)
        obf, b_obf = f.sbuf("obf", [128, 128], BF16)
        ost, b_ost = f.sbuf("ost", [128, 8], F32)
        pTo, b_pTo = f.psum("pTo", [128, 1024], BF16)

        def qk_project(wt, bwt, ncols, dstT, bdst, gcol, is_k):
            nb = (ncols + 511) // 512
            for bi in range(nb):
                c0 = bi * 512
                n = min(512, ncols - c0)
                gc0 = c0 if is_k else c0 + NPRE
                ht, bht, lc = hT_cols(gc0, n)
                for cmp_ in range(2):
                    pq, bpq = pqr.next()
                    for c in range(8):
                        f.op(pe, lambda c=c: T.matmul(out=pq[:, 0:n], lhsT=wt[:, c, cmp_ * 64:(cmp_ + 1) * 64],
                                                      rhs=ht[:, c, lc:lc + n], start=(c == 0), stop=(c == 7)),
                             reads=[bht, bwt], writes=[bpq])
                    f.op(act, lambda: S.activation(out=sqt[:, 0:n], in_=pq[:, 0:n], func=AF.Square),
                         reads=[bpq], writes=[b_sqt])
                    f.op(pe, lambda: T.matmul(out=psq[:, 0:n], lhsT=bones[0:64, 0:64], rhs=sqt[:, 0:n], start=True, stop=True),
                         reads=[b_sqt, b_bones], writes=[b_psq])
                    f.op(dve, lambda: V.tensor_scalar(out=rnt[:, 0:n], in0=psq[:, 0:n], scalar1=1.0 / 64, scalar2=1e-6,
                                                      op0=ALU.mult, op1=ALU.add), reads=[b_psq], writes=[b_rnt])
                    f.op(act, lambda: S.activation(out=rnt[:, 0:n], in_=rnt[:, 0:n], func=AF.Sqrt), reads=[b_rnt], writes=[b_rnt])
                    f.op(dve, lambda: V.reciprocal(out=rnt[:, 0:n], in_=rnt[:, 0:n]), reads=[b_rnt], writes=[b_rnt])
                    f.op(dve, lambda: V.scalar_tensor_tensor(out=dstT[0:64, cmp_, c0:c0 + n], in0=pq[:, 0:n], scalar=gcol,
                                                             in1=rnt[:, 0:n], op0=ALU.mult, op1=ALU.mult),
                         reads=[bpq, b_rnt, b_pvec, b_pv2], writes=[bdst])
                    if is_k and gc0 >= NPRE:
                        ko, bko = kor.next()
                        f.op(pool if False else dve, lambda: V.scalar_tensor_tensor(out=ko[:, 0:n], in0=pq[:, 0:n], scalar=gcol,
                                                                 in1=rnt[:, 0:n], op0=ALU.mult, op1=ALU.mult),
                             reads=[bpq, b_rnt, b_pvec], writes=[bko])
                        r0 = cur_h[0] * 128 + cmp_ * 64
                        f.dma(sp, k_out[r0:r0 + 64, gc0 - NPRE:gc0 - NPRE + n], ko[:, 0:n], bko, reads=[bko], is_output=True)

        qs_tm, b_qs = f.sbuf("qs_tm", [128, 2, 512], BF16)
        ks_tm, b_ks = f.sbuf("ks_tm", [128, 2, 512], BF16)
        vs_tm, b_vs = f.sbuf("vs_tm", [128, 2, 512], BF16)
        cur_h = [0]
        for h in range((4 if dbg is None else 1) if stage >= 1.6 else 0):
            cur_h[0] = h
            wload(wq[:], b_wq, w_in[:, h * 128:(h + 1) * 128].rearrange("(c p) n -> p c n", p=128))
            wload(wk[:], b_wk, w_in[:, 512 + h * 128:512 + (h + 1) * 128].rearrange("(c p) n -> p c n", p=128))
            for cmp_ in range(2):
                f.dma(pool, KT_[64:68, cmp_, 0:4096], kb_d[h, :, :], b_KT, writes=[b_KT])
                f.dma(pool, QT_[64:68, cmp_, 0:2048], qb_d[h, :, :], b_QT, writes=[b_QT])
            if dbg == "attn":
                f.op(dve, lambda: V.memset(KT_[:], 0.125), writes=[b_KT])
                f.op(dve, lambda: V.memset(QT_[:], 0.125), writes=[b_QT])
            elif stage >= 2.0:
                v_project(h)
            if stage >= 2.1 and dbg is None:
                qk_project(wk, b_wk, NCOL, KT_, b_KT, pvec[0:64, KG_C:KG_C + 1], True)
                qk_project(wq, b_wq, NLOC, QT_, b_QT, pv2[0:64, 22:23], False)
            if "samp" in parts:
                for tl in range(2):
                    for cmp_ in range(2):
                        f.op(pe, lambda: T.transpose(out=pTo[:, 0:64], in_=QT_[0:64, cmp_, 2048 + tl * 128:2048 + (tl + 1) * 128], identity=ident_b[0:64, 0:64]),
                             reads=[b_QT, b_identb], writes=[b_pTo])
                        f.op(act, lambda: S.copy(out=qs_tm[:, tl, h * 128 + cmp_ * 64:h * 128 + (cmp_ + 1) * 64], in_=pTo[:, 0:64]),
                             reads=[b_pTo], writes=[b_qs])
                        f.op(pe, lambda: T.transpose(out=pTo[:, 0:64], in_=KT_[0:64, cmp_, 4096 + tl * 128:4096 + (tl + 1) * 128], identity=ident_b[0:64, 0:64]),
                             reads=[b_KT, b_identb], writes=[b_pTo])
                        f.op(act, lambda: S.copy(out=ks_tm[:, tl, h * 128 + cmp_ * 64:h * 128 + (cmp_ + 1) * 64], in_=pTo[:, 0:64]),
                             reads=[b_pTo], writes=[b_ks])
                    f.op(pool, lambda: G.tensor_copy(out=vs_tm[:, tl, h * 128:(h + 1) * 128], in_=Vh[:, 32 + tl, 0:128]),
                         reads=[b_Vh], writes=[b_vs])
            if stage < 2.2:
                continue
            def emit_S(g, kt):
                d1 = (kt == 16 + 2 * g + 1)
                q0 = 128 if d1 else 0
                pS, bpS = pS_r.next()
                for cmp_ in range(2):
                    f.op(pe, lambda cmp_=cmp_: T.matmul(out=pS[:, cmp_, q0:256], lhsT=KT_[:, cmp_, kt * 128:(kt + 1) * 128],
                                                        rhs=QT_[:, cmp_, g * 256 + q0:(g + 1) * 256], start=True, stop=True),
                         reads=[b_KT, b_QT], writes=[bpS])
                return (g, kt, pS, bpS)

            def emit_rest(st_):
                g, kt, pS, bpS = st_
                d0 = (kt == 16 + 2 * g)
                d1 = (kt == 16 + 2 * g + 1)
                q0 = 128 if d1 else 0
                PT, bPT = PTr.next()
                f.op(act, lambda: S.activation(out=PT[:, :, q0:256], in_=pS[:, :, q0:256], func=AF.Exp),
                     reads=[bpS], writes=[bPT])
                if d0 or d1:
                    for cmp_ in range(2):
                        f.op(pool, lambda cmp_=cmp_: G.tensor_tensor(out=PT[:, cmp_, q0:q0 + 128], in0=PT[:, cmp_, q0:q0 + 128],
                                                                     in1=tri_b[:], op=ALU.mult),
                             reads=[bPT, b_tri], writes=[bPT])
                for cmp_ in range(2):
                    for sub in range(2):
                        if d1 and sub == 0:
                            continue
                        last = (kt == 16 + 2 * g + sub)
                        f.op(pe, lambda cmp_=cmp_, sub=sub, last=last: T.matmul(
                            out=pO[:, cmp_, sub, 0:129], lhsT=PT[:, cmp_, sub * 128:(sub + 1) * 128], rhs=Vh[:, kt, :],
                            start=(kt == 0), stop=last), reads=[bPT, b_Vh], writes=[b_pO])

            steps = [(g, kt) for g in range(ng) for kt in range(16 + 2 * g + 2)]
            pend = None
            for si in range(len(steps) + 1):
                cur = emit_S(*steps[si]) if si < len(steps) else None
                if pend is not None:
                    emit_rest(pend)
                    g = pend[0]
                    if pend[1] != 16 + 2 * g + 1:
                        pend = cur
                        continue
                else:
                    pend = cur
                    continue
                pend = cur
                for sub in range(2):
                    lcq = g * 256 + sub * 128
                    f.op(dve, lambda: V.reciprocal(out=ost[:, 0:2], in_=pO[:, :, sub, 128]), reads=[b_pO], writes=[b_ost])
                    f.op(dve, lambda: V.tensor_tensor(out=ost[:, 2:3], in0=ost[:, 1:2], in1=NLAM, op=ALU.mult),
                         reads=[b_ost, b_lt], writes=[b_ost])
                    f.op(dve, lambda: V.tensor_scalar(out=osb[:], in0=pO[:, 0, sub, 0:128], scalar1=ost[:, 0:1], scalar2=None,
                                                      op0=ALU.mult), reads=[b_pO, b_ost], writes=[b_osb])
                    f.op(dve, lambda: V.scalar_tensor_tensor(out=osb[:], in0=pO[:, 1, sub, 0:128], scalar=ost[:, 2:3], in1=osb[:],
                                                             op0=ALU.mult, op1=ALU.add), reads=[b_pO, b_ost, b_osb], writes=[b_osb])
                    finalize_o(f, nc, osb, b_osb, osb2, b_osb2, obf, b_obf, ost, b_ost, prow, b_prow, GAO, lam_init,
                               pTo, b_pTo, ident_b, b_identb, oT, b_oT, h, lcq)

        if "samp" in parts:
            selb, b_selb = cload("selb", sel_d[:, :], [128, 512], BF16)
            sel0b, b_sel0b = cload("sel0b", sel0_d[:, :], [128, 256], BF16)
            sbias, b_sbias = cload("sbias", sbias_d[:, :], [128, 520])
            hmask, b_hmask = cload("hmask", hmask_d[:, :], [8, 4])
            e0t, b_e0 = cload("e0t", e0_d[:, :], [8, 4])
            e1t, b_e1 = cload("e1t", e1_d[:, :], [8, 4])
            iop, b_iop = cload("iop", iota_d[:, :], [128, 1], I32)
            pti, b_pti = f.sbuf("pti", [128, 256], I32)
            f.dma(sp, pti[:], ptab_d[0:1, :].partition_broadcast(128), b_pti, writes=[b_pti])
            ptf, b_ptf = f.sbuf("ptf", [128, 256], F32)
            iof, b_iof = f.sbuf("iof", [128, 1], F32)
            idx, b_idx = f.sbuf("idx", [128, 256], I32)
            f.op(dve, lambda: V.tensor_copy(out=ptf[:], in_=pti[:]), reads=[b_pti], writes=[b_ptf])
            f.op(dve, lambda: V.tensor_copy(out=iof[:], in_=iop[:]), reads=[b_iop], writes=[b_iof])
            f.op(dve, lambda: V.tensor_scalar(out=ptf[:], in0=ptf[:], scalar1=128.0, scalar2=iof[:, 0:1], op0=ALU.mult, op1=ALU.add),
                 reads=[b_ptf, b_iof], writes=[b_ptf])
            f.op(dve, lambda: V.tensor_copy(out=idx[:], in_=ptf[:]), reads=[b_ptf], writes=[b_idx])
            cmb, b_cmb = f.sbuf("cmb", [8, 4], F32)
            f.op(dve, lambda: V.scalar_tensor_tensor(out=cmb[:], in0=e1t[:], scalar=lt[0:8, 5:6], in1=e0t[:], op0=ALU.mult, op1=ALU.add),
                 reads=[b_e1, b_e0, b_lt], writes=[b_cmb])
            onesc, b_onesc = f.sbuf("onesc", [128, 1], F32)
            f.op(dve, lambda: V.memset(onesc[:], 1.0), writes=[b_onesc])
            Ktr = Ring([f.sbuf("Kt%d" % i, [128, 512], F32) for i in range(2)])
            Vtr = Ring([f.sbuf("Vt%d" % i, [128, 512], F32) for i in range(2)])
            prodr = Ring([f.sbuf("prod%d" % i, [128, 512], F32) for i in range(1)])
            qbc, b_qbc = f.sbuf("qbc", [128, 512], F32)
            spg_r = Ring([f.sbuf("spg%d" % i, [128, 16], F32) for i in range(3)])
            osm, b_osm = f.sbuf("osm", [8, 512], F32)
            osel, b_osel = f.sbuf("osel", [8, 128], F32)
            ofin, b_ofin = f.sbuf("ofin", [4, 128], F32)
            ofin2, b_ofin2 = f.sbuf("ofin2", [4, 128], F32)
            ofb, b_ofb = f.sbuf("ofb", [4, 128], BF16)
            sst, b_sst = f.sbuf("sst", [8, 8], F32)
            pS0, bpS0 = pS_r.items[0]
            pS0f = pS0[:].rearrange("p a b -> p (a b)")
            pso = pO[0:8, 0, :, :].rearrange("p a b -> p (a b)")
            psz = pO[0:8, 1, 0, 0:1]
            pvr0, bpvr0 = pvr.items[0]
            for s in range(4):
                tl = s // 2
                col = 2048 + 128 * tl + 64 * (s % 2) + 1
                f.op(pe, lambda: T.matmul(out=pS0f, lhsT=selb[:, s * 128:(s + 1) * 128], rhs=qs_tm[:, tl, :], start=True, stop=True),
                     reads=[b_selb, b_qs], writes=[bpS0])
                f.op(act, lambda: S.copy(out=qbc[:], in_=pS0f), reads=[bpS0], writes=[b_qbc])
                for pg in range(65):
                    Kt, bKt = Ktr.next(); Vt, bVt = Vtr.next(); prod, bprod = prodr.next(); spg, bspg = spg_r.next()
                    if pg < 64:
                        ic = s * 64 + pg
                        f.dma(pool, None, None, bKt, reads=[b_idx], writes=[bKt],
                              fn=lambda: G.indirect_dma_start(out=Kt[:, :], out_offset=None, in_=ck_d[:, :],
                                                              in_offset=bass.IndirectOffsetOnAxis(ap=idx[:, ic:ic + 1], axis=0)))
                        f.dma(pool, None, None, bVt, reads=[b_idx], writes=[bVt],
                              fn=lambda: G.indirect_dma_start(out=Vt[:, :], out_offset=None, in_=cv_d[:, :],
                                                              in_offset=bass.IndirectOffsetOnAxis(ap=idx[:, ic:ic + 1], axis=0)))
                    else:
                        f.op(pe, lambda: T.matmul(out=pS0f, lhsT=sel0b[:, (s % 2) * 128:(s % 2 + 1) * 128], rhs=ks_tm[:, tl, :], start=True, stop=True),
                             reads=[b_sel0b, b_ks], writes=[bpS0])
                        f.op(act, lambda: S.copy(out=Kt[:], in_=pS0f), reads=[bpS0], writes=[bKt])
                        f.op(pe, lambda: T.matmul(out=pS0f, lhsT=sel0b[:, (s % 2) * 128:(s % 2 + 1) * 128], rhs=vs_tm[:, tl, :], start=True, stop=True),
                             reads=[b_sel0b, b_vs], writes=[bpS0])
                        f.op(act, lambda: S.copy(out=Vt[:], in_=pS0f), reads=[bpS0], writes=[bVt])
                    if pg % 2 == 0:
                        f.op(pool, lambda: G.tensor_tensor(out=prod[:], in0=Kt[:], in1=qbc[:], op=ALU.mult), reads=[bKt, b_qbc], writes=[bprod])
                    else:
                        f.op(dve, lambda: V.tensor_tensor(out=prod[:], in0=Kt[:], in1=qbc[:], op=ALU.mult), reads=[bKt, b_qbc], writes=[bprod])
                    f.op(dve, lambda: V.reduce_sum(out=spg[:, 0:8], in_=prod[:].rearrange("p (g d) -> p g d", g=8), axis=AX.X),
                         reads=[bprod], writes=[bspg])
                    f.op(dve, lambda: V.tensor_tensor(out=spg[:, 0:8], in0=spg[:, 0:8], in1=sbias[:, pg * 8:(pg + 1) * 8], op=ALU.add),
                         reads=[bspg, b_sbias], writes=[bspg])
                    f.op(act, lambda: S.activation(out=spg[:, 8:16], in_=spg[:, 0:8], func=AF.Exp), reads=[bspg], writes=[bspg])
                    f.op(pe, lambda: T.matmul(out=pso, lhsT=spg[:, 8:16], rhs=Vt[:], start=(pg == 0), stop=(pg == 64)),
                         reads=[bspg, bVt], writes=[b_pO])
                    f.op(pe, lambda: T.matmul(out=psz, lhsT=spg[:, 8:16], rhs=onesc[:], start=(pg == 0), stop=(pg == 64)),
                         reads=[bspg, b_onesc], writes=[b_pO])
                f.op(dve, lambda: V.reciprocal(out=sst[:, 0:1], in_=psz), reads=[b_pO], writes=[b_sst])
                f.op(dve, lambda: V.tensor_scalar(out=osm[:], in0=pso, scalar1=sst[:, 0:1], scalar2=None, op0=ALU.mult),
                     reads=[b_pO, b_sst], writes=[b_osm])
                f.op(dve, lambda: V.tensor_tensor(out=osm[:].rearrange("p (h d) -> p h d", h=4), in0=osm[:].rearrange("p (h d) -> p h d", h=4),
                                                  in1=hmask[:].unsqueeze(2).to_broadcast([8, 4, 128]), op=ALU.mult),
                     reads=[b_osm, b_hmask], writes=[b_osm])
                f.op(dve, lambda: V.reduce_sum(out=osel[:], in_=osm[:].rearrange("p (h d) -> p d h", h=4), axis=AX.X),
                     reads=[b_osm], writes=[b_osel])
                f.op(pe, lambda: T.matmul(out=pvr0[0:4, 0:128], lhsT=cmb[:], rhs=osel[:], start=True, stop=True),
                     reads=[b_cmb, b_osel], writes=[bpvr0])
                f.op(dve, lambda: V.tensor_copy(out=ofin[:], in_=pvr0[0:4, 0:128]), reads=[bpvr0], writes=[b_ofin])
                f.op(dve, lambda: V.memset(sst[0:4, 1:2], 0.0), writes=[b_sst])
                f.op(act, lambda: S.activation(out=ofin2[:], in_=ofin[:], func=AF.Square, accum_out=sst[0:4, 1:2]),
                     reads=[b_ofin, b_sst], writes=[b_ofin2, b_sst])
                f.op(dve, lambda: V.tensor_scalar(out=sst[0:4, 2:3], in0=sst[0:4, 1:2], scalar1=1.0 / 128, scalar2=1e-6, op0=ALU.mult, op1=ALU.add),
                     reads=[b_sst], writes=[b_sst])
                f.op(act, lambda: S.activation(out=sst[0:4, 2:3], in_=sst[0:4, 2:3], func=AF.Sqrt), reads=[b_sst], writes=[b_sst])
                f.op(dve, lambda: V.reciprocal(out=sst[0:4, 2:3], in_=sst[0:4, 2:3]), reads=[b_sst], writes=[b_sst])
                f.op(dve, lambda: V.tensor_scalar(out=sst[0:4, 3:4], in0=sst[0:4, 2:3], scalar1=(1.0 - lam_init), scalar2=None, op0=ALU.mult),
                     reads=[b_sst], writes=[b_sst])
                f.op(dve, lambda: V.scalar_tensor_tensor(out=ofb[:], in0=ofin[:], scalar=sst[0:4, 3:4], in1=prow[0:4, GAO:GAO + 128],
                                                         op0=ALU.mult, op1=ALU.mult), reads=[b_ofin, b_sst, b_prow], writes=[b_ofb])
                f.op(pe, lambda: T.transpose(out=pTo[:, 0:4], in_=ofb[:], identity=ident_b[0:4, 0:4]), reads=[b_ofb, b_identb], writes=[b_pTo])
                f.op(act, lambda: S.copy(out=oT[:, :, col], in_=pTo[:, 0:4]), reads=[b_pTo], writes=[b_oT])
        f.release(m2)
        f.barrier_all()

    if "epi" in parts:
        epilogue()
    f.release(m_pre)
    f.finish()
    f.close()
    return nc


def finalize_o(f, nc, osb, b_osb, osb2, b_osb2, obf, b_obf, ost, b_ost, prow, b_prow, GAO, lam_init,
               pTo, b_pTo, ident_b, b_identb, oT, b_oT, h, lcq):
    V, S, T = nc.vector, nc.scalar, nc.tensor
    dve, act, pe = f.dve, f.act, f.pe
    f.op(dve, lambda: V.memset(ost[:, 4:5], 0.0), writes=[b_ost])
    f.op(act, lambda: S.activation(out=osb2[:], in_=osb[:], func=AF.Square, accum_out=ost[:, 4:5]),
         reads=[b_osb, b_ost], writes=[b_osb2, b_ost])
    f.op(dve, lambda: V.tensor_scalar(out=ost[:, 5:6], in0=ost[:, 4:5], scalar1=1.0 / 128, scalar2=1e-6,
                                      op0=ALU.mult, op1=ALU.add), reads=[b_ost], writes=[b_ost])
    f.op(act, lambda: S.activation(out=ost[:, 5:6], in_=ost[:, 5:6], func=AF.Sqrt), reads=[b_ost], writes=[b_ost])
    f.op(dve, lambda: V.reciprocal(out=ost[:, 5:6], in_=ost[:, 5:6]), reads=[b_ost], writes=[b_ost])
    f.op(dve, lambda: V.tensor_scalar(out=ost[:, 6:7], in0=ost[:, 5:6], scalar1=(1.0 - lam_init), scalar2=None,
                                      op0=ALU.mult), reads=[b_ost], writes=[b_ost])
    f.op(dve, lambda: V.scalar_tensor_tensor(out=obf[:], in0=osb[:], scalar=ost[:, 6:7], in1=prow[:, GAO:GAO + 128],
                                             op0=ALU.mult, op1=ALU.mult), reads=[b_osb, b_ost, b_prow], writes=[b_obf])
    f.op(pe, lambda: T.transpose(out=pTo[:, 0:128], in_=obf[:], identity=ident_b[:]), reads=[b_obf, b_identb], writes=[b_pTo])
    f.op(act, lambda: S.copy(out=oT[:, h, lcq:lcq + 128], in_=pTo[:, 0:128]), reads=[b_pTo], writes=[b_oT])


def sample_attention(f, nc, L):
    pass


def _consts(half):
    c = {}
    c["ident"] = np.eye(128, dtype=np.float32)
    s_idx = np.arange(128)[:, None]; t_idx = np.arange(128)[None, :]
    same = (s_idx // 64) == (t_idx // 64)
    mu = ((s_idx < t_idx) & same).astype(np.float32)
    mui = ((s_idx <= t_idx) & same).astype(np.float32)
    ml = mu.T.copy()
    c["mu4"] = np.tile(mu, (1, 4)); c["ml4"] = np.tile(ml, (1, 4)); c["mui4"] = np.tile(mui, (1, 4))
    c["tri"] = (s_idx <= t_idx).astype(np.float32)
    cmk = np.zeros((128, 256), np.float32)
    cmk[:, [1, 65, 129, 193]] = 1.0
    c["colmask"] = cmk
    rm = np.ones((128, 512), np.float32); rm[:, ::64] = 0.0
    c["resetm"] = rm
    col = np.arange(512)[None, :]
    c["cm0"] = np.broadcast_to(((col % 128) < 64).astype(np.float32), (128, 512)).copy()
    c["cm1"] = np.broadcast_to(((col % 128) >= 64).astype(np.float32), (128, 512)).copy()
    c["bones"] = same.astype(np.float32)
    sel = np.zeros((128, 4, 128), np.float32)
    sel0 = np.zeros((128, 2, 128), np.float32)
    for s in range(4):
        sel[1 + 64 * (s % 2), s, :] = 1.0
    for r in range(2):
        sel0[1 + 64 * r, r, 0] = 1.0
    c["sel"] = sel.reshape(128, 512); c["sel0"] = sel0.reshape(128, 256)
    c["iotap"] = np.arange(128, dtype=np.int32).reshape(128, 1)
    hm = np.zeros((8, 4), np.float32); e0 = np.zeros((8, 4), np.float32); e1 = np.zeros((8, 4), np.float32)
    for h in range(4):
        for cc in range(2):
            hm[h * 2 + cc, h] = 1.0
        e0[h * 2, h] = 1.0; e1[h * 2 + 1, h] = 1.0
    c["hmask"] = hm; c["e0"] = e0; c["e1"] = e1
    slopes = np.array([2.0 ** (-8.0 * (h + 1) / 4) for h in range(4)], np.float64)
    kcol = np.arange(4096)
    kb = np.zeros((4, 4, 4096), np.float32)
    qb = np.zeros((4, 4, 2048), np.float32)
    qpos = 2048 + np.arange(2048)
    for h in range(4):
        kb[h, 0] = slopes[h] * 128 * (kcol // 128)
        if half == 0:
            kb[h, 0, :2048] = NEG
        kb[h, 1] = slopes[h] * (kcol % 128)
        kb[h, 2] = 1.0; kb[h, 3] = 1.0
        qb[h, 0] = 1.0; qb[h, 1] = 1.0
        qb[h, 2] = -slopes[h] * 128 * (qpos // 128)
        qb[h, 3] = -slopes[h] * (qpos % 128)
    c["kb"] = kb; c["qb"] = qb
    sb = np.zeros((128, 65, 8), np.float32)
    slot = np.arange(128)[:, None]
    for pg in range(64):
        dist = 8192 - (128 * pg + slot)
        for h in range(4):
            sb[:, pg, 2 * h] = (-slopes[h] * dist)[:, 0]; sb[:, pg, 2 * h + 1] = (-slopes[h] * dist)[:, 0]
    sb[1:, 64, :] = NEG
    c["sbias"] = sb.reshape(128, 65 * 8)
    return c


_NC_CACHE = {}
_LAST = None


def kernel(**inp):
    f32 = np.float32
    xp = np.asarray(inp["x_prompt"], f32); xs = np.asarray(inp["x_sample"], f32)
    g = lambda k: np.ascontiguousarray(np.asarray(inp[k], f32)[0])
    w_in = g("w_in")
    shared = {
        "w_in": w_in, "w_pa": g("w_pa"), "w_pb": g("w_pb"), "w_out": g("w_out"),
        "w_gate": g("w_gate"), "w_up": g("w_up"), "w_down": g("w_down"),
        "wa2": np.ascontiguousarray(np.concatenate([g("w2"), g("a2")], 0)), "g2": g("g2"),
        "ck": np.ascontiguousarray(np.asarray(inp["cache_k"], f32).reshape(2560 * 128, 512)),
        "cv": np.ascontiguousarray(np.asarray(inp["cache_v"], f32).reshape(2560 * 128, 512)),
    }
    pvec = np.zeros((128, 36), f32)
    pvec[:, 0:14] = g("shift_mu").reshape(14, 128).T
    pvec[:, 14:18] = g("w0").reshape(4, 128).T
    pvec[:, 18:22] = g("a0").reshape(4, 128).T
    pvec[:, 22:26] = g("k_k").reshape(4, 128).T
    pvec[:, 26:30] = g("k_a").reshape(4, 128).T
    pvec[:, 30:34] = g("r_k").reshape(4, 128).T
    pvec[:, 34] = np.tile(g("q_gain"), 2); pvec[:, 35] = np.tile(g("k_gain"), 2)
    prow = np.concatenate([g("norm_mix"), g("norm_ffn"), g("attn_out_gain"), g("gn_w"), g("gn_b"),
                           g("lambda_q1"), g("lambda_k1"), g("lambda_q2"), g("lambda_k2")]).reshape(1, 3456).astype(f32)
    shared["pvec"] = pvec; shared["prow"] = prow
    consts = [_consts(0), _consts(1)]
    ptab = np.asarray(inp["page_table"], np.int32)
    swkv = np.asarray(inp["state_wkv"], f32)[0]
    sshift = np.asarray(inp["state_shift"], f32)[0]
    in_maps = []
    for c in range(8):
        b, half = c // 2, c % 2
        xin = np.zeros((NCOL, D), f32)
        if half == 1:
            xin[0:2048] = xp[b, 0:2048]
        xin[2048:4096] = xp[b, half * 2048:(half + 1) * 2048]
        for s in range(4):
            xin[4096 + 128 * (s // 2) + 64 * (s % 2) + 1] = xs[4 * c + s, 0]
        m = dict(shared)
        m.update(consts[half])
        m["xin"] = xin
        m["ptab"] = np.ascontiguousarray(ptab[4 * c:4 * c + 4].reshape(1, 256))
        sw = swkv[4 * c:4 * c + 4].transpose(0, 1, 3, 2).reshape(4, 4, 128, 64)
        m["swkv"] = np.ascontiguousarray(sw)
        m["sshift"] = np.ascontiguousarray(sshift[4 * c:4 * c + 4].reshape(4, 14, 128).transpose(2, 1, 0))
        in_maps.append(m)
    if "nc" not in _NC_CACHE:
        _NC_CACHE["nc"] = build()
        _NC_CACHE["small"] = False
    if _NC_CACHE.get("small"):
        for m in in_maps:
            m["ck"] = m["ck"][:128]; m["cv"] = m["cv"][:128]
    nc = _NC_CACHE["nc"]
    res = run_bass_kernel_spmd(nc, in_maps, core_ids=list(range(8)))
    R = res.results
    global _LAST
    _LAST = R
    y_p = np.zeros((4, 4096, 1024), f32); y_s = np.zeros((32, 1, 1024), f32)
    k_p = np.zeros((1, 4, 4096, 4, 128), f32); v_p = np.zeros((1, 4, 4096, 4, 128), f32)
    wkv_p = np.zeros((1, 4, 8, 64, 64), f32); sh_p = np.zeros((1, 4, 1792), f32)
    k_s = np.zeros((1, 32, 1, 4, 128), f32); v_s = np.zeros((1, 32, 1, 4, 128), f32)
    wkv_s = np.zeros((1, 32, 8, 64, 64), f32); sh_s = np.zeros((1, 32, 1792), f32)
    for c in range(8):
        b, half = c // 2, c % 2
        r = R[c]
        sl = slice(half * 2048, (half + 1) * 2048)
        y_p[b, sl] = r["y_out"][0:2048]
        kT = r["k_out"]
        k_p[0, b, sl] = kT[:, 0:2048].T.reshape(2048, 4, 128)
        v_p[0, b, sl] = r["v_out"][0:2048].reshape(2048, 4, 128)
        wk = r["wkv_out"].reshape(5, 8, 64, 64)
        po = r["p_out"]
        if half == 1:
            wkv_p[0, b] = wk[0].transpose(0, 2, 1)
            sh_p[0, b] = po[:, :, 127].reshape(1792)
        for s in range(4):
            col = 128 * (s // 2) + 64 * (s % 2) + 1
            y_s[4 * c + s, 0] = r["y_out"][2048 + col]
            k_s[0, 4 * c + s, 0] = kT[:, 2048 + col].reshape(4, 128)
            v_s[0, 4 * c + s, 0] = r["v_out"][2048 + col].reshape(4, 128)
            wkv_s[0, 4 * c + s] = wk[1 + s].transpose(0, 2, 1)
            sh_s[0, 4 * c + s] = po[:, :, 128 + col].reshape(1792)
    return (y_p, y_s, k_p, v_p, wkv_p, sh_p, k_s, v_s, wkv_s, sh_s)
```
